# Optimizing a Trainium2 kernel written in Bass

```python
import math
import jax
import jax.numpy as jnp
from jax import lax
import numpy as np

D_MODEL = 1024
BATCH = 8
SEQ = 2048
DEPTH = 4
DEC_BATCH = 32
DEC_SEQ = 2048
PAST_LEN = 128

N_MIXERS = 4
D_FF = 2816
RMS_EPS = 1e-6
CONV_K = 5

SSD_D_INNER = 2 * D_MODEL
SSD_HEAD_DIM = 64
SSD_N_HEADS = SSD_D_INNER // SSD_HEAD_DIM
SSD_N_GROUPS = 8
SSD_D_STATE = 128
SSD_CHUNK = 128
SSD_CONV_DIM = SSD_D_INNER + 2 * SSD_N_GROUPS * SSD_D_STATE
SSD_IN_DIM = SSD_D_INNER + SSD_CONV_DIM + 2 * SSD_N_HEADS

GDN_N_K_HEADS = 8
GDN_N_V_HEADS = 16
GDN_HEAD_K = 128
GDN_HEAD_V = 128
GDN_CHUNK = 64
GDN_QK_DIM = GDN_N_K_HEADS * GDN_HEAD_K
GDN_V_DIM = GDN_N_V_HEADS * GDN_HEAD_V
GDN_CONV_DIM = 2 * GDN_QK_DIM + GDN_V_DIM
GDN_IN_DIM = GDN_CONV_DIM + GDN_V_DIM + 3 * GDN_N_V_HEADS

ATT_N_HEADS = 16
ATT_N_KV_HEADS = 4
ATT_HEAD_DIM = D_MODEL // ATT_N_HEADS
ATT_WINDOW = 128
ATT_QKV_DIM = (ATT_N_HEADS + 2 * ATT_N_KV_HEADS) * ATT_HEAD_DIM
ROPE_THETA = 10000.0

RWKV_HEAD = 64
RWKV_N_HEADS = D_MODEL // RWKV_HEAD
RWKV_DECAY_LORA = 64
RWKV_AAA_LORA = 64
RWKV_GATE_LORA = 128
RWKV_LN_EPS = 64e-5

kernel_name = 'hybrid_bidir_ssd_gdn_swa_rwkv7_macaron'


def rmsnorm(x, g, eps=RMS_EPS):
    xf = x.astype(jnp.float32)
    y = xf * lax.rsqrt(jnp.mean(xf * xf, axis=-1, keepdims=True) + eps)
    return (y * g.astype(jnp.float32)).astype(x.dtype)


def l2norm(x, eps=1e-6):
    xf = x.astype(jnp.float32)
    return (xf * lax.rsqrt(jnp.sum(xf * xf, axis=-1, keepdims=True) + eps)).astype(x.dtype)


def swiglu(x, w_gu, w_down):
    gate, up = jnp.split(x @ w_gu, 2, axis=-1)
    return (jax.nn.silu(gate) * up) @ w_down


def flip_seq(t):
    return jnp.flip(t, axis=1)


def centred_dwconv(x, w, b):
    pad = CONV_K // 2
    y = lax.conv_general_dilated(x, w[:, None, :], window_strides=(1,), padding=[(pad, pad)],
                                 dimension_numbers=('NWC', 'WIO', 'NWC'),
                                 feature_group_count=x.shape[-1])
    return y + b


def rope(x, pos):
    half = x.shape[-1] // 2
    inv_freq = ROPE_THETA ** (-jnp.arange(half, dtype=jnp.float32) / half)
    ang = pos.astype(jnp.float32)[:, None] * inv_freq[None, :]
    cos = jnp.cos(ang)[None, :, None, :]
    sin = jnp.sin(ang)[None, :, None, :]
    xf = x.astype(jnp.float32)
    x1, x2 = xf[..., :half], xf[..., half:]
    return jnp.concatenate([x1 * cos - x2 * sin, x2 * cos + x1 * sin], axis=-1).astype(x.dtype)


def ssd_scan(x, dt, a, bm, cm):
    b, l, g, r, p = x.shape
    n = bm.shape[-1]
    q = SSD_CHUNK
    c = l // q
    x = x.reshape(b, c, q, g, r, p)
    dt = dt.reshape(b, c, q, g, r)
    bm = bm.reshape(b, c, q, g, n)
    cm = cm.reshape(b, c, q, g, n)
    a_cs = jnp.cumsum(dt * a, axis=2)
    xdt = x * dt[..., None].astype(x.dtype)
    causal = jnp.tril(jnp.ones((q, q), dtype=bool))
    seg = a_cs[:, :, :, None] - a_cs[:, :, None, :]
    decay = jnp.exp(jnp.where(causal[:, :, None, None], seg, -jnp.inf)).astype(x.dtype)
    cb = jnp.einsum('bclgn,bcsgn->bclsg', cm, bm)
    y_diag = jnp.einsum('bclsgr,bcsgrp->bclgrp', cb[..., None] * decay, xdt)
    decay_to_end = jnp.exp(a_cs[:, :, -1:] - a_cs).astype(x.dtype)
    chunk_states = jnp.einsum('bclgn,bclgr,bclgrp->bcgrpn', bm, decay_to_end, xdt)
    chunk_decay = jnp.exp(a_cs[:, :, -1]).astype(x.dtype)

    def step(h, inp):
        s_c, d_c = inp
        return h * d_c[..., None, None] + s_c, h

    h0 = jnp.zeros((b, g, r, p, n), x.dtype)
    _, h_prev = lax.scan(step, h0, (jnp.moveaxis(chunk_states, 1, 0), jnp.moveaxis(chunk_decay, 1, 0)))
    h_prev = jnp.moveaxis(h_prev, 0, 1)
    y_off = jnp.einsum('bclgn,bcgrpn,bclgr->bclgrp', cm, h_prev, jnp.exp(a_cs).astype(x.dtype))
    return (y_diag + y_off).reshape(b, l, g, r, p)


def ssd_mixer(u, w_in, conv_w, conv_b, a_log, dt_bias, d_skip, norm_w, w_out):
    b, l, _ = u.shape
    g, r = SSD_N_GROUPS, SSD_N_HEADS // SSD_N_GROUPS
    gn = SSD_N_GROUPS * SSD_D_STATE
    z, xbc, dt_raw = jnp.split(u @ w_in, [SSD_D_INNER, SSD_D_INNER + SSD_CONV_DIM], axis=-1)
    xbc = jax.nn.silu(centred_dwconv(xbc, conv_w, conv_b))
    xs, bm, cm = jnp.split(xbc, [SSD_D_INNER, SSD_D_INNER + gn], axis=-1)
    xs = xs.reshape(b, l, g, r, SSD_HEAD_DIM)
    bm = bm.reshape(b, l, g, SSD_D_STATE)
    cm = cm.reshape(b, l, g, SSD_D_STATE)
    dt = jax.nn.softplus(dt_raw.astype(jnp.float32).reshape(b, l, 2, g, r)
                         + dt_bias.astype(jnp.float32).reshape(2, g, r))
    a = -jnp.exp(a_log.astype(jnp.float32)).reshape(2, g, r)
    y_fwd = ssd_scan(xs, dt[:, :, 0], a[0], bm, cm)
    y_bwd = flip_seq(ssd_scan(flip_seq(xs), flip_seq(dt[:, :, 1]), a[1], flip_seq(bm), flip_seq(cm)))
    y = (y_fwd + y_bwd + xs * d_skip.reshape(g, r, 1)).reshape(b, l, SSD_D_INNER)
    return rmsnorm(y * jax.nn.silu(z), norm_w) @ w_out


def gdn_chunk_scan(q, k, v, g, beta):
    b, l, h, dk = k.shape
    dv = v.shape[-1]
    cs = GDN_CHUNK
    n = l // cs
    dtype = v.dtype

    def to_chunks(t):
        return jnp.moveaxis(t.reshape(b, n, cs, h, *t.shape[3:]), 3, 2)

    q = to_chunks(q * (dk ** -0.5))
    k = to_chunks(k)
    v = to_chunks(v)
    beta = to_chunks(beta)
    g = jnp.cumsum(to_chunks(g), axis=-1)
    tril = jnp.tril(jnp.ones((cs, cs), dtype=bool))
    strict = jnp.tril(jnp.ones((cs, cs), dtype=bool), -1)
    decay = jnp.exp(jnp.where(tril, g[..., :, None] - g[..., None, :], -jnp.inf)).astype(dtype)
    kk = jnp.einsum('bnhid,bnhjd->bnhij', k, k)
    lower = jnp.where(strict, beta[..., :, None] * kk * decay, 0.0).astype(jnp.float32)
    t_mat = jnp.eye(cs, dtype=jnp.float32) + lower
    rhs = jnp.concatenate([v * beta[..., None],
                           k * (beta * jnp.exp(g).astype(dtype))[..., None]], axis=-1).astype(jnp.float32)
    sol = lax.linalg.triangular_solve(t_mat, rhs, left_side=True, lower=True).astype(dtype)
    u_vals, w_keys = sol[..., :dv], sol[..., dv:]
    qk = jnp.where(tril, jnp.einsum('bnhid,bnhjd->bnhij', q, k) * decay, 0.0).astype(dtype)
    q_dec = q * jnp.exp(g)[..., None].astype(dtype)
    k_dec = k * jnp.exp(g[..., -1:] - g)[..., None].astype(dtype)
    g_end = jnp.exp(g[..., -1]).astype(dtype)

    def step(s, inp):
        qk_c, qd_c, kd_c, u_c, w_c, ge_c = inp
        v_new = u_c - jnp.einsum('bhcd,bhde->bhce', w_c, s)
        o = jnp.einsum('bhcd,bhde->bhce', qd_c, s) + jnp.einsum('bhij,bhje->bhie', qk_c, v_new)
        s = s * ge_c[..., None, None] + jnp.einsum('bhcd,bhce->bhde', kd_c, v_new)
        return s, o

    s0 = jnp.zeros((b, h, dk, dv), dtype)
    xs = tuple(jnp.moveaxis(t, 1, 0) for t in (qk, q_dec, k_dec, u_vals, w_keys, g_end))
    _, o = lax.scan(step, s0, xs)
    return jnp.transpose(o, (1, 0, 3, 2, 4)).reshape(b, l, h, dv)


def gdn_mixer(u, w_in, conv_w, conv_b, a_log, dt_bias, norm_w, w_out):
    b, l, _ = u.shape
    hk, hv = GDN_N_K_HEADS, GDN_N_V_HEADS
    qkv, z, beta_raw, a_raw = jnp.split(
        u @ w_in, [GDN_CONV_DIM, GDN_CONV_DIM + GDN_V_DIM, GDN_CONV_DIM + GDN_V_DIM + hv], axis=-1)
    qkv = jax.nn.silu(centred_dwconv(qkv, conv_w, conv_b))
    q, k, v = jnp.split(qkv, [GDN_QK_DIM, 2 * GDN_QK_DIM], axis=-1)
    rep = hv // hk
    q = jnp.repeat(l2norm(q.reshape(b, l, hk, GDN_HEAD_K)), rep, axis=2)
    k = jnp.repeat(l2norm(k.reshape(b, l, hk, GDN_HEAD_K)), rep, axis=2)
    v = v.reshape(b, l, hv, GDN_HEAD_V)
    beta = jax.nn.sigmoid(beta_raw)
    gdec = -jnp.exp(a_log.astype(jnp.float32)) * jax.nn.softplus(
        a_raw.astype(jnp.float32).reshape(b, l, 2, hv) + dt_bias.astype(jnp.float32))
    o_fwd = gdn_chunk_scan(q, k, v, gdec[:, :, 0], beta)
    o_bwd = flip_seq(gdn_chunk_scan(flip_seq(q), flip_seq(k), flip_seq(v),
                                    flip_seq(gdec[:, :, 1]), flip_seq(beta)))
    o = rmsnorm(o_fwd + o_bwd, norm_w) * jax.nn.silu(z.reshape(b, l, hv, GDN_HEAD_V))
    return o.reshape(b, l, GDN_V_DIM) @ w_out


def window_attention(u, w_qkv, sinks, w_out):
    b, l, _ = u.shape
    h, kvh, dh, win = ATT_N_HEADS, ATT_N_KV_HEADS, ATT_HEAD_DIM, ATT_WINDOW
    rep = h // kvh
    nb = l // win
    q, k, v = jnp.split(u @ w_qkv, [h * dh, (h + kvh) * dh], axis=-1)
    pos = jnp.arange(l)
    q = rope(q.reshape(b, l, h, dh), pos).reshape(b, nb, win, kvh, rep, dh)
    k = rope(k.reshape(b, l, kvh, dh), pos)
    v = v.reshape(b, l, kvh, dh)

    def band(t):
        tp = jnp.pad(t, ((0, 0), (win, win), (0, 0), (0, 0)))
        return jnp.concatenate([tp[:, o * win:o * win + l].reshape(b, nb, win, kvh, dh) for o in range(3)],
                               axis=2)

    kb, vb = band(k), band(v)
    qpos = jnp.arange(nb)[:, None] * win + jnp.arange(win)[None, :]
    kpos = (jnp.arange(nb)[:, None] - 1) * win + jnp.arange(3 * win)[None, :]
    valid = ((jnp.abs(kpos[:, None, :] - qpos[:, :, None]) <= win)
             & (kpos[:, None, :] >= 0) & (kpos[:, None, :] < l))
    s = jnp.einsum('bnqgrd,bnkgd->bngrqk', q, kb).astype(jnp.float32) * (dh ** -0.5)
    s = jnp.where(valid[None, :, None, None], s, -jnp.inf)
    sink = sinks.astype(jnp.float32).reshape(kvh, rep)[None, None, :, :, None, None]
    m = jnp.maximum(jnp.max(s, axis=-1, keepdims=True), sink)
    e = jnp.exp(s - m)
    probs = (e / (jnp.sum(e, axis=-1, keepdims=True) + jnp.exp(sink - m))).astype(v.dtype)
    o = jnp.einsum('bngrqk,bnkgd->bnqgrd', probs, vb).reshape(b, l, h * dh)
    return o @ w_out


def rwkv7_scan(r, w, k, v, kk, a, reverse):
    b, l, nh, n = r.shape

    def step(s, inp):
        r_t, w_t, k_t, v_t, kk_t, a_t = inp
        sa = jnp.einsum('bhvk,bhk->bhv', s, kk_t)
        s = (s * w_t[:, :, None, :] - sa[..., None] * (kk_t * a_t)[:, :, None, :]
             + v_t[..., None] * k_t[:, :, None, :])
        return s, jnp.einsum('bhvk,bhk->bhv', s, r_t)

    xs = tuple(jnp.moveaxis(t, 1, 0) for t in (r, w, k, v, kk, a))
    s0 = jnp.zeros((b, nh, n, n), r.dtype)
    _, o = lax.scan(step, s0, xs, reverse=reverse)
    return jnp.moveaxis(o, 0, 1)


def rwkv7_mixer(u, x_mu, w_rkv, w0, w1, w2, a0, a1, a2, g1, g2, k_k, k_a, r_k, lnx_w, lnx_b, w_out):
    b, l, d = u.shape
    nh, n = RWKV_N_HEADS, RWKV_HEAD
    up = jnp.pad(u, ((0, 0), (1, 1), (0, 0)))
    xx = 0.5 * (up[:, :-2] + up[:, 2:]) - u
    xm = u[None] + xx[None] * x_mu[:, None, None, :]
    r, k, v = jnp.einsum('sbld,sde->sble', xm[:3], w_rkv)
    xw, xa, xg = xm[3], xm[4], xm[5]
    wl = jnp.tanh(jnp.einsum('bld,sdr->sblr', xw, w1))
    wlog = -jax.nn.softplus(-(w0[:, None, None, :] + jnp.einsum('sblr,srd->sbld', wl, w2)).astype(jnp.float32)) - 0.5
    decay = jnp.exp(-jnp.exp(wlog)).astype(u.dtype)
    a = jax.nn.sigmoid(a0 + (xa @ a1) @ a2)
    gate = jax.nn.sigmoid(xg @ g1) @ g2

    def heads(t):
        return t.reshape(*t.shape[:-1], nh, n)

    kk = l2norm(heads(k * k_k))
    k = k * (1.0 + (a - 1.0) * k_a)
    r_h, k_h, v_h, a_h, dec_h = heads(r), heads(k), heads(v), heads(a), heads(decay)
    o = (rwkv7_scan(r_h, dec_h[0], k_h, v_h, kk, a_h, reverse=False)
         + rwkv7_scan(r_h, dec_h[1], k_h, v_h, kk, a_h, reverse=True))
    of = o.astype(jnp.float32)
    mu = jnp.mean(of, axis=-1, keepdims=True)
    var = jnp.mean(jnp.square(of - mu), axis=-1, keepdims=True)
    o = ((of - mu) * lax.rsqrt(var + RWKV_LN_EPS)).astype(u.dtype).reshape(b, l, d) * lnx_w + lnx_b
    bonus = jnp.sum(r_h * k_h * heads(r_k), axis=-1, keepdims=True) * v_h
    return ((o + bonus.reshape(b, l, d)) * gate) @ w_out


def _n_layers_of(m):
    return (DEPTH - m + N_MIXERS - 1) // N_MIXERS


def setup_inputs(seed: int = 0) -> dict:
    key = jax.random.key(seed)
    keys = iter(jax.random.split(key, 64))
    f32 = jnp.float32

    def normal(shape, scale):
        return jax.random.normal(next(keys), shape, f32) * scale

    def gain(shape):
        return 1.0 + normal(shape, 0.02)

    def uniform(shape, lo, hi):
        return jax.random.uniform(next(keys), shape, f32, lo, hi)

    def dt_bias_init(shape):
        dt = jnp.exp(uniform(shape, math.log(1e-3), math.log(1e-1)))
        return dt + jnp.log(-jnp.expm1(-dt))

    na, nb, nc, nd = (_n_layers_of(m) for m in range(N_MIXERS))
    d, f = D_MODEL, D_FF
    return {
        'x_prompt': normal((BATCH, SEQ, d), 1.0),
        'x_sample': normal((DEC_BATCH, DEC_SEQ, d), 1.0),
        'ffn1_norm': gain((DEPTH, d)),
        'ffn1_w_gu': normal((DEPTH, d, 2 * f), d ** -0.5),
        'ffn1_w_down': normal((DEPTH, f, d), f ** -0.5),
        'mix_norm': gain((DEPTH, d)),
        'ffn2_norm': gain((DEPTH, d)),
        'ffn2_w_gu': normal((DEPTH, d, 2 * f), d ** -0.5),
        'ffn2_w_down': normal((DEPTH, f, d), f ** -0.5),
        'ssd_w_in': normal((na, d, SSD_IN_DIM), d ** -0.5),
        'ssd_conv_w': normal((na, CONV_K, SSD_CONV_DIM), CONV_K ** -0.5),
        'ssd_conv_b': normal((na, SSD_CONV_DIM), 0.02),
        'ssd_a_log': jnp.log(uniform((na, 2, SSD_N_HEADS), 1.0, 16.0)),
        'ssd_dt_bias': dt_bias_init((na, 2, SSD_N_HEADS)),
        'ssd_d': gain((na, SSD_N_HEADS)),
        'ssd_norm': gain((na, SSD_D_INNER)),
        'ssd_w_out': normal((na, SSD_D_INNER, d), SSD_D_INNER ** -0.5),
        'gdn_w_in': normal((nb, d, GDN_IN_DIM), d ** -0.5),
        'gdn_conv_w': normal((nb, CONV_K, GDN_CONV_DIM), CONV_K ** -0.5),
        'gdn_conv_b': normal((nb, GDN_CONV_DIM), 0.02),
        'gdn_a_log': jnp.log(uniform((nb, 2, GDN_N_V_HEADS), 1.0, 16.0)),
        'gdn_dt_bias': dt_bias_init((nb, 2, GDN_N_V_HEADS)),
        'gdn_norm': gain((nb, GDN_HEAD_V)),
        'gdn_w_out': normal((nb, GDN_V_DIM, d), GDN_V_DIM ** -0.5),
        'att_w_qkv': normal((nc, d, ATT_QKV_DIM), d ** -0.5),
        'att_sinks': normal((nc, ATT_N_HEADS), 0.5),
        'att_w_out': normal((nc, ATT_N_HEADS * ATT_HEAD_DIM, d), (ATT_N_HEADS * ATT_HEAD_DIM) ** -0.5),
        'rwkv_x_mu': uniform((nd, 6, d), 0.0, 1.0),
        'rwkv_w_rkv': normal((nd, 3, d, d), d ** -0.5),
        'rwkv_w0': uniform((nd, 2, d), -6.0, 1.0),
        'rwkv_w1': normal((nd, 2, d, RWKV_DECAY_LORA), d ** -0.5),
        'rwkv_w2': normal((nd, 2, RWKV_DECAY_LORA, d), 0.1 * RWKV_DECAY_LORA ** -0.5),
        'rwkv_a0': normal((nd, d), 0.1),
        'rwkv_a1': normal((nd, d, RWKV_AAA_LORA), d ** -0.5),
        'rwkv_a2': normal((nd, RWKV_AAA_LORA, d), 0.1 * RWKV_AAA_LORA ** -0.5),
        'rwkv_g1': normal((nd, d, RWKV_GATE_LORA), d ** -0.5),
        'rwkv_g2': normal((nd, RWKV_GATE_LORA, d), RWKV_GATE_LORA ** -0.5),
        'rwkv_k_k': 0.85 + normal((nd, d), 0.02),
        'rwkv_k_a': gain((nd, d)),
        'rwkv_r_k': normal((nd, d), 0.1),
        'rwkv_lnx_w': gain((nd, d)),
        'rwkv_lnx_b': normal((nd, d), 0.02),
        'rwkv_w_out': normal((nd, d, d), d ** -0.5),
        'final_norm': gain((d,)),
    }


def reference(x_prompt, x_sample,
              ffn1_norm, ffn1_w_gu, ffn1_w_down, mix_norm, ffn2_norm, ffn2_w_gu, ffn2_w_down,
              ssd_w_in, ssd_conv_w, ssd_conv_b, ssd_a_log, ssd_dt_bias, ssd_d, ssd_norm, ssd_w_out,
              gdn_w_in, gdn_conv_w, gdn_conv_b, gdn_a_log, gdn_dt_bias, gdn_norm, gdn_w_out,
              att_w_qkv, att_sinks, att_w_out,
              rwkv_x_mu, rwkv_w_rkv, rwkv_w0, rwkv_w1, rwkv_w2, rwkv_a0, rwkv_a1, rwkv_a2,
              rwkv_g1, rwkv_g2, rwkv_k_k, rwkv_k_a, rwkv_r_k, rwkv_lnx_w, rwkv_lnx_b, rwkv_w_out,
              final_norm):
    def token_mixer(h, i):
        m, j = i % N_MIXERS, i // N_MIXERS
        if m == 0:
            return ssd_mixer(h, ssd_w_in[j], ssd_conv_w[j], ssd_conv_b[j], ssd_a_log[j], ssd_dt_bias[j],
                             ssd_d[j], ssd_norm[j], ssd_w_out[j])
        if m == 1:
            return gdn_mixer(h, gdn_w_in[j], gdn_conv_w[j], gdn_conv_b[j], gdn_a_log[j], gdn_dt_bias[j],
                             gdn_norm[j], gdn_w_out[j])
        if m == 2:
            return window_attention(h, att_w_qkv[j], att_sinks[j], att_w_out[j])
        return rwkv7_mixer(h, rwkv_x_mu[j], rwkv_w_rkv[j], rwkv_w0[j], rwkv_w1[j], rwkv_w2[j],
                           rwkv_a0[j], rwkv_a1[j], rwkv_a2[j], rwkv_g1[j], rwkv_g2[j], rwkv_k_k[j],
                           rwkv_k_a[j], rwkv_r_k[j], rwkv_lnx_w[j], rwkv_lnx_b[j], rwkv_w_out[j])

    def trunk(x):
        for i in range(DEPTH):
            x = x + 0.5 * swiglu(rmsnorm(x, ffn1_norm[i]), ffn1_w_gu[i], ffn1_w_down[i])
            x = x + token_mixer(rmsnorm(x, mix_norm[i]), i)
            x = x + 0.5 * swiglu(rmsnorm(x, ffn2_norm[i]), ffn2_w_gu[i], ffn2_w_down[i])
        return rmsnorm(x, final_norm)

    y_prompt = trunk(x_prompt)
    y_sample = trunk(x_sample)
    return (y_prompt, y_sample)
```

```python
import contextlib
import numpy as np
import concourse.bass as bass
import concourse.mybir as mybir
from concourse.alu_op_type import AluOpType as ALU
from concourse.bass_utils import run_bass_kernel_spmd

F32 = mybir.dt.float32
BF16 = mybir.dt.bfloat16
AF = mybir.ActivationFunctionType
AX = mybir.AxisListType

NCORES = 8
D = 1024
L = 2048
DFF = 2816
KC = D // 128
FC = DFF // 128
ENG = ['pe', 'dve', 'act', 'pool', 'sp']
SAME_ENG_SYNC = True


PSUM_KEYS = set()


class Prog:
    def __init__(self, nc, es):
        self.nc = nc
        self.es = es
        self.engs = {'pe': nc.tensor, 'dve': nc.vector, 'act': nc.scalar, 'pool': nc.gpsimd, 'sp': nc.sync}
        self.q = {e: [] for e in ENG}
        self.sem = {e: es.enter_context(nc.semaphore("s_" + e)) for e in ENG}
        self.cnt = {e: 0 for e in ENG}
        self.seen = {e: {} for e in ENG}
        self.lastw = {}
        self.rds = {}
        self.ndsem = {'sp': 6, 'pool': 2, 'act': 2}
        self.dsem = {}
        self.dval = {}
        self.drr = {}
        for qn, n in self.ndsem.items():
            self.dsem[qn] = [es.enter_context(nc.semaphore(f"d_{qn}{i}")) for i in range(n)]
            for i in range(n):
                self.dval[(qn, i)] = 0
            self.drr[qn] = 0
        self.ninstr = 0

    def _semh(self, s):
        if isinstance(s, tuple):
            return self.dsem[s[0]][s[1]]
        return self.sem[s]

    def _wait(self, eng, tok):
        s, v = tok
        if s == eng and (eng == 'pe' or not SAME_ENG_SYNC):
            return
        if self.seen[eng].get(s, 0) >= v:
            return
        self.seen[eng][s] = v
        self.q[eng].append(('w', self._semh(s), v))

    def _deps(self, eng, reads, writes, is_dma=False):
        for k in reads:
            for tok in self.lastw.get(k, ()):
                self._wait(eng, tok)
        for k in writes:
            for tok in self.lastw.get(k, ()):
                if is_dma and isinstance(tok[0], tuple):
                    continue
                self._wait(eng, tok)
            for s, v in self.rds.get(k, {}).items():
                self._wait(eng, (s, v))

    def _record(self, tok, reads, writes, is_dma=False):
        s, v = tok
        for k in reads:
            d = self.rds.setdefault(k, {})
            if d.get(s, 0) < v:
                d[s] = v
        for k in writes:
            if is_dma and not self.rds.get(k) and k in self.lastw and all(isinstance(t[0], tuple) for t in self.lastw[k]):
                self.lastw[k] = [t for t in self.lastw[k] if t[0] != s] + [tok]
            else:
                self.lastw[k] = [tok]
            self.rds[k] = {}

    def op(self, eng, fn, reads=(), writes=()):
        xr = [k for k in reads if k in PSUM_KEYS]
        if xr:
            reads = [k for k in reads if k not in PSUM_KEYS]
            writes = list(writes) + [k for k in xr if k not in writes]
        self._deps(eng, reads, writes)
        self.cnt[eng] += 1
        tok = (eng, self.cnt[eng])
        self.q[eng].append(('o', fn, self.sem[eng], 1))
        self._record(tok, reads, writes)
        self.ninstr += 1

    def dma(self, qn, out, in_, reads=(), writes=(), **kw):
        self._deps(qn, reads, writes, is_dma=True)
        i = self.drr[qn]
        self.drr[qn] = (i + 1) % self.ndsem[qn]
        prev = self.dval[(qn, i)]
        if prev > 0:
            self._wait(qn, ((qn, i), prev))
        self.dval[(qn, i)] = prev + 16
        tok = ((qn, i), prev + 16)
        self.q[qn].append(('o', lambda e: e.dma_start(out=out, in_=in_, **kw), self.dsem[qn][i], 16))
        self._record(tok, reads, writes, is_dma=True)
        self.ninstr += 1

    def barrier(self):
        for e in ENG:
            for e2 in ENG:
                if e2 != e and self.cnt[e2] > 0:
                    self._wait(e, (e2, self.cnt[e2]))
            for k, v in self.dval.items():
                if v > 0:
                    self._wait(e, (k, v))
        self.lastw = {}
        self.rds = {}

    def emit(self):
        nc = self.nc
        with nc.Block() as block:
            def run(e, name):
                for it in self.q[name]:
                    if it[0] == 'w':
                        e.wait_ge(it[1], it[2])
                    else:
                        it[1](e).then_inc(it[2], it[3])

            @block.tensor
            def _(e):
                run(e, 'pe')

            @block.vector
            def _(e):
                run(e, 'dve')

            @block.scalar
            def _(e):
                run(e, 'act')

            @block.gpsimd
            def _(e):
                run(e, 'pool')

            @block.sync
            def _(e):
                run(e, 'sp')

    def mm(self, out, lhsT, rhs, start, stop, reads, writes):
        self.op('pe', lambda e: e.matmul(out, lhsT, rhs, start=start, stop=stop), reads, writes)

    def mm64(self, out, lhsT, rhs, start, stop, reads, writes):
        if self.cnt['pe'] > 0:
            self.q['pe'].append(('w', self.sem['pe'], self.cnt['pe']))
        self.mm(out, lhsT, rhs, start, stop, reads, writes)

    def tr(self, out, in_, ident, reads, writes):
        self.op('pe', lambda e: e.transpose(out, in_, ident), reads, writes)

    def act(self, out, in_, func, reads, writes, bias=None, scale=None):
        kw = {}
        if bias is not None:
            kw['bias'] = bias
        if scale is not None:
            kw['scale'] = scale
        self.op('act', lambda e: e.activation(out, in_, func, **kw), reads, writes)

    def tt(self, eng, out, in0, in1, op, reads, writes):
        self.op(eng, lambda e: e.tensor_tensor(out, in0, in1, op), reads, writes)

    def ts(self, eng, out, in0, s1, s2, op0, op1, reads, writes):
        if op1 is None:
            self.op(eng, lambda e: e.tensor_scalar(out, in0, s1, None, op0), reads, writes)
        else:
            self.op(eng, lambda e: e.tensor_scalar(out, in0, s1, s2, op0, op1), reads, writes)

    def stt(self, out, in0, scalar, in1, op0, op1, reads, writes):
        self.op('dve', lambda e: e.scalar_tensor_tensor(out, in0, scalar, in1, op0, op1), reads, writes)

    def copy(self, eng, out, in_, reads, writes):
        if eng == 'act':
            self.op('act', lambda e: e.copy(out, in_), reads, writes)
        else:
            self.op(eng, lambda e: e.tensor_copy(out, in_), reads, writes)

    def memset(self, eng, ap, val, writes):
        self.op(eng, lambda e: e.memset(ap, val), (), writes)


class Ctx:
    pass


_UID = [0]


def sb(c, es, name, shape, dt):
    _UID[0] += 1
    return es.enter_context(c.nc.sbuf_tensor(f"{name}_{_UID[0]}", shape, dt))


def ps(c, es, name, shape, dt=F32):
    _UID[0] += 1
    return es.enter_context(c.nc.psum_tensor(f"{name}_{_UID[0]}", shape, dt))


def stage_in(c, srcs):
    p = c.p
    with contextlib.ExitStack() as es:
        xin = [sb(c, es, f"in_x{i}", [128, D], F32) for i in range(2)]
        xt = [sb(c, es, f"in_xt{i}", [128, KC, 512], F32) for i in range(2)]
        pt = [ps(c, es, f"in_ps{i}", [128, 512]) for i in range(4)]
        n = 0
        for s, (src, b) in enumerate(srcs):
            for g in range(L // 512):
                xo = xt[g % 2]
                ko = f"in_xt{g % 2}"
                for j in range(4):
                    t0 = g * 512 + j * 128
                    xi = xin[n % 2]
                    ki = f"in_x{n % 2}"
                    p.dma('sp', xi[:], src[b, t0:t0 + 128, :], [], [ki])
                    for half in range(2):
                        pp = pt[(2 * n + half) % 4]
                        kp = f"in_ps{(2 * n + half) % 4}"
                        for q4 in range(4):
                            kc = half * 4 + q4
                            p.tr(pp[:, q4 * 128:(q4 + 1) * 128], xi[:, kc * 128:(kc + 1) * 128], c.ident[:],
                                 [ki], [kp])
                        eng = 'act' if half == 0 else 'dve'
                        p.copy(eng, xo[:, half * 4:half * 4 + 4, j * 128:(j + 1) * 128],
                               pp[:].rearrange("p (a b) -> p a b", a=4), [kp], [ko])
                    n += 1
                tok0 = s * L + g * 512
                p.dma('sp', c.XT[:, tok0:tok0 + 512].rearrange("(kc p) t -> p kc t", p=128), xo[:],
                      [ko], [("XT", tok0 // 256), ("XT", tok0 // 256 + 1)])
    p.barrier()


def load_vec(c, dst, src_ap, key, q='sp'):
    c.p.dma(q, dst, src_ap.rearrange("(j p) -> p j", p=128), [], [key], allow_slow_non_contiguous=True)


def rms_stats(c, x, kx, sq, ksq, pss, kps, var, kvar, rstd, krstd, nt, nch=KC, dim=D, eps=1e-6):
    p = c.p
    p.act(sq[:, :nch, :nt], x[:, :nch, :nt], AF.Square, [kx], [ksq])
    for kc in range(nch):
        p.mm(pss[:, :nt], c.ones[:], sq[:, kc, :nt], kc == 0, kc == nch - 1, [ksq], [kps])
    p.ts('dve', var[:, :nt], pss[:, :nt], 1.0 / dim, eps, ALU.mult, ALU.add, [kps], [kvar])
    p.tt('pool', rstd[:, :nt], var[:, :nt], c.mhalf[:, :nt], ALU.pow, [kvar], [krstd])


def stage_out(c, dsts, gvec):
    p = c.p
    NT = 256
    with contextlib.ExitStack() as es:
        g = sb(c, es, "o_g", [128, KC], F32)
        x = [sb(c, es, f"o_x{i}", [128, KC, NT], F32) for i in range(2)]
        sq = sb(c, es, "o_sq", [128, KC, NT], F32)
        var = sb(c, es, "o_var", [128, NT], F32)
        rstd = sb(c, es, "o_rstd", [128, NT], F32)
        xn = sb(c, es, "o_xn", [128, KC, NT], F32)
        yo = [sb(c, es, f"o_y{i}", [128, D], F32) for i in range(2)]
        pss = ps(c, es, "o_pss", [128, 512])
        pt = [ps(c, es, f"o_pt{i}", [128, 512]) for i in range(4)]
        load_vec(c, g[:], gvec, "o_g")
        n = 0
        m = 0
        for s, (dst, b) in enumerate(dsts):
            for gi in range(L // NT):
                tok0 = s * L + gi * NT
                xi = x[gi % 2]
                kx = f"o_x{gi % 2}"
                p.dma('sp', xi[:], c.XT[:, tok0:tok0 + NT].rearrange("(kc p) t -> p kc t", p=128),
                      [("XT", tok0 // 256)], [kx])
                rms_stats(c, xi, kx, sq, "o_sq", pss, "o_pss", var, "o_var", rstd, "o_rstd", NT)
                for kc in range(KC):
                    p.stt(xn[:, kc, :], xi[:, kc, :], g[:, kc:kc + 1], rstd[:], ALU.mult, ALU.mult,
                          [kx, "o_g", "o_rstd"], ["o_xn"])
                for j in range(NT // 128):
                    y = yo[m % 2]
                    ky = f"o_y{m % 2}"
                    for half in range(2):
                        pp = pt[n % 4]
                        kp = f"o_pt{n % 4}"
                        n += 1
                        for q4 in range(4):
                            kc = half * 4 + q4
                            p.tr(pp[:, q4 * 128:(q4 + 1) * 128], xn[:, kc, j * 128:(j + 1) * 128], c.ident[:],
                                 ["o_xn"], [kp])
                        eng = 'act' if half == 0 else 'dve'
                        p.copy(eng, y[:, half * 512:(half + 1) * 512], pp[:], [kp], [ky])
                    t0 = gi * NT + j * 128
                    p.dma('sp', dst[b, t0:t0 + 128, :], y[:], [ky], [])
                    m += 1
    p.barrier()


def stage_ffn(c, gvec, w_gu, w_down, ntok):
    p = c.p
    NT = 256
    with contextlib.ExitStack() as es:
        wgu = sb(c, es, "f_wgu", [128, KC, 2 * DFF], BF16)
        wd = sb(c, es, "f_wd", [128, FC, D], BF16)
        g = sb(c, es, "f_g", [128, KC], F32)
        x = [sb(c, es, f"f_x{i}", [128, KC, NT], F32) for i in range(2)]
        sq = sb(c, es, "f_sq", [128, KC, NT], F32)
        var = sb(c, es, "f_var", [128, NT], F32)
        rstd = sb(c, es, "f_rstd", [128, NT], F32)
        xn = sb(c, es, "f_xn", [128, KC, NT], BF16)
        h = sb(c, es, "f_h", [128, FC, NT], BF16)
        sg = [sb(c, es, f"f_sg{i}", [128, NT], F32) for i in range(2)]
        pss = ps(c, es, "f_pss", [128, 512])
        pg = [ps(c, es, f"f_pg{i}", [128, 512]) for i in range(2)]
        pu = [ps(c, es, f"f_pu{i}", [128, 512]) for i in range(2)]
        py = [ps(c, es, f"f_py{i}", [128, 512]) for i in range(2)]
        load_vec(c, g[:], gvec, "f_g")
        wl = WLoader(c, es, "f_wst", 1408)
        wl.dbg = c.cfg.get('ffn_dbg', 0)
        for kc in range(KC if wl.dbg != 1 else 0):
            for hh in range(4):
                p_ = wl.load(wgu[:, kc, hh * 1408:(hh + 1) * 1408],
                             w_gu[kc * 128:(kc + 1) * 128, hh * 1408:(hh + 1) * 1408], "f_wgu")
        for j in range(FC if wl.dbg != 1 else 0):
            wl.load(wd[:, j, :], w_down[j * 128:(j + 1) * 128, :], "f_wd")
        for gi in range(c.cfg.get('ffn_groups', ntok // NT)):
            tok0 = gi * NT
            xi = x[gi % 2]
            kx = f"f_x{gi % 2}"
            p.dma('sp', xi[:], c.XT[:, tok0:tok0 + NT].rearrange("(kc p) t -> p kc t", p=128),
                  [("XT", gi)], [kx])
            rms_stats(c, xi, kx, sq, "f_sq", pss, "f_pss", var, "f_var", rstd, "f_rstd", NT)
            for kc in range(KC):
                p.stt(xn[:, kc, :], xi[:, kc, :], g[:, kc:kc + 1], rstd[:], ALU.mult, ALU.mult,
                      [kx, "f_g", "f_rstd"], ["f_xn"])
            for j in range(FC):
                pgj, puj, sgj = pg[j % 2], pu[j % 2], sg[j % 2]
                kg, ku, ks = f"f_pg{j % 2}", f"f_pu{j % 2}", f"f_sg{j % 2}"
                for kc in range(KC):
                    p.mm(pgj[:, :NT], wgu[:, kc, j * 128:(j + 1) * 128], xn[:, kc, :], kc == 0, kc == KC - 1,
                         ["f_wgu", "f_xn"], [kg])
                for kc in range(KC):
                    p.mm(puj[:, :NT], wgu[:, kc, DFF + j * 128:DFF + (j + 1) * 128], xn[:, kc, :], kc == 0,
                         kc == KC - 1, ["f_wgu", "f_xn"], [ku])
                p.act(sgj[:], pgj[:, :NT], AF.Silu, [kg], [ks])
                p.tt('dve', h[:, j, :], sgj[:], puj[:, :NT], ALU.mult, [ks, ku], [("f_h", j)])
            for dc in range(KC):
                pyj = py[dc % 2]
                ky = f"f_py{dc % 2}"
                for j in range(FC):
                    p.mm(pyj[:, :NT], wd[:, j, dc * 128:(dc + 1) * 128], h[:, j, :], j == 0, j == FC - 1,
                         ["f_wd", ("f_h", j)], [ky])
                p.stt(sq[:, dc, :], pyj[:, :NT], 0.5, xi[:, dc, :], ALU.mult, ALU.add, [ky, kx], ["f_sq"])
            p.dma('sp', c.XT[:, tok0:tok0 + NT].rearrange("(kc p) t -> p kc t", p=128), sq[:],
                  ["f_sq"], [("XT", gi)])
    p.barrier()


class Ring:
    def __init__(self, tiles, name):
        self.tiles = tiles
        self.name = name
        self.i = 0

    def next(self):
        t = self.tiles[self.i % len(self.tiles)]
        k = f"{self.name}{self.i % len(self.tiles)}"
        self.i += 1
        return t, k


def ring_sb(c, es, name, n, shape, dt):
    return Ring([sb(c, es, f"{name}{i}", shape, dt) for i in range(n)], name)


def ring_ps(c, es, name, n, shape, dt=F32):
    for i in range(n):
        PSUM_KEYS.add(f"{name}{i}")
    return Ring([ps(c, es, f"{name}{i}", shape, dt) for i in range(n)], name)


class WLoader:
    def __init__(self, c, es, name, width, n=2):
        self.c = c
        self.ring = ring_sb(c, es, name, n, [128, width], F32)
        self.width = width
        self.k = 0

    def load(self, dst, src, key, shape=None):
        p = self.c.p
        st, kst = self.ring.next()
        n = 1
        for d_ in dst.shape[1:]:
            n *= d_
        sv = st[:dst.shape[0], :n]
        if len(dst.shape) == 3:
            sv = sv.rearrange("p (a b) -> p a b", a=dst.shape[1])
        elif len(dst.shape) == 4:
            sv = sv.rearrange("p (a b c) -> p a b c", a=dst.shape[1], b=dst.shape[2])
        p.dma('sp', sv, src, [], [kst])
        eng = 'pool' if self.k % 2 == 0 else 'act'
        if getattr(self, 'dbg', 0) == 3:
            eng = 'pool'
        if getattr(self, 'dbg', 0) == 4:
            eng = 'act'
        self.k += 1
        if getattr(self, 'dbg', 0) != 2:
            p.copy(eng, dst, sv, [kst], [key])


def xt_view(c, tok0, nt):
    return c.XT[:, tok0:tok0 + nt].rearrange("(kc p) t -> p kc t", p=128)


def xt_keys(tok0, nt):
    return [("XT", k) for k in range(tok0 // 256, (tok0 + nt + 255) // 256)]


def load_xn(c, tok0, NT, xr, sq, pss, var, rstd, g, kg, xn, kxn, pref):
    p = c.p
    xi, kx = xr.next()
    p.dma('sp', xi[:, :, :NT], xt_view(c, tok0, NT), xt_keys(tok0, NT), [kx])
    rms_stats(c, xi, kx, sq, pref + "sq", pss, pref + "pss", var, pref + "var", rstd, pref + "rstd", NT)
    for kc in range(KC):
        p.stt(xn[:, kc, :NT], xi[:, kc, :NT], g[:, kc:kc + 1], rstd[:, :NT], ALU.mult, ALU.mult,
              [kx, kg, pref + "rstd"], [kxn])
    return xi, kx


def stage_attn(c, layer, j, nseq):
    p = c.p
    W = c.W
    wqkv, wout, sinks_d, gvec = W['att_w_qkv'][j], W['att_w_out'][j], W['att_sinks'][j], W['mix_norm'][layer]
    NEG = -30000.0
    with contextlib.ExitStack() as es:
        wq = sb(c, es, "a_wq", [128, KC, 1024], BF16)
        wqs = sb(c, es, "a_wqs", [128, KC, 1024], BF16)
        wk = sb(c, es, "a_wk", [128, KC, 512], BF16)
        wks = sb(c, es, "a_wks", [128, KC, 512], BF16)
        wv = sb(c, es, "a_wv", [128, KC, 256], BF16)
        wo = sb(c, es, "a_wo", [128, KC, 1024], BF16)
        cos = sb(c, es, "a_cos", [128, L], F32)
        sin = sb(c, es, "a_sin", [128, L], F32)
        g = sb(c, es, "a_g", [128, KC], F32)
        snk = sb(c, es, "a_snk", [128, 16], F32)
        nsnk = sb(c, es, "a_nsnk", [128, 16], F32)
        mask = sb(c, es, "a_mask", [128, 384], F32)
        qT = sb(c, es, "a_qT", [128, KC, L], BF16)
        kT = sb(c, es, "a_kT", [128, 4, L], BF16)
        vtok = sb(c, es, "a_v", [128, 16, 256], BF16)
        load_vec(c, g[:], gvec, "a_g")
        p.dma('sp', cos[:], c.c_cos[:, :], [], ["a_cos"])
        p.dma('sp', sin[:], c.c_sin[:, :], [], ["a_sin"])
        p.dma('sp', snk[:], sinks_d.partition_broadcast(128), [], ["a_snk"])
        p.ts('dve', nsnk[:], snk[:], -1.0, None, ALU.mult, None, ["a_snk"], ["a_nsnk"])
        p.memset('pool', mask[:], 0.0, ["a_mask"])
        p.op('pool', lambda e: e.affine_select(mask[:], mask[:], [[1, 384]], ALU.is_ge, NEG, base=0,
                                               channel_multiplier=-1), ["a_mask"], ["a_mask"])
        p.op('pool', lambda e: e.affine_select(mask[:], mask[:], [[-1, 384]], ALU.is_ge, NEG, base=256,
                                               channel_multiplier=1), ["a_mask"], ["a_mask"])
        wl = WLoader(c, es, "a_wst", 1024)
        for kc in range(KC):
            rows = slice(kc * 128, (kc + 1) * 128)
            wl.load(wq[:, kc, :], wqkv[rows, 0:1024], "a_w")
            src = wqkv[rows, 0:1024].rearrange("p (h r d) -> p h r d", h=16, r=2)
            dst = wqs[:, kc, :].rearrange("p (h r d) -> p h r d", h=16, r=2)
            wl.load(dst[:, :, 0, :], src[:, :, 1, :], "a_w")
            wl.load(dst[:, :, 1, :], src[:, :, 0, :], "a_w")
            srck = wqkv[rows, 1024:1280].rearrange("p (g r d) -> p g r d", g=4, r=2)
            dk = wk[:, kc, :].rearrange("p (g c r d) -> p g c r d", g=4, c=2, r=2)
            dks = wks[:, kc, :].rearrange("p (g c r d) -> p g c r d", g=4, c=2, r=2)
            for cpy in range(2):
                for r in range(2):
                    wl.load(dk[:, :, cpy, r, :], srck[:, :, r, :], "a_w")
                    wl.load(dks[:, :, cpy, r, :], srck[:, :, 1 - r, :], "a_w")
            wl.load(wv[:, kc, :], wqkv[rows, 1280:1536], "a_w")
            wl.load(wo[:, kc, :], wout[rows, :], "a_w")
        p.barrier()
        for s in range(nseq):
            with contextlib.ExitStack() as es2:
                NT = 256
                xr = ring_sb(c, es2, "aA_x", 2, [128, KC, NT], F32)
                sq = sb(c, es2, "aA_sq", [128, KC, NT], F32)
                var = sb(c, es2, "aA_var", [128, NT], F32)
                rstd = sb(c, es2, "aA_rstd", [128, NT], F32)
                xn = sb(c, es2, "aA_xn", [128, KC, NT], BF16)
                t1r = ring_sb(c, es2, "aA_t1", 2, [128, NT], F32)
                t2r = ring_sb(c, es2, "aA_t2", 2, [128, NT], F32)
                pss = ps(c, es2, "aA_pss", [128, 512])
                p1r = ring_ps(c, es2, "aA_p1", 2, [128, 512])
                p2r = ring_ps(c, es2, "aA_p2", 2, [128, 512])
                pvr = ring_ps(c, es2, "aA_pv", 2, [128, 512])
                for gi in range(L // NT):
                    t0 = gi * NT
                    load_xn(c, s * L + t0, NT, xr, sq, pss, var, rstd, g, "a_g", xn, "aA_xn", "aA_")
                    for oc in range(12):
                        if oc < 8:
                            wa, wb, dstT = wq[:, :, oc * 128:(oc + 1) * 128], wqs[:, :, oc * 128:(oc + 1) * 128], qT[:, oc, t0:t0 + NT]
                            kd = ("a_qT", oc)
                        else:
                            gg = oc - 8
                            wa, wb, dstT = wk[:, :, gg * 128:(gg + 1) * 128], wks[:, :, gg * 128:(gg + 1) * 128], kT[:, gg, t0:t0 + NT]
                            kd = ("a_kT", gg)
                        p1, k1 = p1r.next()
                        p2, k2 = p2r.next()
                        for kc in range(KC):
                            p.mm(p1[:, :NT], wa[:, kc, :], xn[:, kc, :], kc == 0, kc == KC - 1, ["a_w", "aA_xn"], [k1])
                        for kc in range(KC):
                            p.mm(p2[:, :NT], wb[:, kc, :], xn[:, kc, :], kc == 0, kc == KC - 1, ["a_w", "aA_xn"], [k2])
                        t1, kt1 = t1r.next()
                        t2, kt2 = t2r.next()
                        p.tt('dve', t1[:], p1[:, :NT], cos[:, t0:t0 + NT], ALU.mult, [k1, "a_cos"], [kt1])
                        p.tt('dve', t2[:], p2[:, :NT], sin[:, t0:t0 + NT], ALU.mult, [k2, "a_sin"], [kt2])
                        p.tt('pool', dstT, t1[:], t2[:], ALU.add, [kt1, kt2], [kd])
                    for tb in range(NT // 128):
                        pv, kv = pvr.next()
                        for kc in range(KC):
                            p.mm(pv[:, :256], xn[:, kc, tb * 128:(tb + 1) * 128], wv[:, kc, :], kc == 0, kc == KC - 1,
                                 ["a_w", "aA_xn"], [kv])
                        p.copy('act', vtok[:, (t0 // 128) + tb, :], pv[:, :256], [kv], [("a_v", (t0 // 128) + tb)])
            p.barrier()
            with contextlib.ExitStack() as es2:
                smr = ring_sb(c, es2, "aB_sm", 2, [128, 384], F32)
                er = ring_sb(c, es2, "aB_e", 2, [128, 384], F32)
                enr = ring_sb(c, es2, "aB_en", 2, [128, 384], BF16)
                eTr = ring_sb(c, es2, "aB_eT", 2, [128, 384], BF16)
                str_ = ring_sb(c, es2, "aB_st", 4, [128, 8], F32)
                oT = sb(c, es2, "aB_oT", [128, KC, 512], BF16)
                xr = ring_sb(c, es2, "aB_x", 1, [128, KC, 512], F32)
                xo = sb(c, es2, "aB_xo", [128, KC, 512], F32)
                spr = ring_ps(c, es2, "aB_sp", 2, [128, 512])
                tpr = ring_ps(c, es2, "aB_tp", 2, [128, 512], BF16)
                opr = ring_ps(c, es2, "aB_op", 2, [128, 512])
                ypr = ring_ps(c, es2, "aB_yp", 2, [128, 512])
                for gi in range(L // 512):
                    tok0 = s * L + gi * 512
                    xi, kx = xr.next()
                    p.dma('sp', xi[:], xt_view(c, tok0, 512), xt_keys(tok0, 512), [kx])
                    for qc in range(KC):
                        op_, kop = opr.next()
                        for qb in range(4):
                            jb = gi * 4 + qb
                            kb0, kb1 = max(jb - 1, 0), min(jb + 1, 15)
                            nk = kb1 - kb0 + 1
                            mo = 128 if jb == 0 else 0
                            nkw = nk * 128
                            for hp in range(2):
                                h = qc * 2 + hp
                                gk = h // 4
                                b0 = hp * 64
                                sp_, ksp = spr.next()
                                p.mm(sp_[:, :nkw], qT[b0:b0 + 64, qc, jb * 128:(jb + 1) * 128],
                                     kT[b0:b0 + 64, gk, kb0 * 128:(kb1 + 1) * 128], True, True,
                                     [("a_qT", qc), ("a_kT", gk)], [ksp])
                                sm, ksm = smr.next()
                                p.tt('dve', sm[:, :nkw], sp_[:, :nkw], mask[:, mo:mo + nkw], ALU.add, [ksp, "a_mask"], [ksm])
                                st, kst = str_.next()
                                p.op('dve', lambda e, st=st, sm=sm, nkw=nkw: e.reduce_max(st[:, 0:1], sm[:, :nkw], AX.X), [ksm], [kst])
                                p.ts('dve', st[:, 1:2], st[:, 0:1], -0.125, nsnk[:, h:h + 1], ALU.mult, ALU.min, ["a_nsnk", kst], [kst])
                                e_, ke = er.next()
                                p.op('act', lambda e, e_=e_, sm=sm, st=st, nkw=nkw: e.activation(
                                    e_[:, :nkw], sm[:, :nkw], AF.Exp, bias=st[:, 1:2], scale=0.125, accum_out=st[:, 2:3]),
                                    [ksm, kst], [ke, kst])
                                p.op('act', lambda e, st=st, h=h: e.activation(st[:, 3:4], st[:, 1:2], AF.Exp, bias=snk[:, h:h + 1]),
                                     [kst, "a_snk"], [kst])
                                p.tt('dve', st[:, 4:5], st[:, 2:3], st[:, 3:4], ALU.add, [kst], [kst])
                                p.op('dve', lambda e, st=st: e.reciprocal(st[:, 5:6], st[:, 4:5]), [kst], [kst])
                                en, ken = enr.next()
                                p.ts('dve', en[:, :nkw], e_[:, :nkw], st[:, 5:6], None, ALU.mult, None, [ke, kst], [ken])
                                tp, ktp = tpr.next()
                                for kb in range(nk):
                                    p.tr(tp[:, kb * 128:(kb + 1) * 128], en[:, kb * 128:(kb + 1) * 128], c.identb[:], [ken], [ktp])
                                eT, keT = eTr.next()
                                p.copy('act', eT[:, :nkw], tp[:, :nkw], [ktp], [keT])
                                for kb in range(nk):
                                    p.mm(op_[b0:b0 + 64, qb * 128:(qb + 1) * 128], vtok[:, kb0 + kb, gk * 64:(gk + 1) * 64],
                                         eT[:, kb * 128:(kb + 1) * 128], kb == 0, kb == nk - 1,
                                         [("a_v", kb0 + kb), keT], [kop])
                        p.copy('act', oT[:, qc, :], op_[:], [kop], [("aB_oT", qc)])
                    for dc in range(KC):
                        yp, kyp = ypr.next()
                        for qc in range(KC):
                            p.mm(yp[:], wo[:, qc, dc * 128:(dc + 1) * 128], oT[:, qc, :], qc == 0, qc == KC - 1,
                                 ["a_w", ("aB_oT", qc)], [kyp])
                        p.tt('dve', xo[:, dc, :], yp[:], xi[:, dc, :], ALU.add, [kyp, kx], ["aB_xo"])
                    p.dma('sp', xt_view(c, tok0, 512), xo[:], ["aB_xo"], xt_keys(tok0, 512))
            p.barrier()


def make_masks(c, es):
    p = c.p
    c.mk = {}
    for name, pat, base, cm, op in [("LE", 1, 0, -1, ALU.is_ge), ("GE", -1, 0, 1, ALU.is_ge),
                                    ("GT", -1, 0, 1, ALU.is_gt), ("LT", 1, 0, -1, ALU.is_gt)]:
        t = sb(c, es, "mk" + name, [128, 128], F32)
        p.memset('pool', t[:], 1.0, ["mk" + name])
        p.op('pool', lambda e, t=t, pat=pat, base=base, cm=cm, op=op: e.affine_select(
            t[:], t[:], [[pat, 128]], op, 0.0, base=base, channel_multiplier=cm), ["mk" + name], ["mk" + name])
        c.mk[name] = t


def bc(ap, shape, axis):
    return ap.unsqueeze(axis).broadcast_to(shape)


def stage_ssd(c, layer, j, nseq):
    p = c.p
    W = c.W
    w_in, conv_w, conv_b = W['ssd_w_in'][j], W['ssd_conv_w'][j], W['ssd_conv_b'][j]
    a_log, dt_bias, d_skip, norm_w, w_out = W['ssd_a_log'][j], W['ssd_dt_bias'][j], W['ssd_d'][j], W['ssd_norm'][j], W['ssd_w_out'][j]
    gvec = W['mix_norm'][layer]
    NB = L // 128
    with contextlib.ExitStack() as es:
        g = sb(c, es, "s_g", [128, KC], F32)
        cw = sb(c, es, "s_cw", [128, 5, 32], F32)
        cb = sb(c, es, "s_cb", [128, 32], F32)
        dtb = sb(c, es, "s_dtb", [128, 64], F32)
        aneg = sb(c, es, "s_aneg", [128, 64], F32)
        dsk = sb(c, es, "s_dsk", [128, 32], F32)
        nw = sb(c, es, "s_nw", [128, 2048], F32)
        xn = sb(c, es, "s_xn", [128, KC, L], BF16)
        dt = sb(c, es, "s_dt", [128, NB, 64], F32)
        load_vec(c, g[:], gvec, "s_g")
        for tap in range(5):
            p.dma('sp', cw[:, tap, :], conv_w[tap].rearrange("(cc p) -> p cc", p=128), [], ["s_cw"], allow_slow_non_contiguous=True)
        p.dma('sp', cb[:], conv_b.rearrange("(cc p) -> p cc", p=128), [], ["s_cb"], allow_slow_non_contiguous=True)
        p.dma('sp', dtb[:], dt_bias.rearrange("a b -> (a b)").partition_broadcast(128), [], ["s_dtb"])
        p.dma('sp', aneg[:], a_log.rearrange("a b -> (a b)").partition_broadcast(128), [], ["s_aneg"])
        p.dma('sp', dsk[:], d_skip.partition_broadcast(128), [], ["s_dsk"])
        p.dma('sp', nw[:], norm_w.partition_broadcast(128), [], ["s_nw"])
        p.act(aneg[:], aneg[:], AF.Exp, ["s_aneg"], ["s_aneg"])
        p.ts('dve', aneg[:], aneg[:], -1.0, None, ALU.mult, None, ["s_aneg"], ["s_aneg"])
        p.barrier()
        for s in range(nseq):
            with contextlib.ExitStack() as es2:
                NT = 256
                xr = ring_sb(c, es2, "sA_x", 2, [128, KC, NT], F32)
                sq = sb(c, es2, "sA_sq", [128, KC, NT], F32)
                var = sb(c, es2, "sA_var", [128, NT], F32)
                rstd = sb(c, es2, "sA_rstd", [128, NT], F32)
                pss = ps(c, es2, "sA_pss", [128, 512])
                for gi in range(L // NT):
                    load_xn(c, s * L + gi * NT, NT, xr, sq, pss, var, rstd, g, "s_g", xn[:, :, gi * NT:(gi + 1) * NT], "s_xn", "sA_")
            p.barrier()
            with contextlib.ExitStack() as es2:
                wzr = ring_sb(c, es2, "sB_wz", 2, [128, KC, 512], BF16)
                wdt = sb(c, es2, "sB_wdt", [128, KC, 64], BF16)
                zr = ring_sb(c, es2, "sB_z", 3, [128, 512], F32)
                pzr = ring_ps(c, es2, "sB_pz", 4, [128, 512])
                pdr = ring_ps(c, es2, "sB_pd", 2, [128, 512])
                wl = WLoader(c, es2, "sB_wst", 4096)
                wl.load(wdt[:], w_in[:, 6144:6208].rearrange("(kc p) c -> p kc c", p=128), "sB_wdt")
                for blk in range(NB):
                    pd, kpd = pdr.next()
                    for kc in range(KC):
                        p.mm(pd[:, :64], xn[:, kc, blk * 128:(blk + 1) * 128], wdt[:, kc, :], kc == 0, kc == KC - 1,
                             ["s_xn", "sB_wdt"], [kpd])
                    p.tt('dve', dt[:, blk, :], pd[:, :64], dtb[:], ALU.add, [kpd, "s_dtb"], ["s_dt"])
                p.act(dt[:], dt[:], AF.Exp, ["s_dt"], ["s_dt"])
                p.act(dt[:], dt[:], AF.Ln, ["s_dt"], ["s_dt"], bias=1.0)
                for zc in range(4):
                    wz, kwz = wzr.next()
                    wl.load(wz[:], w_in[:, zc * 512:(zc + 1) * 512].rearrange("(kc p) c -> p kc c", p=128), kwz)
                    for blk in range(NB):
                        pz, kpz = pzr.next()
                        for kc in range(KC):
                            p.mm(pz[:], xn[:, kc, blk * 128:(blk + 1) * 128], wz[:, kc, :], kc == 0, kc == KC - 1,
                                 ["s_xn", kwz], [kpz])
                        z, kz = zr.next()
                        p.act(z[:], pz[:], AF.Silu, [kpz], [kz])
                        p.dma('sp', c.Zs[blk * 128:(blk + 1) * 128, zc * 512:(zc + 1) * 512], z[:], [kz], [("Zs", blk, zc)])
            p.barrier()
            with contextlib.ExitStack() as es2:
                wcr = ring_sb(c, es2, "sC_wc", 3, [128, KC, 128], BF16)
                wl = WLoader(c, es2, "sC_wst", 1024)
                raw = ring_sb(c, es2, "sC_raw", 2, [128, L + 4], F32)
                acc = ring_sb(c, es2, "sC_acc", 2, [128, L], F32)
                cvT = ring_sb(c, es2, "sC_cvT", 2, [128, L], BF16)
                BT = sb(c, es2, "sC_BT", [128, L], BF16)
                CT = sb(c, es2, "sC_CT", [128, L], BF16)
                Btok = sb(c, es2, "sC_Btok", [128, NB, 128], BF16)
                xtok = sb(c, es2, "sC_xtok", [128, NB, 256], BF16)
                xdt = sb(c, es2, "sC_xdt", [128, NB, 2, 256], BF16)
                dta = sb(c, es2, "sC_dta", [128, NB, 2, 4], F32)
                acs = sb(c, es2, "sC_acs", [128, NB, 2, 4], F32)
                tot = sb(c, es2, "sC_tot", [128, NB, 2, 4], F32)
                ea = sb(c, es2, "sC_ea", [128, NB, 2, 4], F32)
                edec = sb(c, es2, "sC_edec", [128, NB, 2, 4], F32)
                etot = sb(c, es2, "sC_etot", [128, NB, 2, 4], F32)
                Y = sb(c, es2, "sC_Y", [128, NB, 256], F32)
                H = sb(c, es2, "sC_H", [128, 256], F32)
                Hb = sb(c, es2, "sC_Hb", [128, 256], BF16)
                cbm = ring_sb(c, es2, "sC_cbm", 2, [128, 2, 128], F32)
                rhsr = ring_sb(c, es2, "sC_rhs", 2, [128, 4, 128], F32)
                decr = ring_sb(c, es2, "sC_dec", 2, [128, 4, 128], F32)
                mtr = ring_sb(c, es2, "sC_mt", 2, [128, 4, 128], BF16)
                tmpr = ring_sb(c, es2, "sC_tmp", 2, [128, 256], F32)
                xdr = ring_sb(c, es2, "sC_xd", 2, [128, 256], BF16)
                pA = ring_ps(c, es2, "sC_pA", 2, [128, 512])
                pT = ring_ps(c, es2, "sC_pT", 2, [128, 1024], BF16)
                pC = ring_ps(c, es2, "sC_pC", 1, [128, 512])
                pY = ring_ps(c, es2, "sC_pY", 2, [128, 512])
                pH = ring_ps(c, es2, "sC_pH", 1, [128, 512])
                for tl, ktl in zip(raw.tiles, ["sC_raw0", "sC_raw1"]):
                    p.memset('pool', tl[:, 0:2], 0.0, [ktl])
                    p.memset('pool', tl[:, L + 2:L + 4], 0.0, [ktl])
                for gq in range(8):
                    for ci, cc in enumerate([2 * gq, 2 * gq + 1, 16 + gq, 24 + gq]):
                        wc, kwc = wcr.next()
                        col0 = 2048 + cc * 128
                        wl.load(wc[:], w_in[:, col0:col0 + 128].rearrange("(kc p) c -> p kc c", p=128), kwc)
                        rw, krw = raw.next()
                        for tg in range(L // 512):
                            pa, kpa = pA.next()
                            for kc in range(KC):
                                p.mm(pa[:], wc[:, kc, :], xn[:, kc, tg * 512:(tg + 1) * 512], kc == 0, kc == KC - 1,
                                     [kwc, "s_xn"], [kpa])
                            p.copy('act', rw[:, 2 + tg * 512:2 + (tg + 1) * 512], pa[:], [kpa], [krw])
                        ac, kac = acc.next()
                        p.ts('dve', ac[:], rw[:, 0:L], cw[:, 0, cc:cc + 1], cb[:, cc:cc + 1], ALU.mult, ALU.add,
                             [krw, "s_cw", "s_cb"], [kac])
                        for tap in range(1, 5):
                            p.stt(ac[:], rw[:, tap:tap + L], cw[:, tap, cc:cc + 1], ac[:], ALU.mult, ALU.add,
                                  [krw, "s_cw", kac], [kac])
                        if ci < 2:
                            cv, kcv = cvT.next()
                            p.act(cv[:], ac[:], AF.Silu, [kac], [kcv])
                            for b4 in range(NB // 8):
                                pt, kpt = pT.next()
                                for q in range(8):
                                    blk = b4 * 8 + q
                                    p.tr(pt[:, q * 128:(q + 1) * 128], cv[:, blk * 128:(blk + 1) * 128], c.identb[:], [kcv], [kpt])
                                p.copy('act' if b4 % 2 else 'dve', xtok[:, b4 * 8:(b4 + 1) * 8, ci * 128:(ci + 1) * 128],
                                       pt[:].rearrange("p (a b) -> p a b", a=8), [kpt], ["sC_xtok"])
                        elif ci == 2:
                            p.act(BT[:], ac[:], AF.Silu, [kac], ["sC_BT"])
                            for b4 in range(NB // 8):
                                pt, kpt = pT.next()
                                for q in range(8):
                                    blk = b4 * 8 + q
                                    p.tr(pt[:, q * 128:(q + 1) * 128], BT[:, blk * 128:(blk + 1) * 128], c.identb[:], ["sC_BT"], [kpt])
                                p.copy('act' if b4 % 2 else 'dve', Btok[:, b4 * 8:(b4 + 1) * 8, :],
                                       pt[:].rearrange("p (a b) -> p a b", a=8), [kpt], ["sC_Btok"])
                        else:
                            p.act(CT[:], ac[:], AF.Silu, [kac], ["sC_CT"])
                    for d in range(2):
                        c0 = d * 32 + gq * 4
                        p.tt('dve', dta[:, :, d, :], dt[:, :, c0:c0 + 4], bc(aneg[:, c0:c0 + 4], [128, NB, 4], 1), ALU.mult,
                             ["s_dt", "s_aneg"], ["sC_dta"])
                        p.tt('dve', xdt[:, :, d, :].rearrange("p b (h q) -> p b h q", h=4),
                             xtok[:].rearrange("p b (h q) -> p b h q", h=4),
                             bc(dt[:, :, c0:c0 + 4], [128, NB, 4, 64], 3), ALU.mult, ["sC_xtok", "s_dt"], ["sC_xdt"])
                    pc, kpc = pC.next()
                    for d in range(2):
                        msk = c.mk["LE"] if d == 0 else c.mk["GE"]
                        for blk in range(NB):
                            p.mm(pc[:, (blk * 2 + d) * 4:(blk * 2 + d) * 4 + 4], msk[:], dta[:, blk, d, :], True, True,
                                 ["sC_dta", "mk"], [kpc])
                    p.copy('dve', acs[:].rearrange("p b d h -> p (b d h)"), pc[:, :NB * 8], [kpc], ["sC_acs"])
                    pc, kpc = pC.next()
                    p.mm(pc[:, :NB * 8], c.ones[:], dta[:].rearrange("p b d h -> p (b d h)"), True, True, ["sC_dta"], [kpc])
                    p.copy('dve', tot[:].rearrange("p b d h -> p (b d h)"), pc[:, :NB * 8], [kpc], ["sC_tot"])
                    p.act(ea[:].rearrange("p b d h -> p (b d h)"), acs[:].rearrange("p b d h -> p (b d h)"), AF.Exp, ["sC_acs"], ["sC_ea"])
                    p.act(etot[:].rearrange("p b d h -> p (b d h)"), tot[:].rearrange("p b d h -> p (b d h)"), AF.Exp, ["sC_tot"], ["sC_etot"])
                    p.tt('dve', edec[:].rearrange("p b d h -> p (b d h)"), tot[:].rearrange("p b d h -> p (b d h)"),
                         acs[:].rearrange("p b d h -> p (b d h)"), ALU.subtract, ["sC_tot", "sC_acs"], ["sC_edec"])
                    p.act(edec[:].rearrange("p b d h -> p (b d h)"), edec[:].rearrange("p b d h -> p (b d h)"), AF.Exp, ["sC_edec"], ["sC_edec"])
                    for d in range(2):
                        order = range(NB) if d == 0 else range(NB - 1, -1, -1)
                        mk_in, mk_l, mk_r = (("LE", "GT", "LE") if d == 0 else ("GE", "LT", "GE"))
                        for ci, blk in enumerate(order):
                            tsl = slice(blk * 128, (blk + 1) * 128)
                            first = ci == 0
                            pc, kpc = pC.next()
                            p.mm(pc[:, :128], BT[:, tsl], CT[:, tsl], True, True, ["sC_BT", "sC_CT"], [kpc])
                            cm_, kcm = cbm.next()
                            p.tt('dve', cm_[:, 0, :], pc[:, :128], c.mk[mk_in][:], ALU.mult, [kpc, "mk"], [kcm])
                            rh, krh = rhsr.next()
                            p.tt('pool', rh[:], bc(c.mk[mk_r][:], [128, 4, 128], 1), bc(dta[:, blk, d, :], [128, 4, 128], 2), ALU.mult,
                                 ["mk", "sC_dta"], [krh])
                            pa, kpa = pA.next()
                            p.mm(pa[:], c.mk[mk_l][:], rh[:].rearrange("p h l -> p (h l)"), True, True, [krh, "mk"], [kpa])
                            dc_, kdc = decr.next()
                            p.act(dc_[:].rearrange("p h l -> p (h l)"), pa[:], AF.Exp, [kpa], [kdc])
                            mt, kmt = mtr.next()
                            p.tt('dve', mt[:], dc_[:], bc(cm_[:, 0, :], [128, 4, 128], 1), ALU.mult, [kdc, kcm], [kmt])
                            py, kpy = pY.next()
                            for h in range(4):
                                p.mm(py[:, h * 64:(h + 1) * 64], mt[:, h, :], xdt[:, blk, d, h * 64:(h + 1) * 64], True, True,
                                     [kmt, "sC_xdt"], [kpy])
                            if not first:
                                p.mm(py[:, 256:512], CT[:, tsl], Hb[:], True, True, ["sC_CT", "sC_Hb"], [kpy])
                                tm, ktm = tmpr.next()
                                p.tt('dve', tm[:].rearrange("p (h q) -> p h q", h=4), py[:, 256:512].rearrange("p (h q) -> p h q", h=4),
                                     bc(ea[:, blk, d, :], [128, 4, 64], 2), ALU.mult, [kpy, "sC_ea"], [ktm])
                                if d == 0:
                                    p.tt('dve', Y[:, blk, :], tm[:], py[:, 0:256], ALU.add, [ktm, kpy], [("sC_Y", blk)])
                                else:
                                    p.tt('dve', tm[:], tm[:], py[:, 0:256], ALU.add, [ktm, kpy], [ktm])
                                    p.tt('pool', Y[:, blk, :], Y[:, blk, :], tm[:], ALU.add, [ktm, ("sC_Y", blk)], [("sC_Y", blk)])
                            else:
                                if d == 0:
                                    p.copy('dve', Y[:, blk, :], py[:, 0:256], [kpy], [("sC_Y", blk)])
                                else:
                                    p.tt('dve', Y[:, blk, :], Y[:, blk, :], py[:, 0:256], ALU.add, [kpy, ("sC_Y", blk)], [("sC_Y", blk)])
                            if ci < NB - 1:
                                xd, kxd = xdr.next()
                                p.tt('pool', xd[:].rearrange("p (h q) -> p h q", h=4),
                                     xdt[:, blk, d, :].rearrange("p (h q) -> p h q", h=4),
                                     bc(edec[:, blk, d, :], [128, 4, 64], 2), ALU.mult, ["sC_xdt", "sC_edec"], [kxd])
                                ph, kph = pH.next()
                                p.mm(ph[:, :256], Btok[:, blk, :], xd[:], True, True, ["sC_Btok", kxd], [kph])
                                if first:
                                    p.copy('dve', H[:], ph[:, :256], [kph], ["sC_H"])
                                else:
                                    p.tt('pool', H[:].rearrange("p (h q) -> p h q", h=4), H[:].rearrange("p (h q) -> p h q", h=4),
                                         bc(etot[:, blk, d, :], [128, 4, 64], 2), ALU.mult, ["sC_H", "sC_etot"], ["sC_H"])
                                    p.tt('dve', H[:], H[:], ph[:, :256], ALU.add, ["sC_H", kph], ["sC_H"])
                                p.copy('act', Hb[:], H[:], ["sC_H"], ["sC_Hb"])
                    for blk in range(NB):
                        tm, ktm = tmpr.next()
                        p.tt('pool', tm[:].rearrange("p (h q) -> p h q", h=4), xtok[:, blk, :].rearrange("p (h q) -> p h q", h=4),
                             bc(dsk[:, gq * 4:gq * 4 + 4], [128, 4, 64], 2), ALU.mult, ["sC_xtok", "s_dsk"], [ktm])
                        p.tt('pool', Y[:, blk, :], Y[:, blk, :], tm[:], ALU.add, [ktm, ("sC_Y", blk)], [("sC_Y", blk)])
                    p.dma('sp', c.Ys[:, gq * 256:(gq + 1) * 256].rearrange("(b p) q -> p b q", p=128), Y[:],
                          [("sC_Y", blk) for blk in range(NB)], [("Ys", gq)])
            p.barrier()
            with contextlib.ExitStack() as es2:
                wo = sb(c, es2, "sD_wo", [128, 16, D], BF16)
                yr = ring_sb(c, es2, "sD_y", 2, [128, 2048], F32)
                zr = ring_sb(c, es2, "sD_z", 2, [128, 2048], F32)
                junk = sb(c, es2, "sD_junk", [128, 2048], BF16)
                yn = ring_sb(c, es2, "sD_yn", 2, [128, 2048], BF16)
                st = ring_sb(c, es2, "sD_st", 2, [128, 4], F32)
                ynT = sb(c, es2, "sD_ynT", [128, 16, 512], BF16)
                xr = ring_sb(c, es2, "sD_x", 1, [128, KC, 512], F32)
                xo = sb(c, es2, "sD_xo", [128, KC, 512], F32)
                pT = ring_ps(c, es2, "sD_pT", 2, [128, 1024], BF16)
                pyr = ring_ps(c, es2, "sD_py", 2, [128, 512])
                wl = WLoader(c, es2, "sD_wst", 1024)
                for cc in range(16):
                    wl.load(wo[:, cc, :], w_out[cc * 128:(cc + 1) * 128, :], "sD_wo")
                for gi in range(L // 512):
                    tok0 = s * L + gi * 512
                    xi, kx = xr.next()
                    p.dma('sp', xi[:], xt_view(c, tok0, 512), xt_keys(tok0, 512), [kx])
                    for qb in range(4):
                        blk = gi * 4 + qb
                        y, ky = yr.next()
                        z, kz = zr.next()
                        p.dma('sp', y[:], c.Ys[blk * 128:(blk + 1) * 128, :], [("Ys", q) for q in range(8)], [ky])
                        p.dma('sp', z[:], c.Zs[blk * 128:(blk + 1) * 128, :], [("Zs", blk, q) for q in range(4)], [kz])
                        p.tt('dve', y[:], y[:], z[:], ALU.mult, [ky, kz], [ky])
                        s_, ks = st.next()
                        p.op('act', lambda e, y=y, s_=s_: e.activation(junk[:], y[:], AF.Square, accum_out=s_[:, 0:1]),
                             [ky], ["sD_junk", ks])
                        p.ts('dve', s_[:, 1:2], s_[:, 0:1], 1.0 / 2048, 1e-6, ALU.mult, ALU.add, [ks], [ks])
                        p.tt('pool', s_[:, 2:3], s_[:, 1:2], c.mhalf[:, 0:1], ALU.pow, [ks], [ks])
                        yn_, kyn = yn.next()
                        p.stt(yn_[:], y[:], s_[:, 2:3], nw[:], ALU.mult, ALU.mult, [ky, ks, "s_nw"], [kyn])
                        for b2 in range(2):
                            pt, kpt = pT.next()
                            for q in range(8):
                                cc = b2 * 8 + q
                                p.tr(pt[:, q * 128:(q + 1) * 128], yn_[:, cc * 128:(cc + 1) * 128], c.identb[:], [kyn], [kpt])
                            p.copy('act' if b2 else 'dve', ynT[:, b2 * 8:(b2 + 1) * 8, qb * 128:(qb + 1) * 128],
                                   pt[:].rearrange("p (a b) -> p a b", a=8), [kpt], ["sD_ynT"])
                    for dc in range(KC):
                        py, kpy = pyr.next()
                        for cc in range(16):
                            p.mm(py[:], wo[:, cc, dc * 128:(dc + 1) * 128], ynT[:, cc, :], cc == 0, cc == 15, ["sD_wo", "sD_ynT"], [kpy])
                        p.tt('dve', xo[:, dc, :], py[:], xi[:, dc, :], ALU.add, [kpy, kx], ["sD_xo"])
                    p.dma('sp', xt_view(c, tok0, 512), xo[:], ["sD_xo"], xt_keys(tok0, 512))
            p.barrier()


def conv_chunk(c, wc_ap, kwc, xn, cw, cb, cc, rw, krw, ac, kac, pA, keypre):
    p = c.p
    for tg in range(L // 512):
        pa, kpa = pA.next()
        for kc in range(KC):
            p.mm(pa[:], wc_ap[:, kc, :], xn[:, kc, tg * 512:(tg + 1) * 512], kc == 0, kc == KC - 1, [kwc, keypre + "xn"], [kpa])
        p.copy('act', rw[:, 2 + tg * 512:2 + (tg + 1) * 512], pa[:], [kpa], [krw])
    p.ts('dve', ac[:], rw[:, 0:L], cw[:, 0, cc:cc + 1], cb[:, cc:cc + 1], ALU.mult, ALU.add, [krw, keypre + "cw", keypre + "cb"], [kac])
    for tap in range(1, 5):
        p.stt(ac[:], rw[:, tap:tap + L], cw[:, tap, cc:cc + 1], ac[:], ALU.mult, ALU.add, [krw, keypre + "cw", kac], [kac])


def neumann_inverse_T(c, units, nlev):
    p = c.p
    for u in units:
        p.tr(u['pN'][:, 384:512], u['Mt'][0][:], c.ident[:], [u['k'] + "Mt0"], [u['k'] + "pN"])
        p.copy('act', u['M'][0][:], u['pN'][:, 384:512], [u['k'] + "pN"], [u['k'] + "M0"])
        p.tt('pool', u['Y'][0][:], u['Mt'][0][:], c.ident[:], ALU.add, [u['k'] + "Mt0"], [u['k'] + "Y0"])
    for k in range(nlev):
        a, b = k % 2, (k + 1) % 2
        last = k == nlev - 1
        for u in units:
            kk = u['k']
            p.mm(u['pN'][:, 0:128], u['Mt'][a][:], u['M'][a][:], True, True, [kk + f"Mt{a}", kk + f"M{a}"], [kk + "pN"])
            if not last:
                p.mm(u['pN'][:, 128:256], u['M'][a][:], u['Mt'][a][:], True, True, [kk + f"Mt{a}", kk + f"M{a}"], [kk + "pN"])
        for u in units:
            kk = u['k']
            p.copy('act', u['M'][b][:], u['pN'][:, 0:128], [kk + "pN"], [kk + f"M{b}"])
            if not last:
                p.copy('dve', u['Mt'][b][:], u['pN'][:, 128:256], [kk + "pN"], [kk + f"Mt{b}"])
        for u in units:
            kk = u['k']
            p.mm(u['pN'][:, 256:384], u['M'][b][:], u['Y'][a][:], True, True, [kk + f"M{b}", kk + f"Y{a}"], [kk + "pN"])
        for u in units:
            kk = u['k']
            if last:
                p.tt('dve', u['XT'][:], u['Y'][a][:], u['pN'][:, 256:384], ALU.add, [kk + f"Y{a}", kk + "pN"], [kk + "XT"])
            else:
                p.tt('dve', u['Y'][b][:], u['Y'][a][:], u['pN'][:, 256:384], ALU.add, [kk + f"Y{a}", kk + "pN"], [kk + f"Y{b}"])


def stage_gdn(c, layer, j, nseq):
    p = c.p
    W = c.W
    w_in, conv_w, conv_b = W['gdn_w_in'][j], W['gdn_conv_w'][j], W['gdn_conv_b'][j]
    a_log, dt_bias, norm_w, w_out = W['gdn_a_log'][j], W['gdn_dt_bias'][j], W['gdn_norm'][j], W['gdn_w_out'][j]
    gvec = W['mix_norm'][layer]
    NB = L // 128
    with contextlib.ExitStack() as es:
        g = sb(c, es, "g_g", [128, KC], F32)
        cw = sb(c, es, "g_cw", [128, 5, 32], F32)
        cb = sb(c, es, "g_cb", [128, 32], F32)
        dtb = sb(c, es, "g_dtb", [128, 32], F32)
        aneg = sb(c, es, "g_aneg", [128, 32], F32)
        nw = sb(c, es, "g_nw", [128, 128], F32)
        xn = sb(c, es, "g_xn", [128, KC, L], BF16)
        bga = sb(c, es, "g_bga", [128, NB, 48], F32)
        load_vec(c, g[:], gvec, "g_g")
        for tap in range(5):
            p.dma('sp', cw[:, tap, :], conv_w[tap].rearrange("(cc p) -> p cc", p=128), [], ["g_cw"], allow_slow_non_contiguous=True)
        p.dma('sp', cb[:], conv_b.rearrange("(cc p) -> p cc", p=128), [], ["g_cb"], allow_slow_non_contiguous=True)
        p.dma('sp', dtb[:], dt_bias.rearrange("a b -> (a b)").partition_broadcast(128), [], ["g_dtb"])
        p.dma('sp', aneg[:], a_log.rearrange("a b -> (a b)").partition_broadcast(128), [], ["g_aneg"])
        p.dma('sp', nw[:], norm_w.partition_broadcast(128), [], ["g_nw"])
        p.act(aneg[:], aneg[:], AF.Exp, ["g_aneg"], ["g_aneg"])
        p.ts('dve', aneg[:], aneg[:], -1.0, None, ALU.mult, None, ["g_aneg"], ["g_aneg"])
        p.barrier()
        for s in range(nseq):
            with contextlib.ExitStack() as es2:
                NT = 256
                xr = ring_sb(c, es2, "gA_x", 2, [128, KC, NT], F32)
                sq = sb(c, es2, "gA_sq", [128, KC, NT], F32)
                var = sb(c, es2, "gA_var", [128, NT], F32)
                rstd = sb(c, es2, "gA_rstd", [128, NT], F32)
                pss = ps(c, es2, "gA_pss", [128, 512])
                for gi in range(L // NT):
                    load_xn(c, s * L + gi * NT, NT, xr, sq, pss, var, rstd, g, "g_g", xn[:, :, gi * NT:(gi + 1) * NT], "g_xn", "gA_")
            p.barrier()
            with contextlib.ExitStack() as es2:
                wzr = ring_sb(c, es2, "gB_wz", 2, [128, KC, 512], BF16)
                wdt = sb(c, es2, "gB_wdt", [128, KC, 48], BF16)
                zr = ring_sb(c, es2, "gB_z", 3, [128, 512], F32)
                pzr = ring_ps(c, es2, "gB_pz", 4, [128, 512])
                pdr = ring_ps(c, es2, "gB_pd", 2, [128, 512])
                wl = WLoader(c, es2, "gB_wst", 4096)
                wl.load(wdt[:], w_in[:, 6144:6192].rearrange("(kc p) c -> p kc c", p=128), "gB_wdt")
                for blk in range(NB):
                    pd, kpd = pdr.next()
                    for kc in range(KC):
                        p.mm(pd[:, :48], xn[:, kc, blk * 128:(blk + 1) * 128], wdt[:, kc, :], kc == 0, kc == KC - 1,
                             ["g_xn", "gB_wdt"], [kpd])
                    p.copy('dve', bga[:, blk, 0:16], pd[:, 0:16], [kpd], ["g_bga"])
                    p.tt('dve', bga[:, blk, 16:48], pd[:, 16:48], dtb[:], ALU.add, [kpd, "g_dtb"], ["g_bga"])
                p.act(bga[:, :, 0:16], bga[:, :, 0:16], AF.Exp, ["g_bga"], ["g_bga"], scale=-1.0)
                p.ts('dve', bga[:, :, 0:16], bga[:, :, 0:16], 1.0, None, ALU.add, None, ["g_bga"], ["g_bga"])
                p.op('dve', lambda e: e.reciprocal(bga[:, :, 0:16], bga[:, :, 0:16]), ["g_bga"], ["g_bga"])
                p.act(bga[:, :, 16:48], bga[:, :, 16:48], AF.Exp, ["g_bga"], ["g_bga"])
                p.act(bga[:, :, 16:48], bga[:, :, 16:48], AF.Ln, ["g_bga"], ["g_bga"], bias=1.0)
                p.tt('dve', bga[:, :, 16:48], bga[:, :, 16:48], bc(aneg[:], [128, NB, 32], 1), ALU.mult, ["g_bga", "g_aneg"], ["g_bga"])
                for zc in range(4):
                    wz, kwz = wzr.next()
                    wl.load(wz[:], w_in[:, 4096 + zc * 512:4096 + (zc + 1) * 512].rearrange("(kc p) c -> p kc c", p=128), kwz)
                    for blk in range(NB):
                        pz, kpz = pzr.next()
                        for kc in range(KC):
                            p.mm(pz[:], xn[:, kc, blk * 128:(blk + 1) * 128], wz[:, kc, :], kc == 0, kc == KC - 1,
                                 ["g_xn", kwz], [kpz])
                        z, kz = zr.next()
                        p.act(z[:], pz[:], AF.Silu, [kpz], [kz])
                        p.dma('sp', c.Zs[blk * 128:(blk + 1) * 128, zc * 512:(zc + 1) * 512], z[:], [kz], [("Zs", blk, zc)])
            p.barrier()
            with contextlib.ExitStack() as es2:
                wcr = ring_sb(c, es2, "gC_wc", 3, [128, KC, 128], BF16)
                wl = WLoader(c, es2, "gC_wst", 1024)
                raw = ring_sb(c, es2, "gC_raw", 2, [128, L + 4], F32)
                acc = ring_sb(c, es2, "gC_acc", 2, [128, L], F32)
                t32a = sb(c, es2, "gC_t32a", [128, L], F32)
                t32b = sb(c, es2, "gC_t32b", [128, L], F32)
                cvT = ring_sb(c, es2, "gC_cvT", 2, [128, L], BF16)
                QhT = sb(c, es2, "gC_QhT", [128, L], BF16)
                KhT = sb(c, es2, "gC_KhT", [128, L], BF16)
                Ktok = sb(c, es2, "gC_Ktok", [128, NB, 128], BF16)
                Vtok = sb(c, es2, "gC_Vtok", [128, NB, 256], BF16)
                KKT = sb(c, es2, "gC_KKT", [128, NB, 128], F32)
                QKT = sb(c, es2, "gC_QKT", [128, NB, 128], F32)
                gq = sb(c, es2, "gC_gq", [128, NB, 2, 2], F32)
                G = sb(c, es2, "gC_G", [128, NB, 2, 2], F32)
                tot = sb(c, es2, "gC_tot", [128, NB, 2, 2], F32)
                eG = sb(c, es2, "gC_eG", [128, NB, 2, 2], F32)
                neG = sb(c, es2, "gC_neG", [128, NB, 2, 2], F32)
                edec = sb(c, es2, "gC_edec", [128, NB, 2, 2], F32)
                etot = sb(c, es2, "gC_etot", [128, NB, 2, 2], F32)
                nbeta = sb(c, es2, "gC_nbeta", [128, NB, 2], F32)
                O = sb(c, es2, "gC_O", [128, NB, 256], F32)
                units = []
                for u in range(4):
                    ud = {'k': f"gU{u}_", 'd': u // 2, 'e': u % 2}
                    ud['M'] = [sb(c, es2, f"gC_M{u}{i}", [128, 128], F32) for i in range(2)]
                    ud['Mt'] = [sb(c, es2, f"gC_Mt{u}{i}", [128, 128], F32) for i in range(2)]
                    ud['Y'] = [sb(c, es2, f"gC_Y{u}{i}", [128, 128], F32) for i in range(2)]
                    ud['XT'] = sb(c, es2, f"gC_XT{u}", [128, 128], BF16)
                    ud['S'] = sb(c, es2, f"gC_S{u}", [128, 128], F32)
                    ud['Sb'] = sb(c, es2, f"gC_Sb{u}", [128, 128], BF16)
                    ud['AT'] = sb(c, es2, f"gC_AT{u}", [128, 128], BF16)
                    ud['R'] = sb(c, es2, f"gC_R{u}", [128, 128], BF16)
                    ud['Vn'] = sb(c, es2, f"gC_Vn{u}", [128, 128], BF16)
                    ud['Kd'] = sb(c, es2, f"gC_Kd{u}", [128, 128], BF16)
                    ud['tmp'] = sb(c, es2, f"gC_tmp{u}", [128, 128], F32)
                    ud['pN'] = ps(c, es2, f"gC_pN{u}", [128, 512])
                    PSUM_KEYS.add(ud['k'] + "pN")
                    units.append(ud)
                rhsd = [sb(c, es2, f"gC_rhs{d}", [128, 2, 128], F32) for d in range(2)]
                decd = [sb(c, es2, f"gC_dec{d}", [128, 2, 128], F32) for d in range(2)]
                decm = [sb(c, es2, f"gC_decm{d}", [128, 2, 128], F32) for d in range(2)]
                decs = [sb(c, es2, f"gC_decs{d}", [128, 2, 128], F32) for d in range(2)]
                pSeg = ps(c, es2, "gC_pSeg", [128, 512])
                PSUM_KEYS.add("gC_pSeg")
                PSUM_KEYS.add("gC_pM0")
                pA = ring_ps(c, es2, "gC_pA", 2, [128, 512])
                pT = ps(c, es2, "gC_pT", [128, 1024], BF16) if False else None
                for tl, ktl in zip(raw.tiles, ["gC_raw0", "gC_raw1"]):
                    p.memset('pool', tl[:, 0:2], 0.0, [ktl])
                    p.memset('pool', tl[:, L + 2:L + 4], 0.0, [ktl])
                slot_i = [0]

                pM = ring_ps(c, es2, "gC_pM", 1, [128, 512])

                def slot():
                    i = slot_i[0] % 3
                    slot_i[0] += 1
                    if i < 2:
                        return pA.tiles[i][:, 0:128], pA.name + str(i)
                    return pM.tiles[0][:, 0:128], "gC_pM0"

                def tslot():
                    return slot()

                for hk in range(8):
                    for ci, cc in enumerate([hk, 8 + hk, 16 + 2 * hk, 17 + 2 * hk]):
                        wc, kwc = wcr.next()
                        wl.load(wc[:], w_in[:, cc * 128:(cc + 1) * 128].rearrange("(kc p) c -> p kc c", p=128), kwc)
                        rw, krw = raw.next()
                        ac, kac = acc.next()
                        conv_chunk(c, wc, kwc, xn, cw, cb, cc, rw, krw, ac, kac, pA, "g_")
                        if ci < 2:
                            p.act(t32a[:], ac[:], AF.Silu, [kac], ["gC_t32a"])
                            p.act(t32b[:], t32a[:], AF.Square, ["gC_t32a"], ["gC_t32b"])
                            for tg in range(L // 512):
                                pa, kpa = pA.next()
                                p.mm(pa[:], c.ones[:], t32b[:, tg * 512:(tg + 1) * 512], True, True, ["gC_t32b"], [kpa])
                                p.ts('dve', ac[:, tg * 512:(tg + 1) * 512], pa[:], 1e-6, None, ALU.add, None, [kpa], [kac])
                            p.tt('pool', ac[:], ac[:], bc(c.mhalf[:, 0:1], [128, L], 1) if False else c.mhalf[:, 0:1].broadcast_to([128, L]), ALU.pow, [kac], [kac])
                            dstT, kd = (QhT, "gC_QhT") if ci == 0 else (KhT, "gC_KhT")
                            scl = 128.0 ** -0.5 if ci == 0 else 1.0
                            p.stt(dstT[:], t32a[:], scl, ac[:], ALU.mult, ALU.mult, ["gC_t32a", kac], [kd])
                            if ci == 1:
                                for blk in range(NB):
                                    sl, ksl = slot()
                                    p.mm(sl, KhT[:, blk * 128:(blk + 1) * 128], c.identb[:], True, True, ["gC_KhT"], [ksl])
                                    p.copy('act' if blk % 2 else 'dve', Ktok[:, blk, :], sl, [ksl], ["gC_Ktok"])
                        else:
                            cv, kcv = cvT.next()
                            p.act(cv[:], ac[:], AF.Silu, [kac], [kcv])
                            for blk in range(NB):
                                sl, ksl = slot()
                                p.mm(sl, cv[:, blk * 128:(blk + 1) * 128], c.identb[:], True, True, [kcv], [ksl])
                                p.copy('act' if blk % 2 else 'dve', Vtok[:, blk, (ci - 2) * 128:(ci - 1) * 128], sl, [ksl], ["gC_Vtok"])
                    for d in range(2):
                        p.copy('dve', gq[:, :, d, :], bga[:, :, 16 + d * 16 + 2 * hk:16 + d * 16 + 2 * hk + 2], ["g_bga"], ["gC_gq"])
                    p.ts('dve', nbeta[:], bga[:, :, 2 * hk:2 * hk + 2], -1.0, None, ALU.mult, None, ["g_bga"], ["gC_nbeta"])
                    pa, kpa = pA.next()
                    for d in range(2):
                        msk = c.mk["LE"] if d == 0 else c.mk["GE"]
                        for blk in range(NB):
                            p.mm(pa[:, (blk * 2 + d) * 2:(blk * 2 + d) * 2 + 2], msk[:], gq[:, blk, d, :], True, True, ["gC_gq", "mk"], [kpa])
                    p.copy('dve', G[:].rearrange("p b d h -> p (b d h)"), pa[:, :NB * 4], [kpa], ["gC_G"])
                    pa, kpa = pA.next()
                    p.mm(pa[:, :NB * 4], c.ones[:], gq[:].rearrange("p b d h -> p (b d h)"), True, True, ["gC_gq"], [kpa])
                    p.copy('dve', tot[:].rearrange("p b d h -> p (b d h)"), pa[:, :NB * 4], [kpa], ["gC_tot"])
                    fl = "p b d h -> p (b d h)"
                    p.act(eG[:].rearrange(fl), G[:].rearrange(fl), AF.Exp, ["gC_G"], ["gC_eG"])
                    p.ts('dve', neG[:].rearrange(fl), eG[:].rearrange(fl), -1.0, None, ALU.mult, None, ["gC_eG"], ["gC_neG"])
                    p.act(etot[:].rearrange(fl), tot[:].rearrange(fl), AF.Exp, ["gC_tot"], ["gC_etot"])
                    p.tt('dve', edec[:].rearrange(fl), tot[:].rearrange(fl), G[:].rearrange(fl), ALU.subtract, ["gC_tot", "gC_G"], ["gC_edec"])
                    p.act(edec[:].rearrange(fl), edec[:].rearrange(fl), AF.Exp, ["gC_edec"], ["gC_edec"])
                    for blk in range(NB):
                        tsl = slice(blk * 128, (blk + 1) * 128)
                        sl, ksl = slot()
                        p.mm(sl, KhT[:, tsl], KhT[:, tsl], True, True, ["gC_KhT"], [ksl])
                        p.copy('act', KKT[:, blk, :], sl, [ksl], ["gC_KKT"])
                        sl, ksl = slot()
                        p.mm(sl, KhT[:, tsl], QhT[:, tsl], True, True, ["gC_KhT", "gC_QhT"], [ksl])
                        p.copy('dve', QKT[:, blk, :], sl, [ksl], ["gC_QKT"])
                    for step in range(NB):
                        first = step == 0
                        lastc = step == NB - 1
                        blks = [step, NB - 1 - step]
                        for d in range(2):
                            blk = blks[d]
                            mk_l, mk_r, mk_i, mk_s = (("GT", "LE", "LE", "LT") if d == 0 else ("LT", "GE", "GE", "GT"))
                            p.tt('pool', rhsd[d][:], bc(c.mk[mk_r][:], [128, 2, 128], 1), bc(gq[:, blk, d, :], [128, 2, 128], 2), ALU.mult,
                                 ["mk", "gC_gq"], [f"gC_rhs{d}"])
                            p.mm(pSeg[:, d * 256:(d + 1) * 256], c.mk[mk_l][:], rhsd[d][:].rearrange("p h l -> p (h l)"), True, True,
                                 [f"gC_rhs{d}", "mk"], ["gC_pSeg"])
                            p.act(decd[d][:].rearrange("p h l -> p (h l)"), pSeg[:, d * 256:(d + 1) * 256], AF.Exp, ["gC_pSeg"], [f"gC_dec{d}"])
                            p.tt('dve', decm[d][:], decd[d][:], bc(c.mk[mk_i][:], [128, 2, 128], 1), ALU.mult, [f"gC_dec{d}", "mk"], [f"gC_decm{d}"])
                            p.tt('pool', decs[d][:], decd[d][:], bc(c.mk[mk_s][:], [128, 2, 128], 1), ALU.mult, [f"gC_dec{d}", "mk"], [f"gC_decs{d}"])
                        for u in units:
                            d, e, kk = u['d'], u['e'], u['k']
                            blk = blks[d]
                            p.tt('dve', u['AT'][:], QKT[:, blk, :], decm[d][:, e, :], ALU.mult, ["gC_QKT", f"gC_decm{d}"], [kk + "AT"])
                            p.stt(u['Mt'][0][:], KKT[:, blk, :], nbeta[:, blk, e:e + 1], decs[d][:, e, :], ALU.mult, ALU.mult,
                                  ["gC_KKT", "gC_nbeta", f"gC_decs{d}"], [kk + "Mt0"])
                        neumann_inverse_T(c, units, 6)
                        sls = {}
                        for u in units:
                            d, e, kk = u['d'], u['e'], u['k']
                            blk = blks[d]
                            tsl = slice(blk * 128, (blk + 1) * 128)
                            vt = Vtok[:, blk, e * 128:(e + 1) * 128]
                            if not first:
                                sl, ksl = slot()
                                p.mm(sl, KhT[:, tsl], u['Sb'][:], True, True, ["gC_KhT", kk + "Sb"], [ksl])
                                p.stt(u['R'][:], sl, neG[:, blk, d, e:e + 1], vt, ALU.mult, ALU.add, [ksl, "gC_neG", "gC_Vtok"], [kk + "R"])
                                rr = u['R'][:]
                                krr = kk + "R"
                            else:
                                rr = vt
                                krr = "gC_Vtok"
                            sl, ksl = slot()
                            p.mm(sl, u['XT'][:], rr, True, True, [kk + "XT", krr], [ksl])
                            p.ts('dve', u['Vn'][:], sl, bga[:, blk, 2 * hk + e:2 * hk + e + 1], None, ALU.mult, None, [ksl, "g_bga"], [kk + "Vn"])
                        for u in units:
                            d, e, kk = u['d'], u['e'], u['k']
                            blk = blks[d]
                            tsl = slice(blk * 128, (blk + 1) * 128)
                            okey = ("gC_O", blk, e)
                            sl2, ksl2 = slot()
                            p.mm(sl2, u['AT'][:], u['Vn'][:], True, True, [kk + "AT", kk + "Vn"], [ksl2])
                            oap = O[:, blk, e * 128:(e + 1) * 128]
                            if not first:
                                sl, ksl = slot()
                                p.mm(sl, QhT[:, tsl], u['Sb'][:], True, True, ["gC_QhT", kk + "Sb"], [ksl])
                                p.ts('dve', u['tmp'][:], sl, eG[:, blk, d, e:e + 1], None, ALU.mult, None, [ksl, "gC_eG"], [kk + "tmp"])
                                p.tt('dve', u['tmp'][:], u['tmp'][:], sl2, ALU.add, [kk + "tmp", ksl2], [kk + "tmp"])
                                src, ksrc = u['tmp'][:], kk + "tmp"
                            else:
                                src, ksrc = sl2, ksl2
                            first_visit = (d == 0 and blk < NB // 2) or (d == 1 and blk >= NB // 2)
                            if first_visit:
                                p.copy('dve' if first else 'pool', oap, src, [ksrc], [okey]) if not first else p.copy('dve', oap, src, [ksrc], [okey])
                            else:
                                p.tt('dve' if first else 'pool', oap, oap, src, ALU.add, [ksrc, okey], [okey]) if not first else p.tt('dve', oap, oap, src, ALU.add, [ksrc, okey], [okey])
                            if not lastc:
                                p.ts('pool', u['Kd'][:], Ktok[:, blk, :], edec[:, blk, d, e:e + 1], None, ALU.mult, None, ["gC_Ktok", "gC_edec"], [kk + "Kd"])
                                sl, ksl = slot()
                                p.mm(sl, u['Kd'][:], u['Vn'][:], True, True, [kk + "Kd", kk + "Vn"], [ksl])
                                if first:
                                    p.copy('dve', u['S'][:], sl, [ksl], [kk + "S"])
                                else:
                                    p.stt(u['S'][:], u['S'][:], etot[:, blk, d, e:e + 1], sl, ALU.mult, ALU.add, [kk + "S", "gC_etot", ksl], [kk + "S"])
                                p.copy('act', u['Sb'][:], u['S'][:], [kk + "S"], [kk + "Sb"])
                    p.dma('sp', c.Ys[:, hk * 256:(hk + 1) * 256].rearrange("(b p) q -> p b q", p=128), O[:],
                          [("gC_O", blk, e) for blk in range(NB) for e in range(2)], [("Ys", hk)])
            p.barrier()
            with contextlib.ExitStack() as es2:
                wo = sb(c, es2, "gD_wo", [128, 16, D], BF16)
                yr = ring_sb(c, es2, "gD_y", 2, [128, 2048], F32)
                zr = ring_sb(c, es2, "gD_z", 2, [128, 2048], F32)
                y2 = sb(c, es2, "gD_y2", [128, 2048], F32)
                yn = ring_sb(c, es2, "gD_yn", 2, [128, 2048], BF16)
                st = ring_sb(c, es2, "gD_st", 2, [128, 16], F32)
                ynT = sb(c, es2, "gD_ynT", [128, 16, 512], BF16)
                xr = ring_sb(c, es2, "gD_x", 1, [128, KC, 512], F32)
                xo = sb(c, es2, "gD_xo", [128, KC, 512], F32)
                pT = ring_ps(c, es2, "gD_pT", 2, [128, 1024], BF16)
                pyr = ring_ps(c, es2, "gD_py", 2, [128, 512])
                wl = WLoader(c, es2, "gD_wst", 1024)
                for cc in range(16):
                    wl.load(wo[:, cc, :], w_out[cc * 128:(cc + 1) * 128, :], "gD_wo")
                for gi in range(L // 512):
                    tok0 = s * L + gi * 512
                    xi, kx = xr.next()
                    p.dma('sp', xi[:], xt_view(c, tok0, 512), xt_keys(tok0, 512), [kx])
                    for qb in range(4):
                        blk = gi * 4 + qb
                        y, ky = yr.next()
                        z, kz = zr.next()
                        p.dma('sp', y[:], c.Ys[blk * 128:(blk + 1) * 128, :], [("Ys", q) for q in range(8)], [ky])
                        p.dma('sp', z[:], c.Zs[blk * 128:(blk + 1) * 128, :], [("Zs", blk, q) for q in range(4)], [kz])
                        p.act(y2[:], y[:], AF.Square, [ky], ["gD_y2"])
                        s_, ks = st.next()
                        p.op('dve', lambda e, s_=s_: e.reduce_sum(s_[:], y2[:].rearrange("p (h v) -> p h v", h=16), AX.X), ["gD_y2"], [ks])
                        p.ts('dve', s_[:], s_[:], 1.0 / 128, 1e-6, ALU.mult, ALU.add, [ks], [ks])
                        p.tt('pool', s_[:], s_[:], c.mhalf[:, 0:16], ALU.pow, [ks], [ks])
                        h3 = "p (h v) -> p h v"
                        p.tt('dve', y[:].rearrange(h3, h=16), y[:].rearrange(h3, h=16), bc(s_[:], [128, 16, 128], 2), ALU.mult, [ky, ks], [ky])
                        p.tt('pool', z[:].rearrange(h3, h=16), z[:].rearrange(h3, h=16), bc(nw[:], [128, 16, 128], 1), ALU.mult, [kz, "g_nw"], [kz])
                        yn_, kyn = yn.next()
                        p.tt('dve', yn_[:], y[:], z[:], ALU.mult, [ky, kz], [kyn])
                        for b2 in range(2):
                            pt, kpt = pT.next()
                            for q in range(8):
                                cc = b2 * 8 + q
                                p.tr(pt[:, q * 128:(q + 1) * 128], yn_[:, cc * 128:(cc + 1) * 128], c.identb[:], [kyn], [kpt])
                            p.copy('act' if b2 else 'dve', ynT[:, b2 * 8:(b2 + 1) * 8, qb * 128:(qb + 1) * 128],
                                   pt[:].rearrange("p (a b) -> p a b", a=8), [kpt], ["gD_ynT"])
                    for dc in range(KC):
                        py, kpy = pyr.next()
                        for cc in range(16):
                            p.mm(py[:], wo[:, cc, dc * 128:(dc + 1) * 128], ynT[:, cc, :], cc == 0, cc == 15, ["gD_wo", "gD_ynT"], [kpy])
                        p.tt('dve', xo[:, dc, :], py[:], xi[:, dc, :], ALU.add, [kpy, kx], ["gD_xo"])
                    p.dma('sp', xt_view(c, tok0, 512), xo[:], ["gD_xo"], xt_keys(tok0, 512))
            p.barrier()


def stage_rwkv(c, layer, j, nseq):
    p = c.p
    W = c.W
    gvec = W['mix_norm'][layer]
    x_mu, w_rkv, w0, w1, w2 = W['rwkv_x_mu'][j], W['rwkv_w_rkv'][j], W['rwkv_w0'][j], W['rwkv_w1'][j], W['rwkv_w2'][j]
    a0, a1, a2, g1, g2 = W['rwkv_a0'][j], W['rwkv_a1'][j], W['rwkv_a2'][j], W['rwkv_g1'][j], W['rwkv_g2'][j]
    k_k, k_a, r_k, lnx_w, lnx_b, w_out = (W['rwkv_k_k'][j], W['rwkv_k_a'][j], W['rwkv_r_k'][j], W['rwkv_lnx_w'][j],
                                            W['rwkv_lnx_b'][j], W['rwkv_w_out'][j])
    CH = 64
    NCH = L // CH
    RW = c.RW
    with contextlib.ExitStack() as es:
        g = sb(c, es, "r_g", [128, KC], F32)
        mu = sb(c, es, "r_mu", [128, 6, KC], F32)
        nw0 = sb(c, es, "r_nw0", [128, 2, KC], F32)
        na0 = sb(c, es, "r_na0", [128, KC], F32)
        kkv = sb(c, es, "r_kk", [128, KC], F32)
        kav = sb(c, es, "r_ka", [128, KC], F32)
        omka = sb(c, es, "r_omka", [128, KC], F32)
        rkv = sb(c, es, "r_rk", [128, KC], F32)
        lnw = sb(c, es, "r_lnw", [128, KC], F32)
        lnb = sb(c, es, "r_lnb", [128, KC], F32)
        onesbd = sb(c, es, "r_onesbd", [128, 128], F32)
        mS = [sb(c, es, f"r_mS{d}", [128, 64], F32) for d in range(2)]
        mSn = [sb(c, es, f"r_mSn{d}", [128, 64], F32) for d in range(2)]
        mI = [sb(c, es, f"r_mI{d}", [128, 64], F32) for d in range(2)]
        load_vec(c, g[:], gvec, "r_c")
        for s_ in range(6):
            load_vec(c, mu[:, s_, :], x_mu[s_], "r_c")
        for d in range(2):
            load_vec(c, nw0[:, d, :], w0[d], "r_c")
        for t_, src in [(na0, a0), (kkv, k_k), (kav, k_a), (rkv, r_k), (lnw, lnx_w), (lnb, lnx_b)]:
            load_vec(c, t_[:], src, "r_c")
        p.ts('dve', nw0[:], nw0[:], -1.0, None, ALU.mult, None, ["r_c"], ["r_c"])
        p.ts('dve', na0[:], na0[:], -1.0, None, ALU.mult, None, ["r_c"], ["r_c"])
        p.ts('dve', omka[:], kav[:], -1.0, 1.0, ALU.mult, ALU.add, ["r_c"], ["r_c"])
        p.memset('pool', onesbd[:], 0.0, ["r_c"])
        p.memset('pool', onesbd[0:64, 0:64], 1.0, ["r_c"])
        p.memset('pool', onesbd[64:128, 64:128], 1.0, ["r_c"])
        for d in range(2):
            ns, ni = (("LT", "LE") if d == 0 else ("GT", "GE"))
            for hs in range(2):
                sl_ = slice(hs * 64, (hs + 1) * 64)
                p.copy('pool', mS[d][sl_, :], c.mk[ns][sl_, hs * 64:(hs + 1) * 64], ["mk"], ["r_c"])
                p.copy('pool', mI[d][sl_, :], c.mk[ni][sl_, hs * 64:(hs + 1) * 64], ["mk"], ["r_c"])
            p.ts('dve', mSn[d][:], mS[d][:], -1.0, None, ALU.mult, None, ["r_c"], ["r_c"])
        p.barrier()
        for s in range(nseq):
            with contextlib.ExitStack() as es2:
                NT = 256
                u = sb(c, es2, "rA_u", [128, KC, L + 2], F32)
                xm = sb(c, es2, "rA_xm", [128, KC, L], BF16)
                pss = ps(c, es2, "rA_pss", [128, 512])
                pA = ring_ps(c, es2, "rA_pA", 3, [128, 512])
                with contextlib.ExitStack() as es3:
                    xr = ring_sb(c, es3, "rA_x", 2, [128, KC, NT], F32)
                    sq = sb(c, es3, "rA_sq", [128, KC, NT], F32)
                    var = sb(c, es3, "rA_var", [128, NT], F32)
                    rstd = sb(c, es3, "rA_rstd", [128, NT], F32)
                    p.memset('pool', u[:, :, 0:1], 0.0, ["rA_u"])
                    p.memset('pool', u[:, :, L + 1:L + 2], 0.0, ["rA_u"])
                    for gi in range(L // NT):
                        t0 = gi * NT
                        xi, kx = xr.next()
                        p.dma('sp', xi[:], xt_view(c, s * L + t0, NT), xt_keys(s * L + t0, NT), [kx])
                        rms_stats(c, xi, kx, sq, "rA_sq", pss, "rA_pss", var, "rA_var", rstd, "rA_rstd", NT)
                        for kc in range(KC):
                            p.stt(u[:, kc, 1 + t0:1 + t0 + NT], xi[:, kc, :], g[:, kc:kc + 1], rstd[:], ALU.mult, ALU.mult,
                                  [kx, "r_c", "rA_rstd"], ["rA_u"])

                p.barrier()
                t1 = sb(c, es2, "rA_t1", [128, L], F32)
                t2 = sb(c, es2, "rA_t2", [128, L], F32)
                ost = ring_sb(c, es2, "rA_ost", 2, [128, L], F32)
                wt = sb(c, es2, "rA_wt", [128, KC, 1024], BF16)
                wl1 = sb(c, es2, "rA_wl1", [128, KC, 128], BF16)
                wl2 = sb(c, es2, "rA_wl2", [128, 1024], BF16)
                lT = sb(c, es2, "rA_lT", [128, L], BF16)
                wl = WLoader(c, es2, "rA_wst", 1024)

                def build_xm(si):
                    for kc in range(KC):
                        p.tt('pool', t1[:], u[:, kc, 0:L], u[:, kc, 2:L + 2], ALU.add, ["rA_u"], ["rA_t1"])
                        p.stt(t2[:], t1[:], 0.5, u[:, kc, 1:L + 1], ALU.mult, ALU.subtract, ["rA_t1", "rA_u"], ["rA_t2"])
                        p.stt(xm[:, kc, :], t2[:], mu[:, si, kc:kc + 1], u[:, kc, 1:L + 1], ALU.mult, ALU.add,
                              ["rA_t2", "r_c", "rA_u"], ["rA_xm"])

                def sigmoid_from(o, pa, bias_ap, scale_out, ko, kpa):
                    if bias_ap is not None:
                        p.op('act', lambda e: e.activation(o, pa, AF.Exp, bias=bias_ap, scale=-1.0), [kpa, "r_c"], [ko])
                    else:
                        p.op('act', lambda e: e.activation(o, pa, AF.Exp, scale=-1.0), [kpa], [ko])
                    p.ts('dve', o, o, 1.0, None, ALU.add, None, [ko], [ko])
                    p.op('dve', lambda e: e.reciprocal(o, o), [ko], [ko])
                    if scale_out != 1.0:
                        p.ts('dve', o, o, scale_out, None, ALU.mult, None, [ko], [ko])

                for si in range(3):
                    build_xm(si)
                    for kc in range(KC):
                        wl.load(wt[:, kc, :], w_rkv[si][kc * 128:(kc + 1) * 128, :], "rA_wt")
                    for oc in range(KC):
                        o_, ko = ost.next()
                        for tg in range(L // 512):
                            pa, kpa = pA.next()
                            for kc in range(KC):
                                p.mm(pa[:], wt[:, kc, oc * 128:(oc + 1) * 128], xm[:, kc, tg * 512:(tg + 1) * 512], kc == 0, kc == KC - 1,
                                     ["rA_wt", "rA_xm"], [kpa])
                            p.copy('act' if tg % 2 else 'dve', o_[:, tg * 512:(tg + 1) * 512], pa[:], [kpa], [ko])
                        p.dma('sp', RW[si, oc * 128:(oc + 1) * 128, :], o_[:], [ko], [("RW", si, oc)])
                for si, nm in [(3, 'w0'), (3, 'w1'), (4, 'a'), (5, 'g')]:
                    if nm in ('w0', 'a', 'g'):
                        build_xm(si)
                    if nm[0] == 'w':
                        d = int(nm[1])
                        l1, l2, rank, dsti, bias_t = w1[d], w2[d], 64, 5 + d, nw0[:, d, :]
                    elif nm == 'a':
                        l1, l2, rank, dsti, bias_t = a1, a2, 64, 3, na0
                    else:
                        l1, l2, rank, dsti, bias_t = g1, g2, 128, 4, None
                    wl.load(wl1[:, :, :rank], l1.rearrange("(kc p) r -> p kc r", p=128), "rA_wl1")
                    wl.load(wl2[:rank, :], l2, "rA_wl2")
                    for tg in range(L // 512):
                        pa, kpa = pA.next()
                        for kc in range(KC):
                            p.mm(pa[:rank, :], wl1[:, kc, :rank], xm[:, kc, tg * 512:(tg + 1) * 512], kc == 0, kc == KC - 1,
                                 ["rA_wl1", "rA_xm"], [kpa])
                        dst_ = lT[:rank, tg * 512:(tg + 1) * 512]
                        if nm[0] == 'w':
                            p.act(dst_, pa[:rank, :], AF.Tanh, [kpa], ["rA_lT"])
                        elif nm == 'a':
                            p.copy('act', dst_, pa[:rank, :], [kpa], ["rA_lT"])
                        else:
                            p.op('act', lambda e, dst_=dst_, pa=pa: e.activation(t1[:, :512], pa[:, :], AF.Exp, scale=-1.0), [kpa], ["rA_t1"])
                            p.ts('dve', t1[:, :512], t1[:, :512], 1.0, None, ALU.add, None, ["rA_t1"], ["rA_t1"])
                            p.op('dve', lambda e: e.reciprocal(t1[:, :512], t1[:, :512]), ["rA_t1"], ["rA_t1"])
                            p.copy('dve', dst_, t1[:, :512], ["rA_t1"], ["rA_lT"])
                    for oc in range(KC):
                        o_, ko = ost.next()
                        for tg in range(L // 512):
                            pa, kpa = pA.next()
                            p.mm(pa[:], wl2[:rank, oc * 128:(oc + 1) * 128], lT[:rank, tg * 512:(tg + 1) * 512], True, True,
                                 ["rA_wl2", "rA_lT"], [kpa])
                            osl = o_[:, tg * 512:(tg + 1) * 512]
                            if nm[0] == 'w':
                                sigmoid_from(osl, pa[:], bias_t[:, oc:oc + 1], -0.6065306597126334, ko, kpa)
                            elif nm == 'a':
                                sigmoid_from(osl, pa[:], bias_t[:, oc:oc + 1], 1.0, ko, kpa)
                            else:
                                p.copy('act', osl, pa[:], [kpa], [ko])
                        p.dma('sp', RW[dsti, oc * 128:(oc + 1) * 128, :], o_[:], [ko], [("RW", dsti, oc)])
            p.barrier()
            with contextlib.ExitStack() as es2:
                yT = sb(c, es2, "rC_yT", [128, KC, L], BF16)
                pA = ring_ps(c, es2, "rC_pA", 3, [128, 512])
                es3 = contextlib.ExitStack()
                F = [sb(c, es3, f"rC_F{i}", [128, L], F32) for i in range(8)]
                kF = [f"rC_F{i}" for i in range(8)]
                AR = [sb(c, es3, f"rC_AR{d}", [128, NCH, 2, CH], BF16) for d in range(2)]
                Kt = [sb(c, es3, f"rC_Kt{d}", [128, L], BF16) for d in range(2)]
                Bt = [sb(c, es3, f"rC_Bt{d}", [128, L], BF16) for d in range(2)]
                Kd = sb(c, es3, "rC_Kd", [128, L], BF16)
                Bd = sb(c, es3, "rC_Bd", [128, L], BF16)
                Kdtok = [sb(c, es3, f"rC_Kdtok{d}", [128, NCH, CH], BF16) for d in range(2)]
                Bdtok = [sb(c, es3, f"rC_Bdtok{d}", [128, NCH, CH], BF16) for d in range(2)]
                PC = [sb(c, es3, f"rC_PC{d}", [128, NCH], F32) for d in range(2)]
                Vp = sb(c, es3, "rC_Vp", [128, NCH, CH], BF16)
                Ost = sb(c, es3, "rC_Ost", [128, NCH, CH], F32)
                Obf = sb(c, es3, "rC_Obf", [128, NCH, CH], BF16)
                st = sb(c, es3, "rC_st", [128, NCH, 4], F32)
                units = []
                for uu in range(2):
                    ud = {'k': f"rU{uu}_", 'd': uu}
                    ud['M'] = [sb(c, es3, f"rC_M{uu}{i}", [128, 128], F32) for i in range(2)]
                    ud['Mt'] = [sb(c, es3, f"rC_Mt{uu}{i}", [128, 128], F32) for i in range(2)]
                    ud['Y'] = [sb(c, es3, f"rC_Y{uu}{i}", [128, 128], F32) for i in range(2)]
                    ud['XT'] = sb(c, es3, f"rC_XT{uu}", [128, 128], BF16)
                    ud['AkT'] = sb(c, es3, f"rC_AkT{uu}", [128, 128], BF16)
                    ud['RkT'] = sb(c, es3, f"rC_RkT{uu}", [128, 128], BF16)
                    ud['RbT'] = sb(c, es3, f"rC_RbT{uu}", [128, 128], BF16)
                    ud['S'] = sb(c, es3, f"rC_S{uu}", [128, CH], F32)
                    ud['Sb'] = sb(c, es3, f"rC_Sb{uu}", [128, CH], BF16)
                    ud['R1'] = sb(c, es3, f"rC_R1{uu}", [128, CH], BF16)
                    ud['NU'] = sb(c, es3, f"rC_NU{uu}", [128, CH], BF16)
                    ud['pN'] = ps(c, es3, f"rC_pN{uu}", [128, 512])
                    PSUM_KEYS.add(ud['k'] + "pN")
                    ud['pX'] = ps(c, es3, f"rC_pX{uu}", [128, 512])
                    PSUM_KEYS.add(ud['k'] + "pX")
                    for nm_ in ['Mt', 'AkT', 'RkT', 'RbT']:
                        tl_ = ud[nm_][0] if nm_ == 'Mt' else ud[nm_]
                        p.memset('pool', tl_[:], 0.0, [ud['k'] + (nm_ + "0" if nm_ == 'Mt' else nm_)])
                    units.append(ud)
                pB = ring_ps(c, es3, "rC_pB", 1, [128, 1024], BF16)
                for hp in range(8):
                    rows = slice(hp * 128, (hp + 1) * 128)
                    cs3 = "p (n q) -> p n q"

                    def ld(dstF, idx):
                        p.dma('sp', F[dstF][:], RW[idx, rows, :], [("RW", idx, hp)], [kF[dstF]])

                    def onesbd_bcast(srcF, dstF, add_eps):
                        for tg in range(L // 512):
                            pa, kpa = pA.next()
                            p.mm(pa[:], onesbd[:], F[srcF][:, tg * 512:(tg + 1) * 512], True, True, [kF[srcF], "r_c"], [kpa])
                            if add_eps is not None:
                                p.ts('dve', F[dstF][:, tg * 512:(tg + 1) * 512], pa[:], add_eps, None, ALU.add, None, [kpa], [kF[dstF]])
                            else:
                                p.copy('dve', F[dstF][:, tg * 512:(tg + 1) * 512], pa[:], [kpa], [kF[dstF]])

                    ld(0, 1)
                    ld(1, 3)
                    p.ts('dve', F[6][:], F[0][:], kkv[:, hp:hp + 1], None, ALU.mult, None, [kF[0], "r_c"], [kF[6]])
                    p.act(F[7][:], F[6][:], AF.Square, [kF[6]], [kF[7]])
                    onesbd_bcast(7, 7, 1e-6) if False else None
                    onesbd_bcast(7, 2, 1e-6)
                    p.tt('pool', F[2][:], F[2][:], c.mhalf[:, 0:1].broadcast_to([128, L]), ALU.pow, [kF[2]], [kF[2]])
                    p.tt('dve', F[2][:], F[6][:], F[2][:], ALU.mult, [kF[6], kF[2]], [kF[2]])
                    p.ts('dve', F[6][:], F[1][:], kav[:, hp:hp + 1], omka[:, hp:hp + 1], ALU.mult, ALU.add, [kF[1], "r_c"], [kF[6]])
                    p.tt('dve', F[3][:], F[0][:], F[6][:], ALU.mult, [kF[0], kF[6]], [kF[3]])
                    p.tt('dve', F[4][:], F[2][:], F[1][:], ALU.mult, [kF[2], kF[1]], [kF[4]])
                    ld(5, 0)
                    ld(0, 2)
                    p.stt(F[6][:], F[5][:], rkv[:, hp:hp + 1], F[3][:], ALU.mult, ALU.mult, [kF[5], "r_c", kF[3]], [kF[6]])
                    onesbd_bcast(6, 7, None)
                    p.tt('dve', F[1][:], F[7][:], F[0][:], ALU.mult, [kF[7], kF[0]], [kF[1]])
                    p.copy('act', Kd[:], F[0][:], [kF[0]], ["rC_Kd"])

                    def to_stacked(srcT, ksrc, dst, kdst):
                        for c8 in range(NCH // 8):
                            pb, kpb = pB.next()
                            for q in range(8):
                                ch = c8 * 8 + q
                                for hs in range(2):
                                    sl_ = slice(hs * 64, (hs + 1) * 64)
                                    p.mm64(pA.tiles[0][sl_, q * 64:(q + 1) * 64], srcT[sl_, ch * CH:(ch + 1) * CH], c.identb[sl_, sl_], True, True,
                                         [ksrc], ["rC_pA0"])
                            p.copy('act' if c8 % 2 else 'dve', dst[:, c8 * 8:(c8 + 1) * 8, :],
                                   pA.tiles[0][:, :].rearrange("p (a b) -> p a b", a=8), ["rC_pA0"], [kdst])

                    to_stacked(Kd, "rC_Kd", Vp, "rC_Vp")
                    for d in range(2):
                        ld(0, 5 + d)
                        p.op('dve', lambda e: e.tensor_tensor_scan(F[6][:], c.ones[:, 0:1].broadcast_to([128, L]), F[0][:], 0.0, ALU.mult, ALU.add),
                             [kF[0]], [kF[6]])
                        cs = F[6][:].rearrange(cs3, q=CH)
                        lw = F[0][:].rearrange(cs3, q=CH)
                        li = F[7][:].rearrange(cs3, q=CH)
                        if d == 0:
                            p.tt('dve', st[:, :, 0:1], cs[:, :, 0:1], lw[:, :, 0:1], ALU.subtract, [kF[6], kF[0]], ["rC_st"])
                            p.tt('dve', li, cs, bc(st[:, :, 0], [128, NCH, CH], 2) if False else st[:, :, 0:1].broadcast_to([128, NCH, CH]),
                                 ALU.subtract, [kF[6], "rC_st"], [kF[7]])
                            p.tt('dve', F[6][:], F[7][:], F[0][:], ALU.subtract, [kF[7], kF[0]], [kF[6]])
                            last = CH - 1
                        else:
                            p.copy('dve', st[:, :, 0:1], cs[:, :, CH - 1:CH], [kF[6]], ["rC_st"])
                            p.tt('dve', cs, st[:, :, 0:1].broadcast_to([128, NCH, CH]), cs, ALU.subtract, [kF[6], "rC_st"], [kF[6]])
                            p.tt('dve', F[7][:], F[6][:], F[0][:], ALU.add, [kF[6], kF[0]], [kF[7]])
                            last = 0
                        p.act(F[6][:], F[6][:], AF.Exp, [kF[6]], [kF[6]])
                        p.tt('dve', AR[d][:, :, 0, :], F[2][:].rearrange(cs3, q=CH), F[6][:].rearrange(cs3, q=CH), ALU.mult,
                             [kF[2], kF[6]], [f"rC_AR{d}"])
                        p.act(F[0][:], F[7][:], AF.Exp, [kF[7]], [kF[0]])
                        p.tt('dve', AR[d][:, :, 1, :], F[5][:].rearrange(cs3, q=CH), F[0][:].rearrange(cs3, q=CH), ALU.mult,
                             [kF[5], kF[0]], [f"rC_AR{d}"])
                        p.copy('dve', PC[d][:].unsqueeze(2), F[0][:].rearrange(cs3, q=CH)[:, :, last:last + 1], [kF[0]], [f"rC_PC{d}"])
                        p.act(F[6][:], F[7][:], AF.Exp, [kF[7]], [kF[6]], scale=-1.0)
                        p.tt('dve', Kt[d][:], F[3][:], F[6][:], ALU.mult, [kF[3], kF[6]], [f"rC_Kt{d}"])
                        p.tt('pool', Bt[d][:], F[4][:], F[6][:], ALU.mult, [kF[4], kF[6]], [f"rC_Bt{d}"])
                        p.tt('dve', F[0][:].rearrange(cs3, q=CH), li[:, :, last:last + 1].broadcast_to([128, NCH, CH]), li, ALU.subtract,
                             [kF[7]], [kF[0]])
                        p.act(F[0][:], F[0][:], AF.Exp, [kF[0]], [kF[0]])
                        p.tt('dve', Kd[:], F[3][:], F[0][:], ALU.mult, [kF[3], kF[0]], ["rC_Kd"])
                        p.tt('pool', Bd[:], F[4][:], F[0][:], ALU.mult, [kF[4], kF[0]], ["rC_Bd"])
                        to_stacked(Kd, "rC_Kd", Kdtok[d], f"rC_Kdtok{d}")
                        to_stacked(Bd, "rC_Bd", Bdtok[d], f"rC_Bdtok{d}")
                    for step in range(NCH):
                        first = step == 0
                        lastc = step == NCH - 1
                        chs = [step, NCH - 1 - step]
                        for u_ in units:
                            d, kk_ = u_['d'], u_['k']
                            ch = chs[d]
                            tsl = slice(ch * CH, (ch + 1) * CH)
                            pX = u_['pX']
                            for hs in range(2):
                                sl_ = slice(hs * 64, (hs + 1) * 64)
                                p.mm64(pX[sl_, 0:128], Kt[d][sl_, tsl], AR[d][sl_, ch, :, :].rearrange("p a q -> p (a q)"), True, True,
                                     [f"rC_Kt{d}", f"rC_AR{d}"], [kk_ + "pX"])
                                p.mm64(pX[sl_, 128:256], Bt[d][sl_, tsl], AR[d][sl_, ch, :, :].rearrange("p a q -> p (a q)"), True, True,
                                     [f"rC_Bt{d}", f"rC_AR{d}"], [kk_ + "pX"])
                            for hs in range(2):
                                sl_ = slice(hs * 64, (hs + 1) * 64)
                                cs_ = slice(hs * 64, (hs + 1) * 64)
                                p.tt('dve', u_['AkT'][sl_, cs_], pX[sl_, 0:64], mS[d][sl_, :], ALU.mult, [kk_ + "pX", "r_c"], [kk_ + "AkT"])
                                p.tt('dve', u_['RkT'][sl_, cs_], pX[sl_, 64:128], mI[d][sl_, :], ALU.mult, [kk_ + "pX", "r_c"], [kk_ + "RkT"])
                                p.tt('dve', u_['Mt'][0][sl_, cs_], pX[sl_, 128:192], mSn[d][sl_, :], ALU.mult, [kk_ + "pX", "r_c"], [kk_ + "Mt0"])
                                p.tt('dve', u_['RbT'][sl_, cs_], pX[sl_, 192:256], mI[d][sl_, :], ALU.mult, [kk_ + "pX", "r_c"], [kk_ + "RbT"])
                        neumann_inverse_T(c, units, 5)
                        for u_ in units:
                            d, kk_ = u_['d'], u_['k']
                            ch = chs[d]
                            pX = u_['pX']
                            if not first:
                                for hs in range(2):
                                    sl_ = slice(hs * 64, (hs + 1) * 64)
                                    p.mm64(pX[sl_, 256:320], AR[d][sl_, ch, 0, :], u_['Sb'][sl_, :], True, False, [f"rC_AR{d}", kk_ + "Sb"], [kk_ + "pX"])
                            p.mm(pX[:, 256:320], u_['AkT'][:], Vp[:, ch, :], first, True, [kk_ + "AkT", "rC_Vp"], [kk_ + "pX"])
                            p.copy('act', u_['R1'][:], pX[:, 256:320], [kk_ + "pX"], [kk_ + "R1"])
                        for u_ in units:
                            d, kk_ = u_['d'], u_['k']
                            pX = u_['pX']
                            p.mm(pX[:, 320:384], u_['XT'][:], u_['R1'][:], True, True, [kk_ + "XT", kk_ + "R1"], [kk_ + "pX"])
                            p.op('act', lambda e, u_=u_, pX=pX: e.activation(u_['NU'][:], pX[:, 320:384], AF.Copy, scale=-1.0), [kk_ + "pX"], [kk_ + "NU"])
                        for u_ in units:
                            d, kk_ = u_['d'], u_['k']
                            ch = chs[d]
                            pX = u_['pX']
                            if not first:
                                for hs in range(2):
                                    sl_ = slice(hs * 64, (hs + 1) * 64)
                                    p.mm64(pX[sl_, 384:448], AR[d][sl_, ch, 1, :], u_['Sb'][sl_, :], True, False, [f"rC_AR{d}", kk_ + "Sb"], [kk_ + "pX"])
                            p.mm(pX[:, 384:448], u_['RkT'][:], Vp[:, ch, :], first, False, [kk_ + "RkT", "rC_Vp"], [kk_ + "pX"])
                            p.mm(pX[:, 384:448], u_['RbT'][:], u_['NU'][:], False, True, [kk_ + "RbT", kk_ + "NU"], [kk_ + "pX"])
                            okey = ("rC_O", ch)
                            first_visit = (d == 0 and ch < NCH // 2) or (d == 1 and ch >= NCH // 2)
                            if first_visit:
                                p.copy('dve', Ost[:, ch, :], pX[:, 384:448], [kk_ + "pX"], [okey])
                            else:
                                p.tt('dve', Ost[:, ch, :], Ost[:, ch, :], pX[:, 384:448], ALU.add, [kk_ + "pX", okey], [okey])
                            if not lastc:
                                for hs in range(2):
                                    sl_ = slice(hs * 64, (hs + 1) * 64)
                                    p.mm64(pX[sl_, 448:512], Kdtok[d][sl_, ch, :], Vp[sl_, ch, :], True, False, [f"rC_Kdtok{d}", "rC_Vp"], [kk_ + "pX"])
                                    p.mm64(pX[sl_, 448:512], Bdtok[d][sl_, ch, :], u_['NU'][sl_, :], False, True, [f"rC_Bdtok{d}", kk_ + "NU"], [kk_ + "pX"])
                                if first:
                                    p.copy('dve', u_['S'][:], pX[:, 448:512], [kk_ + "pX"], [kk_ + "S"])
                                else:
                                    p.stt(u_['S'][:], u_['S'][:], PC[d][:, ch:ch + 1], pX[:, 448:512], ALU.mult, ALU.add,
                                          [kk_ + "S", f"rC_PC{d}", kk_ + "pX"], [kk_ + "S"])
                                p.copy('act', u_['Sb'][:], u_['S'][:], [kk_ + "S"], [kk_ + "Sb"])
                    okeys = [("rC_O", ch) for ch in range(NCH)]
                    p.op('dve', lambda e: e.reduce_sum(st[:, :, 0], Ost[:], AX.X), okeys, ["rC_st"])
                    p.ts('dve', st[:, :, 0], st[:, :, 0], 1.0 / CH, None, ALU.mult, None, ["rC_st"], ["rC_st"])
                    p.tt('dve', Ost[:], Ost[:], st[:, :, 0:1].broadcast_to([128, NCH, CH]), ALU.subtract, okeys + ["rC_st"], ["rC_Oc"])
                    F6v = F[6][:].rearrange(cs3, q=CH)
                    p.act(F6v, Ost[:], AF.Square, ["rC_Oc"], [kF[6]])
                    p.op('dve', lambda e: e.reduce_sum(st[:, :, 1], F6v, AX.X), [kF[6]], ["rC_st"])
                    p.ts('dve', st[:, :, 1], st[:, :, 1], 1.0 / CH, 64e-5, ALU.mult, ALU.add, ["rC_st"], ["rC_st"])
                    p.tt('pool', st[:, :, 2], st[:, :, 1], c.mhalf[:, 0:NCH], ALU.pow, ["rC_st"], ["rC_st"])
                    p.tt('dve', Obf[:], Ost[:], st[:, :, 2:3].broadcast_to([128, NCH, CH]), ALU.mult, ["rC_Oc", "rC_st"], ["rC_Obf"])
                    ld(0, 4)
                    for c8 in range(NCH // 8):
                        for q in range(8):
                            ch = c8 * 8 + q
                            for hs in range(2):
                                sl_ = slice(hs * 64, (hs + 1) * 64)
                                p.mm64(pA.tiles[1][sl_, q * 64:(q + 1) * 64], Obf[sl_, ch, :], c.identb[sl_, sl_], True, True, ["rC_Obf"], ["rC_pA1"])
                        fs = slice(c8 * 512, (c8 + 1) * 512)
                        p.ts('dve', F[6][:, fs], pA.tiles[1][:, :], lnw[:, hp:hp + 1], lnb[:, hp:hp + 1], ALU.mult, ALU.add, ["rC_pA1", "r_c"], [kF[6]])
                    p.tt('dve', F[6][:], F[6][:], F[1][:], ALU.add, [kF[6], kF[1]], [kF[6]])
                    p.tt('dve', yT[:, hp, :], F[6][:], F[0][:], ALU.mult, [kF[6], kF[0]], [("rC_yT", hp)])

                p.barrier()
                es3.close()
                wl = WLoader(c, es2, "rD_wst", 1024)
                wo = sb(c, es2, "rD_wo", [128, KC, D], BF16)
                xr = ring_sb(c, es2, "rD_x", 1, [128, KC, 512], F32)
                xo = sb(c, es2, "rD_xo", [128, KC, 512], F32)
                for kc in range(KC):
                    wl.load(wo[:, kc, :], w_out[kc * 128:(kc + 1) * 128, :], "rD_wo")
                for gi in range(L // 512):
                    tok0 = s * L + gi * 512
                    xi, kx = xr.next()
                    p.dma('sp', xi[:], xt_view(c, tok0, 512), xt_keys(tok0, 512), [kx])
                    for dc in range(KC):
                        pa, kpa = pA.next()
                        for kc in range(KC):
                            p.mm(pa[:], wo[:, kc, dc * 128:(dc + 1) * 128], yT[:, kc, gi * 512:(gi + 1) * 512], kc == 0, kc == KC - 1,
                                 ["rD_wo"] + [("rC_yT", kc)], [kpa])
                        p.tt('dve', xo[:, dc, :], pa[:], xi[:, dc, :], ALU.add, [kpa, kx], ["rD_xo"])
                    p.dma('sp', xt_view(c, tok0, 512), xo[:], ["rD_xo"], xt_keys(tok0, 512))
            p.barrier()


def build(nseq_prompt=1, nseq_sample=4, cfg=None):
    cfg = cfg or {}
    depth = cfg.get('depth', 4)
    nseq = nseq_prompt + nseq_sample
    ntok = nseq * L
    nc = bass.Bass("TRN2", target_bir_lowering=False)
    c = Ctx()
    c.nc = nc
    c.cfg = cfg
    din = {}

    def inp(name, shape):
        din[name] = nc.dram_tensor(name, list(shape), F32, kind="ExternalInput").ap()
        return din[name]

    c.xp = inp("x_prompt", [max(nseq_prompt, 1), L, D])
    c.xs = inp("x_sample", [max(nseq_sample, 1), L, D])
    W = {}
    for name, shape in WEIGHT_SHAPES.items():
        W[name] = inp(name, shape)
    c.W = W
    c_ident = inp("c_ident", [128, 128])
    c.c_cos = inp("c_cos", [128, L])
    c.c_sin = inp("c_sin", [128, L])
    c.yp = nc.dram_tensor("y_prompt", [max(nseq_prompt, 1), L, D], F32, kind="ExternalOutput").ap()
    c.ys = nc.dram_tensor("y_sample", [max(nseq_sample, 1), L, D], F32, kind="ExternalOutput").ap()
    c.XT = nc.dram_tensor("XT", [D, ntok], F32).ap()
    c.Zs = nc.dram_tensor("Zs", [L, 2048], F32).ap()
    c.Ys = nc.dram_tensor("Ys", [L, 2048], F32).ap()
    c.RW = nc.dram_tensor("RW", [7, D, L], F32).ap()

    with contextlib.ExitStack() as es:
        p = Prog(nc, es)
        c.p = p
        c.ident = sb(c, es, "ident", [128, 128], F32)
        c.ones = sb(c, es, "ones", [128, 128], F32)
        c.mhalf = sb(c, es, "mhalf", [128, 512], F32)
        c.identb = sb(c, es, "identb", [128, 128], BF16)
        p.dma('sp', c.ident[:], c_ident[:, :], [], ["ident"])
        p.copy('dve', c.identb[:], c.ident[:], ["ident"], ["identb"])
        p.memset('pool', c.ones[:], 1.0, ["ones"])
        p.memset('pool', c.mhalf[:], -0.5, ["mhalf"])
        make_masks(c, es)
        p.barrier()

        srcs = [(c.xp, b) for b in range(nseq_prompt)] + [(c.xs, b) for b in range(nseq_sample)]
        dsts = [(c.yp, b) for b in range(nseq_prompt)] + [(c.ys, b) for b in range(nseq_sample)]
        stage_in(c, srcs)
        for i in range(depth):
            if cfg.get('ffn', True) and cfg.get('ffn1', True):
                stage_ffn(c, W['ffn1_norm'][i], W['ffn1_w_gu'][i], W['ffn1_w_down'][i], ntok)
            if i in cfg.get('mixers', [0, 1, 2, 3]):
                m, jj = i % 4, i // 4
                if m == 0:
                    stage_ssd(c, i, jj, nseq)
                if m == 1:
                    stage_gdn(c, i, jj, nseq)
                if m == 2:
                    stage_attn(c, i, jj, nseq)
                if m == 3:
                    stage_rwkv(c, i, jj, nseq)
            if cfg.get('ffn', True) and cfg.get('ffn2', True):
                stage_ffn(c, W['ffn2_norm'][i], W['ffn2_w_gu'][i], W['ffn2_w_down'][i], ntok)
        stage_out(c, dsts, W['final_norm'])
        p.emit()
    return nc


WEIGHT_SHAPES = {
    'ffn1_norm': (4, 1024), 'ffn1_w_gu': (4, 1024, 5632), 'ffn1_w_down': (4, 2816, 1024),
    'mix_norm': (4, 1024), 'ffn2_norm': (4, 1024), 'ffn2_w_gu': (4, 1024, 5632), 'ffn2_w_down': (4, 2816, 1024),
    'ssd_w_in': (1, 1024, 6208), 'ssd_conv_w': (1, 5, 4096), 'ssd_conv_b': (1, 4096), 'ssd_a_log': (1, 2, 32),
    'ssd_dt_bias': (1, 2, 32), 'ssd_d': (1, 32), 'ssd_norm': (1, 2048), 'ssd_w_out': (1, 2048, 1024),
    'gdn_w_in': (1, 1024, 6192), 'gdn_conv_w': (1, 5, 4096), 'gdn_conv_b': (1, 4096), 'gdn_a_log': (1, 2, 16),
    'gdn_dt_bias': (1, 2, 16), 'gdn_norm': (1, 128), 'gdn_w_out': (1, 2048, 1024),
    'att_w_qkv': (1, 1024, 1536), 'att_sinks': (1, 16), 'att_w_out': (1, 1024, 1024),
    'rwkv_x_mu': (1, 6, 1024), 'rwkv_w_rkv': (1, 3, 1024, 1024), 'rwkv_w0': (1, 2, 1024),
    'rwkv_w1': (1, 2, 1024, 64), 'rwkv_w2': (1, 2, 64, 1024), 'rwkv_a0': (1, 1024), 'rwkv_a1': (1, 1024, 64),
    'rwkv_a2': (1, 64, 1024), 'rwkv_g1': (1, 1024, 128), 'rwkv_g2': (1, 128, 1024), 'rwkv_k_k': (1, 1024),
    'rwkv_k_a': (1, 1024), 'rwkv_r_k': (1, 1024), 'rwkv_lnx_w': (1, 1024), 'rwkv_lnx_b': (1, 1024),
    'rwkv_w_out': (1, 1024, 1024), 'final_norm': (1024,),
}


def consts():
    r = np.arange(128) % 64
    i = (r % 32).astype(np.float64)
    inv_freq = 10000.0 ** (-i / 32.0)
    ang = (np.arange(L, dtype=np.float64)[None, :] * inv_freq[:, None]).astype(np.float32).astype(np.float64)
    sgn = np.where(r < 32, -1.0, 1.0)[:, None]
    return {"c_ident": np.eye(128, dtype=np.float32),
            "c_cos": np.cos(ang).astype(np.float32),
            "c_sin": (np.sin(ang) * sgn).astype(np.float32)}


def kernel(**inputs):
    nc = build(1, 4)
    xp = np.ascontiguousarray(inputs['x_prompt'], dtype=np.float32)
    xs = np.ascontiguousarray(inputs['x_sample'], dtype=np.float32)
    shared = {k: np.ascontiguousarray(inputs[k], dtype=np.float32) for k in WEIGHT_SHAPES}
    shared.update(consts())
    in_maps = []
    for c in range(NCORES):
        m = dict(shared)
        m['x_prompt'] = xp[c:c + 1]
        m['x_sample'] = xs[4 * c:4 * c + 4]
        in_maps.append(m)
    res = run_bass_kernel_spmd(nc, in_maps, core_ids=list(range(NCORES)))
    yp = np.concatenate([r['y_prompt'] for r in res.results], axis=0)
    ys = np.concatenate([r['y_sample'] for r in res.results], axis=0)
    return yp.astype(np.float32), ys.astype(np.float32)
```

```python
import contextlib
import numpy as np
import concourse.bass as bass
import concourse.mybir as mybir
from concourse.alu_op_type import AluOpType as ALU
from concourse.bass_utils import run_bass_kernel_spmd

F32 = mybir.dt.float32
BF16 = mybir.dt.bfloat16
AF = mybir.ActivationFunctionType
AX = mybir.AxisListType

NCORES = 8
D = 1024
L = 2048
DFF = 2816
KC = D // 128
FC = DFF // 128
ENG = ['pe', 'dve', 'act', 'pool', 'sp']
SAME_ENG_SYNC = True


PSUM_KEYS = set()


class Prog:
    def __init__(self, nc, es):
        self.nc = nc
        self.es = es
        self.engs = {'pe': nc.tensor, 'dve': nc.vector, 'act': nc.scalar, 'pool': nc.gpsimd, 'sp': nc.sync}
        self.q = {e: [] for e in ENG}
        self.sem = {e: es.enter_context(nc.semaphore("s_" + e)) for e in ENG}
        self.cnt = {e: 0 for e in ENG}
        self.seen = {e: {} for e in ENG}
        self.lastw = {}
        self.rds = {}
        self.ndsem = {'sp': 6, 'pool': 2, 'act': 2}
        self.dsem = {}
        self.dval = {}
        self.drr = {}
        for qn, n in self.ndsem.items():
            self.dsem[qn] = [es.enter_context(nc.semaphore(f"d_{qn}{i}")) for i in range(n)]
            for i in range(n):
                self.dval[(qn, i)] = 0
            self.drr[qn] = 0
        self.ninstr = 0

    def _semh(self, s):
        if isinstance(s, tuple):
            return self.dsem[s[0]][s[1]]
        return self.sem[s]

    def _wait(self, eng, tok):
        s, v = tok
        if s == eng and (eng == 'pe' or not SAME_ENG_SYNC):
            return
        if self.seen[eng].get(s, 0) >= v:
            return
        self.seen[eng][s] = v
        self.q[eng].append(('w', self._semh(s), v))

    def _deps(self, eng, reads, writes, is_dma=False):
        for k in reads:
            for tok in self.lastw.get(k, ()):
                self._wait(eng, tok)
        for k in writes:
            for tok in self.lastw.get(k, ()):
                if is_dma and isinstance(tok[0], tuple):
                    continue
                self._wait(eng, tok)
            for s, v in self.rds.get(k, {}).items():
                self._wait(eng, (s, v))

    def _record(self, tok, reads, writes, is_dma=False):
        s, v = tok
        for k in reads:
            d = self.rds.setdefault(k, {})
            if d.get(s, 0) < v:
                d[s] = v
        for k in writes:
            if is_dma and not self.rds.get(k) and k in self.lastw and all(isinstance(t[0], tuple) for t in self.lastw[k]):
                self.lastw[k] = [t for t in self.lastw[k] if t[0] != s] + [tok]
            else:
                self.lastw[k] = [tok]
            self.rds[k] = {}

    def op(self, eng, fn, reads=(), writes=()):
        xr = [k for k in reads if k in PSUM_KEYS]
        if xr:
            reads = [k for k in reads if k not in PSUM_KEYS]
            writes = list(writes) + [k for k in xr if k not in writes]
        self._deps(eng, reads, writes)
        self.cnt[eng] += 1
        tok = (eng, self.cnt[eng])
        self.q[eng].append(('o', fn, self.sem[eng], 1))
        self._record(tok, reads, writes)
        self.ninstr += 1

    def dma(self, qn, out, in_, reads=(), writes=(), **kw):
        self._deps(qn, reads, writes, is_dma=True)
        i = self.drr[qn]
        self.drr[qn] = (i + 1) % self.ndsem[qn]
        prev = self.dval[(qn, i)]
        if prev > 0:
            self._wait(qn, ((qn, i), prev))
        self.dval[(qn, i)] = prev + 16
        tok = ((qn, i), prev + 16)
        self.q[qn].append(('o', lambda e: e.dma_start(out=out, in_=in_, **kw), self.dsem[qn][i], 16))
        self._record(tok, reads, writes, is_dma=True)
        self.ninstr += 1

    def barrier(self):
        for e in ENG:
            for e2 in ENG:
                if e2 != e and self.cnt[e2] > 0:
                    self._wait(e, (e2, self.cnt[e2]))
            for k, v in self.dval.items():
                if v > 0:
                    self._wait(e, (k, v))
        self.lastw = {}
        self.rds = {}

    def emit(self):
        nc = self.nc
        with nc.Block() as block:
            def run(e, name):
                for it in self.q[name]:
                    if it[0] == 'w':
                        e.wait_ge(it[1], it[2])
                    else:
                        it[1](e).then_inc(it[2], it[3])

            @block.tensor
            def _(e):
                run(e, 'pe')

            @block.vector
            def _(e):
                run(e, 'dve')

            @block.scalar
            def _(e):
                run(e, 'act')

            @block.gpsimd
            def _(e):
                run(e, 'pool')

            @block.sync
            def _(e):
                run(e, 'sp')

    def mm(self, out, lhsT, rhs, start, stop, reads, writes):
        self.op('pe', lambda e: e.matmul(out, lhsT, rhs, start=start, stop=stop), reads, writes)

    def mm64(self, out, lhsT, rhs, start, stop, reads, writes):
        if self.cnt['pe'] > 0:
            self.q['pe'].append(('w', self.sem['pe'], self.cnt['pe']))
        self.mm(out, lhsT, rhs, start, stop, reads, writes)

    def tr(self, out, in_, ident, reads, writes):
        self.op('pe', lambda e: e.transpose(out, in_, ident), reads, writes)

    def act(self, out, in_, func, reads, writes, bias=None, scale=None):
        kw = {}
        if bias is not None:
            kw['bias'] = bias
        if scale is not None:
            kw['scale'] = scale
        self.op('act', lambda e: e.activation(out, in_, func, **kw), reads, writes)

    def tt(self, eng, out, in0, in1, op, reads, writes):
        self.op(eng, lambda e: e.tensor_tensor(out, in0, in1, op), reads, writes)

    def ts(self, eng, out, in0, s1, s2, op0, op1, reads, writes):
        if op1 is None:
            self.op(eng, lambda e: e.tensor_scalar(out, in0, s1, None, op0), reads, writes)
        else:
            self.op(eng, lambda e: e.tensor_scalar(out, in0, s1, s2, op0, op1), reads, writes)

    def stt(self, out, in0, scalar, in1, op0, op1, reads, writes):
        self.op('dve', lambda e: e.scalar_tensor_tensor(out, in0, scalar, in1, op0, op1), reads, writes)

    def copy(self, eng, out, in_, reads, writes):
        if eng == 'act':
            self.op('act', lambda e: e.copy(out, in_), reads, writes)
        else:
            self.op(eng, lambda e: e.tensor_copy(out, in_), reads, writes)

    def memset(self, eng, ap, val, writes):
        self.op(eng, lambda e: e.memset(ap, val), (), writes)


class Ctx:
    pass


_UID = [0]


def sb(c, es, name, shape, dt):
    _UID[0] += 1
    return es.enter_context(c.nc.sbuf_tensor(f"{name}_{_UID[0]}", shape, dt))


def ps(c, es, name, shape, dt=F32):
    _UID[0] += 1
    return es.enter_context(c.nc.psum_tensor(f"{name}_{_UID[0]}", shape, dt))


def stage_in(c, srcs):
    p = c.p
    with contextlib.ExitStack() as es:
        xin = [sb(c, es, f"in_x{i}", [128, D], F32) for i in range(2)]
        xt = [sb(c, es, f"in_xt{i}", [128, KC, 512], F32) for i in range(2)]
        pt = [ps(c, es, f"in_ps{i}", [128, 512]) for i in range(4)]
        n = 0
        for s, (src, b) in enumerate(srcs):
            for g in range(L // 512):
                xo = xt[g % 2]
                ko = f"in_xt{g % 2}"
                for j in range(4):
                    t0 = g * 512 + j * 128
                    xi = xin[n % 2]
                    ki = f"in_x{n % 2}"
                    p.dma('sp', xi[:], src[b, t0:t0 + 128, :], [], [ki])
                    for half in range(2):
                        pp = pt[(2 * n + half) % 4]
                        kp = f"in_ps{(2 * n + half) % 4}"
                        for q4 in range(4):
                            kc = half * 4 + q4
                            p.tr(pp[:, q4 * 128:(q4 + 1) * 128], xi[:, kc * 128:(kc + 1) * 128], c.ident[:],
                                 [ki], [kp])
                        eng = 'act' if half == 0 else 'dve'
                        p.copy(eng, xo[:, half * 4:half * 4 + 4, j * 128:(j + 1) * 128],
                               pp[:].rearrange("p (a b) -> p a b", a=4), [kp], [ko])
                    n += 1
                tok0 = s * L + g * 512
                p.dma('sp', c.XT[:, tok0:tok0 + 512].rearrange("(kc p) t -> p kc t", p=128), xo[:],
                      [ko], [("XT", tok0 // 256), ("XT", tok0 // 256 + 1)])
    p.barrier()


def load_vec(c, dst, src_ap, key, q='sp'):
    c.p.dma(q, dst, src_ap.rearrange("(j p) -> p j", p=128), [], [key], allow_slow_non_contiguous=True)


def rms_stats(c, x, kx, sq, ksq, pss, kps, var, kvar, rstd, krstd, nt, nch=KC, dim=D, eps=1e-6):
    p = c.p
    p.act(sq[:, :nch, :nt], x[:, :nch, :nt], AF.Square, [kx], [ksq])
    for kc in range(nch):
        p.mm(pss[:, :nt], c.ones[:], sq[:, kc, :nt], kc == 0, kc == nch - 1, [ksq], [kps])
    p.ts('dve', var[:, :nt], pss[:, :nt], 1.0 / dim, eps, ALU.mult, ALU.add, [kps], [kvar])
    p.tt('pool', rstd[:, :nt], var[:, :nt], c.mhalf[:, :nt], ALU.pow, [kvar], [krstd])


def stage_out(c, dsts, gvec):
    p = c.p
    NT = 256
    with contextlib.ExitStack() as es:
        g = sb(c, es, "o_g", [128, KC], F32)
        x = [sb(c, es, f"o_x{i}", [128, KC, NT], F32) for i in range(2)]
        sq = sb(c, es, "o_sq", [128, KC, NT], F32)
        var = sb(c, es, "o_var", [128, NT], F32)
        rstd = sb(c, es, "o_rstd", [128, NT], F32)
        xn = sb(c, es, "o_xn", [128, KC, NT], F32)
        yo = [sb(c, es, f"o_y{i}", [128, D], F32) for i in range(2)]
        pss = ps(c, es, "o_pss", [128, 512])
        pt = [ps(c, es, f"o_pt{i}", [128, 512]) for i in range(4)]
        load_vec(c, g[:], gvec, "o_g")
        n = 0
        m = 0
        for s, (dst, b) in enumerate(dsts):
            for gi in range(L // NT):
                tok0 = s * L + gi * NT
                xi = x[gi % 2]
                kx = f"o_x{gi % 2}"
                p.dma('sp', xi[:], c.XT[:, tok0:tok0 + NT].rearrange("(kc p) t -> p kc t", p=128),
                      [("XT", tok0 // 256)], [kx])
                rms_stats(c, xi, kx, sq, "o_sq", pss, "o_pss", var, "o_var", rstd, "o_rstd", NT)
                for kc in range(KC):
                    p.stt(xn[:, kc, :], xi[:, kc, :], g[:, kc:kc + 1], rstd[:], ALU.mult, ALU.mult,
                          [kx, "o_g", "o_rstd"], ["o_xn"])
                for j in range(NT // 128):
                    y = yo[m % 2]
                    ky = f"o_y{m % 2}"
                    for half in range(2):
                        pp = pt[n % 4]
                        kp = f"o_pt{n % 4}"
                        n += 1
                        for q4 in range(4):
                            kc = half * 4 + q4
                            p.tr(pp[:, q4 * 128:(q4 + 1) * 128], xn[:, kc, j * 128:(j + 1) * 128], c.ident[:],
                                 ["o_xn"], [kp])
                        eng = 'act' if half == 0 else 'dve'
                        p.copy(eng, y[:, half * 512:(half + 1) * 512], pp[:], [kp], [ky])
                    t0 = gi * NT + j * 128
                    p.dma('sp', dst[b, t0:t0 + 128, :], y[:], [ky], [])
                    m += 1
    p.barrier()


def stage_ffn(c, gvec, w_gu, w_down, ntok):
    p = c.p
    NT = 256
    with contextlib.ExitStack() as es:
        wgu = sb(c, es, "f_wgu", [128, KC, 2 * DFF], BF16)
        wd = sb(c, es, "f_wd", [128, FC, D], BF16)
        g = sb(c, es, "f_g", [128, KC], F32)
        x = [sb(c, es, f"f_x{i}", [128, KC, NT], F32) for i in range(2)]
        sq = sb(c, es, "f_sq", [128, KC, NT], F32)
        var = sb(c, es, "f_var", [128, NT], F32)
        rstd = sb(c, es, "f_rstd", [128, NT], F32)
        xn = sb(c, es, "f_xn", [128, KC, NT], BF16)
        h = sb(c, es, "f_h", [128, FC, NT], BF16)
        sg = [sb(c, es, f"f_sg{i}", [128, NT], F32) for i in range(2)]
        pss = ps(c, es, "f_pss", [128, 512])
        pg = [ps(c, es, f"f_pg{i}", [128, 512]) for i in range(2)]
        pu = [ps(c, es, f"f_pu{i}", [128, 512]) for i in range(2)]
        py = [ps(c, es, f"f_py{i}", [128, 512]) for i in range(2)]
        load_vec(c, g[:], gvec, "f_g")
        wl = WLoader(c, es, "f_wst", 1408)
        wl.dbg = c.cfg.get('ffn_dbg', 0)
        for kc in range(KC if wl.dbg != 1 else 0):
            for hh in range(4):
                p_ = wl.load(wgu[:, kc, hh * 1408:(hh + 1) * 1408],
                             w_gu[kc * 128:(kc + 1) * 128, hh * 1408:(hh + 1) * 1408], "f_wgu")
        for j in range(FC if wl.dbg != 1 else 0):
            wl.load(wd[:, j, :], w_down[j * 128:(j + 1) * 128, :], "f_wd")
        for gi in range(c.cfg.get('ffn_groups', ntok // NT)):
            tok0 = gi * NT
            xi = x[gi % 2]
            kx = f"f_x{gi % 2}"
            p.dma('sp', xi[:], c.XT[:, tok0:tok0 + NT].rearrange("(kc p) t -> p kc t", p=128),
                  [("XT", gi)], [kx])
            rms_stats(c, xi, kx, sq, "f_sq", pss, "f_pss", var, "f_var", rstd, "f_rstd", NT)
            for kc in range(KC):
                p.stt(xn[:, kc, :], xi[:, kc, :], g[:, kc:kc + 1], rstd[:], ALU.mult, ALU.mult,
                      [kx, "f_g", "f_rstd"], ["f_xn"])
            for j in range(FC):
                pgj, puj, sgj = pg[j % 2], pu[j % 2], sg[j % 2]
                kg, ku, ks = f"f_pg{j % 2}", f"f_pu{j % 2}", f"f_sg{j % 2}"
                for kc in range(KC):
                    p.mm(pgj[:, :NT], wgu[:, kc, j * 128:(j + 1) * 128], xn[:, kc, :], kc == 0, kc == KC - 1,
                         ["f_wgu", "f_xn"], [kg])
                for kc in range(KC):
                    p.mm(puj[:, :NT], wgu[:, kc, DFF + j * 128:DFF + (j + 1) * 128], xn[:, kc, :], kc == 0,
                         kc == KC - 1, ["f_wgu", "f_xn"], [ku])
                p.act(sgj[:], pgj[:, :NT], AF.Silu, [kg], [ks])
                p.tt('dve', h[:, j, :], sgj[:], puj[:, :NT], ALU.mult, [ks, ku], [("f_h", j)])
            for dc in range(KC):
                pyj = py[dc % 2]
                ky = f"f_py{dc % 2}"
                for j in range(FC):
                    p.mm(pyj[:, :NT], wd[:, j, dc * 128:(dc + 1) * 128], h[:, j, :], j == 0, j == FC - 1,
                         ["f_wd", ("f_h", j)], [ky])
                p.stt(sq[:, dc, :], pyj[:, :NT], 0.5, xi[:, dc, :], ALU.mult, ALU.add, [ky, kx], ["f_sq"])
            p.dma('sp', c.XT[:, tok0:tok0 + NT].rearrange("(kc p) t -> p kc t", p=128), sq[:],
                  ["f_sq"], [("XT", gi)])
    p.barrier()


class Ring:
    def __init__(self, tiles, name):
        self.tiles = tiles
        self.name = name
        self.i = 0

    def next(self):
        t = self.tiles[self.i % len(self.tiles)]
        k = f"{self.name}{self.i % len(self.tiles)}"
        self.i += 1
        return t, k


def ring_sb(c, es, name, n, shape, dt):
    return Ring([sb(c, es, f"{name}{i}", shape, dt) for i in range(n)], name)


def ring_ps(c, es, name, n, shape, dt=F32):
    for i in range(n):
        PSUM_KEYS.add(f"{name}{i}")
    return Ring([ps(c, es, f"{name}{i}", shape, dt) for i in range(n)], name)


class WLoader:
    def __init__(self, c, es, name, width, n=2):
        self.c = c
        self.ring = ring_sb(c, es, name, n, [128, width], F32)
        self.width = width
        self.k = 0

    def load(self, dst, src, key, shape=None):
        p = self.c.p
        st, kst = self.ring.next()
        n = 1
        for d_ in dst.shape[1:]:
            n *= d_
        sv = st[:dst.shape[0], :n]
        if len(dst.shape) == 3:
            sv = sv.rearrange("p (a b) -> p a b", a=dst.shape[1])
        elif len(dst.shape) == 4:
            sv = sv.rearrange("p (a b c) -> p a b c", a=dst.shape[1], b=dst.shape[2])
        p.dma('sp', sv, src, [], [kst])
        eng = 'dve' if self.k % 2 == 0 else 'act'
        if getattr(self, 'dbg', 0) == 3:
            eng = 'pool'
        if getattr(self, 'dbg', 0) == 4:
            eng = 'act'
        self.k += 1
        if getattr(self, 'dbg', 0) != 2:
            p.copy(eng, dst, sv, [kst], [key])


def xt_view(c, tok0, nt):
    return c.XT[:, tok0:tok0 + nt].rearrange("(kc p) t -> p kc t", p=128)


def xt_keys(tok0, nt):
    return [("XT", k) for k in range(tok0 // 256, (tok0 + nt + 255) // 256)]


def load_xn(c, tok0, NT, xr, sq, pss, var, rstd, g, kg, xn, kxn, pref):
    p = c.p
    xi, kx = xr.next()
    p.dma('sp', xi[:, :, :NT], xt_view(c, tok0, NT), xt_keys(tok0, NT), [kx])
    rms_stats(c, xi, kx, sq, pref + "sq", pss, pref + "pss", var, pref + "var", rstd, pref + "rstd", NT)
    for kc in range(KC):
        p.stt(xn[:, kc, :NT], xi[:, kc, :NT], g[:, kc:kc + 1], rstd[:, :NT], ALU.mult, ALU.mult,
              [kx, kg, pref + "rstd"], [kxn])
    return xi, kx


def stage_attn(c, layer, j, nseq):
    p = c.p
    W = c.W
    wqkv, wout, sinks_d, gvec = W['att_w_qkv'][j], W['att_w_out'][j], W['att_sinks'][j], W['mix_norm'][layer]
    NEG = -30000.0
    with contextlib.ExitStack() as es:
        wq = sb(c, es, "a_wq", [128, KC, 1024], BF16)
        wqs = sb(c, es, "a_wqs", [128, KC, 1024], BF16)
        wk = sb(c, es, "a_wk", [128, KC, 512], BF16)
        wks = sb(c, es, "a_wks", [128, KC, 512], BF16)
        wv = sb(c, es, "a_wv", [128, KC, 256], BF16)
        wo = sb(c, es, "a_wo", [128, KC, 1024], BF16)
        cos = sb(c, es, "a_cos", [128, L], F32)
        sin = sb(c, es, "a_sin", [128, L], F32)
        g = sb(c, es, "a_g", [128, KC], F32)
        snk = sb(c, es, "a_snk", [128, 16], F32)
        nsnk = sb(c, es, "a_nsnk", [128, 16], F32)
        mask = sb(c, es, "a_mask", [128, 384], F32)
        qT = sb(c, es, "a_qT", [128, KC, L], BF16)
        kT = sb(c, es, "a_kT", [128, 4, L], BF16)
        vtok = sb(c, es, "a_v", [128, 16, 256], BF16)
        load_vec(c, g[:], gvec, "a_g")
        p.dma('sp', cos[:], c.c_cos[:, :], [], ["a_cos"])
        p.dma('sp', sin[:], c.c_sin[:, :], [], ["a_sin"])
        p.dma('sp', snk[:], sinks_d.partition_broadcast(128), [], ["a_snk"])
        p.ts('dve', nsnk[:], snk[:], -1.0, None, ALU.mult, None, ["a_snk"], ["a_nsnk"])
        p.memset('pool', mask[:], 0.0, ["a_mask"])
        p.op('pool', lambda e: e.affine_select(mask[:], mask[:], [[1, 384]], ALU.is_ge, NEG, base=0,
                                               channel_multiplier=-1), ["a_mask"], ["a_mask"])
        p.op('pool', lambda e: e.affine_select(mask[:], mask[:], [[-1, 384]], ALU.is_ge, NEG, base=256,
                                               channel_multiplier=1), ["a_mask"], ["a_mask"])
        wl = WLoader(c, es, "a_wst", 1024)
        for kc in range(KC):
            rows = slice(kc * 128, (kc + 1) * 128)
            wl.load(wq[:, kc, :], wqkv[rows, 0:1024], "a_w")
            src = wqkv[rows, 0:1024].rearrange("p (h r d) -> p h r d", h=16, r=2)
            dst = wqs[:, kc, :].rearrange("p (h r d) -> p h r d", h=16, r=2)
            wl.load(dst[:, :, 0, :], src[:, :, 1, :], "a_w")
            wl.load(dst[:, :, 1, :], src[:, :, 0, :], "a_w")
            srck = wqkv[rows, 1024:1280].rearrange("p (g r d) -> p g r d", g=4, r=2)
            dk = wk[:, kc, :].rearrange("p (g c r d) -> p g c r d", g=4, c=2, r=2)
            dks = wks[:, kc, :].rearrange("p (g c r d) -> p g c r d", g=4, c=2, r=2)
            for cpy in range(2):
                for r in range(2):
                    wl.load(dk[:, :, cpy, r, :], srck[:, :, r, :], "a_w")
                    wl.load(dks[:, :, cpy, r, :], srck[:, :, 1 - r, :], "a_w")
            wl.load(wv[:, kc, :], wqkv[rows, 1280:1536], "a_w")
            wl.load(wo[:, kc, :], wout[rows, :], "a_w")
        p.barrier()
        for s in range(nseq):
            with contextlib.ExitStack() as es2:
                NT = 256
                xr = ring_sb(c, es2, "aA_x", 2, [128, KC, NT], F32)
                sq = sb(c, es2, "aA_sq", [128, KC, NT], F32)
                var = sb(c, es2, "aA_var", [128, NT], F32)
                rstd = sb(c, es2, "aA_rstd", [128, NT], F32)
                xn = sb(c, es2, "aA_xn", [128, KC, NT], BF16)
                t1r = ring_sb(c, es2, "aA_t1", 2, [128, NT], F32)
                t2r = ring_sb(c, es2, "aA_t2", 2, [128, NT], F32)
                pss = ps(c, es2, "aA_pss", [128, 512])
                p1r = ring_ps(c, es2, "aA_p1", 2, [128, 512])
                p2r = ring_ps(c, es2, "aA_p2", 2, [128, 512])
                pvr = ring_ps(c, es2, "aA_pv", 2, [128, 512])
                for gi in range(L // NT):
                    t0 = gi * NT
                    load_xn(c, s * L + t0, NT, xr, sq, pss, var, rstd, g, "a_g", xn, "aA_xn", "aA_")
                    for oc in range(12):
                        if oc < 8:
                            wa, wb, dstT = wq[:, :, oc * 128:(oc + 1) * 128], wqs[:, :, oc * 128:(oc + 1) * 128], qT[:, oc, t0:t0 + NT]
                            kd = ("a_qT", oc)
                        else:
                            gg = oc - 8
                            wa, wb, dstT = wk[:, :, gg * 128:(gg + 1) * 128], wks[:, :, gg * 128:(gg + 1) * 128], kT[:, gg, t0:t0 + NT]
                            kd = ("a_kT", gg)
                        p1, k1 = p1r.next()
                        p2, k2 = p2r.next()
                        for kc in range(KC):
                            p.mm(p1[:, :NT], wa[:, kc, :], xn[:, kc, :], kc == 0, kc == KC - 1, ["a_w", "aA_xn"], [k1])
                        for kc in range(KC):
                            p.mm(p2[:, :NT], wb[:, kc, :], xn[:, kc, :], kc == 0, kc == KC - 1, ["a_w", "aA_xn"], [k2])
                        t1, kt1 = t1r.next()
                        t2, kt2 = t2r.next()
                        p.tt('dve', t1[:], p1[:, :NT], cos[:, t0:t0 + NT], ALU.mult, [k1, "a_cos"], [kt1])
                        p.tt('dve', t2[:], p2[:, :NT], sin[:, t0:t0 + NT], ALU.mult, [k2, "a_sin"], [kt2])
                        p.tt('dve', dstT, t1[:], t2[:], ALU.add, [kt1, kt2], [kd])
                    for tb in range(NT // 128):
                        pv, kv = pvr.next()
                        for kc in range(KC):
                            p.mm(pv[:, :256], xn[:, kc, tb * 128:(tb + 1) * 128], wv[:, kc, :], kc == 0, kc == KC - 1,
                                 ["a_w", "aA_xn"], [kv])
                        p.copy('act', vtok[:, (t0 // 128) + tb, :], pv[:, :256], [kv], [("a_v", (t0 // 128) + tb)])
            p.barrier()
            with contextlib.ExitStack() as es2:
                smr = ring_sb(c, es2, "aB_sm", 2, [128, 384], F32)
                er = ring_sb(c, es2, "aB_e", 2, [128, 384], F32)
                enr = ring_sb(c, es2, "aB_en", 2, [128, 384], BF16)
                eTr = ring_sb(c, es2, "aB_eT", 2, [128, 384], BF16)
                str_ = ring_sb(c, es2, "aB_st", 4, [128, 8], F32)
                oT = sb(c, es2, "aB_oT", [128, KC, 512], BF16)
                xr = ring_sb(c, es2, "aB_x", 1, [128, KC, 512], F32)
                xo = sb(c, es2, "aB_xo", [128, KC, 512], F32)
                spr = ring_ps(c, es2, "aB_sp", 2, [128, 512])
                tpr = ring_ps(c, es2, "aB_tp", 2, [128, 512], BF16)
                opr = ring_ps(c, es2, "aB_op", 2, [128, 512])
                ypr = ring_ps(c, es2, "aB_yp", 2, [128, 512])
                for gi in range(L // 512):
                    tok0 = s * L + gi * 512
                    xi, kx = xr.next()
                    p.dma('sp', xi[:], xt_view(c, tok0, 512), xt_keys(tok0, 512), [kx])
                    for qc in range(KC):
                        op_, kop = opr.next()
                        for qb in range(4):
                            jb = gi * 4 + qb
                            kb0, kb1 = max(jb - 1, 0), min(jb + 1, 15)
                            nk = kb1 - kb0 + 1
                            mo = 128 if jb == 0 else 0
                            nkw = nk * 128
                            for hp in range(2):
                                h = qc * 2 + hp
                                gk = h // 4
                                b0 = hp * 64
                                sp_, ksp = spr.next()
                                p.mm(sp_[:, :nkw], qT[b0:b0 + 64, qc, jb * 128:(jb + 1) * 128],
                                     kT[b0:b0 + 64, gk, kb0 * 128:(kb1 + 1) * 128], True, True,
                                     [("a_qT", qc), ("a_kT", gk)], [ksp])
                                sm, ksm = smr.next()
                                p.tt('dve', sm[:, :nkw], sp_[:, :nkw], mask[:, mo:mo + nkw], ALU.add, [ksp, "a_mask"], [ksm])
                                st, kst = str_.next()
                                p.op('dve', lambda e, st=st, sm=sm, nkw=nkw: e.reduce_max(st[:, 0:1], sm[:, :nkw], AX.X), [ksm], [kst])
                                p.ts('dve', st[:, 1:2], st[:, 0:1], -0.125, nsnk[:, h:h + 1], ALU.mult, ALU.min, ["a_nsnk", kst], [kst])
                                e_, ke = er.next()
                                p.op('act', lambda e, e_=e_, sm=sm, st=st, nkw=nkw: e.activation(
                                    e_[:, :nkw], sm[:, :nkw], AF.Exp, bias=st[:, 1:2], scale=0.125, accum_out=st[:, 2:3]),
                                    [ksm, kst], [ke, kst])
                                p.op('act', lambda e, st=st, h=h: e.activation(st[:, 3:4], st[:, 1:2], AF.Exp, bias=snk[:, h:h + 1]),
                                     [kst, "a_snk"], [kst])
                                p.tt('dve', st[:, 4:5], st[:, 2:3], st[:, 3:4], ALU.add, [kst], [kst])
                                p.op('dve', lambda e, st=st: e.reciprocal(st[:, 5:6], st[:, 4:5]), [kst], [kst])
                                en, ken = enr.next()
                                p.ts('dve', en[:, :nkw], e_[:, :nkw], st[:, 5:6], None, ALU.mult, None, [ke, kst], [ken])
                                tp, ktp = tpr.next()
                                for kb in range(nk):
                                    p.tr(tp[:, kb * 128:(kb + 1) * 128], en[:, kb * 128:(kb + 1) * 128], c.identb[:], [ken], [ktp])
                                eT, keT = eTr.next()
                                p.copy('act', eT[:, :nkw], tp[:, :nkw], [ktp], [keT])
                                for kb in range(nk):
                                    p.mm(op_[b0:b0 + 64, qb * 128:(qb + 1) * 128], vtok[:, kb0 + kb, gk * 64:(gk + 1) * 64],
                                         eT[:, kb * 128:(kb + 1) * 128], kb == 0, kb == nk - 1,
                                         [("a_v", kb0 + kb), keT], [kop])
                        p.copy('act', oT[:, qc, :], op_[:], [kop], [("aB_oT", qc)])
                    for dc in range(KC):
                        yp, kyp = ypr.next()
                        for qc in range(KC):
                            p.mm(yp[:], wo[:, qc, dc * 128:(dc + 1) * 128], oT[:, qc, :], qc == 0, qc == KC - 1,
                                 ["a_w", ("aB_oT", qc)], [kyp])
                        p.tt('dve', xo[:, dc, :], yp[:], xi[:, dc, :], ALU.add, [kyp, kx], ["aB_xo"])
                    p.dma('sp', xt_view(c, tok0, 512), xo[:], ["aB_xo"], xt_keys(tok0, 512))
            p.barrier()


def make_masks(c, es):
    p = c.p
    c.mk = {}
    for name, pat, base, cm, op in [("LE", 1, 0, -1, ALU.is_ge), ("GE", -1, 0, 1, ALU.is_ge),
                                    ("GT", -1, 0, 1, ALU.is_gt), ("LT", 1, 0, -1, ALU.is_gt)]:
        t = sb(c, es, "mk" + name, [128, 128], F32)
        p.memset('pool', t[:], 1.0, ["mk" + name])
        p.op('pool', lambda e, t=t, pat=pat, base=base, cm=cm, op=op: e.affine_select(
            t[:], t[:], [[pat, 128]], op, 0.0, base=base, channel_multiplier=cm), ["mk" + name], ["mk" + name])
        c.mk[name] = t


def bc(ap, shape, axis):
    return ap.unsqueeze(axis).broadcast_to(shape)


def stage_ssd(c, layer, j, nseq):
    p = c.p
    W = c.W
    w_in, conv_w, conv_b = W['ssd_w_in'][j], W['ssd_conv_w'][j], W['ssd_conv_b'][j]
    a_log, dt_bias, d_skip, norm_w, w_out = W['ssd_a_log'][j], W['ssd_dt_bias'][j], W['ssd_d'][j], W['ssd_norm'][j], W['ssd_w_out'][j]
    gvec = W['mix_norm'][layer]
    NB = L // 128
    with contextlib.ExitStack() as es:
        g = sb(c, es, "s_g", [128, KC], F32)
        cw = sb(c, es, "s_cw", [128, 5, 32], F32)
        cb = sb(c, es, "s_cb", [128, 32], F32)
        dtb = sb(c, es, "s_dtb", [128, 64], F32)
        aneg = sb(c, es, "s_aneg", [128, 64], F32)
        dsk = sb(c, es, "s_dsk", [128, 32], F32)
        nw = sb(c, es, "s_nw", [128, 2048], F32)
        xn = sb(c, es, "s_xn", [128, KC, L], BF16)
        dt = sb(c, es, "s_dt", [128, NB, 64], F32)
        load_vec(c, g[:], gvec, "s_g")
        for tap in range(5):
            p.dma('sp', cw[:, tap, :], conv_w[tap].rearrange("(cc p) -> p cc", p=128), [], ["s_cw"], allow_slow_non_contiguous=True)
        p.dma('sp', cb[:], conv_b.rearrange("(cc p) -> p cc", p=128), [], ["s_cb"], allow_slow_non_contiguous=True)
        p.dma('sp', dtb[:], dt_bias.rearrange("a b -> (a b)").partition_broadcast(128), [], ["s_dtb"])
        p.dma('sp', aneg[:], a_log.rearrange("a b -> (a b)").partition_broadcast(128), [], ["s_aneg"])
        p.dma('sp', dsk[:], d_skip.partition_broadcast(128), [], ["s_dsk"])
        p.dma('sp', nw[:], norm_w.partition_broadcast(128), [], ["s_nw"])
        p.act(aneg[:], aneg[:], AF.Exp, ["s_aneg"], ["s_aneg"])
        p.ts('dve', aneg[:], aneg[:], -1.0, None, ALU.mult, None, ["s_aneg"], ["s_aneg"])
        p.barrier()
        for s in range(nseq):
            with contextlib.ExitStack() as es2:
                NT = 256
                xr = ring_sb(c, es2, "sA_x", 2, [128, KC, NT], F32)
                sq = sb(c, es2, "sA_sq", [128, KC, NT], F32)
                var = sb(c, es2, "sA_var", [128, NT], F32)
                rstd = sb(c, es2, "sA_rstd", [128, NT], F32)
                pss = ps(c, es2, "sA_pss", [128, 512])
                for gi in range(L // NT):
                    load_xn(c, s * L + gi * NT, NT, xr, sq, pss, var, rstd, g, "s_g", xn[:, :, gi * NT:(gi + 1) * NT], "s_xn", "sA_")
            p.barrier()
            with contextlib.ExitStack() as es2:
                wzr = ring_sb(c, es2, "sB_wz", 2, [128, KC, 512], BF16)
                wdt = sb(c, es2, "sB_wdt", [128, KC, 64], BF16)
                zr = ring_sb(c, es2, "sB_z", 3, [128, 512], F32)
                pzr = ring_ps(c, es2, "sB_pz", 4, [128, 512])
                pdr = ring_ps(c, es2, "sB_pd", 2, [128, 512])
                wl = WLoader(c, es2, "sB_wst", 4096)
                wl.load(wdt[:], w_in[:, 6144:6208].rearrange("(kc p) c -> p kc c", p=128), "sB_wdt")
                for blk in range(NB):
                    pd, kpd = pdr.next()
                    for kc in range(KC):
                        p.mm(pd[:, :64], xn[:, kc, blk * 128:(blk + 1) * 128], wdt[:, kc, :], kc == 0, kc == KC - 1,
                             ["s_xn", "sB_wdt"], [kpd])
                    p.tt('dve', dt[:, blk, :], pd[:, :64], dtb[:], ALU.add, [kpd, "s_dtb"], ["s_dt"])
                p.act(dt[:], dt[:], AF.Exp, ["s_dt"], ["s_dt"])
                p.act(dt[:], dt[:], AF.Ln, ["s_dt"], ["s_dt"], bias=1.0)
                for zc in range(4):
                    wz, kwz = wzr.next()
                    wl.load(wz[:], w_in[:, zc * 512:(zc + 1) * 512].rearrange("(kc p) c -> p kc c", p=128), kwz)
                    for blk in range(NB):
                        pz, kpz = pzr.next()
                        for kc in range(KC):
                            p.mm(pz[:], xn[:, kc, blk * 128:(blk + 1) * 128], wz[:, kc, :], kc == 0, kc == KC - 1,
                                 ["s_xn", kwz], [kpz])
                        z, kz = zr.next()
                        p.act(z[:], pz[:], AF.Silu, [kpz], [kz])
                        p.dma('sp', c.Zs[blk * 128:(blk + 1) * 128, zc * 512:(zc + 1) * 512], z[:], [kz], [("Zs", blk, zc)])
            p.barrier()
            with contextlib.ExitStack() as es2:
                wcr = ring_sb(c, es2, "sC_wc", 3, [128, KC, 128], BF16)
                wl = WLoader(c, es2, "sC_wst", 1024)
                raw = ring_sb(c, es2, "sC_raw", 2, [128, L + 4], F32)
                acc = ring_sb(c, es2, "sC_acc", 2, [128, L], F32)
                cvT = ring_sb(c, es2, "sC_cvT", 2, [128, L], BF16)
                BT = sb(c, es2, "sC_BT", [128, L], BF16)
                CT = sb(c, es2, "sC_CT", [128, L], BF16)
                Btok = sb(c, es2, "sC_Btok", [128, NB, 128], BF16)
                xtok = sb(c, es2, "sC_xtok", [128, NB, 256], BF16)
                xdt = sb(c, es2, "sC_xdt", [128, NB, 2, 256], BF16)
                dta = sb(c, es2, "sC_dta", [128, NB, 2, 4], F32)
                acs = sb(c, es2, "sC_acs", [128, NB, 2, 4], F32)
                tot = sb(c, es2, "sC_tot", [128, NB, 2, 4], F32)
                ea = sb(c, es2, "sC_ea", [128, NB, 2, 4], F32)
                edec = sb(c, es2, "sC_edec", [128, NB, 2, 4], F32)
                etot = sb(c, es2, "sC_etot", [128, NB, 2, 4], F32)
                Y = sb(c, es2, "sC_Y", [128, NB, 256], F32)
                H = sb(c, es2, "sC_H", [128, 256], F32)
                Hb = sb(c, es2, "sC_Hb", [128, 256], BF16)
                cbm = ring_sb(c, es2, "sC_cbm", 2, [128, 2, 128], F32)
                rhsr = ring_sb(c, es2, "sC_rhs", 2, [128, 4, 128], F32)
                decr = ring_sb(c, es2, "sC_dec", 2, [128, 4, 128], F32)
                mtr = ring_sb(c, es2, "sC_mt", 2, [128, 4, 128], BF16)
                tmpr = ring_sb(c, es2, "sC_tmp", 2, [128, 256], F32)
                xdr = ring_sb(c, es2, "sC_xd", 2, [128, 256], BF16)
                pA = ring_ps(c, es2, "sC_pA", 2, [128, 512])
                pT = ring_ps(c, es2, "sC_pT", 2, [128, 1024], BF16)
                pC = ring_ps(c, es2, "sC_pC", 1, [128, 512])
                pY = ring_ps(c, es2, "sC_pY", 2, [128, 512])
                pH = ring_ps(c, es2, "sC_pH", 1, [128, 512])
                for tl, ktl in zip(raw.tiles, ["sC_raw0", "sC_raw1"]):
                    p.memset('pool', tl[:, 0:2], 0.0, [ktl])
                    p.memset('pool', tl[:, L + 2:L + 4], 0.0, [ktl])
                for gq in range(8):
                    for ci, cc in enumerate([2 * gq, 2 * gq + 1, 16 + gq, 24 + gq]):
                        wc, kwc = wcr.next()
                        col0 = 2048 + cc * 128
                        wl.load(wc[:], w_in[:, col0:col0 + 128].rearrange("(kc p) c -> p kc c", p=128), kwc)
                        rw, krw = raw.next()
                        for tg in range(L // 512):
                            pa, kpa = pA.next()
                            for kc in range(KC):
                                p.mm(pa[:], wc[:, kc, :], xn[:, kc, tg * 512:(tg + 1) * 512], kc == 0, kc == KC - 1,
                                     [kwc, "s_xn"], [kpa])
                            p.copy('act', rw[:, 2 + tg * 512:2 + (tg + 1) * 512], pa[:], [kpa], [krw])
                        ac, kac = acc.next()
                        p.ts('dve', ac[:], rw[:, 0:L], cw[:, 0, cc:cc + 1], cb[:, cc:cc + 1], ALU.mult, ALU.add,
                             [krw, "s_cw", "s_cb"], [kac])
                        for tap in range(1, 5):
                            p.stt(ac[:], rw[:, tap:tap + L], cw[:, tap, cc:cc + 1], ac[:], ALU.mult, ALU.add,
                                  [krw, "s_cw", kac], [kac])
                        if ci < 2:
                            cv, kcv = cvT.next()
                            p.act(cv[:], ac[:], AF.Silu, [kac], [kcv])
                            for b4 in range(NB // 8):
                                pt, kpt = pT.next()
                                for q in range(8):
                                    blk = b4 * 8 + q
                                    p.tr(pt[:, q * 128:(q + 1) * 128], cv[:, blk * 128:(blk + 1) * 128], c.identb[:], [kcv], [kpt])
                                p.copy('act' if b4 % 2 else 'dve', xtok[:, b4 * 8:(b4 + 1) * 8, ci * 128:(ci + 1) * 128],
                                       pt[:].rearrange("p (a b) -> p a b", a=8), [kpt], ["sC_xtok"])
                        elif ci == 2:
                            p.act(BT[:], ac[:], AF.Silu, [kac], ["sC_BT"])
                            for b4 in range(NB // 8):
                                pt, kpt = pT.next()
                                for q in range(8):
                                    blk = b4 * 8 + q
                                    p.tr(pt[:, q * 128:(q + 1) * 128], BT[:, blk * 128:(blk + 1) * 128], c.identb[:], ["sC_BT"], [kpt])
                                p.copy('act' if b4 % 2 else 'dve', Btok[:, b4 * 8:(b4 + 1) * 8, :],
                                       pt[:].rearrange("p (a b) -> p a b", a=8), [kpt], ["sC_Btok"])
                        else:
                            p.act(CT[:], ac[:], AF.Silu, [kac], ["sC_CT"])
                    for d in range(2):
                        c0 = d * 32 + gq * 4
                        p.tt('dve', dta[:, :, d, :], dt[:, :, c0:c0 + 4], bc(aneg[:, c0:c0 + 4], [128, NB, 4], 1), ALU.mult,
                             ["s_dt", "s_aneg"], ["sC_dta"])
                        p.tt('dve', xdt[:, :, d, :].rearrange("p b (h q) -> p b h q", h=4),
                             xtok[:].rearrange("p b (h q) -> p b h q", h=4),
                             bc(dt[:, :, c0:c0 + 4], [128, NB, 4, 64], 3), ALU.mult, ["sC_xtok", "s_dt"], ["sC_xdt"])
                    pc, kpc = pC.next()
                    for d in range(2):
                        msk = c.mk["LE"] if d == 0 else c.mk["GE"]
                        for blk in range(NB):
                            p.mm(pc[:, (blk * 2 + d) * 4:(blk * 2 + d) * 4 + 4], msk[:], dta[:, blk, d, :], True, True,
                                 ["sC_dta", "mk"], [kpc])
                    p.copy('dve', acs[:].rearrange("p b d h -> p (b d h)"), pc[:, :NB * 8], [kpc], ["sC_acs"])
                    pc, kpc = pC.next()
                    p.mm(pc[:, :NB * 8], c.ones[:], dta[:].rearrange("p b d h -> p (b d h)"), True, True, ["sC_dta"], [kpc])
                    p.copy('dve', tot[:].rearrange("p b d h -> p (b d h)"), pc[:, :NB * 8], [kpc], ["sC_tot"])
                    p.act(ea[:].rearrange("p b d h -> p (b d h)"), acs[:].rearrange("p b d h -> p (b d h)"), AF.Exp, ["sC_acs"], ["sC_ea"])
                    p.act(etot[:].rearrange("p b d h -> p (b d h)"), tot[:].rearrange("p b d h -> p (b d h)"), AF.Exp, ["sC_tot"], ["sC_etot"])
                    p.tt('dve', edec[:].rearrange("p b d h -> p (b d h)"), tot[:].rearrange("p b d h -> p (b d h)"),
                         acs[:].rearrange("p b d h -> p (b d h)"), ALU.subtract, ["sC_tot", "sC_acs"], ["sC_edec"])
                    p.act(edec[:].rearrange("p b d h -> p (b d h)"), edec[:].rearrange("p b d h -> p (b d h)"), AF.Exp, ["sC_edec"], ["sC_edec"])
                    for d in range(2):
                        order = range(NB) if d == 0 else range(NB - 1, -1, -1)
                        mk_in, mk_l, mk_r = (("LE", "GT", "LE") if d == 0 else ("GE", "LT", "GE"))
                        for ci, blk in enumerate(order):
                            tsl = slice(blk * 128, (blk + 1) * 128)
                            first = ci == 0
                            pc, kpc = pC.next()
                            p.mm(pc[:, :128], BT[:, tsl], CT[:, tsl], True, True, ["sC_BT", "sC_CT"], [kpc])
                            cm_, kcm = cbm.next()
                            p.tt('dve', cm_[:, 0, :], pc[:, :128], c.mk[mk_in][:], ALU.mult, [kpc, "mk"], [kcm])
                            rh, krh = rhsr.next()
                            p.tt('dve', rh[:], bc(c.mk[mk_r][:], [128, 4, 128], 1), bc(dta[:, blk, d, :], [128, 4, 128], 2), ALU.mult,
                                 ["mk", "sC_dta"], [krh])
                            pa, kpa = pA.next()
                            p.mm(pa[:], c.mk[mk_l][:], rh[:].rearrange("p h l -> p (h l)"), True, True, [krh, "mk"], [kpa])
                            dc_, kdc = decr.next()
                            p.act(dc_[:].rearrange("p h l -> p (h l)"), pa[:], AF.Exp, [kpa], [kdc])
                            mt, kmt = mtr.next()
                            p.tt('dve', mt[:], dc_[:], bc(cm_[:, 0, :], [128, 4, 128], 1), ALU.mult, [kdc, kcm], [kmt])
                            py, kpy = pY.next()
                            for h in range(4):
                                p.mm(py[:, h * 64:(h + 1) * 64], mt[:, h, :], xdt[:, blk, d, h * 64:(h + 1) * 64], True, True,
                                     [kmt, "sC_xdt"], [kpy])
                            if not first:
                                p.mm(py[:, 256:512], CT[:, tsl], Hb[:], True, True, ["sC_CT", "sC_Hb"], [kpy])
                                tm, ktm = tmpr.next()
                                p.tt('dve', tm[:].rearrange("p (h q) -> p h q", h=4), py[:, 256:512].rearrange("p (h q) -> p h q", h=4),
                                     bc(ea[:, blk, d, :], [128, 4, 64], 2), ALU.mult, [kpy, "sC_ea"], [ktm])
                                if d == 0:
                                    p.tt('dve', Y[:, blk, :], tm[:], py[:, 0:256], ALU.add, [ktm, kpy], [("sC_Y", blk)])
                                else:
                                    p.tt('dve', tm[:], tm[:], py[:, 0:256], ALU.add, [ktm, kpy], [ktm])
                                    p.tt('dve', Y[:, blk, :], Y[:, blk, :], tm[:], ALU.add, [ktm, ("sC_Y", blk)], [("sC_Y", blk)])
                            else:
                                if d == 0:
                                    p.copy('dve', Y[:, blk, :], py[:, 0:256], [kpy], [("sC_Y", blk)])
                                else:
                                    p.tt('dve', Y[:, blk, :], Y[:, blk, :], py[:, 0:256], ALU.add, [kpy, ("sC_Y", blk)], [("sC_Y", blk)])
                            if ci < NB - 1:
                                xd, kxd = xdr.next()
                                p.tt('dve', xd[:].rearrange("p (h q) -> p h q", h=4),
                                     xdt[:, blk, d, :].rearrange("p (h q) -> p h q", h=4),
                                     bc(edec[:, blk, d, :], [128, 4, 64], 2), ALU.mult, ["sC_xdt", "sC_edec"], [kxd])
                                ph, kph = pH.next()
                                p.mm(ph[:, :256], Btok[:, blk, :], xd[:], True, True, ["sC_Btok", kxd], [kph])
                                if first:
                                    p.copy('dve', H[:], ph[:, :256], [kph], ["sC_H"])
                                else:
                                    p.tt('dve', H[:].rearrange("p (h q) -> p h q", h=4), H[:].rearrange("p (h q) -> p h q", h=4),
                                         bc(etot[:, blk, d, :], [128, 4, 64], 2), ALU.mult, ["sC_H", "sC_etot"], ["sC_H"])
                                    p.tt('dve', H[:], H[:], ph[:, :256], ALU.add, ["sC_H", kph], ["sC_H"])
                                p.copy('act', Hb[:], H[:], ["sC_H"], ["sC_Hb"])
                    for blk in range(NB):
                        tm, ktm = tmpr.next()
                        p.tt('dve', tm[:].rearrange("p (h q) -> p h q", h=4), xtok[:, blk, :].rearrange("p (h q) -> p h q", h=4),
                             bc(dsk[:, gq * 4:gq * 4 + 4], [128, 4, 64], 2), ALU.mult, ["sC_xtok", "s_dsk"], [ktm])
                        p.tt('dve', Y[:, blk, :], Y[:, blk, :], tm[:], ALU.add, [ktm, ("sC_Y", blk)], [("sC_Y", blk)])
                    p.dma('sp', c.Ys[:, gq * 256:(gq + 1) * 256].rearrange("(b p) q -> p b q", p=128), Y[:],
                          [("sC_Y", blk) for blk in range(NB)], [("Ys", gq)])
            p.barrier()
            with contextlib.ExitStack() as es2:
                wo = sb(c, es2, "sD_wo", [128, 16, D], BF16)
                yr = ring_sb(c, es2, "sD_y", 2, [128, 2048], F32)
                zr = ring_sb(c, es2, "sD_z", 2, [128, 2048], F32)
                junk = sb(c, es2, "sD_junk", [128, 2048], BF16)
                yn = ring_sb(c, es2, "sD_yn", 2, [128, 2048], BF16)
                st = ring_sb(c, es2, "sD_st", 2, [128, 4], F32)
                ynT = sb(c, es2, "sD_ynT", [128, 16, 512], BF16)
                xr = ring_sb(c, es2, "sD_x", 1, [128, KC, 512], F32)
                xo = sb(c, es2, "sD_xo", [128, KC, 512], F32)
                pT = ring_ps(c, es2, "sD_pT", 2, [128, 1024], BF16)
                pyr = ring_ps(c, es2, "sD_py", 2, [128, 512])
                wl = WLoader(c, es2, "sD_wst", 1024)
                for cc in range(16):
                    wl.load(wo[:, cc, :], w_out[cc * 128:(cc + 1) * 128, :], "sD_wo")
                for gi in range(L // 512):
                    tok0 = s * L + gi * 512
                    xi, kx = xr.next()
                    p.dma('sp', xi[:], xt_view(c, tok0, 512), xt_keys(tok0, 512), [kx])
                    for qb in range(4):
                        blk = gi * 4 + qb
                        y, ky = yr.next()
                        z, kz = zr.next()
                        p.dma('sp', y[:], c.Ys[blk * 128:(blk + 1) * 128, :], [("Ys", q) for q in range(8)], [ky])
                        p.dma('sp', z[:], c.Zs[blk * 128:(blk + 1) * 128, :], [("Zs", blk, q) for q in range(4)], [kz])
                        p.tt('dve', y[:], y[:], z[:], ALU.mult, [ky, kz], [ky])
                        s_, ks = st.next()
                        p.op('act', lambda e, y=y, s_=s_: e.activation(junk[:], y[:], AF.Square, accum_out=s_[:, 0:1]),
                             [ky], ["sD_junk", ks])
                        p.ts('dve', s_[:, 1:2], s_[:, 0:1], 1.0 / 2048, 1e-6, ALU.mult, ALU.add, [ks], [ks])
                        p.tt('pool', s_[:, 2:3], s_[:, 1:2], c.mhalf[:, 0:1], ALU.pow, [ks], [ks])
                        yn_, kyn = yn.next()
                        p.stt(yn_[:], y[:], s_[:, 2:3], nw[:], ALU.mult, ALU.mult, [ky, ks, "s_nw"], [kyn])
                        for b2 in range(2):
                            pt, kpt = pT.next()
                            for q in range(8):
                                cc = b2 * 8 + q
                                p.tr(pt[:, q * 128:(q + 1) * 128], yn_[:, cc * 128:(cc + 1) * 128], c.identb[:], [kyn], [kpt])
                            p.copy('act' if b2 else 'dve', ynT[:, b2 * 8:(b2 + 1) * 8, qb * 128:(qb + 1) * 128],
                                   pt[:].rearrange("p (a b) -> p a b", a=8), [kpt], ["sD_ynT"])
                    for dc in range(KC):
                        py, kpy = pyr.next()
                        for cc in range(16):
                            p.mm(py[:], wo[:, cc, dc * 128:(dc + 1) * 128], ynT[:, cc, :], cc == 0, cc == 15, ["sD_wo", "sD_ynT"], [kpy])
                        p.tt('dve', xo[:, dc, :], py[:], xi[:, dc, :], ALU.add, [kpy, kx], ["sD_xo"])
                    p.dma('sp', xt_view(c, tok0, 512), xo[:], ["sD_xo"], xt_keys(tok0, 512))
            p.barrier()


def conv_chunk(c, wc_ap, kwc, xn, cw, cb, cc, rw, krw, ac, kac, pA, keypre):
    p = c.p
    for tg in range(L // 512):
        pa, kpa = pA.next()
        for kc in range(KC):
            p.mm(pa[:], wc_ap[:, kc, :], xn[:, kc, tg * 512:(tg + 1) * 512], kc == 0, kc == KC - 1, [kwc, keypre + "xn"], [kpa])
        p.copy('act', rw[:, 2 + tg * 512:2 + (tg + 1) * 512], pa[:], [kpa], [krw])
    p.ts('dve', ac[:], rw[:, 0:L], cw[:, 0, cc:cc + 1], cb[:, cc:cc + 1], ALU.mult, ALU.add, [krw, keypre + "cw", keypre + "cb"], [kac])
    for tap in range(1, 5):
        p.stt(ac[:], rw[:, tap:tap + L], cw[:, tap, cc:cc + 1], ac[:], ALU.mult, ALU.add, [krw, keypre + "cw", kac], [kac])


def neumann_inverse_T(c, units, nlev):
    p = c.p
    for u in units:
        p.tr(u['pN'][:, 384:512], u['Mt'][0][:], c.ident[:], [u['k'] + "Mt0"], [u['k'] + "pN"])
        p.copy('act', u['M'][0][:], u['pN'][:, 384:512], [u['k'] + "pN"], [u['k'] + "M0"])
        p.tt('dve', u['Y'][0][:], u['Mt'][0][:], c.ident[:], ALU.add, [u['k'] + "Mt0"], [u['k'] + "Y0"])
    for k in range(nlev):
        a, b = k % 2, (k + 1) % 2
        last = k == nlev - 1
        for u in units:
            kk = u['k']
            p.mm(u['pN'][:, 0:128], u['Mt'][a][:], u['M'][a][:], True, True, [kk + f"Mt{a}", kk + f"M{a}"], [kk + "pN"])
            if not last:
                p.mm(u['pN'][:, 128:256], u['M'][a][:], u['Mt'][a][:], True, True, [kk + f"Mt{a}", kk + f"M{a}"], [kk + "pN"])
        for u in units:
            kk = u['k']
            p.copy('act', u['M'][b][:], u['pN'][:, 0:128], [kk + "pN"], [kk + f"M{b}"])
            if not last:
                p.copy('dve', u['Mt'][b][:], u['pN'][:, 128:256], [kk + "pN"], [kk + f"Mt{b}"])
        for u in units:
            kk = u['k']
            p.mm(u['pN'][:, 256:384], u['M'][b][:], u['Y'][a][:], True, True, [kk + f"M{b}", kk + f"Y{a}"], [kk + "pN"])
        for u in units:
            kk = u['k']
            if last:
                p.tt('dve', u['XT'][:], u['Y'][a][:], u['pN'][:, 256:384], ALU.add, [kk + f"Y{a}", kk + "pN"], [kk + "XT"])
            else:
                p.tt('dve', u['Y'][b][:], u['Y'][a][:], u['pN'][:, 256:384], ALU.add, [kk + f"Y{a}", kk + "pN"], [kk + f"Y{b}"])


def stage_gdn(c, layer, j, nseq):
    p = c.p
    W = c.W
    w_in, conv_w, conv_b = W['gdn_w_in'][j], W['gdn_conv_w'][j], W['gdn_conv_b'][j]
    a_log, dt_bias, norm_w, w_out = W['gdn_a_log'][j], W['gdn_dt_bias'][j], W['gdn_norm'][j], W['gdn_w_out'][j]
    gvec = W['mix_norm'][layer]
    NB = L // 128
    with contextlib.ExitStack() as es:
        g = sb(c, es, "g_g", [128, KC], F32)
        cw = sb(c, es, "g_cw", [128, 5, 32], F32)
        cb = sb(c, es, "g_cb", [128, 32], F32)
        dtb = sb(c, es, "g_dtb", [128, 32], F32)
        aneg = sb(c, es, "g_aneg", [128, 32], F32)
        nw = sb(c, es, "g_nw", [128, 128], F32)
        xn = sb(c, es, "g_xn", [128, KC, L], BF16)
        bga = sb(c, es, "g_bga", [128, NB, 48], F32)
        load_vec(c, g[:], gvec, "g_g")
        for tap in range(5):
            p.dma('sp', cw[:, tap, :], conv_w[tap].rearrange("(cc p) -> p cc", p=128), [], ["g_cw"], allow_slow_non_contiguous=True)
        p.dma('sp', cb[:], conv_b.rearrange("(cc p) -> p cc", p=128), [], ["g_cb"], allow_slow_non_contiguous=True)
        p.dma('sp', dtb[:], dt_bias.rearrange("a b -> (a b)").partition_broadcast(128), [], ["g_dtb"])
        p.dma('sp', aneg[:], a_log.rearrange("a b -> (a b)").partition_broadcast(128), [], ["g_aneg"])
        p.dma('sp', nw[:], norm_w.partition_broadcast(128), [], ["g_nw"])
        p.act(aneg[:], aneg[:], AF.Exp, ["g_aneg"], ["g_aneg"])
        p.ts('dve', aneg[:], aneg[:], -1.0, None, ALU.mult, None, ["g_aneg"], ["g_aneg"])
        p.barrier()
        for s in range(nseq):
            with contextlib.ExitStack() as es2:
                NT = 256
                xr = ring_sb(c, es2, "gA_x", 2, [128, KC, NT], F32)
                sq = sb(c, es2, "gA_sq", [128, KC, NT], F32)
                var = sb(c, es2, "gA_var", [128, NT], F32)
                rstd = sb(c, es2, "gA_rstd", [128, NT], F32)
                pss = ps(c, es2, "gA_pss", [128, 512])
                for gi in range(L // NT):
                    load_xn(c, s * L + gi * NT, NT, xr, sq, pss, var, rstd, g, "g_g", xn[:, :, gi * NT:(gi + 1) * NT], "g_xn", "gA_")
            p.barrier()
            with contextlib.ExitStack() as es2:
                wzr = ring_sb(c, es2, "gB_wz", 2, [128, KC, 512], BF16)
                wdt = sb(c, es2, "gB_wdt", [128, KC, 48], BF16)
                zr = ring_sb(c, es2, "gB_z", 3, [128, 512], F32)
                pzr = ring_ps(c, es2, "gB_pz", 4, [128, 512])
                pdr = ring_ps(c, es2, "gB_pd", 2, [128, 512])
                wl = WLoader(c, es2, "gB_wst", 4096)
                wl.load(wdt[:], w_in[:, 6144:6192].rearrange("(kc p) c -> p kc c", p=128), "gB_wdt")
                for blk in range(NB):
                    pd, kpd = pdr.next()
                    for kc in range(KC):
                        p.mm(pd[:, :48], xn[:, kc, blk * 128:(blk + 1) * 128], wdt[:, kc, :], kc == 0, kc == KC - 1,
                             ["g_xn", "gB_wdt"], [kpd])
                    p.copy('dve', bga[:, blk, 0:16], pd[:, 0:16], [kpd], ["g_bga"])
                    p.tt('dve', bga[:, blk, 16:48], pd[:, 16:48], dtb[:], ALU.add, [kpd, "g_dtb"], ["g_bga"])
                p.act(bga[:, :, 0:16], bga[:, :, 0:16], AF.Exp, ["g_bga"], ["g_bga"], scale=-1.0)
                p.ts('dve', bga[:, :, 0:16], bga[:, :, 0:16], 1.0, None, ALU.add, None, ["g_bga"], ["g_bga"])
                p.op('dve', lambda e: e.reciprocal(bga[:, :, 0:16], bga[:, :, 0:16]), ["g_bga"], ["g_bga"])
                p.act(bga[:, :, 16:48], bga[:, :, 16:48], AF.Exp, ["g_bga"], ["g_bga"])
                p.act(bga[:, :, 16:48], bga[:, :, 16:48], AF.Ln, ["g_bga"], ["g_bga"], bias=1.0)
                p.tt('dve', bga[:, :, 16:48], bga[:, :, 16:48], bc(aneg[:], [128, NB, 32], 1), ALU.mult, ["g_bga", "g_aneg"], ["g_bga"])
                for zc in range(4):
                    wz, kwz = wzr.next()
                    wl.load(wz[:], w_in[:, 4096 + zc * 512:4096 + (zc + 1) * 512].rearrange("(kc p) c -> p kc c", p=128), kwz)
                    for blk in range(NB):
                        pz, kpz = pzr.next()
                        for kc in range(KC):
                            p.mm(pz[:], xn[:, kc, blk * 128:(blk + 1) * 128], wz[:, kc, :], kc == 0, kc == KC - 1,
                                 ["g_xn", kwz], [kpz])
                        z, kz = zr.next()
                        p.act(z[:], pz[:], AF.Silu, [kpz], [kz])
                        p.dma('sp', c.Zs[blk * 128:(blk + 1) * 128, zc * 512:(zc + 1) * 512], z[:], [kz], [("Zs", blk, zc)])
            p.barrier()
            with contextlib.ExitStack() as es2:
                wcr = ring_sb(c, es2, "gC_wc", 3, [128, KC, 128], BF16)
                wl = WLoader(c, es2, "gC_wst", 1024)
                raw = ring_sb(c, es2, "gC_raw", 2, [128, L + 4], F32)
                acc = ring_sb(c, es2, "gC_acc", 2, [128, L], F32)
                t32a = sb(c, es2, "gC_t32a", [128, L], F32)
                t32b = sb(c, es2, "gC_t32b", [128, L], F32)
                cvT = ring_sb(c, es2, "gC_cvT", 2, [128, L], BF16)
                QhT = sb(c, es2, "gC_QhT", [128, L], BF16)
                KhT = sb(c, es2, "gC_KhT", [128, L], BF16)
                Ktok = sb(c, es2, "gC_Ktok", [128, NB, 128], BF16)
                Vtok = sb(c, es2, "gC_Vtok", [128, NB, 256], BF16)
                KKT = sb(c, es2, "gC_KKT", [128, NB, 128], F32)
                QKT = sb(c, es2, "gC_QKT", [128, NB, 128], F32)
                gq = sb(c, es2, "gC_gq", [128, NB, 2, 2], F32)
                G = sb(c, es2, "gC_G", [128, NB, 2, 2], F32)
                tot = sb(c, es2, "gC_tot", [128, NB, 2, 2], F32)
                eG = sb(c, es2, "gC_eG", [128, NB, 2, 2], F32)
                neG = sb(c, es2, "gC_neG", [128, NB, 2, 2], F32)
                edec = sb(c, es2, "gC_edec", [128, NB, 2, 2], F32)
                etot = sb(c, es2, "gC_etot", [128, NB, 2, 2], F32)
                nbeta = sb(c, es2, "gC_nbeta", [128, NB, 2], F32)
                O = sb(c, es2, "gC_O", [128, NB, 256], F32)
                units = []
                for u in range(4):
                    ud = {'k': f"gU{u}_", 'd': u // 2, 'e': u % 2}
                    ud['M'] = [sb(c, es2, f"gC_M{u}{i}", [128, 128], F32) for i in range(2)]
                    ud['Mt'] = [sb(c, es2, f"gC_Mt{u}{i}", [128, 128], F32) for i in range(2)]
                    ud['Y'] = [sb(c, es2, f"gC_Y{u}{i}", [128, 128], F32) for i in range(2)]
                    ud['XT'] = sb(c, es2, f"gC_XT{u}", [128, 128], BF16)
                    ud['S'] = sb(c, es2, f"gC_S{u}", [128, 128], F32)
                    ud['Sb'] = sb(c, es2, f"gC_Sb{u}", [128, 128], BF16)
                    ud['AT'] = sb(c, es2, f"gC_AT{u}", [128, 128], BF16)
                    ud['R'] = sb(c, es2, f"gC_R{u}", [128, 128], BF16)
                    ud['Vn'] = sb(c, es2, f"gC_Vn{u}", [128, 128], BF16)
                    ud['Kd'] = sb(c, es2, f"gC_Kd{u}", [128, 128], BF16)
                    ud['tmp'] = sb(c, es2, f"gC_tmp{u}", [128, 128], F32)
                    ud['pN'] = ps(c, es2, f"gC_pN{u}", [128, 512])
                    PSUM_KEYS.add(ud['k'] + "pN")
                    units.append(ud)
                rhsd = [sb(c, es2, f"gC_rhs{d}", [128, 2, 128], F32) for d in range(2)]
                decd = [sb(c, es2, f"gC_dec{d}", [128, 2, 128], F32) for d in range(2)]
                decm = [sb(c, es2, f"gC_decm{d}", [128, 2, 128], F32) for d in range(2)]
                decs = [sb(c, es2, f"gC_decs{d}", [128, 2, 128], F32) for d in range(2)]
                pSeg = ps(c, es2, "gC_pSeg", [128, 512])
                PSUM_KEYS.add("gC_pSeg")
                PSUM_KEYS.add("gC_pM0")
                pA = ring_ps(c, es2, "gC_pA", 2, [128, 512])
                pT = ps(c, es2, "gC_pT", [128, 1024], BF16) if False else None
                for tl, ktl in zip(raw.tiles, ["gC_raw0", "gC_raw1"]):
                    p.memset('pool', tl[:, 0:2], 0.0, [ktl])
                    p.memset('pool', tl[:, L + 2:L + 4], 0.0, [ktl])
                slot_i = [0]

                pM = ring_ps(c, es2, "gC_pM", 1, [128, 512])

                def slot():
                    i = slot_i[0] % 3
                    slot_i[0] += 1
                    if i < 2:
                        return pA.tiles[i][:, 0:128], pA.name + str(i)
                    return pM.tiles[0][:, 0:128], "gC_pM0"

                def tslot():
                    return slot()

                for hk in range(8):
                    for ci, cc in enumerate([hk, 8 + hk, 16 + 2 * hk, 17 + 2 * hk]):
                        wc, kwc = wcr.next()
                        wl.load(wc[:], w_in[:, cc * 128:(cc + 1) * 128].rearrange("(kc p) c -> p kc c", p=128), kwc)
                        rw, krw = raw.next()
                        ac, kac = acc.next()
                        conv_chunk(c, wc, kwc, xn, cw, cb, cc, rw, krw, ac, kac, pA, "g_")
                        if ci < 2:
                            p.act(t32a[:], ac[:], AF.Silu, [kac], ["gC_t32a"])
                            p.act(t32b[:], t32a[:], AF.Square, ["gC_t32a"], ["gC_t32b"])
                            for tg in range(L // 512):
                                pa, kpa = pA.next()
                                p.mm(pa[:], c.ones[:], t32b[:, tg * 512:(tg + 1) * 512], True, True, ["gC_t32b"], [kpa])
                                p.ts('dve', ac[:, tg * 512:(tg + 1) * 512], pa[:], 1e-6, None, ALU.add, None, [kpa], [kac])
                            p.act(ac[:], ac[:], AF.Ln, [kac], [kac])
                            p.act(ac[:], ac[:], AF.Exp, [kac], [kac], scale=-0.5)
                            dstT, kd = (QhT, "gC_QhT") if ci == 0 else (KhT, "gC_KhT")
                            scl = 128.0 ** -0.5 if ci == 0 else 1.0
                            p.stt(dstT[:], t32a[:], scl, ac[:], ALU.mult, ALU.mult, ["gC_t32a", kac], [kd])
                            if ci == 1:
                                for blk in range(NB):
                                    sl, ksl = slot()
                                    p.mm(sl, KhT[:, blk * 128:(blk + 1) * 128], c.identb[:], True, True, ["gC_KhT"], [ksl])
                                    p.copy('act' if blk % 2 else 'dve', Ktok[:, blk, :], sl, [ksl], ["gC_Ktok"])
                        else:
                            cv, kcv = cvT.next()
                            p.act(cv[:], ac[:], AF.Silu, [kac], [kcv])
                            for blk in range(NB):
                                sl, ksl = slot()
                                p.mm(sl, cv[:, blk * 128:(blk + 1) * 128], c.identb[:], True, True, [kcv], [ksl])
                                p.copy('act' if blk % 2 else 'dve', Vtok[:, blk, (ci - 2) * 128:(ci - 1) * 128], sl, [ksl], ["gC_Vtok"])
                    for d in range(2):
                        p.copy('dve', gq[:, :, d, :], bga[:, :, 16 + d * 16 + 2 * hk:16 + d * 16 + 2 * hk + 2], ["g_bga"], ["gC_gq"])
                    p.ts('dve', nbeta[:], bga[:, :, 2 * hk:2 * hk + 2], -1.0, None, ALU.mult, None, ["g_bga"], ["gC_nbeta"])
                    pa, kpa = pA.next()
                    for d in range(2):
                        msk = c.mk["LE"] if d == 0 else c.mk["GE"]
                        for blk in range(NB):
                            p.mm(pa[:, (blk * 2 + d) * 2:(blk * 2 + d) * 2 + 2], msk[:], gq[:, blk, d, :], True, True, ["gC_gq", "mk"], [kpa])
                    p.copy('dve', G[:].rearrange("p b d h -> p (b d h)"), pa[:, :NB * 4], [kpa], ["gC_G"])
                    pa, kpa = pA.next()
                    p.mm(pa[:, :NB * 4], c.ones[:], gq[:].rearrange("p b d h -> p (b d h)"), True, True, ["gC_gq"], [kpa])
                    p.copy('dve', tot[:].rearrange("p b d h -> p (b d h)"), pa[:, :NB * 4], [kpa], ["gC_tot"])
                    fl = "p b d h -> p (b d h)"
                    p.act(eG[:].rearrange(fl), G[:].rearrange(fl), AF.Exp, ["gC_G"], ["gC_eG"])
                    p.ts('dve', neG[:].rearrange(fl), eG[:].rearrange(fl), -1.0, None, ALU.mult, None, ["gC_eG"], ["gC_neG"])
                    p.act(etot[:].rearrange(fl), tot[:].rearrange(fl), AF.Exp, ["gC_tot"], ["gC_etot"])
                    p.tt('dve', edec[:].rearrange(fl), tot[:].rearrange(fl), G[:].rearrange(fl), ALU.subtract, ["gC_tot", "gC_G"], ["gC_edec"])
                    p.act(edec[:].rearrange(fl), edec[:].rearrange(fl), AF.Exp, ["gC_edec"], ["gC_edec"])
                    for blk in range(NB):
                        tsl = slice(blk * 128, (blk + 1) * 128)
                        sl, ksl = slot()
                        p.mm(sl, KhT[:, tsl], KhT[:, tsl], True, True, ["gC_KhT"], [ksl])
                        p.copy('act', KKT[:, blk, :], sl, [ksl], ["gC_KKT"])
                        sl, ksl = slot()
                        p.mm(sl, KhT[:, tsl], QhT[:, tsl], True, True, ["gC_KhT", "gC_QhT"], [ksl])
                        p.copy('dve', QKT[:, blk, :], sl, [ksl], ["gC_QKT"])
                    for step in range(NB):
                        first = step == 0
                        lastc = step == NB - 1
                        blks = [step, NB - 1 - step]
                        for d in range(2):
                            blk = blks[d]
                            mk_l, mk_r, mk_i, mk_s = (("GT", "LE", "LE", "LT") if d == 0 else ("LT", "GE", "GE", "GT"))
                            p.tt('dve', rhsd[d][:], bc(c.mk[mk_r][:], [128, 2, 128], 1), bc(gq[:, blk, d, :], [128, 2, 128], 2), ALU.mult,
                                 ["mk", "gC_gq"], [f"gC_rhs{d}"])
                            p.mm(pSeg[:, d * 256:(d + 1) * 256], c.mk[mk_l][:], rhsd[d][:].rearrange("p h l -> p (h l)"), True, True,
                                 [f"gC_rhs{d}", "mk"], ["gC_pSeg"])
                            p.act(decd[d][:].rearrange("p h l -> p (h l)"), pSeg[:, d * 256:(d + 1) * 256], AF.Exp, ["gC_pSeg"], [f"gC_dec{d}"])
                            p.tt('dve', decm[d][:], decd[d][:], bc(c.mk[mk_i][:], [128, 2, 128], 1), ALU.mult, [f"gC_dec{d}", "mk"], [f"gC_decm{d}"])
                            p.tt('dve', decs[d][:], decd[d][:], bc(c.mk[mk_s][:], [128, 2, 128], 1), ALU.mult, [f"gC_dec{d}", "mk"], [f"gC_decs{d}"])
                        for u in units:
                            d, e, kk = u['d'], u['e'], u['k']
                            blk = blks[d]
                            p.tt('dve', u['AT'][:], QKT[:, blk, :], decm[d][:, e, :], ALU.mult, ["gC_QKT", f"gC_decm{d}"], [kk + "AT"])
                            p.stt(u['Mt'][0][:], KKT[:, blk, :], nbeta[:, blk, e:e + 1], decs[d][:, e, :], ALU.mult, ALU.mult,
                                  ["gC_KKT", "gC_nbeta", f"gC_decs{d}"], [kk + "Mt0"])
                        neumann_inverse_T(c, units, 6)
                        sls = {}
                        for u in units:
                            d, e, kk = u['d'], u['e'], u['k']
                            blk = blks[d]
                            tsl = slice(blk * 128, (blk + 1) * 128)
                            vt = Vtok[:, blk, e * 128:(e + 1) * 128]
                            if not first:
                                sl, ksl = slot()
                                p.mm(sl, KhT[:, tsl], u['Sb'][:], True, True, ["gC_KhT", kk + "Sb"], [ksl])
                                p.stt(u['R'][:], sl, neG[:, blk, d, e:e + 1], vt, ALU.mult, ALU.add, [ksl, "gC_neG", "gC_Vtok"], [kk + "R"])
                                rr = u['R'][:]
                                krr = kk + "R"
                            else:
                                rr = vt
                                krr = "gC_Vtok"
                            sl, ksl = slot()
                            p.mm(sl, u['XT'][:], rr, True, True, [kk + "XT", krr], [ksl])
                            p.ts('dve', u['Vn'][:], sl, bga[:, blk, 2 * hk + e:2 * hk + e + 1], None, ALU.mult, None, [ksl, "g_bga"], [kk + "Vn"])
                        for u in units:
                            d, e, kk = u['d'], u['e'], u['k']
                            blk = blks[d]
                            tsl = slice(blk * 128, (blk + 1) * 128)
                            okey = ("gC_O", blk, e)
                            sl2, ksl2 = slot()
                            p.mm(sl2, u['AT'][:], u['Vn'][:], True, True, [kk + "AT", kk + "Vn"], [ksl2])
                            oap = O[:, blk, e * 128:(e + 1) * 128]
                            if not first:
                                sl, ksl = slot()
                                p.mm(sl, QhT[:, tsl], u['Sb'][:], True, True, ["gC_QhT", kk + "Sb"], [ksl])
                                p.ts('dve', u['tmp'][:], sl, eG[:, blk, d, e:e + 1], None, ALU.mult, None, [ksl, "gC_eG"], [kk + "tmp"])
                                p.tt('dve', u['tmp'][:], u['tmp'][:], sl2, ALU.add, [kk + "tmp", ksl2], [kk + "tmp"])
                                src, ksrc = u['tmp'][:], kk + "tmp"
                            else:
                                src, ksrc = sl2, ksl2
                            first_visit = (d == 0 and blk < NB // 2) or (d == 1 and blk >= NB // 2)
                            if first_visit:
                                p.copy('dve', oap, src, [ksrc], [okey]) if not first else p.copy('dve', oap, src, [ksrc], [okey])
                            else:
                                p.tt('dve', oap, oap, src, ALU.add, [ksrc, okey], [okey]) if not first else p.tt('dve', oap, oap, src, ALU.add, [ksrc, okey], [okey])
                            if not lastc:
                                p.ts('dve', u['Kd'][:], Ktok[:, blk, :], edec[:, blk, d, e:e + 1], None, ALU.mult, None, ["gC_Ktok", "gC_edec"], [kk + "Kd"])
                                sl, ksl = slot()
                                p.mm(sl, u['Kd'][:], u['Vn'][:], True, True, [kk + "Kd", kk + "Vn"], [ksl])
                                if first:
                                    p.copy('dve', u['S'][:], sl, [ksl], [kk + "S"])
                                else:
                                    p.stt(u['S'][:], u['S'][:], etot[:, blk, d, e:e + 1], sl, ALU.mult, ALU.add, [kk + "S", "gC_etot", ksl], [kk + "S"])
                                p.copy('act', u['Sb'][:], u['S'][:], [kk + "S"], [kk + "Sb"])
                    p.dma('sp', c.Ys[:, hk * 256:(hk + 1) * 256].rearrange("(b p) q -> p b q", p=128), O[:],
                          [("gC_O", blk, e) for blk in range(NB) for e in range(2)], [("Ys", hk)])
            p.barrier()
            with contextlib.ExitStack() as es2:
                wo = sb(c, es2, "gD_wo", [128, 16, D], BF16)
                yr = ring_sb(c, es2, "gD_y", 2, [128, 2048], F32)
                zr = ring_sb(c, es2, "gD_z", 2, [128, 2048], F32)
                y2 = sb(c, es2, "gD_y2", [128, 2048], F32)
                yn = ring_sb(c, es2, "gD_yn", 2, [128, 2048], BF16)
                st = ring_sb(c, es2, "gD_st", 2, [128, 16], F32)
                ynT = sb(c, es2, "gD_ynT", [128, 16, 512], BF16)
                xr = ring_sb(c, es2, "gD_x", 1, [128, KC, 512], F32)
                xo = sb(c, es2, "gD_xo", [128, KC, 512], F32)
                pT = ring_ps(c, es2, "gD_pT", 2, [128, 1024], BF16)
                pyr = ring_ps(c, es2, "gD_py", 2, [128, 512])
                wl = WLoader(c, es2, "gD_wst", 1024)
                for cc in range(16):
                    wl.load(wo[:, cc, :], w_out[cc * 128:(cc + 1) * 128, :], "gD_wo")
                for gi in range(L // 512):
                    tok0 = s * L + gi * 512
                    xi, kx = xr.next()
                    p.dma('sp', xi[:], xt_view(c, tok0, 512), xt_keys(tok0, 512), [kx])
                    for qb in range(4):
                        blk = gi * 4 + qb
                        y, ky = yr.next()
                        z, kz = zr.next()
                        p.dma('sp', y[:], c.Ys[blk * 128:(blk + 1) * 128, :], [("Ys", q) for q in range(8)], [ky])
                        p.dma('sp', z[:], c.Zs[blk * 128:(blk + 1) * 128, :], [("Zs", blk, q) for q in range(4)], [kz])
                        p.act(y2[:], y[:], AF.Square, [ky], ["gD_y2"])
                        s_, ks = st.next()
                        p.op('dve', lambda e, s_=s_: e.reduce_sum(s_[:], y2[:].rearrange("p (h v) -> p h v", h=16), AX.X), ["gD_y2"], [ks])
                        p.ts('dve', s_[:], s_[:], 1.0 / 128, 1e-6, ALU.mult, ALU.add, [ks], [ks])
                        p.tt('pool', s_[:], s_[:], c.mhalf[:, 0:16], ALU.pow, [ks], [ks])
                        h3 = "p (h v) -> p h v"
                        p.tt('dve', y[:].rearrange(h3, h=16), y[:].rearrange(h3, h=16), bc(s_[:], [128, 16, 128], 2), ALU.mult, [ky, ks], [ky])
                        p.tt('dve', z[:].rearrange(h3, h=16), z[:].rearrange(h3, h=16), bc(nw[:], [128, 16, 128], 1), ALU.mult, [kz, "g_nw"], [kz])
                        yn_, kyn = yn.next()
                        p.tt('dve', yn_[:], y[:], z[:], ALU.mult, [ky, kz], [kyn])
                        for b2 in range(2):
                            pt, kpt = pT.next()
                            for q in range(8):
                                cc = b2 * 8 + q
                                p.tr(pt[:, q * 128:(q + 1) * 128], yn_[:, cc * 128:(cc + 1) * 128], c.identb[:], [kyn], [kpt])
                            p.copy('act' if b2 else 'dve', ynT[:, b2 * 8:(b2 + 1) * 8, qb * 128:(qb + 1) * 128],
                                   pt[:].rearrange("p (a b) -> p a b", a=8), [kpt], ["gD_ynT"])
                    for dc in range(KC):
                        py, kpy = pyr.next()
                        for cc in range(16):
                            p.mm(py[:], wo[:, cc, dc * 128:(dc + 1) * 128], ynT[:, cc, :], cc == 0, cc == 15, ["gD_wo", "gD_ynT"], [kpy])
                        p.tt('dve', xo[:, dc, :], py[:], xi[:, dc, :], ALU.add, [kpy, kx], ["gD_xo"])
                    p.dma('sp', xt_view(c, tok0, 512), xo[:], ["gD_xo"], xt_keys(tok0, 512))
            p.barrier()


def stage_rwkv(c, layer, j, nseq):
    p = c.p
    W = c.W
    gvec = W['mix_norm'][layer]
    x_mu, w_rkv, w0, w1, w2 = W['rwkv_x_mu'][j], W['rwkv_w_rkv'][j], W['rwkv_w0'][j], W['rwkv_w1'][j], W['rwkv_w2'][j]
    a0, a1, a2, g1, g2 = W['rwkv_a0'][j], W['rwkv_a1'][j], W['rwkv_a2'][j], W['rwkv_g1'][j], W['rwkv_g2'][j]
    k_k, k_a, r_k, lnx_w, lnx_b, w_out = (W['rwkv_k_k'][j], W['rwkv_k_a'][j], W['rwkv_r_k'][j], W['rwkv_lnx_w'][j],
                                            W['rwkv_lnx_b'][j], W['rwkv_w_out'][j])
    CH = 64
    NCH = L // CH
    RW = c.RW
    with contextlib.ExitStack() as es:
        g = sb(c, es, "r_g", [128, KC], F32)
        mu = sb(c, es, "r_mu", [128, 6, KC], F32)
        nw0 = sb(c, es, "r_nw0", [128, 2, KC], F32)
        na0 = sb(c, es, "r_na0", [128, KC], F32)
        kkv = sb(c, es, "r_kk", [128, KC], F32)
        kav = sb(c, es, "r_ka", [128, KC], F32)
        omka = sb(c, es, "r_omka", [128, KC], F32)
        rkv = sb(c, es, "r_rk", [128, KC], F32)
        lnw = sb(c, es, "r_lnw", [128, KC], F32)
        lnb = sb(c, es, "r_lnb", [128, KC], F32)
        onesbd = sb(c, es, "r_onesbd", [128, 128], F32)
        mS = [sb(c, es, f"r_mS{d}", [128, 64], F32) for d in range(2)]
        mSn = [sb(c, es, f"r_mSn{d}", [128, 64], F32) for d in range(2)]
        mI = [sb(c, es, f"r_mI{d}", [128, 64], F32) for d in range(2)]
        load_vec(c, g[:], gvec, "r_c")
        for s_ in range(6):
            load_vec(c, mu[:, s_, :], x_mu[s_], "r_c")
        for d in range(2):
            load_vec(c, nw0[:, d, :], w0[d], "r_c")
        for t_, src in [(na0, a0), (kkv, k_k), (kav, k_a), (rkv, r_k), (lnw, lnx_w), (lnb, lnx_b)]:
            load_vec(c, t_[:], src, "r_c")
        p.ts('dve', nw0[:], nw0[:], -1.0, None, ALU.mult, None, ["r_c"], ["r_c"])
        p.ts('dve', na0[:], na0[:], -1.0, None, ALU.mult, None, ["r_c"], ["r_c"])
        p.ts('dve', omka[:], kav[:], -1.0, 1.0, ALU.mult, ALU.add, ["r_c"], ["r_c"])
        p.memset('pool', onesbd[:], 0.0, ["r_c"])
        p.memset('pool', onesbd[0:64, 0:64], 1.0, ["r_c"])
        p.memset('pool', onesbd[64:128, 64:128], 1.0, ["r_c"])
        for d in range(2):
            ns, ni = (("LT", "LE") if d == 0 else ("GT", "GE"))
            for hs in range(2):
                sl_ = slice(hs * 64, (hs + 1) * 64)
                p.copy('pool', mS[d][sl_, :], c.mk[ns][sl_, hs * 64:(hs + 1) * 64], ["mk"], ["r_c"])
                p.copy('pool', mI[d][sl_, :], c.mk[ni][sl_, hs * 64:(hs + 1) * 64], ["mk"], ["r_c"])
            p.ts('dve', mSn[d][:], mS[d][:], -1.0, None, ALU.mult, None, ["r_c"], ["r_c"])
        p.barrier()
        for s in range(nseq):
            with contextlib.ExitStack() as es2:
                NT = 256
                u = sb(c, es2, "rA_u", [128, KC, L + 2], F32)
                xm = sb(c, es2, "rA_xm", [128, KC, L], BF16)
                pss = ps(c, es2, "rA_pss", [128, 512])
                pA = ring_ps(c, es2, "rA_pA", 3, [128, 512])
                with contextlib.ExitStack() as es3:
                    xr = ring_sb(c, es3, "rA_x", 2, [128, KC, NT], F32)
                    sq = sb(c, es3, "rA_sq", [128, KC, NT], F32)
                    var = sb(c, es3, "rA_var", [128, NT], F32)
                    rstd = sb(c, es3, "rA_rstd", [128, NT], F32)
                    p.memset('pool', u[:, :, 0:1], 0.0, ["rA_u"])
                    p.memset('pool', u[:, :, L + 1:L + 2], 0.0, ["rA_u"])
                    for gi in range(L // NT):
                        t0 = gi * NT
                        xi, kx = xr.next()
                        p.dma('sp', xi[:], xt_view(c, s * L + t0, NT), xt_keys(s * L + t0, NT), [kx])
                        rms_stats(c, xi, kx, sq, "rA_sq", pss, "rA_pss", var, "rA_var", rstd, "rA_rstd", NT)
                        for kc in range(KC):
                            p.stt(u[:, kc, 1 + t0:1 + t0 + NT], xi[:, kc, :], g[:, kc:kc + 1], rstd[:], ALU.mult, ALU.mult,
                                  [kx, "r_c", "rA_rstd"], ["rA_u"])

                p.barrier()
                t1 = sb(c, es2, "rA_t1", [128, L], F32)
                t2 = sb(c, es2, "rA_t2", [128, L], F32)
                ost = ring_sb(c, es2, "rA_ost", 2, [128, L], F32)
                wt = sb(c, es2, "rA_wt", [128, KC, 1024], BF16)
                wl1 = sb(c, es2, "rA_wl1", [128, KC, 128], BF16)
                wl2 = sb(c, es2, "rA_wl2", [128, 1024], BF16)
                lT = sb(c, es2, "rA_lT", [128, L], BF16)
                wl = WLoader(c, es2, "rA_wst", 1024)

                def build_xm(si):
                    for kc in range(KC):
                        p.tt('dve', t1[:], u[:, kc, 0:L], u[:, kc, 2:L + 2], ALU.add, ["rA_u"], ["rA_t1"])
                        p.stt(t2[:], t1[:], 0.5, u[:, kc, 1:L + 1], ALU.mult, ALU.subtract, ["rA_t1", "rA_u"], ["rA_t2"])
                        p.stt(xm[:, kc, :], t2[:], mu[:, si, kc:kc + 1], u[:, kc, 1:L + 1], ALU.mult, ALU.add,
                              ["rA_t2", "r_c", "rA_u"], ["rA_xm"])

                def sigmoid_from(o, pa, bias_ap, scale_out, ko, kpa):
                    if bias_ap is not None:
                        p.op('act', lambda e: e.activation(o, pa, AF.Exp, bias=bias_ap, scale=-1.0), [kpa, "r_c"], [ko])
                    else:
                        p.op('act', lambda e: e.activation(o, pa, AF.Exp, scale=-1.0), [kpa], [ko])
                    p.ts('dve', o, o, 1.0, None, ALU.add, None, [ko], [ko])
                    p.op('dve', lambda e: e.reciprocal(o, o), [ko], [ko])
                    if scale_out != 1.0:
                        p.ts('dve', o, o, scale_out, None, ALU.mult, None, [ko], [ko])

                for si in range(3):
                    build_xm(si)
                    for kc in range(KC):
                        wl.load(wt[:, kc, :], w_rkv[si][kc * 128:(kc + 1) * 128, :], "rA_wt")
                    for oc in range(KC):
                        o_, ko = ost.next()
                        for tg in range(L // 512):
                            pa, kpa = pA.next()
                            for kc in range(KC):
                                p.mm(pa[:], wt[:, kc, oc * 128:(oc + 1) * 128], xm[:, kc, tg * 512:(tg + 1) * 512], kc == 0, kc == KC - 1,
                                     ["rA_wt", "rA_xm"], [kpa])
                            p.copy('act' if tg % 2 else 'dve', o_[:, tg * 512:(tg + 1) * 512], pa[:], [kpa], [ko])
                        p.dma('sp', RW[si, oc * 128:(oc + 1) * 128, :], o_[:], [ko], [("RW", si, oc)])
                for si, nm in [(3, 'w0'), (3, 'w1'), (4, 'a'), (5, 'g')]:
                    if nm in ('w0', 'a', 'g'):
                        build_xm(si)
                    if nm[0] == 'w':
                        d = int(nm[1])
                        l1, l2, rank, dsti, bias_t = w1[d], w2[d], 64, 5 + d, nw0[:, d, :]
                    elif nm == 'a':
                        l1, l2, rank, dsti, bias_t = a1, a2, 64, 3, na0
                    else:
                        l1, l2, rank, dsti, bias_t = g1, g2, 128, 4, None
                    wl.load(wl1[:, :, :rank], l1.rearrange("(kc p) r -> p kc r", p=128), "rA_wl1")
                    wl.load(wl2[:rank, :], l2, "rA_wl2")
                    for tg in range(L // 512):
                        pa, kpa = pA.next()
                        for kc in range(KC):
                            p.mm(pa[:rank, :], wl1[:, kc, :rank], xm[:, kc, tg * 512:(tg + 1) * 512], kc == 0, kc == KC - 1,
                                 ["rA_wl1", "rA_xm"], [kpa])
                        dst_ = lT[:rank, tg * 512:(tg + 1) * 512]
                        if nm[0] == 'w':
                            p.act(dst_, pa[:rank, :], AF.Tanh, [kpa], ["rA_lT"])
                        elif nm == 'a':
                            p.copy('act', dst_, pa[:rank, :], [kpa], ["rA_lT"])
                        else:
                            p.op('act', lambda e, dst_=dst_, pa=pa: e.activation(t1[:, :512], pa[:, :], AF.Exp, scale=-1.0), [kpa], ["rA_t1"])
                            p.ts('dve', t1[:, :512], t1[:, :512], 1.0, None, ALU.add, None, ["rA_t1"], ["rA_t1"])
                            p.op('dve', lambda e: e.reciprocal(t1[:, :512], t1[:, :512]), ["rA_t1"], ["rA_t1"])
                            p.copy('dve', dst_, t1[:, :512], ["rA_t1"], ["rA_lT"])
                    for oc in range(KC):
                        o_, ko = ost.next()
                        for tg in range(L // 512):
                            pa, kpa = pA.next()
                            p.mm(pa[:], wl2[:rank, oc * 128:(oc + 1) * 128], lT[:rank, tg * 512:(tg + 1) * 512], True, True,
                                 ["rA_wl2", "rA_lT"], [kpa])
                            osl = o_[:, tg * 512:(tg + 1) * 512]
                            if nm[0] == 'w':
                                sigmoid_from(osl, pa[:], bias_t[:, oc:oc + 1], -0.6065306597126334, ko, kpa)
                            elif nm == 'a':
                                sigmoid_from(osl, pa[:], bias_t[:, oc:oc + 1], 1.0, ko, kpa)
                            else:
                                p.copy('act', osl, pa[:], [kpa], [ko])
                        p.dma('sp', RW[dsti, oc * 128:(oc + 1) * 128, :], o_[:], [ko], [("RW", dsti, oc)])
            p.barrier()
            with contextlib.ExitStack() as es2:
                yT = sb(c, es2, "rC_yT", [128, KC, L], BF16)
                pA = ring_ps(c, es2, "rC_pA", 3, [128, 512])
                es3 = contextlib.ExitStack()
                F = [sb(c, es3, f"rC_F{i}", [128, L], F32) for i in range(8)]
                kF = [f"rC_F{i}" for i in range(8)]
                AR = [sb(c, es3, f"rC_AR{d}", [128, NCH, 2, CH], BF16) for d in range(2)]
                Kt = [sb(c, es3, f"rC_Kt{d}", [128, L], BF16) for d in range(2)]
                Bt = [sb(c, es3, f"rC_Bt{d}", [128, L], BF16) for d in range(2)]
                Kd = sb(c, es3, "rC_Kd", [128, L], BF16)
                Bd = sb(c, es3, "rC_Bd", [128, L], BF16)
                Kdtok = [sb(c, es3, f"rC_Kdtok{d}", [128, NCH, CH], BF16) for d in range(2)]
                Bdtok = [sb(c, es3, f"rC_Bdtok{d}", [128, NCH, CH], BF16) for d in range(2)]
                PC = [sb(c, es3, f"rC_PC{d}", [128, NCH], F32) for d in range(2)]
                Vp = sb(c, es3, "rC_Vp", [128, NCH, CH], BF16)
                Ost = sb(c, es3, "rC_Ost", [128, NCH, CH], F32)
                Obf = sb(c, es3, "rC_Obf", [128, NCH, CH], BF16)
                st = sb(c, es3, "rC_st", [128, NCH, 4], F32)
                units = []
                for uu in range(2):
                    ud = {'k': f"rU{uu}_", 'd': uu}
                    ud['M'] = [sb(c, es3, f"rC_M{uu}{i}", [128, 128], F32) for i in range(2)]
                    ud['Mt'] = [sb(c, es3, f"rC_Mt{uu}{i}", [128, 128], F32) for i in range(2)]
                    ud['Y'] = [sb(c, es3, f"rC_Y{uu}{i}", [128, 128], F32) for i in range(2)]
                    ud['XT'] = sb(c, es3, f"rC_XT{uu}", [128, 128], BF16)
                    ud['AkT'] = sb(c, es3, f"rC_AkT{uu}", [128, 128], BF16)
                    ud['RkT'] = sb(c, es3, f"rC_RkT{uu}", [128, 128], BF16)
                    ud['RbT'] = sb(c, es3, f"rC_RbT{uu}", [128, 128], BF16)
                    ud['S'] = sb(c, es3, f"rC_S{uu}", [128, CH], F32)
                    ud['Sb'] = sb(c, es3, f"rC_Sb{uu}", [128, CH], BF16)
                    ud['R1'] = sb(c, es3, f"rC_R1{uu}", [128, CH], BF16)
                    ud['NU'] = sb(c, es3, f"rC_NU{uu}", [128, CH], BF16)
                    ud['pN'] = ps(c, es3, f"rC_pN{uu}", [128, 512])
                    PSUM_KEYS.add(ud['k'] + "pN")
                    ud['pX'] = ps(c, es3, f"rC_pX{uu}", [128, 512])
                    PSUM_KEYS.add(ud['k'] + "pX")
                    for nm_ in ['Mt', 'AkT', 'RkT', 'RbT']:
                        tl_ = ud[nm_][0] if nm_ == 'Mt' else ud[nm_]
                        p.memset('pool', tl_[:], 0.0, [ud['k'] + (nm_ + "0" if nm_ == 'Mt' else nm_)])
                    units.append(ud)
                pB = ring_ps(c, es3, "rC_pB", 1, [128, 1024], BF16)
                for hp in range(8):
                    rows = slice(hp * 128, (hp + 1) * 128)
                    cs3 = "p (n q) -> p n q"

                    def ld(dstF, idx):
                        p.dma('sp', F[dstF][:], RW[idx, rows, :], [("RW", idx, hp)], [kF[dstF]])

                    def onesbd_bcast(srcF, dstF, add_eps):
                        for tg in range(L // 512):
                            pa, kpa = pA.next()
                            p.mm(pa[:], onesbd[:], F[srcF][:, tg * 512:(tg + 1) * 512], True, True, [kF[srcF], "r_c"], [kpa])
                            if add_eps is not None:
                                p.ts('dve', F[dstF][:, tg * 512:(tg + 1) * 512], pa[:], add_eps, None, ALU.add, None, [kpa], [kF[dstF]])
                            else:
                                p.copy('dve', F[dstF][:, tg * 512:(tg + 1) * 512], pa[:], [kpa], [kF[dstF]])

                    ld(0, 1)
                    ld(1, 3)
                    p.ts('dve', F[6][:], F[0][:], kkv[:, hp:hp + 1], None, ALU.mult, None, [kF[0], "r_c"], [kF[6]])
                    p.act(F[7][:], F[6][:], AF.Square, [kF[6]], [kF[7]])
                    onesbd_bcast(7, 7, 1e-6) if False else None
                    onesbd_bcast(7, 2, 1e-6)
                    p.act(F[2][:], F[2][:], AF.Ln, [kF[2]], [kF[2]])
                    p.act(F[2][:], F[2][:], AF.Exp, [kF[2]], [kF[2]], scale=-0.5)
                    p.tt('dve', F[2][:], F[6][:], F[2][:], ALU.mult, [kF[6], kF[2]], [kF[2]])
                    p.ts('dve', F[6][:], F[1][:], kav[:, hp:hp + 1], omka[:, hp:hp + 1], ALU.mult, ALU.add, [kF[1], "r_c"], [kF[6]])
                    p.tt('dve', F[3][:], F[0][:], F[6][:], ALU.mult, [kF[0], kF[6]], [kF[3]])
                    p.tt('dve', F[4][:], F[2][:], F[1][:], ALU.mult, [kF[2], kF[1]], [kF[4]])
                    ld(5, 0)
                    ld(0, 2)
                    p.stt(F[6][:], F[5][:], rkv[:, hp:hp + 1], F[3][:], ALU.mult, ALU.mult, [kF[5], "r_c", kF[3]], [kF[6]])
                    onesbd_bcast(6, 7, None)
                    p.tt('dve', F[1][:], F[7][:], F[0][:], ALU.mult, [kF[7], kF[0]], [kF[1]])
                    p.copy('act', Kd[:], F[0][:], [kF[0]], ["rC_Kd"])

                    def to_stacked(srcT, ksrc, dst, kdst):
                        for c8 in range(NCH // 8):
                            pb, kpb = pB.next()
                            for q in range(8):
                                ch = c8 * 8 + q
                                for hs in range(2):
                                    sl_ = slice(hs * 64, (hs + 1) * 64)
                                    p.mm64(pA.tiles[0][sl_, q * 64:(q + 1) * 64], srcT[sl_, ch * CH:(ch + 1) * CH], c.identb[sl_, sl_], True, True,
                                         [ksrc], ["rC_pA0"])
                            p.copy('act' if c8 % 2 else 'dve', dst[:, c8 * 8:(c8 + 1) * 8, :],
                                   pA.tiles[0][:, :].rearrange("p (a b) -> p a b", a=8), ["rC_pA0"], [kdst])

                    to_stacked(Kd, "rC_Kd", Vp, "rC_Vp")
                    for d in range(2):
                        ld(0, 5 + d)
                        p.op('dve', lambda e: e.tensor_tensor_scan(F[6][:], c.ones[:, 0:1].broadcast_to([128, L]), F[0][:], 0.0, ALU.mult, ALU.add),
                             [kF[0]], [kF[6]])
                        cs = F[6][:].rearrange(cs3, q=CH)
                        lw = F[0][:].rearrange(cs3, q=CH)
                        li = F[7][:].rearrange(cs3, q=CH)
                        if d == 0:
                            p.tt('dve', st[:, :, 0:1], cs[:, :, 0:1], lw[:, :, 0:1], ALU.subtract, [kF[6], kF[0]], ["rC_st"])
                            p.tt('dve', li, cs, bc(st[:, :, 0], [128, NCH, CH], 2) if False else st[:, :, 0:1].broadcast_to([128, NCH, CH]),
                                 ALU.subtract, [kF[6], "rC_st"], [kF[7]])
                            p.tt('dve', F[6][:], F[7][:], F[0][:], ALU.subtract, [kF[7], kF[0]], [kF[6]])
                            last = CH - 1
                        else:
                            p.copy('dve', st[:, :, 0:1], cs[:, :, CH - 1:CH], [kF[6]], ["rC_st"])
                            p.tt('dve', cs, st[:, :, 0:1].broadcast_to([128, NCH, CH]), cs, ALU.subtract, [kF[6], "rC_st"], [kF[6]])
                            p.tt('dve', F[7][:], F[6][:], F[0][:], ALU.add, [kF[6], kF[0]], [kF[7]])
                            last = 0
                        p.act(F[6][:], F[6][:], AF.Exp, [kF[6]], [kF[6]])
                        p.tt('dve', AR[d][:, :, 0, :], F[2][:].rearrange(cs3, q=CH), F[6][:].rearrange(cs3, q=CH), ALU.mult,
                             [kF[2], kF[6]], [f"rC_AR{d}"])
                        p.act(F[0][:], F[7][:], AF.Exp, [kF[7]], [kF[0]])
                        p.tt('dve', AR[d][:, :, 1, :], F[5][:].rearrange(cs3, q=CH), F[0][:].rearrange(cs3, q=CH), ALU.mult,
                             [kF[5], kF[0]], [f"rC_AR{d}"])
                        p.copy('dve', PC[d][:].unsqueeze(2), F[0][:].rearrange(cs3, q=CH)[:, :, last:last + 1], [kF[0]], [f"rC_PC{d}"])
                        p.act(F[6][:], F[7][:], AF.Exp, [kF[7]], [kF[6]], scale=-1.0)
                        p.tt('dve', Kt[d][:], F[3][:], F[6][:], ALU.mult, [kF[3], kF[6]], [f"rC_Kt{d}"])
                        p.tt('dve', Bt[d][:], F[4][:], F[6][:], ALU.mult, [kF[4], kF[6]], [f"rC_Bt{d}"])
                        p.tt('dve', F[0][:].rearrange(cs3, q=CH), li[:, :, last:last + 1].broadcast_to([128, NCH, CH]), li, ALU.subtract,
                             [kF[7]], [kF[0]])
                        p.act(F[0][:], F[0][:], AF.Exp, [kF[0]], [kF[0]])
                        p.tt('dve', Kd[:], F[3][:], F[0][:], ALU.mult, [kF[3], kF[0]], ["rC_Kd"])
                        p.tt('dve', Bd[:], F[4][:], F[0][:], ALU.mult, [kF[4], kF[0]], ["rC_Bd"])
                        to_stacked(Kd, "rC_Kd", Kdtok[d], f"rC_Kdtok{d}")
                        to_stacked(Bd, "rC_Bd", Bdtok[d], f"rC_Bdtok{d}")
                    for step in range(NCH):
                        first = step == 0
                        lastc = step == NCH - 1
                        chs = [step, NCH - 1 - step]
                        for u_ in units:
                            d, kk_ = u_['d'], u_['k']
                            ch = chs[d]
                            tsl = slice(ch * CH, (ch + 1) * CH)
                            pX = u_['pX']
                            for hs in range(2):
                                sl_ = slice(hs * 64, (hs + 1) * 64)
                                p.mm64(pX[sl_, 0:128], Kt[d][sl_, tsl], AR[d][sl_, ch, :, :].rearrange("p a q -> p (a q)"), True, True,
                                     [f"rC_Kt{d}", f"rC_AR{d}"], [kk_ + "pX"])
                                p.mm64(pX[sl_, 128:256], Bt[d][sl_, tsl], AR[d][sl_, ch, :, :].rearrange("p a q -> p (a q)"), True, True,
                                     [f"rC_Bt{d}", f"rC_AR{d}"], [kk_ + "pX"])
                            for hs in range(2):
                                sl_ = slice(hs * 64, (hs + 1) * 64)
                                cs_ = slice(hs * 64, (hs + 1) * 64)
                                p.tt('dve', u_['AkT'][sl_, cs_], pX[sl_, 0:64], mS[d][sl_, :], ALU.mult, [kk_ + "pX", "r_c"], [kk_ + "AkT"])
                                p.tt('dve', u_['RkT'][sl_, cs_], pX[sl_, 64:128], mI[d][sl_, :], ALU.mult, [kk_ + "pX", "r_c"], [kk_ + "RkT"])
                                p.tt('dve', u_['Mt'][0][sl_, cs_], pX[sl_, 128:192], mSn[d][sl_, :], ALU.mult, [kk_ + "pX", "r_c"], [kk_ + "Mt0"])
                                p.tt('dve', u_['RbT'][sl_, cs_], pX[sl_, 192:256], mI[d][sl_, :], ALU.mult, [kk_ + "pX", "r_c"], [kk_ + "RbT"])
                        neumann_inverse_T(c, units, 5)
                        for u_ in units:
                            d, kk_ = u_['d'], u_['k']
                            ch = chs[d]
                            pX = u_['pX']
                            if not first:
                                for hs in range(2):
                                    sl_ = slice(hs * 64, (hs + 1) * 64)
                                    p.mm64(pX[sl_, 256:320], AR[d][sl_, ch, 0, :], u_['Sb'][sl_, :], True, False, [f"rC_AR{d}", kk_ + "Sb"], [kk_ + "pX"])
                            p.mm(pX[:, 256:320], u_['AkT'][:], Vp[:, ch, :], first, True, [kk_ + "AkT", "rC_Vp"], [kk_ + "pX"])
                            p.copy('act', u_['R1'][:], pX[:, 256:320], [kk_ + "pX"], [kk_ + "R1"])
                        for u_ in units:
                            d, kk_ = u_['d'], u_['k']
                            pX = u_['pX']
                            p.mm(pX[:, 320:384], u_['XT'][:], u_['R1'][:], True, True, [kk_ + "XT", kk_ + "R1"], [kk_ + "pX"])
                            p.op('act', lambda e, u_=u_, pX=pX: e.activation(u_['NU'][:], pX[:, 320:384], AF.Copy, scale=-1.0), [kk_ + "pX"], [kk_ + "NU"])
                        for u_ in units:
                            d, kk_ = u_['d'], u_['k']
                            ch = chs[d]
                            pX = u_['pX']
                            if not first:
                                for hs in range(2):
                                    sl_ = slice(hs * 64, (hs + 1) * 64)
                                    p.mm64(pX[sl_, 384:448], AR[d][sl_, ch, 1, :], u_['Sb'][sl_, :], True, False, [f"rC_AR{d}", kk_ + "Sb"], [kk_ + "pX"])
                            p.mm(pX[:, 384:448], u_['RkT'][:], Vp[:, ch, :], first, False, [kk_ + "RkT", "rC_Vp"], [kk_ + "pX"])
                            p.mm(pX[:, 384:448], u_['RbT'][:], u_['NU'][:], False, True, [kk_ + "RbT", kk_ + "NU"], [kk_ + "pX"])
                            okey = ("rC_O", ch)
                            first_visit = (d == 0 and ch < NCH // 2) or (d == 1 and ch >= NCH // 2)
                            if first_visit:
                                p.copy('dve', Ost[:, ch, :], pX[:, 384:448], [kk_ + "pX"], [okey])
                            else:
                                p.tt('dve', Ost[:, ch, :], Ost[:, ch, :], pX[:, 384:448], ALU.add, [kk_ + "pX", okey], [okey])
                            if not lastc:
                                for hs in range(2):
                                    sl_ = slice(hs * 64, (hs + 1) * 64)
                                    p.mm64(pX[sl_, 448:512], Kdtok[d][sl_, ch, :], Vp[sl_, ch, :], True, False, [f"rC_Kdtok{d}", "rC_Vp"], [kk_ + "pX"])
                                    p.mm64(pX[sl_, 448:512], Bdtok[d][sl_, ch, :], u_['NU'][sl_, :], False, True, [f"rC_Bdtok{d}", kk_ + "NU"], [kk_ + "pX"])
                                if first:
                                    p.copy('dve', u_['S'][:], pX[:, 448:512], [kk_ + "pX"], [kk_ + "S"])
                                else:
                                    p.stt(u_['S'][:], u_['S'][:], PC[d][:, ch:ch + 1], pX[:, 448:512], ALU.mult, ALU.add,
                                          [kk_ + "S", f"rC_PC{d}", kk_ + "pX"], [kk_ + "S"])
                                p.copy('act', u_['Sb'][:], u_['S'][:], [kk_ + "S"], [kk_ + "Sb"])
                    okeys = [("rC_O", ch) for ch in range(NCH)]
                    p.op('dve', lambda e: e.reduce_sum(st[:, :, 0], Ost[:], AX.X), okeys, ["rC_st"])
                    p.ts('dve', st[:, :, 0], st[:, :, 0], 1.0 / CH, None, ALU.mult, None, ["rC_st"], ["rC_st"])
                    p.tt('dve', Ost[:], Ost[:], st[:, :, 0:1].broadcast_to([128, NCH, CH]), ALU.subtract, okeys + ["rC_st"], ["rC_Oc"])
                    F6v = F[6][:].rearrange(cs3, q=CH)
                    p.act(F6v, Ost[:], AF.Square, ["rC_Oc"], [kF[6]])
                    p.op('dve', lambda e: e.reduce_sum(st[:, :, 1], F6v, AX.X), [kF[6]], ["rC_st"])
                    p.ts('dve', st[:, :, 1], st[:, :, 1], 1.0 / CH, 64e-5, ALU.mult, ALU.add, ["rC_st"], ["rC_st"])
                    p.tt('pool', st[:, :, 2], st[:, :, 1], c.mhalf[:, 0:NCH], ALU.pow, ["rC_st"], ["rC_st"])
                    p.tt('dve', Obf[:], Ost[:], st[:, :, 2:3].broadcast_to([128, NCH, CH]), ALU.mult, ["rC_Oc", "rC_st"], ["rC_Obf"])
                    ld(0, 4)
                    for c8 in range(NCH // 8):
                        for q in range(8):
                            ch = c8 * 8 + q
                            for hs in range(2):
                                sl_ = slice(hs * 64, (hs + 1) * 64)
                                p.mm64(pA.tiles[1][sl_, q * 64:(q + 1) * 64], Obf[sl_, ch, :], c.identb[sl_, sl_], True, True, ["rC_Obf"], ["rC_pA1"])
                        fs = slice(c8 * 512, (c8 + 1) * 512)
                        p.ts('dve', F[6][:, fs], pA.tiles[1][:, :], lnw[:, hp:hp + 1], lnb[:, hp:hp + 1], ALU.mult, ALU.add, ["rC_pA1", "r_c"], [kF[6]])
                    p.tt('dve', F[6][:], F[6][:], F[1][:], ALU.add, [kF[6], kF[1]], [kF[6]])
                    p.tt('dve', yT[:, hp, :], F[6][:], F[0][:], ALU.mult, [kF[6], kF[0]], [("rC_yT", hp)])

                p.barrier()
                es3.close()
                wl = WLoader(c, es2, "rD_wst", 1024)
                wo = sb(c, es2, "rD_wo", [128, KC, D], BF16)
                xr = ring_sb(c, es2, "rD_x", 1, [128, KC, 512], F32)
                xo = sb(c, es2, "rD_xo", [128, KC, 512], F32)
                for kc in range(KC):
                    wl.load(wo[:, kc, :], w_out[kc * 128:(kc + 1) * 128, :], "rD_wo")
                for gi in range(L // 512):
                    tok0 = s * L + gi * 512
                    xi, kx = xr.next()
                    p.dma('sp', xi[:], xt_view(c, tok0, 512), xt_keys(tok0, 512), [kx])
                    for dc in range(KC):
                        pa, kpa = pA.next()
                        for kc in range(KC):
                            p.mm(pa[:], wo[:, kc, dc * 128:(dc + 1) * 128], yT[:, kc, gi * 512:(gi + 1) * 512], kc == 0, kc == KC - 1,
                                 ["rD_wo"] + [("rC_yT", kc)], [kpa])
                        p.tt('dve', xo[:, dc, :], pa[:], xi[:, dc, :], ALU.add, [kpa, kx], ["rD_xo"])
                    p.dma('sp', xt_view(c, tok0, 512), xo[:], ["rD_xo"], xt_keys(tok0, 512))
            p.barrier()


def build(nseq_prompt=1, nseq_sample=4, cfg=None):
    cfg = cfg or {}
    depth = cfg.get('depth', 4)
    nseq = nseq_prompt + nseq_sample
    ntok = nseq * L
    nc = bass.Bass("TRN2", target_bir_lowering=False)
    c = Ctx()
    c.nc = nc
    c.cfg = cfg
    din = {}

    def inp(name, shape):
        din[name] = nc.dram_tensor(name, list(shape), F32, kind="ExternalInput").ap()
        return din[name]

    c.xp = inp("x_prompt", [max(nseq_prompt, 1), L, D])
    c.xs = inp("x_sample", [max(nseq_sample, 1), L, D])
    W = {}
    for name, shape in WEIGHT_SHAPES.items():
        W[name] = inp(name, shape)
    c.W = W
    c_ident = inp("c_ident", [128, 128])
    c.c_cos = inp("c_cos", [128, L])
    c.c_sin = inp("c_sin", [128, L])
    c.yp = nc.dram_tensor("y_prompt", [max(nseq_prompt, 1), L, D], F32, kind="ExternalOutput").ap()
    c.ys = nc.dram_tensor("y_sample", [max(nseq_sample, 1), L, D], F32, kind="ExternalOutput").ap()
    c.XT = nc.dram_tensor("XT", [D, ntok], F32).ap()
    c.Zs = nc.dram_tensor("Zs", [L, 2048], F32).ap()
    c.Ys = nc.dram_tensor("Ys", [L, 2048], F32).ap()
    c.RW = nc.dram_tensor("RW", [7, D, L], F32).ap()

    with contextlib.ExitStack() as es:
        p = Prog(nc, es)
        c.p = p
        c.ident = sb(c, es, "ident", [128, 128], F32)
        c.ones = sb(c, es, "ones", [128, 128], F32)
        c.mhalf = sb(c, es, "mhalf", [128, 512], F32)
        c.identb = sb(c, es, "identb", [128, 128], BF16)
        p.dma('sp', c.ident[:], c_ident[:, :], [], ["ident"])
        p.copy('dve', c.identb[:], c.ident[:], ["ident"], ["identb"])
        p.memset('pool', c.ones[:], 1.0, ["ones"])
        p.memset('pool', c.mhalf[:], -0.5, ["mhalf"])
        make_masks(c, es)
        p.barrier()

        srcs = [(c.xp, b) for b in range(nseq_prompt)] + [(c.xs, b) for b in range(nseq_sample)]
        dsts = [(c.yp, b) for b in range(nseq_prompt)] + [(c.ys, b) for b in range(nseq_sample)]
        stage_in(c, srcs)
        for i in range(depth):
            if cfg.get('ffn', True) and cfg.get('ffn1', True):
                stage_ffn(c, W['ffn1_norm'][i], W['ffn1_w_gu'][i], W['ffn1_w_down'][i], ntok)
            if i in cfg.get('mixers', [0, 1, 2, 3]):
                m, jj = i % 4, i // 4
                if m == 0:
                    stage_ssd(c, i, jj, nseq)
                if m == 1:
                    stage_gdn(c, i, jj, nseq)
                if m == 2:
                    stage_attn(c, i, jj, nseq)
                if m == 3:
                    stage_rwkv(c, i, jj, nseq)
            if cfg.get('ffn', True) and cfg.get('ffn2', True):
                stage_ffn(c, W['ffn2_norm'][i], W['ffn2_w_gu'][i], W['ffn2_w_down'][i], ntok)
        stage_out(c, dsts, W['final_norm'])
        p.emit()
    return nc


WEIGHT_SHAPES = {
    'ffn1_norm': (4, 1024), 'ffn1_w_gu': (4, 1024, 5632), 'ffn1_w_down': (4, 2816, 1024),
    'mix_norm': (4, 1024), 'ffn2_norm': (4, 1024), 'ffn2_w_gu': (4, 1024, 5632), 'ffn2_w_down': (4, 2816, 1024),
    'ssd_w_in': (1, 1024, 6208), 'ssd_conv_w': (1, 5, 4096), 'ssd_conv_b': (1, 4096), 'ssd_a_log': (1, 2, 32),
    'ssd_dt_bias': (1, 2, 32), 'ssd_d': (1, 32), 'ssd_norm': (1, 2048), 'ssd_w_out': (1, 2048, 1024),
    'gdn_w_in': (1, 1024, 6192), 'gdn_conv_w': (1, 5, 4096), 'gdn_conv_b': (1, 4096), 'gdn_a_log': (1, 2, 16),
    'gdn_dt_bias': (1, 2, 16), 'gdn_norm': (1, 128), 'gdn_w_out': (1, 2048, 1024),
    'att_w_qkv': (1, 1024, 1536), 'att_sinks': (1, 16), 'att_w_out': (1, 1024, 1024),
    'rwkv_x_mu': (1, 6, 1024), 'rwkv_w_rkv': (1, 3, 1024, 1024), 'rwkv_w0': (1, 2, 1024),
    'rwkv_w1': (1, 2, 1024, 64), 'rwkv_w2': (1, 2, 64, 1024), 'rwkv_a0': (1, 1024), 'rwkv_a1': (1, 1024, 64),
    'rwkv_a2': (1, 64, 1024), 'rwkv_g1': (1, 1024, 128), 'rwkv_g2': (1, 128, 1024), 'rwkv_k_k': (1, 1024),
    'rwkv_k_a': (1, 1024), 'rwkv_r_k': (1, 1024), 'rwkv_lnx_w': (1, 1024), 'rwkv_lnx_b': (1, 1024),
    'rwkv_w_out': (1, 1024, 1024), 'final_norm': (1024,),
}


def consts():
    r = np.arange(128) % 64
    i = (r % 32).astype(np.float64)
    inv_freq = 10000.0 ** (-i / 32.0)
    ang = (np.arange(L, dtype=np.float64)[None, :] * inv_freq[:, None]).astype(np.float32).astype(np.float64)
    sgn = np.where(r < 32, -1.0, 1.0)[:, None]
    return {"c_ident": np.eye(128, dtype=np.float32),
            "c_cos": np.cos(ang).astype(np.float32),
            "c_sin": (np.sin(ang) * sgn).astype(np.float32)}


def kernel(**inputs):
    nc = build(1, 4)
    xp = np.ascontiguousarray(inputs['x_prompt'], dtype=np.float32)
    xs = np.ascontiguousarray(inputs['x_sample'], dtype=np.float32)
    shared = {k: np.ascontiguousarray(inputs[k], dtype=np.float32) for k in WEIGHT_SHAPES}
    shared.update(consts())
    in_maps = []
    for c in range(NCORES):
        m = dict(shared)
        m['x_prompt'] = xp[c:c + 1]
        m['x_sample'] = xs[4 * c:4 * c + 4]
        in_maps.append(m)
    res = run_bass_kernel_spmd(nc, in_maps, core_ids=list(range(NCORES)))
    yp = np.concatenate([r['y_prompt'] for r in res.results], axis=0)
    ys = np.concatenate([r['y_sample'] for r in res.results], axis=0)
    return yp.astype(np.float32), ys.astype(np.float32)
```

```python
import contextlib
import numpy as np
import concourse.bass as bass
import concourse.mybir as mybir
from concourse.alu_op_type import AluOpType as ALU
from concourse.bass_utils import run_bass_kernel_spmd

F32 = mybir.dt.float32
BF16 = mybir.dt.bfloat16
AF = mybir.ActivationFunctionType
AX = mybir.AxisListType

NCORES = 8
D = 1024
L = 2048
DFF = 2816
KC = D // 128
FC = DFF // 128
ENG = ['pe', 'dve', 'act', 'pool', 'sp']
SAME_ENG_SYNC = True


PSUM_KEYS = set()


class Prog:
    def __init__(self, nc, es):
        self.nc = nc
        self.es = es
        self.engs = {'pe': nc.tensor, 'dve': nc.vector, 'act': nc.scalar, 'pool': nc.gpsimd, 'sp': nc.sync}
        self.q = {e: [] for e in ENG}
        self.sem = {e: es.enter_context(nc.semaphore("s_" + e)) for e in ENG}
        self.cnt = {e: 0 for e in ENG}
        self.seen = {e: {} for e in ENG}
        self.lastw = {}
        self.rds = {}
        self.ndsem = {'sp': 6, 'pool': 2, 'act': 2}
        self.dsem = {}
        self.dval = {}
        self.drr = {}
        for qn, n in self.ndsem.items():
            self.dsem[qn] = [es.enter_context(nc.semaphore(f"d_{qn}{i}")) for i in range(n)]
            for i in range(n):
                self.dval[(qn, i)] = 0
            self.drr[qn] = 0
        self.ninstr = 0

    def _semh(self, s):
        if isinstance(s, tuple):
            return self.dsem[s[0]][s[1]]
        return self.sem[s]

    def _wait(self, eng, tok):
        s, v = tok
        if s == eng and (eng == 'pe' or not SAME_ENG_SYNC):
            return
        if self.seen[eng].get(s, 0) >= v:
            return
        self.seen[eng][s] = v
        self.q[eng].append(('w', self._semh(s), v))

    def _deps(self, eng, reads, writes, is_dma=False):
        for k in reads:
            for tok in self.lastw.get(k, ()):
                self._wait(eng, tok)
        for k in writes:
            for tok in self.lastw.get(k, ()):
                if is_dma and isinstance(tok[0], tuple):
                    continue
                self._wait(eng, tok)
            for s, v in self.rds.get(k, {}).items():
                self._wait(eng, (s, v))

    def _record(self, tok, reads, writes, is_dma=False):
        s, v = tok
        for k in reads:
            d = self.rds.setdefault(k, {})
            if d.get(s, 0) < v:
                d[s] = v
        for k in writes:
            if is_dma and not self.rds.get(k) and k in self.lastw and all(isinstance(t[0], tuple) for t in self.lastw[k]):
                self.lastw[k] = [t for t in self.lastw[k] if t[0] != s] + [tok]
            else:
                self.lastw[k] = [tok]
            self.rds[k] = {}

    def op(self, eng, fn, reads=(), writes=()):
        xr = [k for k in reads if k in PSUM_KEYS]
        if xr:
            reads = [k for k in reads if k not in PSUM_KEYS]
            writes = list(writes) + [k for k in xr if k not in writes]
        self._deps(eng, reads, writes)
        self.cnt[eng] += 1
        tok = (eng, self.cnt[eng])
        self.q[eng].append(('o', fn, self.sem[eng], 1))
        self._record(tok, reads, writes)
        self.ninstr += 1

    def dma(self, qn, out, in_, reads=(), writes=(), **kw):
        self._deps(qn, reads, writes, is_dma=True)
        i = self.drr[qn]
        self.drr[qn] = (i + 1) % self.ndsem[qn]
        prev = self.dval[(qn, i)]
        if prev > 0:
            self._wait(qn, ((qn, i), prev))
        self.dval[(qn, i)] = prev + 16
        tok = ((qn, i), prev + 16)
        self.q[qn].append(('o', lambda e: e.dma_start(out=out, in_=in_, **kw), self.dsem[qn][i], 16))
        self._record(tok, reads, writes, is_dma=True)
        self.ninstr += 1

    def barrier(self):
        for e in ENG:
            for e2 in ENG:
                if e2 != e and self.cnt[e2] > 0:
                    self._wait(e, (e2, self.cnt[e2]))
            for k, v in self.dval.items():
                if v > 0:
                    self._wait(e, (k, v))
        self.lastw = {}
        self.rds = {}

    def emit(self):
        nc = self.nc
        with nc.Block() as block:
            def run(e, name):
                for it in self.q[name]:
                    if it[0] == 'w':
                        e.wait_ge(it[1], it[2])
                    else:
                        it[1](e).then_inc(it[2], it[3])

            @block.tensor
            def _(e):
                run(e, 'pe')

            @block.vector
            def _(e):
                run(e, 'dve')

            @block.scalar
            def _(e):
                run(e, 'act')

            @block.gpsimd
            def _(e):
                run(e, 'pool')

            @block.sync
            def _(e):
                run(e, 'sp')

    def mm(self, out, lhsT, rhs, start, stop, reads, writes):
        self.op('pe', lambda e: e.matmul(out, lhsT, rhs, start=start, stop=stop), reads, writes)

    def mm64(self, out, lhsT, rhs, start, stop, reads, writes):
        grp = int(lhsT.base_partition) if hasattr(lhsT, 'base_partition') and not callable(lhsT.base_partition) else int(lhsT.base_partition())
        last = getattr(self, '_last64', None)
        if self.cnt['pe'] > 0 and last is not None and last[0] == self.cnt['pe'] and last[1] != grp:
            self.q['pe'].append(('w', self.sem['pe'], self.cnt['pe']))
        self.mm(out, lhsT, rhs, start, stop, reads, writes)
        self._last64 = (self.cnt['pe'], grp)

    def tr(self, out, in_, ident, reads, writes):
        self.op('pe', lambda e: e.transpose(out, in_, ident), reads, writes)

    def act(self, out, in_, func, reads, writes, bias=None, scale=None):
        kw = {}
        if bias is not None:
            kw['bias'] = bias
        if scale is not None:
            kw['scale'] = scale
        self.op('act', lambda e: e.activation(out, in_, func, **kw), reads, writes)

    def tt(self, eng, out, in0, in1, op, reads, writes):
        self.op(eng, lambda e: e.tensor_tensor(out, in0, in1, op), reads, writes)

    def ts(self, eng, out, in0, s1, s2, op0, op1, reads, writes):
        if op1 is None:
            self.op(eng, lambda e: e.tensor_scalar(out, in0, s1, None, op0), reads, writes)
        else:
            self.op(eng, lambda e: e.tensor_scalar(out, in0, s1, s2, op0, op1), reads, writes)

    def stt(self, out, in0, scalar, in1, op0, op1, reads, writes):
        self.op('dve', lambda e: e.scalar_tensor_tensor(out, in0, scalar, in1, op0, op1), reads, writes)

    def copy(self, eng, out, in_, reads, writes):
        if eng == 'act':
            self.op('act', lambda e: e.copy(out, in_), reads, writes)
        else:
            self.op(eng, lambda e: e.tensor_copy(out, in_), reads, writes)

    def memset(self, eng, ap, val, writes):
        self.op(eng, lambda e: e.memset(ap, val), (), writes)


class Ctx:
    pass


_UID = [0]


def sb(c, es, name, shape, dt):
    _UID[0] += 1
    return es.enter_context(c.nc.sbuf_tensor(f"{name}_{_UID[0]}", shape, dt))


def ps(c, es, name, shape, dt=F32):
    _UID[0] += 1
    return es.enter_context(c.nc.psum_tensor(f"{name}_{_UID[0]}", shape, dt))


def stage_in(c, srcs):
    p = c.p
    with contextlib.ExitStack() as es:
        xin = [sb(c, es, f"in_x{i}", [128, D], F32) for i in range(2)]
        xt = [sb(c, es, f"in_xt{i}", [128, KC, 512], F32) for i in range(2)]
        pt = [ps(c, es, f"in_ps{i}", [128, 512]) for i in range(4)]
        n = 0
        for s, (src, b) in enumerate(srcs):
            for g in range(L // 512):
                xo = xt[g % 2]
                ko = f"in_xt{g % 2}"
                for j in range(4):
                    t0 = g * 512 + j * 128
                    xi = xin[n % 2]
                    ki = f"in_x{n % 2}"
                    p.dma('sp', xi[:], src[b, t0:t0 + 128, :], [], [ki])
                    for half in range(2):
                        pp = pt[(2 * n + half) % 4]
                        kp = f"in_ps{(2 * n + half) % 4}"
                        for q4 in range(4):
                            kc = half * 4 + q4
                            p.tr(pp[:, q4 * 128:(q4 + 1) * 128], xi[:, kc * 128:(kc + 1) * 128], c.ident[:],
                                 [ki], [kp])
                        eng = 'act' if half == 0 else 'dve'
                        p.copy(eng, xo[:, half * 4:half * 4 + 4, j * 128:(j + 1) * 128],
                               pp[:].rearrange("p (a b) -> p a b", a=4), [kp], [ko])
                    n += 1
                tok0 = s * L + g * 512
                p.dma('sp', c.XT[:, tok0:tok0 + 512].rearrange("(kc p) t -> p kc t", p=128), xo[:],
                      [ko], [("XT", tok0 // 256), ("XT", tok0 // 256 + 1)])
    p.barrier()


def load_vec(c, dst, src_ap, key, q='sp'):
    c.p.dma(q, dst, src_ap.rearrange("(j p) -> p j", p=128), [], [key], allow_slow_non_contiguous=True)


def rms_stats(c, x, kx, sq, ksq, pss, kps, var, kvar, rstd, krstd, nt, nch=KC, dim=D, eps=1e-6):
    p = c.p
    p.act(sq[:, :nch, :nt], x[:, :nch, :nt], AF.Square, [kx], [ksq])
    for kc in range(nch):
        p.mm(pss[:, :nt], c.ones[:], sq[:, kc, :nt], kc == 0, kc == nch - 1, [ksq], [kps])
    p.ts('dve', var[:, :nt], pss[:, :nt], 1.0 / dim, eps, ALU.mult, ALU.add, [kps], [kvar])
    p.tt('pool', rstd[:, :nt], var[:, :nt], c.mhalf[:, :nt], ALU.pow, [kvar], [krstd])


def stage_out(c, dsts, gvec):
    p = c.p
    NT = 256
    with contextlib.ExitStack() as es:
        g = sb(c, es, "o_g", [128, KC], F32)
        x = [sb(c, es, f"o_x{i}", [128, KC, NT], F32) for i in range(2)]
        sq = sb(c, es, "o_sq", [128, KC, NT], F32)
        var = sb(c, es, "o_var", [128, NT], F32)
        rstd = sb(c, es, "o_rstd", [128, NT], F32)
        xn = sb(c, es, "o_xn", [128, KC, NT], F32)
        yo = [sb(c, es, f"o_y{i}", [128, D], F32) for i in range(2)]
        pss = ps(c, es, "o_pss", [128, 512])
        pt = [ps(c, es, f"o_pt{i}", [128, 512]) for i in range(4)]
        load_vec(c, g[:], gvec, "o_g")
        n = 0
        m = 0
        for s, (dst, b) in enumerate(dsts):
            for gi in range(L // NT):
                tok0 = s * L + gi * NT
                xi = x[gi % 2]
                kx = f"o_x{gi % 2}"
                p.dma('sp', xi[:], c.XT[:, tok0:tok0 + NT].rearrange("(kc p) t -> p kc t", p=128),
                      [("XT", tok0 // 256)], [kx])
                rms_stats(c, xi, kx, sq, "o_sq", pss, "o_pss", var, "o_var", rstd, "o_rstd", NT)
                for kc in range(KC):
                    p.stt(xn[:, kc, :], xi[:, kc, :], g[:, kc:kc + 1], rstd[:], ALU.mult, ALU.mult,
                          [kx, "o_g", "o_rstd"], ["o_xn"])
                for j in range(NT // 128):
                    y = yo[m % 2]
                    ky = f"o_y{m % 2}"
                    for half in range(2):
                        pp = pt[n % 4]
                        kp = f"o_pt{n % 4}"
                        n += 1
                        for q4 in range(4):
                            kc = half * 4 + q4
                            p.tr(pp[:, q4 * 128:(q4 + 1) * 128], xn[:, kc, j * 128:(j + 1) * 128], c.ident[:],
                                 ["o_xn"], [kp])
                        eng = 'act' if half == 0 else 'dve'
                        p.copy(eng, y[:, half * 512:(half + 1) * 512], pp[:], [kp], [ky])
                    t0 = gi * NT + j * 128
                    p.dma('sp', dst[b, t0:t0 + 128, :], y[:], [ky], [])
                    m += 1
    p.barrier()


def stage_ffn(c, gvec, w_gu, w_down, ntok):
    p = c.p
    NT = 256
    with contextlib.ExitStack() as es:
        wgu = sb(c, es, "f_wgu", [128, KC, 2 * DFF], BF16)
        wd = sb(c, es, "f_wd", [128, FC, D], BF16)
        g = sb(c, es, "f_g", [128, KC], F32)
        x = [sb(c, es, f"f_x{i}", [128, KC, NT], F32) for i in range(2)]
        sq = sb(c, es, "f_sq", [128, KC, NT], F32)
        var = sb(c, es, "f_var", [128, NT], F32)
        rstd = sb(c, es, "f_rstd", [128, NT], F32)
        xn = sb(c, es, "f_xn", [128, KC, NT], BF16)
        h = sb(c, es, "f_h", [128, FC, NT], BF16)
        sg = [sb(c, es, f"f_sg{i}", [128, NT], F32) for i in range(2)]
        pss = ps(c, es, "f_pss", [128, 512])
        pg = [ps(c, es, f"f_pg{i}", [128, 512]) for i in range(2)]
        pu = [ps(c, es, f"f_pu{i}", [128, 512]) for i in range(2)]
        py = [ps(c, es, f"f_py{i}", [128, 512]) for i in range(2)]
        load_vec(c, g[:], gvec, "f_g")
        wl = WLoader(c, es, "f_wst", 1408)
        wl.dbg = c.cfg.get('ffn_dbg', 0)
        for kc in range(KC if wl.dbg != 1 else 0):
            for hh in range(4):
                p_ = wl.load(wgu[:, kc, hh * 1408:(hh + 1) * 1408],
                             w_gu[kc * 128:(kc + 1) * 128, hh * 1408:(hh + 1) * 1408], "f_wgu")
        for j in range(FC if wl.dbg != 1 else 0):
            wl.load(wd[:, j, :], w_down[j * 128:(j + 1) * 128, :], "f_wd")
        for gi in range(c.cfg.get('ffn_groups', ntok // NT)):
            tok0 = gi * NT
            xi = x[gi % 2]
            kx = f"f_x{gi % 2}"
            p.dma('sp', xi[:], c.XT[:, tok0:tok0 + NT].rearrange("(kc p) t -> p kc t", p=128),
                  [("XT", gi)], [kx])
            rms_stats(c, xi, kx, sq, "f_sq", pss, "f_pss", var, "f_var", rstd, "f_rstd", NT)
            for kc in range(KC):
                p.stt(xn[:, kc, :], xi[:, kc, :], g[:, kc:kc + 1], rstd[:], ALU.mult, ALU.mult,
                      [kx, "f_g", "f_rstd"], ["f_xn"])
            for j in range(FC):
                pgj, puj, sgj = pg[j % 2], pu[j % 2], sg[j % 2]
                kg, ku, ks = f"f_pg{j % 2}", f"f_pu{j % 2}", f"f_sg{j % 2}"
                for kc in range(KC):
                    p.mm(pgj[:, :NT], wgu[:, kc, j * 128:(j + 1) * 128], xn[:, kc, :], kc == 0, kc == KC - 1,
                         ["f_wgu", "f_xn"], [kg])
                for kc in range(KC):
                    p.mm(puj[:, :NT], wgu[:, kc, DFF + j * 128:DFF + (j + 1) * 128], xn[:, kc, :], kc == 0,
                         kc == KC - 1, ["f_wgu", "f_xn"], [ku])
                p.act(sgj[:], pgj[:, :NT], AF.Silu, [kg], [ks])
                p.tt('dve', h[:, j, :], sgj[:], puj[:, :NT], ALU.mult, [ks, ku], [("f_h", j)])
            for dc in range(KC):
                pyj = py[dc % 2]
                ky = f"f_py{dc % 2}"
                for j in range(FC):
                    p.mm(pyj[:, :NT], wd[:, j, dc * 128:(dc + 1) * 128], h[:, j, :], j == 0, j == FC - 1,
                         ["f_wd", ("f_h", j)], [ky])
                p.stt(sq[:, dc, :], pyj[:, :NT], 0.5, xi[:, dc, :], ALU.mult, ALU.add, [ky, kx], ["f_sq"])
            p.dma('sp', c.XT[:, tok0:tok0 + NT].rearrange("(kc p) t -> p kc t", p=128), sq[:],
                  ["f_sq"], [("XT", gi)])
    p.barrier()


class Ring:
    def __init__(self, tiles, name):
        self.tiles = tiles
        self.name = name
        self.i = 0

    def next(self):
        t = self.tiles[self.i % len(self.tiles)]
        k = f"{self.name}{self.i % len(self.tiles)}"
        self.i += 1
        return t, k


def ring_sb(c, es, name, n, shape, dt):
    return Ring([sb(c, es, f"{name}{i}", shape, dt) for i in range(n)], name)


def ring_ps(c, es, name, n, shape, dt=F32):
    for i in range(n):
        PSUM_KEYS.add(f"{name}{i}")
    return Ring([ps(c, es, f"{name}{i}", shape, dt) for i in range(n)], name)


class WLoader:
    def __init__(self, c, es, name, width, n=2):
        self.c = c
        self.ring = ring_sb(c, es, name, n, [128, width], F32)
        self.width = width
        self.k = 0

    def load(self, dst, src, key, shape=None):
        p = self.c.p
        st, kst = self.ring.next()
        n = 1
        for d_ in dst.shape[1:]:
            n *= d_
        sv = st[:dst.shape[0], :n]
        if len(dst.shape) == 3:
            sv = sv.rearrange("p (a b) -> p a b", a=dst.shape[1])
        elif len(dst.shape) == 4:
            sv = sv.rearrange("p (a b c) -> p a b c", a=dst.shape[1], b=dst.shape[2])
        p.dma('sp', sv, src, [], [kst])
        eng = 'dve' if self.k % 2 == 0 else 'act'
        if getattr(self, 'dbg', 0) == 3:
            eng = 'pool'
        if getattr(self, 'dbg', 0) == 4:
            eng = 'act'
        self.k += 1
        if getattr(self, 'dbg', 0) != 2:
            p.copy(eng, dst, sv, [kst], [key])


def xt_view(c, tok0, nt):
    return c.XT[:, tok0:tok0 + nt].rearrange("(kc p) t -> p kc t", p=128)


def xt_keys(tok0, nt):
    return [("XT", k) for k in range(tok0 // 256, (tok0 + nt + 255) // 256)]


def load_xn(c, tok0, NT, xr, sq, pss, var, rstd, g, kg, xn, kxn, pref):
    p = c.p
    xi, kx = xr.next()
    p.dma('sp', xi[:, :, :NT], xt_view(c, tok0, NT), xt_keys(tok0, NT), [kx])
    rms_stats(c, xi, kx, sq, pref + "sq", pss, pref + "pss", var, pref + "var", rstd, pref + "rstd", NT)
    for kc in range(KC):
        p.stt(xn[:, kc, :NT], xi[:, kc, :NT], g[:, kc:kc + 1], rstd[:, :NT], ALU.mult, ALU.mult,
              [kx, kg, pref + "rstd"], [kxn])
    return xi, kx


def stage_attn(c, layer, j, nseq):
    p = c.p
    W = c.W
    wqkv, wout, sinks_d, gvec = W['att_w_qkv'][j], W['att_w_out'][j], W['att_sinks'][j], W['mix_norm'][layer]
    NEG = -30000.0
    with contextlib.ExitStack() as es:
        wq = sb(c, es, "a_wq", [128, KC, 1024], BF16)
        wqs = sb(c, es, "a_wqs", [128, KC, 1024], BF16)
        wk = sb(c, es, "a_wk", [128, KC, 512], BF16)
        wks = sb(c, es, "a_wks", [128, KC, 512], BF16)
        wv = sb(c, es, "a_wv", [128, KC, 256], BF16)
        wo = sb(c, es, "a_wo", [128, KC, 1024], BF16)
        cos = sb(c, es, "a_cos", [128, L], F32)
        sin = sb(c, es, "a_sin", [128, L], F32)
        g = sb(c, es, "a_g", [128, KC], F32)
        snk = sb(c, es, "a_snk", [128, 16], F32)
        nsnk = sb(c, es, "a_nsnk", [128, 16], F32)
        mask = sb(c, es, "a_mask", [128, 384], F32)
        qT = sb(c, es, "a_qT", [128, KC, L], BF16)
        kT = sb(c, es, "a_kT", [128, 4, L], BF16)
        vtok = sb(c, es, "a_v", [128, 16, 256], BF16)
        load_vec(c, g[:], gvec, "a_g")
        p.dma('sp', cos[:], c.c_cos[:, :], [], ["a_cos"])
        p.dma('sp', sin[:], c.c_sin[:, :], [], ["a_sin"])
        p.dma('sp', snk[:], sinks_d.partition_broadcast(128), [], ["a_snk"])
        p.ts('dve', nsnk[:], snk[:], -1.0, None, ALU.mult, None, ["a_snk"], ["a_nsnk"])
        p.memset('pool', mask[:], 0.0, ["a_mask"])
        p.op('pool', lambda e: e.affine_select(mask[:], mask[:], [[1, 384]], ALU.is_ge, NEG, base=0,
                                               channel_multiplier=-1), ["a_mask"], ["a_mask"])
        p.op('pool', lambda e: e.affine_select(mask[:], mask[:], [[-1, 384]], ALU.is_ge, NEG, base=256,
                                               channel_multiplier=1), ["a_mask"], ["a_mask"])
        wl = WLoader(c, es, "a_wst", 1024)
        for kc in range(KC):
            rows = slice(kc * 128, (kc + 1) * 128)
            wl.load(wq[:, kc, :], wqkv[rows, 0:1024], "a_w")
            src = wqkv[rows, 0:1024].rearrange("p (h r d) -> p h r d", h=16, r=2)
            dst = wqs[:, kc, :].rearrange("p (h r d) -> p h r d", h=16, r=2)
            wl.load(dst[:, :, 0, :], src[:, :, 1, :], "a_w")
            wl.load(dst[:, :, 1, :], src[:, :, 0, :], "a_w")
            srck = wqkv[rows, 1024:1280].rearrange("p (g r d) -> p g r d", g=4, r=2)
            dk = wk[:, kc, :].rearrange("p (g c r d) -> p g c r d", g=4, c=2, r=2)
            dks = wks[:, kc, :].rearrange("p (g c r d) -> p g c r d", g=4, c=2, r=2)
            for cpy in range(2):
                for r in range(2):
                    wl.load(dk[:, :, cpy, r, :], srck[:, :, r, :], "a_w")
                    wl.load(dks[:, :, cpy, r, :], srck[:, :, 1 - r, :], "a_w")
            wl.load(wv[:, kc, :], wqkv[rows, 1280:1536], "a_w")
            wl.load(wo[:, kc, :], wout[rows, :], "a_w")
        p.barrier()
        for s in range(nseq):
            with contextlib.ExitStack() as es2:
                NT = 256
                xr = ring_sb(c, es2, "aA_x", 2, [128, KC, NT], F32)
                sq = sb(c, es2, "aA_sq", [128, KC, NT], F32)
                var = sb(c, es2, "aA_var", [128, NT], F32)
                rstd = sb(c, es2, "aA_rstd", [128, NT], F32)
                xn = sb(c, es2, "aA_xn", [128, KC, NT], BF16)
                t1r = ring_sb(c, es2, "aA_t1", 2, [128, NT], F32)
                t2r = ring_sb(c, es2, "aA_t2", 2, [128, NT], F32)
                pss = ps(c, es2, "aA_pss", [128, 512])
                p1r = ring_ps(c, es2, "aA_p1", 2, [128, 512])
                p2r = ring_ps(c, es2, "aA_p2", 2, [128, 512])
                pvr = ring_ps(c, es2, "aA_pv", 2, [128, 512])
                for gi in range(L // NT):
                    t0 = gi * NT
                    load_xn(c, s * L + t0, NT, xr, sq, pss, var, rstd, g, "a_g", xn, "aA_xn", "aA_")
                    for oc in range(12):
                        if oc < 8:
                            wa, wb, dstT = wq[:, :, oc * 128:(oc + 1) * 128], wqs[:, :, oc * 128:(oc + 1) * 128], qT[:, oc, t0:t0 + NT]
                            kd = ("a_qT", oc)
                        else:
                            gg = oc - 8
                            wa, wb, dstT = wk[:, :, gg * 128:(gg + 1) * 128], wks[:, :, gg * 128:(gg + 1) * 128], kT[:, gg, t0:t0 + NT]
                            kd = ("a_kT", gg)
                        p1, k1 = p1r.next()
                        p2, k2 = p2r.next()
                        for kc in range(KC):
                            p.mm(p1[:, :NT], wa[:, kc, :], xn[:, kc, :], kc == 0, kc == KC - 1, ["a_w", "aA_xn"], [k1])
                        for kc in range(KC):
                            p.mm(p2[:, :NT], wb[:, kc, :], xn[:, kc, :], kc == 0, kc == KC - 1, ["a_w", "aA_xn"], [k2])
                        t1, kt1 = t1r.next()
                        t2, kt2 = t2r.next()
                        p.tt('dve', t1[:], p1[:, :NT], cos[:, t0:t0 + NT], ALU.mult, [k1, "a_cos"], [kt1])
                        p.tt('dve', t2[:], p2[:, :NT], sin[:, t0:t0 + NT], ALU.mult, [k2, "a_sin"], [kt2])
                        p.tt('dve', dstT, t1[:], t2[:], ALU.add, [kt1, kt2], [kd])
                    for tb in range(NT // 128):
                        pv, kv = pvr.next()
                        for kc in range(KC):
                            p.mm(pv[:, :256], xn[:, kc, tb * 128:(tb + 1) * 128], wv[:, kc, :], kc == 0, kc == KC - 1,
                                 ["a_w", "aA_xn"], [kv])
                        p.copy('act', vtok[:, (t0 // 128) + tb, :], pv[:, :256], [kv], [("a_v", (t0 // 128) + tb)])
            p.barrier()
            with contextlib.ExitStack() as es2:
                smr = ring_sb(c, es2, "aB_sm", 2, [128, 384], F32)
                er = ring_sb(c, es2, "aB_e", 2, [128, 384], F32)
                enr = ring_sb(c, es2, "aB_en", 2, [128, 384], BF16)
                eTr = ring_sb(c, es2, "aB_eT", 2, [128, 384], BF16)
                str_ = ring_sb(c, es2, "aB_st", 4, [128, 8], F32)
                oT = sb(c, es2, "aB_oT", [128, KC, 512], BF16)
                xr = ring_sb(c, es2, "aB_x", 1, [128, KC, 512], F32)
                xo = sb(c, es2, "aB_xo", [128, KC, 512], F32)
                spr = ring_ps(c, es2, "aB_sp", 2, [128, 512])
                tpr = ring_ps(c, es2, "aB_tp", 2, [128, 512], BF16)
                opr = ring_ps(c, es2, "aB_op", 2, [128, 512])
                ypr = ring_ps(c, es2, "aB_yp", 2, [128, 512])
                for gi in range(L // 512):
                    tok0 = s * L + gi * 512
                    xi, kx = xr.next()
                    p.dma('sp', xi[:], xt_view(c, tok0, 512), xt_keys(tok0, 512), [kx])
                    for qc in range(KC):
                        op_, kop = opr.next()
                        for qb in range(4):
                            jb = gi * 4 + qb
                            kb0, kb1 = max(jb - 1, 0), min(jb + 1, 15)
                            nk = kb1 - kb0 + 1
                            mo = 128 if jb == 0 else 0
                            nkw = nk * 128
                            for hp in range(2):
                                h = qc * 2 + hp
                                gk = h // 4
                                b0 = hp * 64
                                sp_, ksp = spr.next()
                                p.mm(sp_[:, :nkw], qT[b0:b0 + 64, qc, jb * 128:(jb + 1) * 128],
                                     kT[b0:b0 + 64, gk, kb0 * 128:(kb1 + 1) * 128], True, True,
                                     [("a_qT", qc), ("a_kT", gk)], [ksp])
                                sm, ksm = smr.next()
                                p.tt('dve', sm[:, :nkw], sp_[:, :nkw], mask[:, mo:mo + nkw], ALU.add, [ksp, "a_mask"], [ksm])
                                st, kst = str_.next()
                                p.op('dve', lambda e, st=st, sm=sm, nkw=nkw: e.reduce_max(st[:, 0:1], sm[:, :nkw], AX.X), [ksm], [kst])
                                p.ts('dve', st[:, 1:2], st[:, 0:1], -0.125, nsnk[:, h:h + 1], ALU.mult, ALU.min, ["a_nsnk", kst], [kst])
                                e_, ke = er.next()
                                p.op('act', lambda e, e_=e_, sm=sm, st=st, nkw=nkw: e.activation(
                                    e_[:, :nkw], sm[:, :nkw], AF.Exp, bias=st[:, 1:2], scale=0.125, accum_out=st[:, 2:3]),
                                    [ksm, kst], [ke, kst])
                                p.op('act', lambda e, st=st, h=h: e.activation(st[:, 3:4], st[:, 1:2], AF.Exp, bias=snk[:, h:h + 1]),
                                     [kst, "a_snk"], [kst])
                                p.tt('dve', st[:, 4:5], st[:, 2:3], st[:, 3:4], ALU.add, [kst], [kst])
                                p.op('dve', lambda e, st=st: e.reciprocal(st[:, 5:6], st[:, 4:5]), [kst], [kst])
                                en, ken = enr.next()
                                p.ts('dve', en[:, :nkw], e_[:, :nkw], st[:, 5:6], None, ALU.mult, None, [ke, kst], [ken])
                                tp, ktp = tpr.next()
                                for kb in range(nk):
                                    p.tr(tp[:, kb * 128:(kb + 1) * 128], en[:, kb * 128:(kb + 1) * 128], c.identb[:], [ken], [ktp])
                                eT, keT = eTr.next()
                                p.copy('act', eT[:, :nkw], tp[:, :nkw], [ktp], [keT])
                                for kb in range(nk):
                                    p.mm(op_[b0:b0 + 64, qb * 128:(qb + 1) * 128], vtok[:, kb0 + kb, gk * 64:(gk + 1) * 64],
                                         eT[:, kb * 128:(kb + 1) * 128], kb == 0, kb == nk - 1,
                                         [("a_v", kb0 + kb), keT], [kop])
                        p.copy('act', oT[:, qc, :], op_[:], [kop], [("aB_oT", qc)])
                    for dc in range(KC):
                        yp, kyp = ypr.next()
                        for qc in range(KC):
                            p.mm(yp[:], wo[:, qc, dc * 128:(dc + 1) * 128], oT[:, qc, :], qc == 0, qc == KC - 1,
                                 ["a_w", ("aB_oT", qc)], [kyp])
                        p.tt('dve', xo[:, dc, :], yp[:], xi[:, dc, :], ALU.add, [kyp, kx], ["aB_xo"])
                    p.dma('sp', xt_view(c, tok0, 512), xo[:], ["aB_xo"], xt_keys(tok0, 512))
            p.barrier()


def make_masks(c, es):
    p = c.p
    c.mk = {}
    for name, pat, base, cm, op in [("LE", 1, 0, -1, ALU.is_ge), ("GE", -1, 0, 1, ALU.is_ge),
                                    ("GT", -1, 0, 1, ALU.is_gt), ("LT", 1, 0, -1, ALU.is_gt)]:
        t = sb(c, es, "mk" + name, [128, 128], F32)
        p.memset('pool', t[:], 1.0, ["mk" + name])
        p.op('pool', lambda e, t=t, pat=pat, base=base, cm=cm, op=op: e.affine_select(
            t[:], t[:], [[pat, 128]], op, 0.0, base=base, channel_multiplier=cm), ["mk" + name], ["mk" + name])
        c.mk[name] = t


def bc(ap, shape, axis):
    return ap.unsqueeze(axis).broadcast_to(shape)


def stage_ssd(c, layer, j, nseq):
    p = c.p
    W = c.W
    w_in, conv_w, conv_b = W['ssd_w_in'][j], W['ssd_conv_w'][j], W['ssd_conv_b'][j]
    a_log, dt_bias, d_skip, norm_w, w_out = W['ssd_a_log'][j], W['ssd_dt_bias'][j], W['ssd_d'][j], W['ssd_norm'][j], W['ssd_w_out'][j]
    gvec = W['mix_norm'][layer]
    NB = L // 128
    with contextlib.ExitStack() as es:
        g = sb(c, es, "s_g", [128, KC], F32)
        cw = sb(c, es, "s_cw", [128, 5, 32], F32)
        cb = sb(c, es, "s_cb", [128, 32], F32)
        dtb = sb(c, es, "s_dtb", [128, 64], F32)
        aneg = sb(c, es, "s_aneg", [128, 64], F32)
        dsk = sb(c, es, "s_dsk", [128, 32], F32)
        nw = sb(c, es, "s_nw", [128, 2048], F32)
        xn = sb(c, es, "s_xn", [128, KC, L], BF16)
        dt = sb(c, es, "s_dt", [128, NB, 64], F32)
        load_vec(c, g[:], gvec, "s_g")
        for tap in range(5):
            p.dma('sp', cw[:, tap, :], conv_w[tap].rearrange("(cc p) -> p cc", p=128), [], ["s_cw"], allow_slow_non_contiguous=True)
        p.dma('sp', cb[:], conv_b.rearrange("(cc p) -> p cc", p=128), [], ["s_cb"], allow_slow_non_contiguous=True)
        p.dma('sp', dtb[:], dt_bias.rearrange("a b -> (a b)").partition_broadcast(128), [], ["s_dtb"])
        p.dma('sp', aneg[:], a_log.rearrange("a b -> (a b)").partition_broadcast(128), [], ["s_aneg"])
        p.dma('sp', dsk[:], d_skip.partition_broadcast(128), [], ["s_dsk"])
        p.dma('sp', nw[:], norm_w.partition_broadcast(128), [], ["s_nw"])
        p.act(aneg[:], aneg[:], AF.Exp, ["s_aneg"], ["s_aneg"])
        p.ts('dve', aneg[:], aneg[:], -1.0, None, ALU.mult, None, ["s_aneg"], ["s_aneg"])
        p.barrier()
        for s in range(nseq):
            with contextlib.ExitStack() as es2:
                NT = 256
                xr = ring_sb(c, es2, "sA_x", 2, [128, KC, NT], F32)
                sq = sb(c, es2, "sA_sq", [128, KC, NT], F32)
                var = sb(c, es2, "sA_var", [128, NT], F32)
                rstd = sb(c, es2, "sA_rstd", [128, NT], F32)
                pss = ps(c, es2, "sA_pss", [128, 512])
                for gi in range(L // NT):
                    load_xn(c, s * L + gi * NT, NT, xr, sq, pss, var, rstd, g, "s_g", xn[:, :, gi * NT:(gi + 1) * NT], "s_xn", "sA_")
            p.barrier()
            with contextlib.ExitStack() as es2:
                wzr = ring_sb(c, es2, "sB_wz", 2, [128, KC, 512], BF16)
                wdt = sb(c, es2, "sB_wdt", [128, KC, 64], BF16)
                zr = ring_sb(c, es2, "sB_z", 3, [128, 512], F32)
                pzr = ring_ps(c, es2, "sB_pz", 4, [128, 512])
                pdr = ring_ps(c, es2, "sB_pd", 2, [128, 512])
                wl = WLoader(c, es2, "sB_wst", 4096)
                wl.load(wdt[:], w_in[:, 6144:6208].rearrange("(kc p) c -> p kc c", p=128), "sB_wdt")
                for blk in range(NB):
                    pd, kpd = pdr.next()
                    for kc in range(KC):
                        p.mm(pd[:, :64], xn[:, kc, blk * 128:(blk + 1) * 128], wdt[:, kc, :], kc == 0, kc == KC - 1,
                             ["s_xn", "sB_wdt"], [kpd])
                    p.tt('dve', dt[:, blk, :], pd[:, :64], dtb[:], ALU.add, [kpd, "s_dtb"], ["s_dt"])
                p.act(dt[:], dt[:], AF.Exp, ["s_dt"], ["s_dt"])
                p.act(dt[:], dt[:], AF.Ln, ["s_dt"], ["s_dt"], bias=1.0)
                for zc in range(4):
                    wz, kwz = wzr.next()
                    wl.load(wz[:], w_in[:, zc * 512:(zc + 1) * 512].rearrange("(kc p) c -> p kc c", p=128), kwz)
                    for blk in range(NB):
                        pz, kpz = pzr.next()
                        for kc in range(KC):
                            p.mm(pz[:], xn[:, kc, blk * 128:(blk + 1) * 128], wz[:, kc, :], kc == 0, kc == KC - 1,
                                 ["s_xn", kwz], [kpz])
                        z, kz = zr.next()
                        p.act(z[:], pz[:], AF.Silu, [kpz], [kz])
                        p.dma('sp', c.Zs[blk * 128:(blk + 1) * 128, zc * 512:(zc + 1) * 512], z[:], [kz], [("Zs", blk, zc)])
            p.barrier()
            with contextlib.ExitStack() as es2:
                wcr = ring_sb(c, es2, "sC_wc", 3, [128, KC, 128], BF16)
                wl = WLoader(c, es2, "sC_wst", 1024)
                raw = ring_sb(c, es2, "sC_raw", 2, [128, L + 4], F32)
                acc = ring_sb(c, es2, "sC_acc", 2, [128, L], F32)
                cvT = ring_sb(c, es2, "sC_cvT", 2, [128, L], BF16)
                BT = sb(c, es2, "sC_BT", [128, L], BF16)
                CT = sb(c, es2, "sC_CT", [128, L], BF16)
                Btok = sb(c, es2, "sC_Btok", [128, NB, 128], BF16)
                xtok = sb(c, es2, "sC_xtok", [128, NB, 256], BF16)
                xdt = sb(c, es2, "sC_xdt", [128, NB, 2, 256], BF16)
                dta = sb(c, es2, "sC_dta", [128, NB, 2, 4], F32)
                acs = sb(c, es2, "sC_acs", [128, NB, 2, 4], F32)
                tot = sb(c, es2, "sC_tot", [128, NB, 2, 4], F32)
                ea = sb(c, es2, "sC_ea", [128, NB, 2, 4], F32)
                edec = sb(c, es2, "sC_edec", [128, NB, 2, 4], F32)
                etot = sb(c, es2, "sC_etot", [128, NB, 2, 4], F32)
                Y = sb(c, es2, "sC_Y", [128, NB, 256], F32)
                H = sb(c, es2, "sC_H", [128, 256], F32)
                Hb = sb(c, es2, "sC_Hb", [128, 256], BF16)
                cbm = ring_sb(c, es2, "sC_cbm", 2, [128, 2, 128], F32)
                rhsr = ring_sb(c, es2, "sC_rhs", 2, [128, 4, 128], F32)
                decr = ring_sb(c, es2, "sC_dec", 2, [128, 4, 128], F32)
                mtr = ring_sb(c, es2, "sC_mt", 2, [128, 4, 128], BF16)
                tmpr = ring_sb(c, es2, "sC_tmp", 2, [128, 256], F32)
                xdr = ring_sb(c, es2, "sC_xd", 2, [128, 256], BF16)
                pA = ring_ps(c, es2, "sC_pA", 2, [128, 512])
                pT = ring_ps(c, es2, "sC_pT", 2, [128, 1024], BF16)
                pC = ring_ps(c, es2, "sC_pC", 1, [128, 512])
                pY = ring_ps(c, es2, "sC_pY", 2, [128, 512])
                pH = ring_ps(c, es2, "sC_pH", 1, [128, 512])
                for tl, ktl in zip(raw.tiles, ["sC_raw0", "sC_raw1"]):
                    p.memset('pool', tl[:, 0:2], 0.0, [ktl])
                    p.memset('pool', tl[:, L + 2:L + 4], 0.0, [ktl])
                for gq in range(8):
                    for ci, cc in enumerate([2 * gq, 2 * gq + 1, 16 + gq, 24 + gq]):
                        wc, kwc = wcr.next()
                        col0 = 2048 + cc * 128
                        wl.load(wc[:], w_in[:, col0:col0 + 128].rearrange("(kc p) c -> p kc c", p=128), kwc)
                        rw, krw = raw.next()
                        for tg in range(L // 512):
                            pa, kpa = pA.next()
                            for kc in range(KC):
                                p.mm(pa[:], wc[:, kc, :], xn[:, kc, tg * 512:(tg + 1) * 512], kc == 0, kc == KC - 1,
                                     [kwc, "s_xn"], [kpa])
                            p.copy('act', rw[:, 2 + tg * 512:2 + (tg + 1) * 512], pa[:], [kpa], [krw])
                        ac, kac = acc.next()
                        p.ts('dve', ac[:], rw[:, 0:L], cw[:, 0, cc:cc + 1], cb[:, cc:cc + 1], ALU.mult, ALU.add,
                             [krw, "s_cw", "s_cb"], [kac])
                        for tap in range(1, 5):
                            p.stt(ac[:], rw[:, tap:tap + L], cw[:, tap, cc:cc + 1], ac[:], ALU.mult, ALU.add,
                                  [krw, "s_cw", kac], [kac])
                        if ci < 2:
                            cv, kcv = cvT.next()
                            p.act(cv[:], ac[:], AF.Silu, [kac], [kcv])
                            for b4 in range(NB // 8):
                                pt, kpt = pT.next()
                                for q in range(8):
                                    blk = b4 * 8 + q
                                    p.tr(pt[:, q * 128:(q + 1) * 128], cv[:, blk * 128:(blk + 1) * 128], c.identb[:], [kcv], [kpt])
                                p.copy('act' if b4 % 2 else 'dve', xtok[:, b4 * 8:(b4 + 1) * 8, ci * 128:(ci + 1) * 128],
                                       pt[:].rearrange("p (a b) -> p a b", a=8), [kpt], ["sC_xtok"])
                        elif ci == 2:
                            p.act(BT[:], ac[:], AF.Silu, [kac], ["sC_BT"])
                            for b4 in range(NB // 8):
                                pt, kpt = pT.next()
                                for q in range(8):
                                    blk = b4 * 8 + q
                                    p.tr(pt[:, q * 128:(q + 1) * 128], BT[:, blk * 128:(blk + 1) * 128], c.identb[:], ["sC_BT"], [kpt])
                                p.copy('act' if b4 % 2 else 'dve', Btok[:, b4 * 8:(b4 + 1) * 8, :],
                                       pt[:].rearrange("p (a b) -> p a b", a=8), [kpt], ["sC_Btok"])
                        else:
                            p.act(CT[:], ac[:], AF.Silu, [kac], ["sC_CT"])
                    for d in range(2):
                        c0 = d * 32 + gq * 4
                        p.tt('dve', dta[:, :, d, :], dt[:, :, c0:c0 + 4], bc(aneg[:, c0:c0 + 4], [128, NB, 4], 1), ALU.mult,
                             ["s_dt", "s_aneg"], ["sC_dta"])
                        p.tt('dve', xdt[:, :, d, :].rearrange("p b (h q) -> p b h q", h=4),
                             xtok[:].rearrange("p b (h q) -> p b h q", h=4),
                             bc(dt[:, :, c0:c0 + 4], [128, NB, 4, 64], 3), ALU.mult, ["sC_xtok", "s_dt"], ["sC_xdt"])
                    pc, kpc = pC.next()
                    for d in range(2):
                        msk = c.mk["LE"] if d == 0 else c.mk["GE"]
                        for blk in range(NB):
                            p.mm(pc[:, (blk * 2 + d) * 4:(blk * 2 + d) * 4 + 4], msk[:], dta[:, blk, d, :], True, True,
                                 ["sC_dta", "mk"], [kpc])
                    p.copy('dve', acs[:].rearrange("p b d h -> p (b d h)"), pc[:, :NB * 8], [kpc], ["sC_acs"])
                    pc, kpc = pC.next()
                    p.mm(pc[:, :NB * 8], c.ones[:], dta[:].rearrange("p b d h -> p (b d h)"), True, True, ["sC_dta"], [kpc])
                    p.copy('dve', tot[:].rearrange("p b d h -> p (b d h)"), pc[:, :NB * 8], [kpc], ["sC_tot"])
                    p.act(ea[:].rearrange("p b d h -> p (b d h)"), acs[:].rearrange("p b d h -> p (b d h)"), AF.Exp, ["sC_acs"], ["sC_ea"])
                    p.act(etot[:].rearrange("p b d h -> p (b d h)"), tot[:].rearrange("p b d h -> p (b d h)"), AF.Exp, ["sC_tot"], ["sC_etot"])
                    p.tt('dve', edec[:].rearrange("p b d h -> p (b d h)"), tot[:].rearrange("p b d h -> p (b d h)"),
                         acs[:].rearrange("p b d h -> p (b d h)"), ALU.subtract, ["sC_tot", "sC_acs"], ["sC_edec"])
                    p.act(edec[:].rearrange("p b d h -> p (b d h)"), edec[:].rearrange("p b d h -> p (b d h)"), AF.Exp, ["sC_edec"], ["sC_edec"])
                    for d in range(2):
                        order = range(NB) if d == 0 else range(NB - 1, -1, -1)
                        mk_in, mk_l, mk_r = (("LE", "GT", "LE") if d == 0 else ("GE", "LT", "GE"))
                        for ci, blk in enumerate(order):
                            tsl = slice(blk * 128, (blk + 1) * 128)
                            first = ci == 0
                            pc, kpc = pC.next()
                            p.mm(pc[:, :128], BT[:, tsl], CT[:, tsl], True, True, ["sC_BT", "sC_CT"], [kpc])
                            cm_, kcm = cbm.next()
                            p.tt('dve', cm_[:, 0, :], pc[:, :128], c.mk[mk_in][:], ALU.mult, [kpc, "mk"], [kcm])
                            rh, krh = rhsr.next()
                            p.tt('dve', rh[:], bc(c.mk[mk_r][:], [128, 4, 128], 1), bc(dta[:, blk, d, :], [128, 4, 128], 2), ALU.mult,
                                 ["mk", "sC_dta"], [krh])
                            pa, kpa = pA.next()
                            p.mm(pa[:], c.mk[mk_l][:], rh[:].rearrange("p h l -> p (h l)"), True, True, [krh, "mk"], [kpa])
                            dc_, kdc = decr.next()
                            p.act(dc_[:].rearrange("p h l -> p (h l)"), pa[:], AF.Exp, [kpa], [kdc])
                            mt, kmt = mtr.next()
                            p.tt('dve', mt[:], dc_[:], bc(cm_[:, 0, :], [128, 4, 128], 1), ALU.mult, [kdc, kcm], [kmt])
                            py, kpy = pY.next()
                            for h in range(4):
                                p.mm(py[:, h * 64:(h + 1) * 64], mt[:, h, :], xdt[:, blk, d, h * 64:(h + 1) * 64], True, True,
                                     [kmt, "sC_xdt"], [kpy])
                            if not first:
                                p.mm(py[:, 256:512], CT[:, tsl], Hb[:], True, True, ["sC_CT", "sC_Hb"], [kpy])
                                tm, ktm = tmpr.next()
                                p.tt('dve', tm[:].rearrange("p (h q) -> p h q", h=4), py[:, 256:512].rearrange("p (h q) -> p h q", h=4),
                                     bc(ea[:, blk, d, :], [128, 4, 64], 2), ALU.mult, [kpy, "sC_ea"], [ktm])
                                if d == 0:
                                    p.tt('dve', Y[:, blk, :], tm[:], py[:, 0:256], ALU.add, [ktm, kpy], [("sC_Y", blk)])
                                else:
                                    p.tt('dve', tm[:], tm[:], py[:, 0:256], ALU.add, [ktm, kpy], [ktm])
                                    p.tt('dve', Y[:, blk, :], Y[:, blk, :], tm[:], ALU.add, [ktm, ("sC_Y", blk)], [("sC_Y", blk)])
                            else:
                                if d == 0:
                                    p.copy('dve', Y[:, blk, :], py[:, 0:256], [kpy], [("sC_Y", blk)])
                                else:
                                    p.tt('dve', Y[:, blk, :], Y[:, blk, :], py[:, 0:256], ALU.add, [kpy, ("sC_Y", blk)], [("sC_Y", blk)])
                            if ci < NB - 1:
                                xd, kxd = xdr.next()
                                p.tt('dve', xd[:].rearrange("p (h q) -> p h q", h=4),
                                     xdt[:, blk, d, :].rearrange("p (h q) -> p h q", h=4),
                                     bc(edec[:, blk, d, :], [128, 4, 64], 2), ALU.mult, ["sC_xdt", "sC_edec"], [kxd])
                                ph, kph = pH.next()
                                p.mm(ph[:, :256], Btok[:, blk, :], xd[:], True, True, ["sC_Btok", kxd], [kph])
                                if first:
                                    p.copy('dve', H[:], ph[:, :256], [kph], ["sC_H"])
                                else:
                                    p.tt('dve', H[:].rearrange("p (h q) -> p h q", h=4), H[:].rearrange("p (h q) -> p h q", h=4),
                                         bc(etot[:, blk, d, :], [128, 4, 64], 2), ALU.mult, ["sC_H", "sC_etot"], ["sC_H"])
                                    p.tt('dve', H[:], H[:], ph[:, :256], ALU.add, ["sC_H", kph], ["sC_H"])
                                p.copy('act', Hb[:], H[:], ["sC_H"], ["sC_Hb"])
                    for blk in range(NB):
                        tm, ktm = tmpr.next()
                        p.tt('dve', tm[:].rearrange("p (h q) -> p h q", h=4), xtok[:, blk, :].rearrange("p (h q) -> p h q", h=4),
                             bc(dsk[:, gq * 4:gq * 4 + 4], [128, 4, 64], 2), ALU.mult, ["sC_xtok", "s_dsk"], [ktm])
                        p.tt('dve', Y[:, blk, :], Y[:, blk, :], tm[:], ALU.add, [ktm, ("sC_Y", blk)], [("sC_Y", blk)])
                    p.dma('sp', c.Ys[:, gq * 256:(gq + 1) * 256].rearrange("(b p) q -> p b q", p=128), Y[:],
                          [("sC_Y", blk) for blk in range(NB)], [("Ys", gq)])
            p.barrier()
            with contextlib.ExitStack() as es2:
                wo = sb(c, es2, "sD_wo", [128, 16, D], BF16)
                yr = ring_sb(c, es2, "sD_y", 2, [128, 2048], F32)
                zr = ring_sb(c, es2, "sD_z", 2, [128, 2048], F32)
                junk = sb(c, es2, "sD_junk", [128, 2048], BF16)
                yn = ring_sb(c, es2, "sD_yn", 2, [128, 2048], BF16)
                st = ring_sb(c, es2, "sD_st", 2, [128, 4], F32)
                ynT = sb(c, es2, "sD_ynT", [128, 16, 512], BF16)
                xr = ring_sb(c, es2, "sD_x", 1, [128, KC, 512], F32)
                xo = sb(c, es2, "sD_xo", [128, KC, 512], F32)
                pT = ring_ps(c, es2, "sD_pT", 2, [128, 1024], BF16)
                pyr = ring_ps(c, es2, "sD_py", 2, [128, 512])
                wl = WLoader(c, es2, "sD_wst", 1024)
                for cc in range(16):
                    wl.load(wo[:, cc, :], w_out[cc * 128:(cc + 1) * 128, :], "sD_wo")
                for gi in range(L // 512):
                    tok0 = s * L + gi * 512
                    xi, kx = xr.next()
                    p.dma('sp', xi[:], xt_view(c, tok0, 512), xt_keys(tok0, 512), [kx])
                    for qb in range(4):
                        blk = gi * 4 + qb
                        y, ky = yr.next()
                        z, kz = zr.next()
                        p.dma('sp', y[:], c.Ys[blk * 128:(blk + 1) * 128, :], [("Ys", q) for q in range(8)], [ky])
                        p.dma('sp', z[:], c.Zs[blk * 128:(blk + 1) * 128, :], [("Zs", blk, q) for q in range(4)], [kz])
                        p.tt('dve', y[:], y[:], z[:], ALU.mult, [ky, kz], [ky])
                        s_, ks = st.next()
                        p.op('act', lambda e, y=y, s_=s_: e.activation(junk[:], y[:], AF.Square, accum_out=s_[:, 0:1]),
                             [ky], ["sD_junk", ks])
                        p.ts('dve', s_[:, 1:2], s_[:, 0:1], 1.0 / 2048, 1e-6, ALU.mult, ALU.add, [ks], [ks])
                        p.tt('pool', s_[:, 2:3], s_[:, 1:2], c.mhalf[:, 0:1], ALU.pow, [ks], [ks])
                        yn_, kyn = yn.next()
                        p.stt(yn_[:], y[:], s_[:, 2:3], nw[:], ALU.mult, ALU.mult, [ky, ks, "s_nw"], [kyn])
                        for b2 in range(2):
                            pt, kpt = pT.next()
                            for q in range(8):
                                cc = b2 * 8 + q
                                p.tr(pt[:, q * 128:(q + 1) * 128], yn_[:, cc * 128:(cc + 1) * 128], c.identb[:], [kyn], [kpt])
                            p.copy('act' if b2 else 'dve', ynT[:, b2 * 8:(b2 + 1) * 8, qb * 128:(qb + 1) * 128],
                                   pt[:].rearrange("p (a b) -> p a b", a=8), [kpt], ["sD_ynT"])
                    for dc in range(KC):
                        py, kpy = pyr.next()
                        for cc in range(16):
                            p.mm(py[:], wo[:, cc, dc * 128:(dc + 1) * 128], ynT[:, cc, :], cc == 0, cc == 15, ["sD_wo", "sD_ynT"], [kpy])
                        p.tt('dve', xo[:, dc, :], py[:], xi[:, dc, :], ALU.add, [kpy, kx], ["sD_xo"])
                    p.dma('sp', xt_view(c, tok0, 512), xo[:], ["sD_xo"], xt_keys(tok0, 512))
            p.barrier()


def conv_chunk(c, wc_ap, kwc, xn, cw, cb, cc, rw, krw, ac, kac, pA, keypre):
    p = c.p
    for tg in range(L // 512):
        pa, kpa = pA.next()
        for kc in range(KC):
            p.mm(pa[:], wc_ap[:, kc, :], xn[:, kc, tg * 512:(tg + 1) * 512], kc == 0, kc == KC - 1, [kwc, keypre + "xn"], [kpa])
        p.copy('act', rw[:, 2 + tg * 512:2 + (tg + 1) * 512], pa[:], [kpa], [krw])
    p.ts('dve', ac[:], rw[:, 0:L], cw[:, 0, cc:cc + 1], cb[:, cc:cc + 1], ALU.mult, ALU.add, [krw, keypre + "cw", keypre + "cb"], [kac])
    for tap in range(1, 5):
        p.stt(ac[:], rw[:, tap:tap + L], cw[:, tap, cc:cc + 1], ac[:], ALU.mult, ALU.add, [krw, keypre + "cw", kac], [kac])


def neumann_inverse_T(c, units, nlev):
    p = c.p
    for u in units:
        p.tr(u['pN'][:, 384:512], u['Mt'][0][:], c.ident[:], [u['k'] + "Mt0"], [u['k'] + "pN"])
        p.copy('act', u['M'][0][:], u['pN'][:, 384:512], [u['k'] + "pN"], [u['k'] + "M0"])
        p.tt('dve', u['Y'][0][:], u['Mt'][0][:], c.ident[:], ALU.add, [u['k'] + "Mt0"], [u['k'] + "Y0"])
    for k in range(nlev):
        a, b = k % 2, (k + 1) % 2
        last = k == nlev - 1
        for u in units:
            kk = u['k']
            p.mm(u['pN'][:, 0:128], u['Mt'][a][:], u['M'][a][:], True, True, [kk + f"Mt{a}", kk + f"M{a}"], [kk + "pN"])
            if not last:
                p.mm(u['pN'][:, 128:256], u['M'][a][:], u['Mt'][a][:], True, True, [kk + f"Mt{a}", kk + f"M{a}"], [kk + "pN"])
        for u in units:
            kk = u['k']
            p.copy('act', u['M'][b][:], u['pN'][:, 0:128], [kk + "pN"], [kk + f"M{b}"])
            if not last:
                p.copy('dve', u['Mt'][b][:], u['pN'][:, 128:256], [kk + "pN"], [kk + f"Mt{b}"])
        for u in units:
            kk = u['k']
            p.mm(u['pN'][:, 256:384], u['M'][b][:], u['Y'][a][:], True, True, [kk + f"M{b}", kk + f"Y{a}"], [kk + "pN"])
        for u in units:
            kk = u['k']
            if last:
                p.tt('dve', u['XT'][:], u['Y'][a][:], u['pN'][:, 256:384], ALU.add, [kk + f"Y{a}", kk + "pN"], [kk + "XT"])
            else:
                p.tt('dve', u['Y'][b][:], u['Y'][a][:], u['pN'][:, 256:384], ALU.add, [kk + f"Y{a}", kk + "pN"], [kk + f"Y{b}"])


def stage_gdn(c, layer, j, nseq):
    p = c.p
    W = c.W
    w_in, conv_w, conv_b = W['gdn_w_in'][j], W['gdn_conv_w'][j], W['gdn_conv_b'][j]
    a_log, dt_bias, norm_w, w_out = W['gdn_a_log'][j], W['gdn_dt_bias'][j], W['gdn_norm'][j], W['gdn_w_out'][j]
    gvec = W['mix_norm'][layer]
    NB = L // 128
    with contextlib.ExitStack() as es:
        g = sb(c, es, "g_g", [128, KC], F32)
        cw = sb(c, es, "g_cw", [128, 5, 32], F32)
        cb = sb(c, es, "g_cb", [128, 32], F32)
        dtb = sb(c, es, "g_dtb", [128, 32], F32)
        aneg = sb(c, es, "g_aneg", [128, 32], F32)
        nw = sb(c, es, "g_nw", [128, 128], F32)
        xn = sb(c, es, "g_xn", [128, KC, L], BF16)
        bga = sb(c, es, "g_bga", [128, NB, 48], F32)
        load_vec(c, g[:], gvec, "g_g")
        for tap in range(5):
            p.dma('sp', cw[:, tap, :], conv_w[tap].rearrange("(cc p) -> p cc", p=128), [], ["g_cw"], allow_slow_non_contiguous=True)
        p.dma('sp', cb[:], conv_b.rearrange("(cc p) -> p cc", p=128), [], ["g_cb"], allow_slow_non_contiguous=True)
        p.dma('sp', dtb[:], dt_bias.rearrange("a b -> (a b)").partition_broadcast(128), [], ["g_dtb"])
        p.dma('sp', aneg[:], a_log.rearrange("a b -> (a b)").partition_broadcast(128), [], ["g_aneg"])
        p.dma('sp', nw[:], norm_w.partition_broadcast(128), [], ["g_nw"])
        p.act(aneg[:], aneg[:], AF.Exp, ["g_aneg"], ["g_aneg"])
        p.ts('dve', aneg[:], aneg[:], -1.0, None, ALU.mult, None, ["g_aneg"], ["g_aneg"])
        p.barrier()
        for s in range(nseq):
            with contextlib.ExitStack() as es2:
                NT = 256
                xr = ring_sb(c, es2, "gA_x", 2, [128, KC, NT], F32)
                sq = sb(c, es2, "gA_sq", [128, KC, NT], F32)
                var = sb(c, es2, "gA_var", [128, NT], F32)
                rstd = sb(c, es2, "gA_rstd", [128, NT], F32)
                pss = ps(c, es2, "gA_pss", [128, 512])
                for gi in range(L // NT):
                    load_xn(c, s * L + gi * NT, NT, xr, sq, pss, var, rstd, g, "g_g", xn[:, :, gi * NT:(gi + 1) * NT], "g_xn", "gA_")
            p.barrier()
            with contextlib.ExitStack() as es2:
                wzr = ring_sb(c, es2, "gB_wz", 2, [128, KC, 512], BF16)
                wdt = sb(c, es2, "gB_wdt", [128, KC, 48], BF16)
                zr = ring_sb(c, es2, "gB_z", 3, [128, 512], F32)
                pzr = ring_ps(c, es2, "gB_pz", 4, [128, 512])
                pdr = ring_ps(c, es2, "gB_pd", 2, [128, 512])
                wl = WLoader(c, es2, "gB_wst", 4096)
                wl.load(wdt[:], w_in[:, 6144:6192].rearrange("(kc p) c -> p kc c", p=128), "gB_wdt")
                for blk in range(NB):
                    pd, kpd = pdr.next()
                    for kc in range(KC):
                        p.mm(pd[:, :48], xn[:, kc, blk * 128:(blk + 1) * 128], wdt[:, kc, :], kc == 0, kc == KC - 1,
                             ["g_xn", "gB_wdt"], [kpd])
                    p.copy('dve', bga[:, blk, 0:16], pd[:, 0:16], [kpd], ["g_bga"])
                    p.tt('dve', bga[:, blk, 16:48], pd[:, 16:48], dtb[:], ALU.add, [kpd, "g_dtb"], ["g_bga"])
                p.act(bga[:, :, 0:16], bga[:, :, 0:16], AF.Exp, ["g_bga"], ["g_bga"], scale=-1.0)
                p.ts('dve', bga[:, :, 0:16], bga[:, :, 0:16], 1.0, None, ALU.add, None, ["g_bga"], ["g_bga"])
                p.op('dve', lambda e: e.reciprocal(bga[:, :, 0:16], bga[:, :, 0:16]), ["g_bga"], ["g_bga"])
                p.act(bga[:, :, 16:48], bga[:, :, 16:48], AF.Exp, ["g_bga"], ["g_bga"])
                p.act(bga[:, :, 16:48], bga[:, :, 16:48], AF.Ln, ["g_bga"], ["g_bga"], bias=1.0)
                p.tt('dve', bga[:, :, 16:48], bga[:, :, 16:48], bc(aneg[:], [128, NB, 32], 1), ALU.mult, ["g_bga", "g_aneg"], ["g_bga"])
                for zc in range(4):
                    wz, kwz = wzr.next()
                    wl.load(wz[:], w_in[:, 4096 + zc * 512:4096 + (zc + 1) * 512].rearrange("(kc p) c -> p kc c", p=128), kwz)
                    for blk in range(NB):
                        pz, kpz = pzr.next()
                        for kc in range(KC):
                            p.mm(pz[:], xn[:, kc, blk * 128:(blk + 1) * 128], wz[:, kc, :], kc == 0, kc == KC - 1,
                                 ["g_xn", kwz], [kpz])
                        z, kz = zr.next()
                        p.act(z[:], pz[:], AF.Silu, [kpz], [kz])
                        p.dma('sp', c.Zs[blk * 128:(blk + 1) * 128, zc * 512:(zc + 1) * 512], z[:], [kz], [("Zs", blk, zc)])
            p.barrier()
            with contextlib.ExitStack() as es2:
                wcr = ring_sb(c, es2, "gC_wc", 3, [128, KC, 128], BF16)
                wl = WLoader(c, es2, "gC_wst", 1024)
                raw = ring_sb(c, es2, "gC_raw", 2, [128, L + 4], F32)
                acc = ring_sb(c, es2, "gC_acc", 2, [128, L], F32)
                t32a = sb(c, es2, "gC_t32a", [128, L], F32)
                t32b = sb(c, es2, "gC_t32b", [128, L], F32)
                cvT = ring_sb(c, es2, "gC_cvT", 2, [128, L], BF16)
                QhT = sb(c, es2, "gC_QhT", [128, L], BF16)
                KhT = sb(c, es2, "gC_KhT", [128, L], BF16)
                Ktok = sb(c, es2, "gC_Ktok", [128, NB, 128], BF16)
                Vtok = sb(c, es2, "gC_Vtok", [128, NB, 256], BF16)
                KKT = sb(c, es2, "gC_KKT", [128, NB, 128], F32)
                QKT = sb(c, es2, "gC_QKT", [128, NB, 128], F32)
                gq = sb(c, es2, "gC_gq", [128, NB, 2, 2], F32)
                G = sb(c, es2, "gC_G", [128, NB, 2, 2], F32)
                tot = sb(c, es2, "gC_tot", [128, NB, 2, 2], F32)
                eG = sb(c, es2, "gC_eG", [128, NB, 2, 2], F32)
                neG = sb(c, es2, "gC_neG", [128, NB, 2, 2], F32)
                edec = sb(c, es2, "gC_edec", [128, NB, 2, 2], F32)
                etot = sb(c, es2, "gC_etot", [128, NB, 2, 2], F32)
                nbeta = sb(c, es2, "gC_nbeta", [128, NB, 2], F32)
                O = sb(c, es2, "gC_O", [128, NB, 256], F32)
                units = []
                for u in range(4):
                    ud = {'k': f"gU{u}_", 'd': u // 2, 'e': u % 2}
                    ud['M'] = [sb(c, es2, f"gC_M{u}{i}", [128, 128], F32) for i in range(2)]
                    ud['Mt'] = [sb(c, es2, f"gC_Mt{u}{i}", [128, 128], F32) for i in range(2)]
                    ud['Y'] = [sb(c, es2, f"gC_Y{u}{i}", [128, 128], F32) for i in range(2)]
                    ud['XT'] = sb(c, es2, f"gC_XT{u}", [128, 128], BF16)
                    ud['S'] = sb(c, es2, f"gC_S{u}", [128, 128], F32)
                    ud['Sb'] = sb(c, es2, f"gC_Sb{u}", [128, 128], BF16)
                    ud['AT'] = sb(c, es2, f"gC_AT{u}", [128, 128], BF16)
                    ud['R'] = sb(c, es2, f"gC_R{u}", [128, 128], BF16)
                    ud['Vn'] = sb(c, es2, f"gC_Vn{u}", [128, 128], BF16)
                    ud['Kd'] = sb(c, es2, f"gC_Kd{u}", [128, 128], BF16)
                    ud['tmp'] = sb(c, es2, f"gC_tmp{u}", [128, 128], F32)
                    ud['pN'] = ps(c, es2, f"gC_pN{u}", [128, 512])
                    PSUM_KEYS.add(ud['k'] + "pN")
                    units.append(ud)
                rhsd = [sb(c, es2, f"gC_rhs{d}", [128, 2, 128], F32) for d in range(2)]
                decd = [sb(c, es2, f"gC_dec{d}", [128, 2, 128], F32) for d in range(2)]
                decm = [sb(c, es2, f"gC_decm{d}", [128, 2, 128], F32) for d in range(2)]
                decs = [sb(c, es2, f"gC_decs{d}", [128, 2, 128], F32) for d in range(2)]
                pSeg = ps(c, es2, "gC_pSeg", [128, 512])
                PSUM_KEYS.add("gC_pSeg")
                PSUM_KEYS.add("gC_pM0")
                pA = ring_ps(c, es2, "gC_pA", 2, [128, 512])
                pT = ps(c, es2, "gC_pT", [128, 1024], BF16) if False else None
                for tl, ktl in zip(raw.tiles, ["gC_raw0", "gC_raw1"]):
                    p.memset('pool', tl[:, 0:2], 0.0, [ktl])
                    p.memset('pool', tl[:, L + 2:L + 4], 0.0, [ktl])
                slot_i = [0]

                pM = ring_ps(c, es2, "gC_pM", 1, [128, 512])

                def slot():
                    i = slot_i[0] % 3
                    slot_i[0] += 1
                    if i < 2:
                        return pA.tiles[i][:, 0:128], pA.name + str(i)
                    return pM.tiles[0][:, 0:128], "gC_pM0"

                def tslot():
                    return slot()

                for hk in range(8):
                    for ci, cc in enumerate([hk, 8 + hk, 16 + 2 * hk, 17 + 2 * hk]):
                        wc, kwc = wcr.next()
                        wl.load(wc[:], w_in[:, cc * 128:(cc + 1) * 128].rearrange("(kc p) c -> p kc c", p=128), kwc)
                        rw, krw = raw.next()
                        ac, kac = acc.next()
                        conv_chunk(c, wc, kwc, xn, cw, cb, cc, rw, krw, ac, kac, pA, "g_")
                        if ci < 2:
                            p.act(t32a[:], ac[:], AF.Silu, [kac], ["gC_t32a"])
                            p.act(t32b[:], t32a[:], AF.Square, ["gC_t32a"], ["gC_t32b"])
                            for tg in range(L // 512):
                                pa, kpa = pA.next()
                                p.mm(pa[:], c.ones[:], t32b[:, tg * 512:(tg + 1) * 512], True, True, ["gC_t32b"], [kpa])
                                p.ts('dve', ac[:, tg * 512:(tg + 1) * 512], pa[:], 1e-6, None, ALU.add, None, [kpa], [kac])
                            p.act(ac[:], ac[:], AF.Ln, [kac], [kac])
                            p.act(ac[:], ac[:], AF.Exp, [kac], [kac], scale=-0.5)
                            dstT, kd = (QhT, "gC_QhT") if ci == 0 else (KhT, "gC_KhT")
                            scl = 128.0 ** -0.5 if ci == 0 else 1.0
                            p.stt(dstT[:], t32a[:], scl, ac[:], ALU.mult, ALU.mult, ["gC_t32a", kac], [kd])
                            if ci == 1:
                                for blk in range(NB):
                                    sl, ksl = slot()
                                    p.mm(sl, KhT[:, blk * 128:(blk + 1) * 128], c.identb[:], True, True, ["gC_KhT"], [ksl])
                                    p.copy('act' if blk % 2 else 'dve', Ktok[:, blk, :], sl, [ksl], ["gC_Ktok"])
                        else:
                            cv, kcv = cvT.next()
                            p.act(cv[:], ac[:], AF.Silu, [kac], [kcv])
                            for blk in range(NB):
                                sl, ksl = slot()
                                p.mm(sl, cv[:, blk * 128:(blk + 1) * 128], c.identb[:], True, True, [kcv], [ksl])
                                p.copy('act' if blk % 2 else 'dve', Vtok[:, blk, (ci - 2) * 128:(ci - 1) * 128], sl, [ksl], ["gC_Vtok"])
                    for d in range(2):
                        p.copy('dve', gq[:, :, d, :], bga[:, :, 16 + d * 16 + 2 * hk:16 + d * 16 + 2 * hk + 2], ["g_bga"], ["gC_gq"])
                    p.ts('dve', nbeta[:], bga[:, :, 2 * hk:2 * hk + 2], -1.0, None, ALU.mult, None, ["g_bga"], ["gC_nbeta"])
                    pa, kpa = pA.next()
                    for d in range(2):
                        msk = c.mk["LE"] if d == 0 else c.mk["GE"]
                        for blk in range(NB):
                            p.mm(pa[:, (blk * 2 + d) * 2:(blk * 2 + d) * 2 + 2], msk[:], gq[:, blk, d, :], True, True, ["gC_gq", "mk"], [kpa])
                    p.copy('dve', G[:].rearrange("p b d h -> p (b d h)"), pa[:, :NB * 4], [kpa], ["gC_G"])
                    pa, kpa = pA.next()
                    p.mm(pa[:, :NB * 4], c.ones[:], gq[:].rearrange("p b d h -> p (b d h)"), True, True, ["gC_gq"], [kpa])
                    p.copy('dve', tot[:].rearrange("p b d h -> p (b d h)"), pa[:, :NB * 4], [kpa], ["gC_tot"])
                    fl = "p b d h -> p (b d h)"
                    p.act(eG[:].rearrange(fl), G[:].rearrange(fl), AF.Exp, ["gC_G"], ["gC_eG"])
                    p.ts('dve', neG[:].rearrange(fl), eG[:].rearrange(fl), -1.0, None, ALU.mult, None, ["gC_eG"], ["gC_neG"])
                    p.act(etot[:].rearrange(fl), tot[:].rearrange(fl), AF.Exp, ["gC_tot"], ["gC_etot"])
                    p.tt('dve', edec[:].rearrange(fl), tot[:].rearrange(fl), G[:].rearrange(fl), ALU.subtract, ["gC_tot", "gC_G"], ["gC_edec"])
                    p.act(edec[:].rearrange(fl), edec[:].rearrange(fl), AF.Exp, ["gC_edec"], ["gC_edec"])
                    for blk in range(NB):
                        tsl = slice(blk * 128, (blk + 1) * 128)
                        sl, ksl = slot()
                        p.mm(sl, KhT[:, tsl], KhT[:, tsl], True, True, ["gC_KhT"], [ksl])
                        p.copy('act', KKT[:, blk, :], sl, [ksl], ["gC_KKT"])
                        sl, ksl = slot()
                        p.mm(sl, KhT[:, tsl], QhT[:, tsl], True, True, ["gC_KhT", "gC_QhT"], [ksl])
                        p.copy('dve', QKT[:, blk, :], sl, [ksl], ["gC_QKT"])
                    for step in range(NB):
                        first = step == 0
                        lastc = step == NB - 1
                        blks = [step, NB - 1 - step]
                        for d in range(2):
                            blk = blks[d]
                            mk_l, mk_r, mk_i, mk_s = (("GT", "LE", "LE", "LT") if d == 0 else ("LT", "GE", "GE", "GT"))
                            p.tt('dve', rhsd[d][:], bc(c.mk[mk_r][:], [128, 2, 128], 1), bc(gq[:, blk, d, :], [128, 2, 128], 2), ALU.mult,
                                 ["mk", "gC_gq"], [f"gC_rhs{d}"])
                            p.mm(pSeg[:, d * 256:(d + 1) * 256], c.mk[mk_l][:], rhsd[d][:].rearrange("p h l -> p (h l)"), True, True,
                                 [f"gC_rhs{d}", "mk"], ["gC_pSeg"])
                            p.act(decd[d][:].rearrange("p h l -> p (h l)"), pSeg[:, d * 256:(d + 1) * 256], AF.Exp, ["gC_pSeg"], [f"gC_dec{d}"])
                            p.tt('dve', decm[d][:], decd[d][:], bc(c.mk[mk_i][:], [128, 2, 128], 1), ALU.mult, [f"gC_dec{d}", "mk"], [f"gC_decm{d}"])
                            p.tt('dve', decs[d][:], decd[d][:], bc(c.mk[mk_s][:], [128, 2, 128], 1), ALU.mult, [f"gC_dec{d}", "mk"], [f"gC_decs{d}"])
                        for u in units:
                            d, e, kk = u['d'], u['e'], u['k']
                            blk = blks[d]
                            p.tt('dve', u['AT'][:], QKT[:, blk, :], decm[d][:, e, :], ALU.mult, ["gC_QKT", f"gC_decm{d}"], [kk + "AT"])
                            p.stt(u['Mt'][0][:], KKT[:, blk, :], nbeta[:, blk, e:e + 1], decs[d][:, e, :], ALU.mult, ALU.mult,
                                  ["gC_KKT", "gC_nbeta", f"gC_decs{d}"], [kk + "Mt0"])
                        neumann_inverse_T(c, units, 6)
                        sls = {}
                        for u in units:
                            d, e, kk = u['d'], u['e'], u['k']
                            blk = blks[d]
                            tsl = slice(blk * 128, (blk + 1) * 128)
                            vt = Vtok[:, blk, e * 128:(e + 1) * 128]
                            if not first:
                                sl, ksl = slot()
                                p.mm(sl, KhT[:, tsl], u['Sb'][:], True, True, ["gC_KhT", kk + "Sb"], [ksl])
                                p.stt(u['R'][:], sl, neG[:, blk, d, e:e + 1], vt, ALU.mult, ALU.add, [ksl, "gC_neG", "gC_Vtok"], [kk + "R"])
                                rr = u['R'][:]
                                krr = kk + "R"
                            else:
                                rr = vt
                                krr = "gC_Vtok"
                            sl, ksl = slot()
                            p.mm(sl, u['XT'][:], rr, True, True, [kk + "XT", krr], [ksl])
                            p.ts('dve', u['Vn'][:], sl, bga[:, blk, 2 * hk + e:2 * hk + e + 1], None, ALU.mult, None, [ksl, "g_bga"], [kk + "Vn"])
                        for u in units:
                            d, e, kk = u['d'], u['e'], u['k']
                            blk = blks[d]
                            tsl = slice(blk * 128, (blk + 1) * 128)
                            okey = ("gC_O", blk, e)
                            sl2, ksl2 = slot()
                            p.mm(sl2, u['AT'][:], u['Vn'][:], True, True, [kk + "AT", kk + "Vn"], [ksl2])
                            oap = O[:, blk, e * 128:(e + 1) * 128]
                            if not first:
                                sl, ksl = slot()
                                p.mm(sl, QhT[:, tsl], u['Sb'][:], True, True, ["gC_QhT", kk + "Sb"], [ksl])
                                p.ts('dve', u['tmp'][:], sl, eG[:, blk, d, e:e + 1], None, ALU.mult, None, [ksl, "gC_eG"], [kk + "tmp"])
                                p.tt('dve', u['tmp'][:], u['tmp'][:], sl2, ALU.add, [kk + "tmp", ksl2], [kk + "tmp"])
                                src, ksrc = u['tmp'][:], kk + "tmp"
                            else:
                                src, ksrc = sl2, ksl2
                            first_visit = (d == 0 and blk < NB // 2) or (d == 1 and blk >= NB // 2)
                            if first_visit:
                                p.copy('dve', oap, src, [ksrc], [okey]) if not first else p.copy('dve', oap, src, [ksrc], [okey])
                            else:
                                p.tt('dve', oap, oap, src, ALU.add, [ksrc, okey], [okey]) if not first else p.tt('dve', oap, oap, src, ALU.add, [ksrc, okey], [okey])
                            if not lastc:
                                p.ts('dve', u['Kd'][:], Ktok[:, blk, :], edec[:, blk, d, e:e + 1], None, ALU.mult, None, ["gC_Ktok", "gC_edec"], [kk + "Kd"])
                                sl, ksl = slot()
                                p.mm(sl, u['Kd'][:], u['Vn'][:], True, True, [kk + "Kd", kk + "Vn"], [ksl])
                                if first:
                                    p.copy('dve', u['S'][:], sl, [ksl], [kk + "S"])
                                else:
                                    p.stt(u['S'][:], u['S'][:], etot[:, blk, d, e:e + 1], sl, ALU.mult, ALU.add, [kk + "S", "gC_etot", ksl], [kk + "S"])
                                p.copy('act', u['Sb'][:], u['S'][:], [kk + "S"], [kk + "Sb"])
                    p.dma('sp', c.Ys[:, hk * 256:(hk + 1) * 256].rearrange("(b p) q -> p b q", p=128), O[:],
                          [("gC_O", blk, e) for blk in range(NB) for e in range(2)], [("Ys", hk)])
            p.barrier()
            with contextlib.ExitStack() as es2:
                wo = sb(c, es2, "gD_wo", [128, 16, D], BF16)
                yr = ring_sb(c, es2, "gD_y", 2, [128, 2048], F32)
                zr = ring_sb(c, es2, "gD_z", 2, [128, 2048], F32)
                y2 = sb(c, es2, "gD_y2", [128, 2048], F32)
                yn = ring_sb(c, es2, "gD_yn", 2, [128, 2048], BF16)
                st = ring_sb(c, es2, "gD_st", 2, [128, 16], F32)
                ynT = sb(c, es2, "gD_ynT", [128, 16, 512], BF16)
                xr = ring_sb(c, es2, "gD_x", 1, [128, KC, 512], F32)
                xo = sb(c, es2, "gD_xo", [128, KC, 512], F32)
                pT = ring_ps(c, es2, "gD_pT", 2, [128, 1024], BF16)
                pyr = ring_ps(c, es2, "gD_py", 2, [128, 512])
                wl = WLoader(c, es2, "gD_wst", 1024)
                for cc in range(16):
                    wl.load(wo[:, cc, :], w_out[cc * 128:(cc + 1) * 128, :], "gD_wo")
                for gi in range(L // 512):
                    tok0 = s * L + gi * 512
                    xi, kx = xr.next()
                    p.dma('sp', xi[:], xt_view(c, tok0, 512), xt_keys(tok0, 512), [kx])
                    for qb in range(4):
                        blk = gi * 4 + qb
                        y, ky = yr.next()
                        z, kz = zr.next()
                        p.dma('sp', y[:], c.Ys[blk * 128:(blk + 1) * 128, :], [("Ys", q) for q in range(8)], [ky])
                        p.dma('sp', z[:], c.Zs[blk * 128:(blk + 1) * 128, :], [("Zs", blk, q) for q in range(4)], [kz])
                        p.act(y2[:], y[:], AF.Square, [ky], ["gD_y2"])
                        s_, ks = st.next()
                        p.op('dve', lambda e, s_=s_: e.reduce_sum(s_[:], y2[:].rearrange("p (h v) -> p h v", h=16), AX.X), ["gD_y2"], [ks])
                        p.ts('dve', s_[:], s_[:], 1.0 / 128, 1e-6, ALU.mult, ALU.add, [ks], [ks])
                        p.tt('pool', s_[:], s_[:], c.mhalf[:, 0:16], ALU.pow, [ks], [ks])
                        h3 = "p (h v) -> p h v"
                        p.tt('dve', y[:].rearrange(h3, h=16), y[:].rearrange(h3, h=16), bc(s_[:], [128, 16, 128], 2), ALU.mult, [ky, ks], [ky])
                        p.tt('dve', z[:].rearrange(h3, h=16), z[:].rearrange(h3, h=16), bc(nw[:], [128, 16, 128], 1), ALU.mult, [kz, "g_nw"], [kz])
                        yn_, kyn = yn.next()
                        p.tt('dve', yn_[:], y[:], z[:], ALU.mult, [ky, kz], [kyn])
                        for b2 in range(2):
                            pt, kpt = pT.next()
                            for q in range(8):
                                cc = b2 * 8 + q
                                p.tr(pt[:, q * 128:(q + 1) * 128], yn_[:, cc * 128:(cc + 1) * 128], c.identb[:], [kyn], [kpt])
                            p.copy('act' if b2 else 'dve', ynT[:, b2 * 8:(b2 + 1) * 8, qb * 128:(qb + 1) * 128],
                                   pt[:].rearrange("p (a b) -> p a b", a=8), [kpt], ["gD_ynT"])
                    for dc in range(KC):
                        py, kpy = pyr.next()
                        for cc in range(16):
                            p.mm(py[:], wo[:, cc, dc * 128:(dc + 1) * 128], ynT[:, cc, :], cc == 0, cc == 15, ["gD_wo", "gD_ynT"], [kpy])
                        p.tt('dve', xo[:, dc, :], py[:], xi[:, dc, :], ALU.add, [kpy, kx], ["gD_xo"])
                    p.dma('sp', xt_view(c, tok0, 512), xo[:], ["gD_xo"], xt_keys(tok0, 512))
            p.barrier()


def stage_rwkv(c, layer, j, nseq):
    p = c.p
    W = c.W
    gvec = W['mix_norm'][layer]
    x_mu, w_rkv, w0, w1, w2 = W['rwkv_x_mu'][j], W['rwkv_w_rkv'][j], W['rwkv_w0'][j], W['rwkv_w1'][j], W['rwkv_w2'][j]
    a0, a1, a2, g1, g2 = W['rwkv_a0'][j], W['rwkv_a1'][j], W['rwkv_a2'][j], W['rwkv_g1'][j], W['rwkv_g2'][j]
    k_k, k_a, r_k, lnx_w, lnx_b, w_out = (W['rwkv_k_k'][j], W['rwkv_k_a'][j], W['rwkv_r_k'][j], W['rwkv_lnx_w'][j],
                                            W['rwkv_lnx_b'][j], W['rwkv_w_out'][j])
    CH = 64
    NCH = L // CH
    RW = c.RW
    with contextlib.ExitStack() as es:
        g = sb(c, es, "r_g", [128, KC], F32)
        mu = sb(c, es, "r_mu", [128, 6, KC], F32)
        nw0 = sb(c, es, "r_nw0", [128, 2, KC], F32)
        na0 = sb(c, es, "r_na0", [128, KC], F32)
        kkv = sb(c, es, "r_kk", [128, KC], F32)
        kav = sb(c, es, "r_ka", [128, KC], F32)
        omka = sb(c, es, "r_omka", [128, KC], F32)
        rkv = sb(c, es, "r_rk", [128, KC], F32)
        lnw = sb(c, es, "r_lnw", [128, KC], F32)
        lnb = sb(c, es, "r_lnb", [128, KC], F32)
        onesbd = sb(c, es, "r_onesbd", [128, 128], F32)
        mS = [sb(c, es, f"r_mS{d}", [128, 64], F32) for d in range(2)]
        mSn = [sb(c, es, f"r_mSn{d}", [128, 64], F32) for d in range(2)]
        mI = [sb(c, es, f"r_mI{d}", [128, 64], F32) for d in range(2)]
        load_vec(c, g[:], gvec, "r_c")
        for s_ in range(6):
            load_vec(c, mu[:, s_, :], x_mu[s_], "r_c")
        for d in range(2):
            load_vec(c, nw0[:, d, :], w0[d], "r_c")
        for t_, src in [(na0, a0), (kkv, k_k), (kav, k_a), (rkv, r_k), (lnw, lnx_w), (lnb, lnx_b)]:
            load_vec(c, t_[:], src, "r_c")
        p.ts('dve', nw0[:], nw0[:], -1.0, None, ALU.mult, None, ["r_c"], ["r_c"])
        p.ts('dve', na0[:], na0[:], -1.0, None, ALU.mult, None, ["r_c"], ["r_c"])
        p.ts('dve', omka[:], kav[:], -1.0, 1.0, ALU.mult, ALU.add, ["r_c"], ["r_c"])
        p.memset('pool', onesbd[:], 0.0, ["r_c"])
        p.memset('pool', onesbd[0:64, 0:64], 1.0, ["r_c"])
        p.memset('pool', onesbd[64:128, 64:128], 1.0, ["r_c"])
        for d in range(2):
            ns, ni = (("LT", "LE") if d == 0 else ("GT", "GE"))
            for hs in range(2):
                sl_ = slice(hs * 64, (hs + 1) * 64)
                p.copy('pool', mS[d][sl_, :], c.mk[ns][sl_, hs * 64:(hs + 1) * 64], ["mk"], ["r_c"])
                p.copy('pool', mI[d][sl_, :], c.mk[ni][sl_, hs * 64:(hs + 1) * 64], ["mk"], ["r_c"])
            p.ts('dve', mSn[d][:], mS[d][:], -1.0, None, ALU.mult, None, ["r_c"], ["r_c"])
        p.barrier()
        for s in range(nseq):
            with contextlib.ExitStack() as es2:
                NT = 256
                u = sb(c, es2, "rA_u", [128, KC, L + 2], F32)
                xm = sb(c, es2, "rA_xm", [128, KC, L], BF16)
                pss = ps(c, es2, "rA_pss", [128, 512])
                pA = ring_ps(c, es2, "rA_pA", 3, [128, 512])
                with contextlib.ExitStack() as es3:
                    xr = ring_sb(c, es3, "rA_x", 2, [128, KC, NT], F32)
                    sq = sb(c, es3, "rA_sq", [128, KC, NT], F32)
                    var = sb(c, es3, "rA_var", [128, NT], F32)
                    rstd = sb(c, es3, "rA_rstd", [128, NT], F32)
                    p.memset('pool', u[:, :, 0:1], 0.0, ["rA_u"])
                    p.memset('pool', u[:, :, L + 1:L + 2], 0.0, ["rA_u"])
                    for gi in range(L // NT):
                        t0 = gi * NT
                        xi, kx = xr.next()
                        p.dma('sp', xi[:], xt_view(c, s * L + t0, NT), xt_keys(s * L + t0, NT), [kx])
                        rms_stats(c, xi, kx, sq, "rA_sq", pss, "rA_pss", var, "rA_var", rstd, "rA_rstd", NT)
                        for kc in range(KC):
                            p.stt(u[:, kc, 1 + t0:1 + t0 + NT], xi[:, kc, :], g[:, kc:kc + 1], rstd[:], ALU.mult, ALU.mult,
                                  [kx, "r_c", "rA_rstd"], ["rA_u"])

                p.barrier()
                t1 = sb(c, es2, "rA_t1", [128, L], F32)
                t2 = sb(c, es2, "rA_t2", [128, L], F32)
                ost = ring_sb(c, es2, "rA_ost", 2, [128, L], F32)
                wt = sb(c, es2, "rA_wt", [128, KC, 1024], BF16)
                wl1 = sb(c, es2, "rA_wl1", [128, KC, 128], BF16)
                wl2 = sb(c, es2, "rA_wl2", [128, 1024], BF16)
                lT = sb(c, es2, "rA_lT", [128, L], BF16)
                wl = WLoader(c, es2, "rA_wst", 1024)

                def build_xm(si):
                    for kc in range(KC):
                        p.tt('dve', t1[:], u[:, kc, 0:L], u[:, kc, 2:L + 2], ALU.add, ["rA_u"], ["rA_t1"])
                        p.stt(t2[:], t1[:], 0.5, u[:, kc, 1:L + 1], ALU.mult, ALU.subtract, ["rA_t1", "rA_u"], ["rA_t2"])
                        p.stt(xm[:, kc, :], t2[:], mu[:, si, kc:kc + 1], u[:, kc, 1:L + 1], ALU.mult, ALU.add,
                              ["rA_t2", "r_c", "rA_u"], ["rA_xm"])

                def sigmoid_from(o, pa, bias_ap, scale_out, ko, kpa):
                    if bias_ap is not None:
                        p.op('act', lambda e: e.activation(o, pa, AF.Exp, bias=bias_ap, scale=-1.0), [kpa, "r_c"], [ko])
                    else:
                        p.op('act', lambda e: e.activation(o, pa, AF.Exp, scale=-1.0), [kpa], [ko])
                    p.ts('dve', o, o, 1.0, None, ALU.add, None, [ko], [ko])
                    p.op('dve', lambda e: e.reciprocal(o, o), [ko], [ko])
                    if scale_out != 1.0:
                        p.ts('dve', o, o, scale_out, None, ALU.mult, None, [ko], [ko])

                for si in range(3):
                    build_xm(si)
                    for kc in range(KC):
                        wl.load(wt[:, kc, :], w_rkv[si][kc * 128:(kc + 1) * 128, :], "rA_wt")
                    for oc in range(KC):
                        o_, ko = ost.next()
                        for tg in range(L // 512):
                            pa, kpa = pA.next()
                            for kc in range(KC):
                                p.mm(pa[:], wt[:, kc, oc * 128:(oc + 1) * 128], xm[:, kc, tg * 512:(tg + 1) * 512], kc == 0, kc == KC - 1,
                                     ["rA_wt", "rA_xm"], [kpa])
                            p.copy('act' if tg % 2 else 'dve', o_[:, tg * 512:(tg + 1) * 512], pa[:], [kpa], [ko])
                        p.dma('sp', RW[si, oc * 128:(oc + 1) * 128, :], o_[:], [ko], [("RW", si, oc)])
                for si, nm in [(3, 'w0'), (3, 'w1'), (4, 'a'), (5, 'g')]:
                    if nm in ('w0', 'a', 'g'):
                        build_xm(si)
                    if nm[0] == 'w':
                        d = int(nm[1])
                        l1, l2, rank, dsti, bias_t = w1[d], w2[d], 64, 5 + d, nw0[:, d, :]
                    elif nm == 'a':
                        l1, l2, rank, dsti, bias_t = a1, a2, 64, 3, na0
                    else:
                        l1, l2, rank, dsti, bias_t = g1, g2, 128, 4, None
                    wl.load(wl1[:, :, :rank], l1.rearrange("(kc p) r -> p kc r", p=128), "rA_wl1")
                    wl.load(wl2[:rank, :], l2, "rA_wl2")
                    for tg in range(L // 512):
                        pa, kpa = pA.next()
                        for kc in range(KC):
                            p.mm(pa[:rank, :], wl1[:, kc, :rank], xm[:, kc, tg * 512:(tg + 1) * 512], kc == 0, kc == KC - 1,
                                 ["rA_wl1", "rA_xm"], [kpa])
                        dst_ = lT[:rank, tg * 512:(tg + 1) * 512]
                        if nm[0] == 'w':
                            p.act(dst_, pa[:rank, :], AF.Tanh, [kpa], ["rA_lT"])
                        elif nm == 'a':
                            p.copy('act', dst_, pa[:rank, :], [kpa], ["rA_lT"])
                        else:
                            p.op('act', lambda e, dst_=dst_, pa=pa: e.activation(t1[:, :512], pa[:, :], AF.Exp, scale=-1.0), [kpa], ["rA_t1"])
                            p.ts('dve', t1[:, :512], t1[:, :512], 1.0, None, ALU.add, None, ["rA_t1"], ["rA_t1"])
                            p.op('dve', lambda e: e.reciprocal(t1[:, :512], t1[:, :512]), ["rA_t1"], ["rA_t1"])
                            p.copy('dve', dst_, t1[:, :512], ["rA_t1"], ["rA_lT"])
                    for oc in range(KC):
                        o_, ko = ost.next()
                        for tg in range(L // 512):
                            pa, kpa = pA.next()
                            p.mm(pa[:], wl2[:rank, oc * 128:(oc + 1) * 128], lT[:rank, tg * 512:(tg + 1) * 512], True, True,
                                 ["rA_wl2", "rA_lT"], [kpa])
                            osl = o_[:, tg * 512:(tg + 1) * 512]
                            if nm[0] == 'w':
                                sigmoid_from(osl, pa[:], bias_t[:, oc:oc + 1], -0.6065306597126334, ko, kpa)
                            elif nm == 'a':
                                sigmoid_from(osl, pa[:], bias_t[:, oc:oc + 1], 1.0, ko, kpa)
                            else:
                                p.copy('act', osl, pa[:], [kpa], [ko])
                        p.dma('sp', RW[dsti, oc * 128:(oc + 1) * 128, :], o_[:], [ko], [("RW", dsti, oc)])
            p.barrier()
            with contextlib.ExitStack() as es2:
                yT = sb(c, es2, "rC_yT", [128, KC, L], BF16)
                pA = ring_ps(c, es2, "rC_pA", 3, [128, 512])
                es3 = contextlib.ExitStack()
                F = [sb(c, es3, f"rC_F{i}", [128, L], F32) for i in range(8)]
                kF = [f"rC_F{i}" for i in range(8)]
                AR = [sb(c, es3, f"rC_AR{d}", [128, NCH, 2, CH], BF16) for d in range(2)]
                Kt = [sb(c, es3, f"rC_Kt{d}", [128, L], BF16) for d in range(2)]
                Bt = [sb(c, es3, f"rC_Bt{d}", [128, L], BF16) for d in range(2)]
                Kd = sb(c, es3, "rC_Kd", [128, L], BF16)
                Bd = sb(c, es3, "rC_Bd", [128, L], BF16)
                Kdtok = [sb(c, es3, f"rC_Kdtok{d}", [128, NCH, CH], BF16) for d in range(2)]
                Bdtok = [sb(c, es3, f"rC_Bdtok{d}", [128, NCH, CH], BF16) for d in range(2)]
                PC = [sb(c, es3, f"rC_PC{d}", [128, NCH], F32) for d in range(2)]
                Vp = sb(c, es3, "rC_Vp", [128, NCH, CH], BF16)
                Ost = sb(c, es3, "rC_Ost", [128, NCH, CH], F32)
                Obf = sb(c, es3, "rC_Obf", [128, NCH, CH], BF16)
                st = sb(c, es3, "rC_st", [128, NCH, 4], F32)
                units = []
                for uu in range(2):
                    ud = {'k': f"rU{uu}_", 'd': uu}
                    ud['M'] = [sb(c, es3, f"rC_M{uu}{i}", [128, 128], F32) for i in range(2)]
                    ud['Mt'] = [sb(c, es3, f"rC_Mt{uu}{i}", [128, 128], F32) for i in range(2)]
                    ud['Y'] = [sb(c, es3, f"rC_Y{uu}{i}", [128, 128], F32) for i in range(2)]
                    ud['XT'] = sb(c, es3, f"rC_XT{uu}", [128, 128], BF16)
                    ud['AkT'] = sb(c, es3, f"rC_AkT{uu}", [128, 128], BF16)
                    ud['RkT'] = sb(c, es3, f"rC_RkT{uu}", [128, 128], BF16)
                    ud['RbT'] = sb(c, es3, f"rC_RbT{uu}", [128, 128], BF16)
                    ud['S'] = sb(c, es3, f"rC_S{uu}", [128, CH], F32)
                    ud['Sb'] = sb(c, es3, f"rC_Sb{uu}", [128, CH], BF16)
                    ud['R1'] = sb(c, es3, f"rC_R1{uu}", [128, CH], BF16)
                    ud['NU'] = sb(c, es3, f"rC_NU{uu}", [128, CH], BF16)
                    ud['pN'] = ps(c, es3, f"rC_pN{uu}", [128, 512])
                    PSUM_KEYS.add(ud['k'] + "pN")
                    ud['pX'] = ps(c, es3, f"rC_pX{uu}", [128, 512])
                    PSUM_KEYS.add(ud['k'] + "pX")
                    for nm_ in ['Mt', 'AkT', 'RkT', 'RbT']:
                        tl_ = ud[nm_][0] if nm_ == 'Mt' else ud[nm_]
                        p.memset('pool', tl_[:], 0.0, [ud['k'] + (nm_ + "0" if nm_ == 'Mt' else nm_)])
                    units.append(ud)
                pB = ring_ps(c, es3, "rC_pB", 1, [128, 1024], BF16)
                for hp in range(8):
                    rows = slice(hp * 128, (hp + 1) * 128)
                    cs3 = "p (n q) -> p n q"

                    def ld(dstF, idx):
                        p.dma('sp', F[dstF][:], RW[idx, rows, :], [("RW", idx, hp)], [kF[dstF]])

                    def onesbd_bcast(srcF, dstF, add_eps):
                        for tg in range(L // 512):
                            pa, kpa = pA.next()
                            p.mm(pa[:], onesbd[:], F[srcF][:, tg * 512:(tg + 1) * 512], True, True, [kF[srcF], "r_c"], [kpa])
                            if add_eps is not None:
                                p.ts('dve', F[dstF][:, tg * 512:(tg + 1) * 512], pa[:], add_eps, None, ALU.add, None, [kpa], [kF[dstF]])
                            else:
                                p.copy('dve', F[dstF][:, tg * 512:(tg + 1) * 512], pa[:], [kpa], [kF[dstF]])

                    ld(0, 1)
                    ld(1, 3)
                    p.ts('dve', F[6][:], F[0][:], kkv[:, hp:hp + 1], None, ALU.mult, None, [kF[0], "r_c"], [kF[6]])
                    p.act(F[7][:], F[6][:], AF.Square, [kF[6]], [kF[7]])
                    onesbd_bcast(7, 7, 1e-6) if False else None
                    onesbd_bcast(7, 2, 1e-6)
                    p.act(F[2][:], F[2][:], AF.Ln, [kF[2]], [kF[2]])
                    p.act(F[2][:], F[2][:], AF.Exp, [kF[2]], [kF[2]], scale=-0.5)
                    p.tt('dve', F[2][:], F[6][:], F[2][:], ALU.mult, [kF[6], kF[2]], [kF[2]])
                    p.ts('dve', F[6][:], F[1][:], kav[:, hp:hp + 1], omka[:, hp:hp + 1], ALU.mult, ALU.add, [kF[1], "r_c"], [kF[6]])
                    p.tt('dve', F[3][:], F[0][:], F[6][:], ALU.mult, [kF[0], kF[6]], [kF[3]])
                    p.tt('dve', F[4][:], F[2][:], F[1][:], ALU.mult, [kF[2], kF[1]], [kF[4]])
                    ld(5, 0)
                    ld(0, 2)
                    p.stt(F[6][:], F[5][:], rkv[:, hp:hp + 1], F[3][:], ALU.mult, ALU.mult, [kF[5], "r_c", kF[3]], [kF[6]])
                    onesbd_bcast(6, 7, None)
                    p.tt('dve', F[1][:], F[7][:], F[0][:], ALU.mult, [kF[7], kF[0]], [kF[1]])
                    p.copy('act', Kd[:], F[0][:], [kF[0]], ["rC_Kd"])

                    def to_stacked(srcT, ksrc, dst, kdst):
                        for c8 in range(NCH // 8):
                            pb, kpb = pB.next()
                            for hs in range(2):
                                sl_ = slice(hs * 64, (hs + 1) * 64)
                                for q in range(8):
                                    ch = c8 * 8 + q
                                    p.mm64(pA.tiles[0][sl_, q * 64:(q + 1) * 64], srcT[sl_, ch * CH:(ch + 1) * CH], c.identb[sl_, sl_], True, True,
                                         [ksrc], ["rC_pA0"])
                            p.copy('act' if c8 % 2 else 'dve', dst[:, c8 * 8:(c8 + 1) * 8, :],
                                   pA.tiles[0][:, :].rearrange("p (a b) -> p a b", a=8), ["rC_pA0"], [kdst])

                    to_stacked(Kd, "rC_Kd", Vp, "rC_Vp")
                    for d in range(2):
                        ld(0, 5 + d)
                        p.op('dve', lambda e: e.tensor_tensor_scan(F[6][:], c.ones[:, 0:1].broadcast_to([128, L]), F[0][:], 0.0, ALU.mult, ALU.add),
                             [kF[0]], [kF[6]])
                        cs = F[6][:].rearrange(cs3, q=CH)
                        lw = F[0][:].rearrange(cs3, q=CH)
                        li = F[7][:].rearrange(cs3, q=CH)
                        if d == 0:
                            p.tt('dve', st[:, :, 0:1], cs[:, :, 0:1], lw[:, :, 0:1], ALU.subtract, [kF[6], kF[0]], ["rC_st"])
                            p.tt('dve', li, cs, bc(st[:, :, 0], [128, NCH, CH], 2) if False else st[:, :, 0:1].broadcast_to([128, NCH, CH]),
                                 ALU.subtract, [kF[6], "rC_st"], [kF[7]])
                            p.tt('dve', F[6][:], F[7][:], F[0][:], ALU.subtract, [kF[7], kF[0]], [kF[6]])
                            last = CH - 1
                        else:
                            p.copy('dve', st[:, :, 0:1], cs[:, :, CH - 1:CH], [kF[6]], ["rC_st"])
                            p.tt('dve', cs, st[:, :, 0:1].broadcast_to([128, NCH, CH]), cs, ALU.subtract, [kF[6], "rC_st"], [kF[6]])
                            p.tt('dve', F[7][:], F[6][:], F[0][:], ALU.add, [kF[6], kF[0]], [kF[7]])
                            last = 0
                        p.act(F[6][:], F[6][:], AF.Exp, [kF[6]], [kF[6]])
                        p.tt('dve', AR[d][:, :, 0, :], F[2][:].rearrange(cs3, q=CH), F[6][:].rearrange(cs3, q=CH), ALU.mult,
                             [kF[2], kF[6]], [f"rC_AR{d}"])
                        p.act(F[0][:], F[7][:], AF.Exp, [kF[7]], [kF[0]])
                        p.tt('dve', AR[d][:, :, 1, :], F[5][:].rearrange(cs3, q=CH), F[0][:].rearrange(cs3, q=CH), ALU.mult,
                             [kF[5], kF[0]], [f"rC_AR{d}"])
                        p.copy('dve', PC[d][:].unsqueeze(2), F[0][:].rearrange(cs3, q=CH)[:, :, last:last + 1], [kF[0]], [f"rC_PC{d}"])
                        p.act(F[6][:], F[7][:], AF.Exp, [kF[7]], [kF[6]], scale=-1.0)
                        p.tt('dve', Kt[d][:], F[3][:], F[6][:], ALU.mult, [kF[3], kF[6]], [f"rC_Kt{d}"])
                        p.tt('dve', Bt[d][:], F[4][:], F[6][:], ALU.mult, [kF[4], kF[6]], [f"rC_Bt{d}"])
                        p.tt('dve', F[0][:].rearrange(cs3, q=CH), li[:, :, last:last + 1].broadcast_to([128, NCH, CH]), li, ALU.subtract,
                             [kF[7]], [kF[0]])
                        p.act(F[0][:], F[0][:], AF.Exp, [kF[0]], [kF[0]])
                        p.tt('dve', Kd[:], F[3][:], F[0][:], ALU.mult, [kF[3], kF[0]], ["rC_Kd"])
                        p.tt('dve', Bd[:], F[4][:], F[0][:], ALU.mult, [kF[4], kF[0]], ["rC_Bd"])
                        to_stacked(Kd, "rC_Kd", Kdtok[d], f"rC_Kdtok{d}")
                        to_stacked(Bd, "rC_Bd", Bdtok[d], f"rC_Bdtok{d}")
                    for step in range(NCH):
                        first = step == 0
                        lastc = step == NCH - 1
                        chs = [step, NCH - 1 - step]
                        for u_ in units:
                            d, kk_ = u_['d'], u_['k']
                            ch = chs[d]
                            tsl = slice(ch * CH, (ch + 1) * CH)
                            pX = u_['pX']
                            for hs in range(2):
                                sl_ = slice(hs * 64, (hs + 1) * 64)
                                p.mm64(pX[sl_, 0:128], Kt[d][sl_, tsl], AR[d][sl_, ch, :, :].rearrange("p a q -> p (a q)"), True, True,
                                     [f"rC_Kt{d}", f"rC_AR{d}"], [kk_ + "pX"])
                                p.mm64(pX[sl_, 128:256], Bt[d][sl_, tsl], AR[d][sl_, ch, :, :].rearrange("p a q -> p (a q)"), True, True,
                                     [f"rC_Bt{d}", f"rC_AR{d}"], [kk_ + "pX"])
                            for hs in range(2):
                                sl_ = slice(hs * 64, (hs + 1) * 64)
                                cs_ = slice(hs * 64, (hs + 1) * 64)
                                p.tt('dve', u_['AkT'][sl_, cs_], pX[sl_, 0:64], mS[d][sl_, :], ALU.mult, [kk_ + "pX", "r_c"], [kk_ + "AkT"])
                                p.tt('dve', u_['RkT'][sl_, cs_], pX[sl_, 64:128], mI[d][sl_, :], ALU.mult, [kk_ + "pX", "r_c"], [kk_ + "RkT"])
                                p.tt('dve', u_['Mt'][0][sl_, cs_], pX[sl_, 128:192], mSn[d][sl_, :], ALU.mult, [kk_ + "pX", "r_c"], [kk_ + "Mt0"])
                                p.tt('dve', u_['RbT'][sl_, cs_], pX[sl_, 192:256], mI[d][sl_, :], ALU.mult, [kk_ + "pX", "r_c"], [kk_ + "RbT"])
                        neumann_inverse_T(c, units, 5)
                        for u_ in units:
                            d, kk_ = u_['d'], u_['k']
                            ch = chs[d]
                            pX = u_['pX']
                            if not first:
                                for hs in range(2):
                                    sl_ = slice(hs * 64, (hs + 1) * 64)
                                    p.mm64(pX[sl_, 256:320], AR[d][sl_, ch, 0, :], u_['Sb'][sl_, :], True, False, [f"rC_AR{d}", kk_ + "Sb"], [kk_ + "pX"])
                            p.mm(pX[:, 256:320], u_['AkT'][:], Vp[:, ch, :], first, True, [kk_ + "AkT", "rC_Vp"], [kk_ + "pX"])
                            p.copy('act', u_['R1'][:], pX[:, 256:320], [kk_ + "pX"], [kk_ + "R1"])
                        for u_ in units:
                            d, kk_ = u_['d'], u_['k']
                            pX = u_['pX']
                            p.mm(pX[:, 320:384], u_['XT'][:], u_['R1'][:], True, True, [kk_ + "XT", kk_ + "R1"], [kk_ + "pX"])
                            p.op('act', lambda e, u_=u_, pX=pX: e.activation(u_['NU'][:], pX[:, 320:384], AF.Copy, scale=-1.0), [kk_ + "pX"], [kk_ + "NU"])
                        for u_ in units:
                            d, kk_ = u_['d'], u_['k']
                            ch = chs[d]
                            pX = u_['pX']
                            if not first:
                                for hs in range(2):
                                    sl_ = slice(hs * 64, (hs + 1) * 64)
                                    p.mm64(pX[sl_, 384:448], AR[d][sl_, ch, 1, :], u_['Sb'][sl_, :], True, False, [f"rC_AR{d}", kk_ + "Sb"], [kk_ + "pX"])
                            p.mm(pX[:, 384:448], u_['RkT'][:], Vp[:, ch, :], first, False, [kk_ + "RkT", "rC_Vp"], [kk_ + "pX"])
                            p.mm(pX[:, 384:448], u_['RbT'][:], u_['NU'][:], False, True, [kk_ + "RbT", kk_ + "NU"], [kk_ + "pX"])
                            okey = ("rC_O", ch)
                            first_visit = (d == 0 and ch < NCH // 2) or (d == 1 and ch >= NCH // 2)
                            if first_visit:
                                p.copy('dve', Ost[:, ch, :], pX[:, 384:448], [kk_ + "pX"], [okey])
                            else:
                                p.tt('dve', Ost[:, ch, :], Ost[:, ch, :], pX[:, 384:448], ALU.add, [kk_ + "pX", okey], [okey])
                            if not lastc:
                                for hs in range(2):
                                    sl_ = slice(hs * 64, (hs + 1) * 64)
                                    p.mm64(pX[sl_, 448:512], Kdtok[d][sl_, ch, :], Vp[sl_, ch, :], True, False, [f"rC_Kdtok{d}", "rC_Vp"], [kk_ + "pX"])
                                    p.mm64(pX[sl_, 448:512], Bdtok[d][sl_, ch, :], u_['NU'][sl_, :], False, True, [f"rC_Bdtok{d}", kk_ + "NU"], [kk_ + "pX"])
                                if first:
                                    p.copy('dve', u_['S'][:], pX[:, 448:512], [kk_ + "pX"], [kk_ + "S"])
                                else:
                                    p.stt(u_['S'][:], u_['S'][:], PC[d][:, ch:ch + 1], pX[:, 448:512], ALU.mult, ALU.add,
                                          [kk_ + "S", f"rC_PC{d}", kk_ + "pX"], [kk_ + "S"])
                                p.copy('act', u_['Sb'][:], u_['S'][:], [kk_ + "S"], [kk_ + "Sb"])
                    okeys = [("rC_O", ch) for ch in range(NCH)]
                    p.op('dve', lambda e: e.reduce_sum(st[:, :, 0], Ost[:], AX.X), okeys, ["rC_st"])
                    p.ts('dve', st[:, :, 0], st[:, :, 0], 1.0 / CH, None, ALU.mult, None, ["rC_st"], ["rC_st"])
                    p.tt('dve', Ost[:], Ost[:], st[:, :, 0:1].broadcast_to([128, NCH, CH]), ALU.subtract, okeys + ["rC_st"], ["rC_Oc"])
                    F6v = F[6][:].rearrange(cs3, q=CH)
                    p.act(F6v, Ost[:], AF.Square, ["rC_Oc"], [kF[6]])
                    p.op('dve', lambda e: e.reduce_sum(st[:, :, 1], F6v, AX.X), [kF[6]], ["rC_st"])
                    p.ts('dve', st[:, :, 1], st[:, :, 1], 1.0 / CH, 64e-5, ALU.mult, ALU.add, ["rC_st"], ["rC_st"])
                    p.tt('pool', st[:, :, 2], st[:, :, 1], c.mhalf[:, 0:NCH], ALU.pow, ["rC_st"], ["rC_st"])
                    p.tt('dve', Obf[:], Ost[:], st[:, :, 2:3].broadcast_to([128, NCH, CH]), ALU.mult, ["rC_Oc", "rC_st"], ["rC_Obf"])
                    ld(0, 4)
                    for c8 in range(NCH // 8):
                        for hs in range(2):
                            sl_ = slice(hs * 64, (hs + 1) * 64)
                            for q in range(8):
                                ch = c8 * 8 + q
                                p.mm64(pA.tiles[1][sl_, q * 64:(q + 1) * 64], Obf[sl_, ch, :], c.identb[sl_, sl_], True, True, ["rC_Obf"], ["rC_pA1"])
                        fs = slice(c8 * 512, (c8 + 1) * 512)
                        p.ts('dve', F[6][:, fs], pA.tiles[1][:, :], lnw[:, hp:hp + 1], lnb[:, hp:hp + 1], ALU.mult, ALU.add, ["rC_pA1", "r_c"], [kF[6]])
                    p.tt('dve', F[6][:], F[6][:], F[1][:], ALU.add, [kF[6], kF[1]], [kF[6]])
                    p.tt('dve', yT[:, hp, :], F[6][:], F[0][:], ALU.mult, [kF[6], kF[0]], [("rC_yT", hp)])

                p.barrier()
                es3.close()
                wl = WLoader(c, es2, "rD_wst", 1024)
                wo = sb(c, es2, "rD_wo", [128, KC, D], BF16)
                xr = ring_sb(c, es2, "rD_x", 1, [128, KC, 512], F32)
                xo = sb(c, es2, "rD_xo", [128, KC, 512], F32)
                for kc in range(KC):
                    wl.load(wo[:, kc, :], w_out[kc * 128:(kc + 1) * 128, :], "rD_wo")
                for gi in range(L // 512):
                    tok0 = s * L + gi * 512
                    xi, kx = xr.next()
                    p.dma('sp', xi[:], xt_view(c, tok0, 512), xt_keys(tok0, 512), [kx])
                    for dc in range(KC):
                        pa, kpa = pA.next()
                        for kc in range(KC):
                            p.mm(pa[:], wo[:, kc, dc * 128:(dc + 1) * 128], yT[:, kc, gi * 512:(gi + 1) * 512], kc == 0, kc == KC - 1,
                                 ["rD_wo"] + [("rC_yT", kc)], [kpa])
                        p.tt('dve', xo[:, dc, :], pa[:], xi[:, dc, :], ALU.add, [kpa, kx], ["rD_xo"])
                    p.dma('sp', xt_view(c, tok0, 512), xo[:], ["rD_xo"], xt_keys(tok0, 512))
            p.barrier()


def build(nseq_prompt=1, nseq_sample=4, cfg=None):
    cfg = cfg or {}
    depth = cfg.get('depth', 4)
    nseq = nseq_prompt + nseq_sample
    ntok = nseq * L
    nc = bass.Bass("TRN2", target_bir_lowering=False)
    c = Ctx()
    c.nc = nc
    c.cfg = cfg
    din = {}

    def inp(name, shape):
        din[name] = nc.dram_tensor(name, list(shape), F32, kind="ExternalInput").ap()
        return din[name]

    c.xp = inp("x_prompt", [max(nseq_prompt, 1), L, D])
    c.xs = inp("x_sample", [max(nseq_sample, 1), L, D])
    W = {}
    for name, shape in WEIGHT_SHAPES.items():
        W[name] = inp(name, shape)
    c.W = W
    c_ident = inp("c_ident", [128, 128])
    c.c_cos = inp("c_cos", [128, L])
    c.c_sin = inp("c_sin", [128, L])
    c.yp = nc.dram_tensor("y_prompt", [max(nseq_prompt, 1), L, D], F32, kind="ExternalOutput").ap()
    c.ys = nc.dram_tensor("y_sample", [max(nseq_sample, 1), L, D], F32, kind="ExternalOutput").ap()
    c.XT = nc.dram_tensor("XT", [D, ntok], F32).ap()
    c.Zs = nc.dram_tensor("Zs", [L, 2048], F32).ap()
    c.Ys = nc.dram_tensor("Ys", [L, 2048], F32).ap()
    c.RW = nc.dram_tensor("RW", [7, D, L], F32).ap()

    with contextlib.ExitStack() as es:
        p = Prog(nc, es)
        c.p = p
        c.ident = sb(c, es, "ident", [128, 128], F32)
        c.ones = sb(c, es, "ones", [128, 128], F32)
        c.mhalf = sb(c, es, "mhalf", [128, 512], F32)
        c.identb = sb(c, es, "identb", [128, 128], BF16)
        p.dma('sp', c.ident[:], c_ident[:, :], [], ["ident"])
        p.copy('dve', c.identb[:], c.ident[:], ["ident"], ["identb"])
        p.memset('pool', c.ones[:], 1.0, ["ones"])
        p.memset('pool', c.mhalf[:], -0.5, ["mhalf"])
        make_masks(c, es)
        p.barrier()

        srcs = [(c.xp, b) for b in range(nseq_prompt)] + [(c.xs, b) for b in range(nseq_sample)]
        dsts = [(c.yp, b) for b in range(nseq_prompt)] + [(c.ys, b) for b in range(nseq_sample)]
        stage_in(c, srcs)
        for i in range(depth):
            if cfg.get('ffn', True) and cfg.get('ffn1', True):
                stage_ffn(c, W['ffn1_norm'][i], W['ffn1_w_gu'][i], W['ffn1_w_down'][i], ntok)
            if i in cfg.get('mixers', [0, 1, 2, 3]):
                m, jj = i % 4, i // 4
                if m == 0:
                    stage_ssd(c, i, jj, nseq)
                if m == 1:
                    stage_gdn(c, i, jj, nseq)
                if m == 2:
                    stage_attn(c, i, jj, nseq)
                if m == 3:
                    stage_rwkv(c, i, jj, nseq)
            if cfg.get('ffn', True) and cfg.get('ffn2', True):
                stage_ffn(c, W['ffn2_norm'][i], W['ffn2_w_gu'][i], W['ffn2_w_down'][i], ntok)
        stage_out(c, dsts, W['final_norm'])
        p.emit()
    return nc


WEIGHT_SHAPES = {
    'ffn1_norm': (4, 1024), 'ffn1_w_gu': (4, 1024, 5632), 'ffn1_w_down': (4, 2816, 1024),
    'mix_norm': (4, 1024), 'ffn2_norm': (4, 1024), 'ffn2_w_gu': (4, 1024, 5632), 'ffn2_w_down': (4, 2816, 1024),
    'ssd_w_in': (1, 1024, 6208), 'ssd_conv_w': (1, 5, 4096), 'ssd_conv_b': (1, 4096), 'ssd_a_log': (1, 2, 32),
    'ssd_dt_bias': (1, 2, 32), 'ssd_d': (1, 32), 'ssd_norm': (1, 2048), 'ssd_w_out': (1, 2048, 1024),
    'gdn_w_in': (1, 1024, 6192), 'gdn_conv_w': (1, 5, 4096), 'gdn_conv_b': (1, 4096), 'gdn_a_log': (1, 2, 16),
    'gdn_dt_bias': (1, 2, 16), 'gdn_norm': (1, 128), 'gdn_w_out': (1, 2048, 1024),
    'att_w_qkv': (1, 1024, 1536), 'att_sinks': (1, 16), 'att_w_out': (1, 1024, 1024),
    'rwkv_x_mu': (1, 6, 1024), 'rwkv_w_rkv': (1, 3, 1024, 1024), 'rwkv_w0': (1, 2, 1024),
    'rwkv_w1': (1, 2, 1024, 64), 'rwkv_w2': (1, 2, 64, 1024), 'rwkv_a0': (1, 1024), 'rwkv_a1': (1, 1024, 64),
    'rwkv_a2': (1, 64, 1024), 'rwkv_g1': (1, 1024, 128), 'rwkv_g2': (1, 128, 1024), 'rwkv_k_k': (1, 1024),
    'rwkv_k_a': (1, 1024), 'rwkv_r_k': (1, 1024), 'rwkv_lnx_w': (1, 1024), 'rwkv_lnx_b': (1, 1024),
    'rwkv_w_out': (1, 1024, 1024), 'final_norm': (1024,),
}


def consts():
    r = np.arange(128) % 64
    i = (r % 32).astype(np.float64)
    inv_freq = 10000.0 ** (-i / 32.0)
    ang = (np.arange(L, dtype=np.float64)[None, :] * inv_freq[:, None]).astype(np.float32).astype(np.float64)
    sgn = np.where(r < 32, -1.0, 1.0)[:, None]
    return {"c_ident": np.eye(128, dtype=np.float32),
            "c_cos": np.cos(ang).astype(np.float32),
            "c_sin": (np.sin(ang) * sgn).astype(np.float32)}


def kernel(**inputs):
    nc = build(1, 4)
    xp = np.ascontiguousarray(inputs['x_prompt'], dtype=np.float32)
    xs = np.ascontiguousarray(inputs['x_sample'], dtype=np.float32)
    shared = {k: np.ascontiguousarray(inputs[k], dtype=np.float32) for k in WEIGHT_SHAPES}
    shared.update(consts())
    in_maps = []
    for c in range(NCORES):
        m = dict(shared)
        m['x_prompt'] = xp[c:c + 1]
        m['x_sample'] = xs[4 * c:4 * c + 4]
        in_maps.append(m)
    res = run_bass_kernel_spmd(nc, in_maps, core_ids=list(range(NCORES)))
    yp = np.concatenate([r['y_prompt'] for r in res.results], axis=0)
    ys = np.concatenate([r['y_sample'] for r in res.results], axis=0)
    return yp.astype(np.float32), ys.astype(np.float32)
```

```python
import contextlib
import numpy as np
import concourse.bass as bass
import concourse.mybir as mybir
from concourse.alu_op_type import AluOpType as ALU
from concourse.bass_utils import run_bass_kernel_spmd

F32 = mybir.dt.float32
BF16 = mybir.dt.bfloat16
AF = mybir.ActivationFunctionType
AX = mybir.AxisListType

NCORES = 8
D = 1024
L = 2048
DFF = 2816
KC = D // 128
FC = DFF // 128
ENG = ['pe', 'dve', 'act', 'pool', 'sp']
SAME_ENG_SYNC = True


PSUM_KEYS = set()


class Prog:
    def __init__(self, nc, es):
        self.nc = nc
        self.es = es
        self.engs = {'pe': nc.tensor, 'dve': nc.vector, 'act': nc.scalar, 'pool': nc.gpsimd, 'sp': nc.sync}
        self.q = {e: [] for e in ENG}
        self.sem = {e: es.enter_context(nc.semaphore("s_" + e)) for e in ENG}
        self.cnt = {e: 0 for e in ENG}
        self.seen = {e: {} for e in ENG}
        self.lastw = {}
        self.rds = {}
        self.ndsem = {'sp': 6, 'pool': 2, 'act': 2}
        self.dsem = {}
        self.dval = {}
        self.drr = {}
        for qn, n in self.ndsem.items():
            self.dsem[qn] = [es.enter_context(nc.semaphore(f"d_{qn}{i}")) for i in range(n)]
            for i in range(n):
                self.dval[(qn, i)] = 0
            self.drr[qn] = 0
        self.ninstr = 0

    def _semh(self, s):
        if isinstance(s, tuple):
            return self.dsem[s[0]][s[1]]
        return self.sem[s]

    def _wait(self, eng, tok):
        s, v = tok
        if s == eng and (eng == 'pe' or not SAME_ENG_SYNC):
            return
        if self.seen[eng].get(s, 0) >= v:
            return
        self.seen[eng][s] = v
        self.q[eng].append(('w', self._semh(s), v))

    def _deps(self, eng, reads, writes, is_dma=False):
        for k in reads:
            for tok in self.lastw.get(k, ()):
                self._wait(eng, tok)
        for k in writes:
            for tok in self.lastw.get(k, ()):
                if is_dma and isinstance(tok[0], tuple):
                    continue
                self._wait(eng, tok)
            for s, v in self.rds.get(k, {}).items():
                self._wait(eng, (s, v))

    def _record(self, tok, reads, writes, is_dma=False):
        s, v = tok
        for k in reads:
            d = self.rds.setdefault(k, {})
            if d.get(s, 0) < v:
                d[s] = v
        for k in writes:
            if is_dma and not self.rds.get(k) and k in self.lastw and all(isinstance(t[0], tuple) for t in self.lastw[k]):
                self.lastw[k] = [t for t in self.lastw[k] if t[0] != s] + [tok]
            else:
                self.lastw[k] = [tok]
            self.rds[k] = {}

    def op(self, eng, fn, reads=(), writes=()):
        xr = [k for k in reads if k in PSUM_KEYS]
        if xr:
            reads = [k for k in reads if k not in PSUM_KEYS]
            writes = list(writes) + [k for k in xr if k not in writes]
        self._deps(eng, reads, writes)
        self.cnt[eng] += 1
        tok = (eng, self.cnt[eng])
        self.q[eng].append(('o', fn, self.sem[eng], 1))
        self._record(tok, reads, writes)
        self.ninstr += 1

    def dma(self, qn, out, in_, reads=(), writes=(), **kw):
        self._deps(qn, reads, writes, is_dma=True)
        i = self.drr[qn]
        self.drr[qn] = (i + 1) % self.ndsem[qn]
        prev = self.dval[(qn, i)]
        if prev > 0:
            self._wait(qn, ((qn, i), prev))
        self.dval[(qn, i)] = prev + 16
        tok = ((qn, i), prev + 16)
        self.q[qn].append(('o', lambda e: e.dma_start(out=out, in_=in_, **kw), self.dsem[qn][i], 16))
        self._record(tok, reads, writes, is_dma=True)
        self.ninstr += 1

    def barrier(self):
        for e in ENG:
            for e2 in ENG:
                if e2 != e and self.cnt[e2] > 0:
                    self._wait(e, (e2, self.cnt[e2]))
            for k, v in self.dval.items():
                if v > 0:
                    self._wait(e, (k, v))
        self.lastw = {}
        self.rds = {}

    def emit(self):
        nc = self.nc
        with nc.Block() as block:
            def run(e, name):
                for it in self.q[name]:
                    if it[0] == 'w':
                        e.wait_ge(it[1], it[2])
                    else:
                        it[1](e).then_inc(it[2], it[3])

            @block.tensor
            def _(e):
                run(e, 'pe')

            @block.vector
            def _(e):
                run(e, 'dve')

            @block.scalar
            def _(e):
                run(e, 'act')

            @block.gpsimd
            def _(e):
                run(e, 'pool')

            @block.sync
            def _(e):
                run(e, 'sp')

    def mm(self, out, lhsT, rhs, start, stop, reads, writes):
        self.op('pe', lambda e: e.matmul(out, lhsT, rhs, start=start, stop=stop), reads, writes)

    def mm64(self, out, lhsT, rhs, start, stop, reads, writes):
        grp = int(lhsT.base_partition) if hasattr(lhsT, 'base_partition') and not callable(lhsT.base_partition) else int(lhsT.base_partition())
        last = getattr(self, '_last64', None)
        if self.cnt['pe'] > 0 and last is not None and last[0] == self.cnt['pe'] and last[1] != grp:
            self.q['pe'].append(('w', self.sem['pe'], self.cnt['pe']))
        self.mm(out, lhsT, rhs, start, stop, reads, writes)
        self._last64 = (self.cnt['pe'], grp)

    def tr(self, out, in_, ident, reads, writes):
        self.op('pe', lambda e: e.transpose(out, in_, ident), reads, writes)

    def act(self, out, in_, func, reads, writes, bias=None, scale=None):
        kw = {}
        if bias is not None:
            kw['bias'] = bias
        if scale is not None:
            kw['scale'] = scale
        self.op('act', lambda e: e.activation(out, in_, func, **kw), reads, writes)

    def tt(self, eng, out, in0, in1, op, reads, writes):
        self.op(eng, lambda e: e.tensor_tensor(out, in0, in1, op), reads, writes)

    def ts(self, eng, out, in0, s1, s2, op0, op1, reads, writes):
        if op1 is None:
            self.op(eng, lambda e: e.tensor_scalar(out, in0, s1, None, op0), reads, writes)
        else:
            self.op(eng, lambda e: e.tensor_scalar(out, in0, s1, s2, op0, op1), reads, writes)

    def stt(self, out, in0, scalar, in1, op0, op1, reads, writes):
        self.op('dve', lambda e: e.scalar_tensor_tensor(out, in0, scalar, in1, op0, op1), reads, writes)

    def copy(self, eng, out, in_, reads, writes):
        if eng == 'act':
            self.op('act', lambda e: e.copy(out, in_), reads, writes)
        else:
            self.op(eng, lambda e: e.tensor_copy(out, in_), reads, writes)

    def memset(self, eng, ap, val, writes):
        self.op(eng, lambda e: e.memset(ap, val), (), writes)


class Ctx:
    pass


_UID = [0]


def sb(c, es, name, shape, dt):
    _UID[0] += 1
    return es.enter_context(c.nc.sbuf_tensor(f"{name}_{_UID[0]}", shape, dt))


def ps(c, es, name, shape, dt=F32):
    _UID[0] += 1
    return es.enter_context(c.nc.psum_tensor(f"{name}_{_UID[0]}", shape, dt))


def stage_in(c, srcs):
    p = c.p
    with contextlib.ExitStack() as es:
        xin = [sb(c, es, f"in_x{i}", [128, D], F32) for i in range(2)]
        xt = [sb(c, es, f"in_xt{i}", [128, KC, 512], F32) for i in range(2)]
        pt = [ps(c, es, f"in_ps{i}", [128, 512]) for i in range(4)]
        n = 0
        for s, (src, b) in enumerate(srcs):
            for g in range(L // 512):
                xo = xt[g % 2]
                ko = f"in_xt{g % 2}"
                for j in range(4):
                    t0 = g * 512 + j * 128
                    xi = xin[n % 2]
                    ki = f"in_x{n % 2}"
                    p.dma('sp', xi[:], src[b, t0:t0 + 128, :], [], [ki])
                    for half in range(2):
                        pp = pt[(2 * n + half) % 4]
                        kp = f"in_ps{(2 * n + half) % 4}"
                        for q4 in range(4):
                            kc = half * 4 + q4
                            p.tr(pp[:, q4 * 128:(q4 + 1) * 128], xi[:, kc * 128:(kc + 1) * 128], c.ident[:],
                                 [ki], [kp])
                        eng = 'act' if half == 0 else 'dve'
                        p.copy(eng, xo[:, half * 4:half * 4 + 4, j * 128:(j + 1) * 128],
                               pp[:].rearrange("p (a b) -> p a b", a=4), [kp], [ko])
                    n += 1
                tok0 = s * L + g * 512
                p.dma('sp', c.XT[:, tok0:tok0 + 512].rearrange("(kc p) t -> p kc t", p=128), xo[:],
                      [ko], [("XT", tok0 // 256), ("XT", tok0 // 256 + 1)])
    p.barrier()


def load_vec(c, dst, src_ap, key, q='sp'):
    c.p.dma(q, dst, src_ap.rearrange("(j p) -> p j", p=128), [], [key], allow_slow_non_contiguous=True)


def rms_stats(c, x, kx, sq, ksq, pss, kps, var, kvar, rstd, krstd, nt, nch=KC, dim=D, eps=1e-6):
    p = c.p
    p.act(sq[:, :nch, :nt], x[:, :nch, :nt], AF.Square, [kx], [ksq])
    for kc in range(nch):
        p.mm(pss[:, :nt], c.ones[:], sq[:, kc, :nt], kc == 0, kc == nch - 1, [ksq], [kps])
    p.ts('dve', var[:, :nt], pss[:, :nt], 1.0 / dim, eps, ALU.mult, ALU.add, [kps], [kvar])
    p.tt('pool', rstd[:, :nt], var[:, :nt], c.mhalf[:, :nt], ALU.pow, [kvar], [krstd])


def stage_out(c, dsts, gvec):
    p = c.p
    NT = 256
    with contextlib.ExitStack() as es:
        g = sb(c, es, "o_g", [128, KC], F32)
        x = [sb(c, es, f"o_x{i}", [128, KC, NT], F32) for i in range(2)]
        sq = sb(c, es, "o_sq", [128, KC, NT], F32)
        var = sb(c, es, "o_var", [128, NT], F32)
        rstd = sb(c, es, "o_rstd", [128, NT], F32)
        xn = sb(c, es, "o_xn", [128, KC, NT], F32)
        yo = [sb(c, es, f"o_y{i}", [128, D], F32) for i in range(2)]
        pss = ps(c, es, "o_pss", [128, 512])
        pt = [ps(c, es, f"o_pt{i}", [128, 512]) for i in range(4)]
        load_vec(c, g[:], gvec, "o_g")
        n = 0
        m = 0
        for s, (dst, b) in enumerate(dsts):
            for gi in range(L // NT):
                tok0 = s * L + gi * NT
                xi = x[gi % 2]
                kx = f"o_x{gi % 2}"
                p.dma('sp', xi[:], c.XT[:, tok0:tok0 + NT].rearrange("(kc p) t -> p kc t", p=128),
                      [("XT", tok0 // 256)], [kx])
                rms_stats(c, xi, kx, sq, "o_sq", pss, "o_pss", var, "o_var", rstd, "o_rstd", NT)
                for kc in range(KC):
                    p.stt(xn[:, kc, :], xi[:, kc, :], g[:, kc:kc + 1], rstd[:], ALU.mult, ALU.mult,
                          [kx, "o_g", "o_rstd"], ["o_xn"])
                for j in range(NT // 128):
                    y = yo[m % 2]
                    ky = f"o_y{m % 2}"
                    for half in range(2):
                        pp = pt[n % 4]
                        kp = f"o_pt{n % 4}"
                        n += 1
                        for q4 in range(4):
                            kc = half * 4 + q4
                            p.tr(pp[:, q4 * 128:(q4 + 1) * 128], xn[:, kc, j * 128:(j + 1) * 128], c.ident[:],
                                 ["o_xn"], [kp])
                        eng = 'act' if half == 0 else 'dve'
                        p.copy(eng, y[:, half * 512:(half + 1) * 512], pp[:], [kp], [ky])
                    t0 = gi * NT + j * 128
                    p.dma('sp', dst[b, t0:t0 + 128, :], y[:], [ky], [])
                    m += 1
    p.barrier()


def stage_ffn(c, gvec, w_gu, w_down, ntok):
    p = c.p
    NT = 256
    with contextlib.ExitStack() as es:
        wgu = sb(c, es, "f_wgu", [128, KC, 2 * DFF], BF16)
        wd = sb(c, es, "f_wd", [128, FC, D], BF16)
        g = sb(c, es, "f_g", [128, KC], F32)
        x = [sb(c, es, f"f_x{i}", [128, KC, NT], F32) for i in range(2)]
        sq = sb(c, es, "f_sq", [128, KC, NT], F32)
        xo = sb(c, es, "f_xo", [128, KC, NT], F32)
        var = sb(c, es, "f_var", [128, NT], F32)
        rstd = sb(c, es, "f_rstd", [128, NT], F32)
        xn = [sb(c, es, f"f_xn{i}", [128, KC, NT], BF16) for i in range(2)]
        h = sb(c, es, "f_h", [128, FC, NT], BF16)
        sg = [sb(c, es, f"f_sg{i}", [128, NT], F32) for i in range(2)]
        pss = ps(c, es, "f_pss", [128, 512])
        pg = [ps(c, es, f"f_pg{i}", [128, 512]) for i in range(2)]
        pu = [ps(c, es, f"f_pu{i}", [128, 512]) for i in range(2)]
        py = [ps(c, es, f"f_py{i}", [128, 512]) for i in range(2)]
        load_vec(c, g[:], gvec, "f_g")
        wl = WLoader(c, es, "f_wst", 1408)
        for kc in range(KC):
            for hh in range(4):
                wl.load(wgu[:, kc, hh * 1408:(hh + 1) * 1408],
                        w_gu[kc * 128:(kc + 1) * 128, hh * 1408:(hh + 1) * 1408], "f_wgu")
        for j in range(FC):
            wl.load(wd[:, j, :], w_down[j * 128:(j + 1) * 128, :], "f_wd")
        ngrp = c.cfg.get('ffn_groups', ntok // NT)

        def prep(gi):
            xi, kx = x[gi % 2], f"f_x{gi % 2}"
            xg, kxn = xn[gi % 2], f"f_xn{gi % 2}"
            p.dma('sp', xi[:], c.XT[:, gi * NT:(gi + 1) * NT].rearrange("(kc p) t -> p kc t", p=128), [("XT", gi)], [kx])
            rms_stats(c, xi, kx, sq, "f_sq", pss, "f_pss", var, "f_var", rstd, "f_rstd", NT)
            for kc in range(KC):
                p.stt(xg[:, kc, :], xi[:, kc, :], g[:, kc:kc + 1], rstd[:], ALU.mult, ALU.mult, [kx, "f_g", "f_rstd"], [kxn])

        if ngrp > 0:
            prep(0)
        for gi in range(ngrp):
            tok0 = gi * NT
            xi, kx = x[gi % 2], f"f_x{gi % 2}"
            xg, kxn = xn[gi % 2], f"f_xn{gi % 2}"
            for j in range(FC):
                pgj, puj, sgj = pg[j % 2], pu[j % 2], sg[j % 2]
                kg, ku, ks = f"f_pg{j % 2}", f"f_pu{j % 2}", f"f_sg{j % 2}"
                for kc in range(KC):
                    p.mm(pgj[:, :NT], wgu[:, kc, j * 128:(j + 1) * 128], xg[:, kc, :], kc == 0, kc == KC - 1, ["f_wgu", kxn], [kg])
                for kc in range(KC):
                    p.mm(puj[:, :NT], wgu[:, kc, DFF + j * 128:DFF + (j + 1) * 128], xg[:, kc, :], kc == 0, kc == KC - 1, ["f_wgu", kxn], [ku])
                p.act(sgj[:], pgj[:, :NT], AF.Silu, [kg], [ks])
                p.tt('dve', h[:, j, :], sgj[:], puj[:, :NT], ALU.mult, [ks, ku], [("f_h", j)])
            if gi + 1 < ngrp:
                prep(gi + 1)
            for dc in range(KC):
                pyj = py[dc % 2]
                ky = f"f_py{dc % 2}"
                for j in range(FC):
                    p.mm(pyj[:, :NT], wd[:, j, dc * 128:(dc + 1) * 128], h[:, j, :], j == 0, j == FC - 1, ["f_wd", ("f_h", j)], [ky])
                p.stt(xo[:, dc, :], pyj[:, :NT], 0.5, xi[:, dc, :], ALU.mult, ALU.add, [ky, kx], ["f_xo"])
            p.dma('sp', c.XT[:, tok0:tok0 + NT].rearrange("(kc p) t -> p kc t", p=128), xo[:], ["f_xo"], [("XT", gi)])
    p.barrier()


class Ring:
    def __init__(self, tiles, name):
        self.tiles = tiles
        self.name = name
        self.i = 0

    def next(self):
        t = self.tiles[self.i % len(self.tiles)]
        k = f"{self.name}{self.i % len(self.tiles)}"
        self.i += 1
        return t, k


def ring_sb(c, es, name, n, shape, dt):
    return Ring([sb(c, es, f"{name}{i}", shape, dt) for i in range(n)], name)


def ring_ps(c, es, name, n, shape, dt=F32):
    for i in range(n):
        PSUM_KEYS.add(f"{name}{i}")
    return Ring([ps(c, es, f"{name}{i}", shape, dt) for i in range(n)], name)


class WLoader:
    def __init__(self, c, es, name, width, n=2):
        self.c = c
        self.ring = ring_sb(c, es, name, n, [128, width], F32)
        self.width = width
        self.k = 0

    def load(self, dst, src, key, shape=None):
        p = self.c.p
        st, kst = self.ring.next()
        n = 1
        for d_ in dst.shape[1:]:
            n *= d_
        sv = st[:dst.shape[0], :n]
        if len(dst.shape) == 3:
            sv = sv.rearrange("p (a b) -> p a b", a=dst.shape[1])
        elif len(dst.shape) == 4:
            sv = sv.rearrange("p (a b c) -> p a b c", a=dst.shape[1], b=dst.shape[2])
        p.dma('sp', sv, src, [], [kst])
        eng = 'dve' if self.k % 2 == 0 else 'act'
        if getattr(self, 'dbg', 0) == 3:
            eng = 'pool'
        if getattr(self, 'dbg', 0) == 4:
            eng = 'act'
        self.k += 1
        if getattr(self, 'dbg', 0) != 2:
            p.copy(eng, dst, sv, [kst], [key])


def xt_view(c, tok0, nt):
    return c.XT[:, tok0:tok0 + nt].rearrange("(kc p) t -> p kc t", p=128)


def xt_keys(tok0, nt):
    return [("XT", k) for k in range(tok0 // 256, (tok0 + nt + 255) // 256)]


def load_xn(c, tok0, NT, xr, sq, pss, var, rstd, g, kg, xn, kxn, pref):
    p = c.p
    xi, kx = xr.next()
    p.dma('sp', xi[:, :, :NT], xt_view(c, tok0, NT), xt_keys(tok0, NT), [kx])
    rms_stats(c, xi, kx, sq, pref + "sq", pss, pref + "pss", var, pref + "var", rstd, pref + "rstd", NT)
    for kc in range(KC):
        p.stt(xn[:, kc, :NT], xi[:, kc, :NT], g[:, kc:kc + 1], rstd[:, :NT], ALU.mult, ALU.mult,
              [kx, kg, pref + "rstd"], [kxn])
    return xi, kx


def stage_attn(c, layer, j, nseq):
    p = c.p
    W = c.W
    wqkv, wout, sinks_d, gvec = W['att_w_qkv'][j], W['att_w_out'][j], W['att_sinks'][j], W['mix_norm'][layer]
    NEG = -30000.0
    with contextlib.ExitStack() as es:
        wq = sb(c, es, "a_wq", [128, KC, 1024], BF16)
        wqs = sb(c, es, "a_wqs", [128, KC, 1024], BF16)
        wk = sb(c, es, "a_wk", [128, KC, 512], BF16)
        wks = sb(c, es, "a_wks", [128, KC, 512], BF16)
        wv = sb(c, es, "a_wv", [128, KC, 256], BF16)
        wo = sb(c, es, "a_wo", [128, KC, 1024], BF16)
        cos = sb(c, es, "a_cos", [128, L], F32)
        sin = sb(c, es, "a_sin", [128, L], F32)
        g = sb(c, es, "a_g", [128, KC], F32)
        snk = sb(c, es, "a_snk", [128, 16], F32)
        nsnk = sb(c, es, "a_nsnk", [128, 16], F32)
        mask = sb(c, es, "a_mask", [128, 384], F32)
        qT = sb(c, es, "a_qT", [128, KC, L], BF16)
        kT = sb(c, es, "a_kT", [128, 4, L], BF16)
        vtok = sb(c, es, "a_v", [128, 16, 256], BF16)
        load_vec(c, g[:], gvec, "a_g")
        p.dma('sp', cos[:], c.c_cos[:, :], [], ["a_cos"])
        p.dma('sp', sin[:], c.c_sin[:, :], [], ["a_sin"])
        p.dma('sp', snk[:], sinks_d.partition_broadcast(128), [], ["a_snk"])
        p.ts('dve', nsnk[:], snk[:], -1.0, None, ALU.mult, None, ["a_snk"], ["a_nsnk"])
        p.memset('pool', mask[:], 0.0, ["a_mask"])
        p.op('pool', lambda e: e.affine_select(mask[:], mask[:], [[1, 384]], ALU.is_ge, NEG, base=0,
                                               channel_multiplier=-1), ["a_mask"], ["a_mask"])
        p.op('pool', lambda e: e.affine_select(mask[:], mask[:], [[-1, 384]], ALU.is_ge, NEG, base=256,
                                               channel_multiplier=1), ["a_mask"], ["a_mask"])
        wl = WLoader(c, es, "a_wst", 1024)
        for kc in range(KC):
            rows = slice(kc * 128, (kc + 1) * 128)
            wl.load(wq[:, kc, :], wqkv[rows, 0:1024], "a_w")
            src = wqkv[rows, 0:1024].rearrange("p (h r d) -> p h r d", h=16, r=2)
            dst = wqs[:, kc, :].rearrange("p (h r d) -> p h r d", h=16, r=2)
            wl.load(dst[:, :, 0, :], src[:, :, 1, :], "a_w")
            wl.load(dst[:, :, 1, :], src[:, :, 0, :], "a_w")
            srck = wqkv[rows, 1024:1280].rearrange("p (g r d) -> p g r d", g=4, r=2)
            dk = wk[:, kc, :].rearrange("p (g c r d) -> p g c r d", g=4, c=2, r=2)
            dks = wks[:, kc, :].rearrange("p (g c r d) -> p g c r d", g=4, c=2, r=2)
            for cpy in range(2):
                for r in range(2):
                    wl.load(dk[:, :, cpy, r, :], srck[:, :, r, :], "a_w")
                    wl.load(dks[:, :, cpy, r, :], srck[:, :, 1 - r, :], "a_w")
            wl.load(wv[:, kc, :], wqkv[rows, 1280:1536], "a_w")
            wl.load(wo[:, kc, :], wout[rows, :], "a_w")
        p.barrier()
        for s in range(nseq):
            with contextlib.ExitStack() as es2:
                NT = 256
                xr = ring_sb(c, es2, "aA_x", 2, [128, KC, NT], F32)
                sq = sb(c, es2, "aA_sq", [128, KC, NT], F32)
                var = sb(c, es2, "aA_var", [128, NT], F32)
                rstd = sb(c, es2, "aA_rstd", [128, NT], F32)
                xn = sb(c, es2, "aA_xn", [128, KC, NT], BF16)
                t1r = ring_sb(c, es2, "aA_t1", 2, [128, NT], F32)
                t2r = ring_sb(c, es2, "aA_t2", 2, [128, NT], F32)
                pss = ps(c, es2, "aA_pss", [128, 512])
                p1r = ring_ps(c, es2, "aA_p1", 2, [128, 512])
                p2r = ring_ps(c, es2, "aA_p2", 2, [128, 512])
                pvr = ring_ps(c, es2, "aA_pv", 2, [128, 512])
                for gi in range(L // NT):
                    t0 = gi * NT
                    load_xn(c, s * L + t0, NT, xr, sq, pss, var, rstd, g, "a_g", xn, "aA_xn", "aA_")
                    for oc in range(12):
                        if oc < 8:
                            wa, wb, dstT = wq[:, :, oc * 128:(oc + 1) * 128], wqs[:, :, oc * 128:(oc + 1) * 128], qT[:, oc, t0:t0 + NT]
                            kd = ("a_qT", oc)
                        else:
                            gg = oc - 8
                            wa, wb, dstT = wk[:, :, gg * 128:(gg + 1) * 128], wks[:, :, gg * 128:(gg + 1) * 128], kT[:, gg, t0:t0 + NT]
                            kd = ("a_kT", gg)
                        p1, k1 = p1r.next()
                        p2, k2 = p2r.next()
                        for kc in range(KC):
                            p.mm(p1[:, :NT], wa[:, kc, :], xn[:, kc, :], kc == 0, kc == KC - 1, ["a_w", "aA_xn"], [k1])
                        for kc in range(KC):
                            p.mm(p2[:, :NT], wb[:, kc, :], xn[:, kc, :], kc == 0, kc == KC - 1, ["a_w", "aA_xn"], [k2])
                        t1, kt1 = t1r.next()
                        t2, kt2 = t2r.next()
                        p.tt('dve', t1[:], p1[:, :NT], cos[:, t0:t0 + NT], ALU.mult, [k1, "a_cos"], [kt1])
                        p.tt('dve', t2[:], p2[:, :NT], sin[:, t0:t0 + NT], ALU.mult, [k2, "a_sin"], [kt2])
                        p.tt('dve', dstT, t1[:], t2[:], ALU.add, [kt1, kt2], [kd])
                    for tb in range(NT // 128):
                        pv, kv = pvr.next()
                        for kc in range(KC):
                            p.mm(pv[:, :256], xn[:, kc, tb * 128:(tb + 1) * 128], wv[:, kc, :], kc == 0, kc == KC - 1,
                                 ["a_w", "aA_xn"], [kv])
                        p.copy('act', vtok[:, (t0 // 128) + tb, :], pv[:, :256], [kv], [("a_v", (t0 // 128) + tb)])
            p.barrier()
            with contextlib.ExitStack() as es2:
                smr = ring_sb(c, es2, "aB_sm", 2, [128, 384], F32)
                er = ring_sb(c, es2, "aB_e", 2, [128, 384], F32)
                enr = ring_sb(c, es2, "aB_en", 2, [128, 384], BF16)
                eTr = ring_sb(c, es2, "aB_eT", 2, [128, 384], BF16)
                str_ = ring_sb(c, es2, "aB_st", 4, [128, 8], F32)
                oT = sb(c, es2, "aB_oT", [128, KC, 512], BF16)
                xr = ring_sb(c, es2, "aB_x", 1, [128, KC, 512], F32)
                xo = sb(c, es2, "aB_xo", [128, KC, 512], F32)
                spr = ring_ps(c, es2, "aB_sp", 2, [128, 512])
                tpr = ring_ps(c, es2, "aB_tp", 2, [128, 512], BF16)
                opr = ring_ps(c, es2, "aB_op", 2, [128, 512])
                ypr = ring_ps(c, es2, "aB_yp", 2, [128, 512])
                for gi in range(L // 512):
                    tok0 = s * L + gi * 512
                    xi, kx = xr.next()
                    p.dma('sp', xi[:], xt_view(c, tok0, 512), xt_keys(tok0, 512), [kx])
                    for qc in range(KC):
                        op_, kop = opr.next()
                        for qb in range(4):
                            jb = gi * 4 + qb
                            kb0, kb1 = max(jb - 1, 0), min(jb + 1, 15)
                            nk = kb1 - kb0 + 1
                            mo = 128 if jb == 0 else 0
                            nkw = nk * 128
                            for hp in range(2):
                                h = qc * 2 + hp
                                gk = h // 4
                                b0 = hp * 64
                                sp_, ksp = spr.next()
                                p.mm(sp_[:, :nkw], qT[b0:b0 + 64, qc, jb * 128:(jb + 1) * 128],
                                     kT[b0:b0 + 64, gk, kb0 * 128:(kb1 + 1) * 128], True, True,
                                     [("a_qT", qc), ("a_kT", gk)], [ksp])
                                sm, ksm = smr.next()
                                p.tt('dve', sm[:, :nkw], sp_[:, :nkw], mask[:, mo:mo + nkw], ALU.add, [ksp, "a_mask"], [ksm])
                                st, kst = str_.next()
                                p.op('dve', lambda e, st=st, sm=sm, nkw=nkw: e.reduce_max(st[:, 0:1], sm[:, :nkw], AX.X), [ksm], [kst])
                                p.ts('dve', st[:, 1:2], st[:, 0:1], -0.125, nsnk[:, h:h + 1], ALU.mult, ALU.min, ["a_nsnk", kst], [kst])
                                e_, ke = er.next()
                                p.op('act', lambda e, e_=e_, sm=sm, st=st, nkw=nkw: e.activation(
                                    e_[:, :nkw], sm[:, :nkw], AF.Exp, bias=st[:, 1:2], scale=0.125, accum_out=st[:, 2:3]),
                                    [ksm, kst], [ke, kst])
                                p.op('act', lambda e, st=st, h=h: e.activation(st[:, 3:4], st[:, 1:2], AF.Exp, bias=snk[:, h:h + 1]),
                                     [kst, "a_snk"], [kst])
                                p.tt('dve', st[:, 4:5], st[:, 2:3], st[:, 3:4], ALU.add, [kst], [kst])
                                p.op('dve', lambda e, st=st: e.reciprocal(st[:, 5:6], st[:, 4:5]), [kst], [kst])
                                en, ken = enr.next()
                                p.ts('dve', en[:, :nkw], e_[:, :nkw], st[:, 5:6], None, ALU.mult, None, [ke, kst], [ken])
                                tp, ktp = tpr.next()
                                for kb in range(nk):
                                    p.tr(tp[:, kb * 128:(kb + 1) * 128], en[:, kb * 128:(kb + 1) * 128], c.identb[:], [ken], [ktp])
                                eT, keT = eTr.next()
                                p.copy('act', eT[:, :nkw], tp[:, :nkw], [ktp], [keT])
                                for kb in range(nk):
                                    p.mm(op_[b0:b0 + 64, qb * 128:(qb + 1) * 128], vtok[:, kb0 + kb, gk * 64:(gk + 1) * 64],
                                         eT[:, kb * 128:(kb + 1) * 128], kb == 0, kb == nk - 1,
                                         [("a_v", kb0 + kb), keT], [kop])
                        p.copy('act', oT[:, qc, :], op_[:], [kop], [("aB_oT", qc)])
                    for dc in range(KC):
                        yp, kyp = ypr.next()
                        for qc in range(KC):
                            p.mm(yp[:], wo[:, qc, dc * 128:(dc + 1) * 128], oT[:, qc, :], qc == 0, qc == KC - 1,
                                 ["a_w", ("aB_oT", qc)], [kyp])
                        p.tt('dve', xo[:, dc, :], yp[:], xi[:, dc, :], ALU.add, [kyp, kx], ["aB_xo"])
                    p.dma('sp', xt_view(c, tok0, 512), xo[:], ["aB_xo"], xt_keys(tok0, 512))
            p.barrier()


def make_masks(c, es):
    p = c.p
    c.mk = {}
    for name, pat, base, cm, op in [("LE", 1, 0, -1, ALU.is_ge), ("GE", -1, 0, 1, ALU.is_ge),
                                    ("GT", -1, 0, 1, ALU.is_gt), ("LT", 1, 0, -1, ALU.is_gt)]:
        t = sb(c, es, "mk" + name, [128, 128], F32)
        p.memset('pool', t[:], 1.0, ["mk" + name])
        p.op('pool', lambda e, t=t, pat=pat, base=base, cm=cm, op=op: e.affine_select(
            t[:], t[:], [[pat, 128]], op, 0.0, base=base, channel_multiplier=cm), ["mk" + name], ["mk" + name])
        c.mk[name] = t


def bc(ap, shape, axis):
    return ap.unsqueeze(axis).broadcast_to(shape)


def stage_ssd(c, layer, j, nseq):
    p = c.p
    W = c.W
    w_in, conv_w, conv_b = W['ssd_w_in'][j], W['ssd_conv_w'][j], W['ssd_conv_b'][j]
    a_log, dt_bias, d_skip, norm_w, w_out = W['ssd_a_log'][j], W['ssd_dt_bias'][j], W['ssd_d'][j], W['ssd_norm'][j], W['ssd_w_out'][j]
    gvec = W['mix_norm'][layer]
    NB = L // 128
    with contextlib.ExitStack() as es:
        g = sb(c, es, "s_g", [128, KC], F32)
        cw = sb(c, es, "s_cw", [128, 5, 32], F32)
        cb = sb(c, es, "s_cb", [128, 32], F32)
        dtb = sb(c, es, "s_dtb", [128, 64], F32)
        aneg = sb(c, es, "s_aneg", [128, 64], F32)
        dsk = sb(c, es, "s_dsk", [128, 32], F32)
        nw = sb(c, es, "s_nw", [128, 2048], F32)
        xn = sb(c, es, "s_xn", [128, KC, L], BF16)
        dt = sb(c, es, "s_dt", [128, NB, 64], F32)
        load_vec(c, g[:], gvec, "s_g")
        for tap in range(5):
            p.dma('sp', cw[:, tap, :], conv_w[tap].rearrange("(cc p) -> p cc", p=128), [], ["s_cw"], allow_slow_non_contiguous=True)
        p.dma('sp', cb[:], conv_b.rearrange("(cc p) -> p cc", p=128), [], ["s_cb"], allow_slow_non_contiguous=True)
        p.dma('sp', dtb[:], dt_bias.rearrange("a b -> (a b)").partition_broadcast(128), [], ["s_dtb"])
        p.dma('sp', aneg[:], a_log.rearrange("a b -> (a b)").partition_broadcast(128), [], ["s_aneg"])
        p.dma('sp', dsk[:], d_skip.partition_broadcast(128), [], ["s_dsk"])
        p.dma('sp', nw[:], norm_w.partition_broadcast(128), [], ["s_nw"])
        p.act(aneg[:], aneg[:], AF.Exp, ["s_aneg"], ["s_aneg"])
        p.ts('dve', aneg[:], aneg[:], -1.0, None, ALU.mult, None, ["s_aneg"], ["s_aneg"])
        p.barrier()
        for s in range(nseq):
            with contextlib.ExitStack() as es2:
                NT = 256
                xr = ring_sb(c, es2, "sA_x", 2, [128, KC, NT], F32)
                sq = sb(c, es2, "sA_sq", [128, KC, NT], F32)
                var = sb(c, es2, "sA_var", [128, NT], F32)
                rstd = sb(c, es2, "sA_rstd", [128, NT], F32)
                pss = ps(c, es2, "sA_pss", [128, 512])
                for gi in range(L // NT):
                    load_xn(c, s * L + gi * NT, NT, xr, sq, pss, var, rstd, g, "s_g", xn[:, :, gi * NT:(gi + 1) * NT], "s_xn", "sA_")
            p.barrier()
            with contextlib.ExitStack() as es2:
                wzr = ring_sb(c, es2, "sB_wz", 2, [128, KC, 512], BF16)
                wdt = sb(c, es2, "sB_wdt", [128, KC, 64], BF16)
                zr = ring_sb(c, es2, "sB_z", 3, [128, 512], F32)
                pzr = ring_ps(c, es2, "sB_pz", 4, [128, 512])
                pdr = ring_ps(c, es2, "sB_pd", 2, [128, 512])
                wl = WLoader(c, es2, "sB_wst", 4096)
                wl.load(wdt[:], w_in[:, 6144:6208].rearrange("(kc p) c -> p kc c", p=128), "sB_wdt")
                for blk in range(NB):
                    pd, kpd = pdr.next()
                    for kc in range(KC):
                        p.mm(pd[:, :64], xn[:, kc, blk * 128:(blk + 1) * 128], wdt[:, kc, :], kc == 0, kc == KC - 1,
                             ["s_xn", "sB_wdt"], [kpd])
                    p.tt('dve', dt[:, blk, :], pd[:, :64], dtb[:], ALU.add, [kpd, "s_dtb"], ["s_dt"])
                p.act(dt[:], dt[:], AF.Exp, ["s_dt"], ["s_dt"])
                p.act(dt[:], dt[:], AF.Ln, ["s_dt"], ["s_dt"], bias=1.0)
                for zc in range(4):
                    wz, kwz = wzr.next()
                    wl.load(wz[:], w_in[:, zc * 512:(zc + 1) * 512].rearrange("(kc p) c -> p kc c", p=128), kwz)
                    for blk in range(NB):
                        pz, kpz = pzr.next()
                        for kc in range(KC):
                            p.mm(pz[:], xn[:, kc, blk * 128:(blk + 1) * 128], wz[:, kc, :], kc == 0, kc == KC - 1,
                                 ["s_xn", kwz], [kpz])
                        z, kz = zr.next()
                        p.act(z[:], pz[:], AF.Silu, [kpz], [kz])
                        p.dma('sp', c.Zs[blk * 128:(blk + 1) * 128, zc * 512:(zc + 1) * 512], z[:], [kz], [("Zs", blk, zc)])
            p.barrier()
            with contextlib.ExitStack() as es2:
                wcr = ring_sb(c, es2, "sC_wc", 3, [128, KC, 128], BF16)
                wl = WLoader(c, es2, "sC_wst", 1024)
                raw = ring_sb(c, es2, "sC_raw", 2, [128, L + 4], F32)
                acc = ring_sb(c, es2, "sC_acc", 2, [128, L], F32)
                cvT = ring_sb(c, es2, "sC_cvT", 2, [128, L], BF16)
                BT = sb(c, es2, "sC_BT", [128, L], BF16)
                CT = sb(c, es2, "sC_CT", [128, L], BF16)
                Btok = sb(c, es2, "sC_Btok", [128, NB, 128], BF16)
                xtok = sb(c, es2, "sC_xtok", [128, NB, 256], BF16)
                xdt = sb(c, es2, "sC_xdt", [128, NB, 2, 256], BF16)
                dta = sb(c, es2, "sC_dta", [128, NB, 2, 4], F32)
                acs = sb(c, es2, "sC_acs", [128, NB, 2, 4], F32)
                tot = sb(c, es2, "sC_tot", [128, NB, 2, 4], F32)
                ea = sb(c, es2, "sC_ea", [128, NB, 2, 4], F32)
                edec = sb(c, es2, "sC_edec", [128, NB, 2, 4], F32)
                etot = sb(c, es2, "sC_etot", [128, NB, 2, 4], F32)
                Y = sb(c, es2, "sC_Y", [128, NB, 256], F32)
                H = sb(c, es2, "sC_H", [128, 256], F32)
                Hb = sb(c, es2, "sC_Hb", [128, 256], BF16)
                cbm = ring_sb(c, es2, "sC_cbm", 2, [128, 2, 128], F32)
                rhsr = ring_sb(c, es2, "sC_rhs", 2, [128, 4, 128], F32)
                decr = ring_sb(c, es2, "sC_dec", 2, [128, 4, 128], F32)
                mtr = ring_sb(c, es2, "sC_mt", 2, [128, 4, 128], BF16)
                tmpr = ring_sb(c, es2, "sC_tmp", 2, [128, 256], F32)
                xdr = ring_sb(c, es2, "sC_xd", 2, [128, 256], BF16)
                pA = ring_ps(c, es2, "sC_pA", 2, [128, 512])
                pT = ring_ps(c, es2, "sC_pT", 2, [128, 1024], BF16)
                pC = ring_ps(c, es2, "sC_pC", 1, [128, 512])
                pY = ring_ps(c, es2, "sC_pY", 2, [128, 512])
                pH = ring_ps(c, es2, "sC_pH", 1, [128, 512])
                for tl, ktl in zip(raw.tiles, ["sC_raw0", "sC_raw1"]):
                    p.memset('pool', tl[:, 0:2], 0.0, [ktl])
                    p.memset('pool', tl[:, L + 2:L + 4], 0.0, [ktl])
                for gq in range(8):
                    for ci, cc in enumerate([2 * gq, 2 * gq + 1, 16 + gq, 24 + gq]):
                        wc, kwc = wcr.next()
                        col0 = 2048 + cc * 128
                        wl.load(wc[:], w_in[:, col0:col0 + 128].rearrange("(kc p) c -> p kc c", p=128), kwc)
                        rw, krw = raw.next()
                        for tg in range(L // 512):
                            pa, kpa = pA.next()
                            for kc in range(KC):
                                p.mm(pa[:], wc[:, kc, :], xn[:, kc, tg * 512:(tg + 1) * 512], kc == 0, kc == KC - 1,
                                     [kwc, "s_xn"], [kpa])
                            p.copy('act', rw[:, 2 + tg * 512:2 + (tg + 1) * 512], pa[:], [kpa], [krw])
                        ac, kac = acc.next()
                        p.ts('dve', ac[:], rw[:, 0:L], cw[:, 0, cc:cc + 1], cb[:, cc:cc + 1], ALU.mult, ALU.add,
                             [krw, "s_cw", "s_cb"], [kac])
                        for tap in range(1, 5):
                            p.stt(ac[:], rw[:, tap:tap + L], cw[:, tap, cc:cc + 1], ac[:], ALU.mult, ALU.add,
                                  [krw, "s_cw", kac], [kac])
                        if ci < 2:
                            cv, kcv = cvT.next()
                            p.act(cv[:], ac[:], AF.Silu, [kac], [kcv])
                            for b4 in range(NB // 8):
                                pt, kpt = pT.next()
                                for q in range(8):
                                    blk = b4 * 8 + q
                                    p.tr(pt[:, q * 128:(q + 1) * 128], cv[:, blk * 128:(blk + 1) * 128], c.identb[:], [kcv], [kpt])
                                p.copy('act' if b4 % 2 else 'dve', xtok[:, b4 * 8:(b4 + 1) * 8, ci * 128:(ci + 1) * 128],
                                       pt[:].rearrange("p (a b) -> p a b", a=8), [kpt], ["sC_xtok"])
                        elif ci == 2:
                            p.act(BT[:], ac[:], AF.Silu, [kac], ["sC_BT"])
                            for b4 in range(NB // 8):
                                pt, kpt = pT.next()
                                for q in range(8):
                                    blk = b4 * 8 + q
                                    p.tr(pt[:, q * 128:(q + 1) * 128], BT[:, blk * 128:(blk + 1) * 128], c.identb[:], ["sC_BT"], [kpt])
                                p.copy('act' if b4 % 2 else 'dve', Btok[:, b4 * 8:(b4 + 1) * 8, :],
                                       pt[:].rearrange("p (a b) -> p a b", a=8), [kpt], ["sC_Btok"])
                        else:
                            p.act(CT[:], ac[:], AF.Silu, [kac], ["sC_CT"])
                    for d in range(2):
                        c0 = d * 32 + gq * 4
                        p.tt('dve', dta[:, :, d, :], dt[:, :, c0:c0 + 4], bc(aneg[:, c0:c0 + 4], [128, NB, 4], 1), ALU.mult,
                             ["s_dt", "s_aneg"], ["sC_dta"])
                        p.tt('dve', xdt[:, :, d, :].rearrange("p b (h q) -> p b h q", h=4),
                             xtok[:].rearrange("p b (h q) -> p b h q", h=4),
                             bc(dt[:, :, c0:c0 + 4], [128, NB, 4, 64], 3), ALU.mult, ["sC_xtok", "s_dt"], ["sC_xdt"])
                    pc, kpc = pC.next()
                    for d in range(2):
                        msk = c.mk["LE"] if d == 0 else c.mk["GE"]
                        for blk in range(NB):
                            p.mm(pc[:, (blk * 2 + d) * 4:(blk * 2 + d) * 4 + 4], msk[:], dta[:, blk, d, :], True, True,
                                 ["sC_dta", "mk"], [kpc])
                    p.copy('dve', acs[:].rearrange("p b d h -> p (b d h)"), pc[:, :NB * 8], [kpc], ["sC_acs"])
                    pc, kpc = pC.next()
                    p.mm(pc[:, :NB * 8], c.ones[:], dta[:].rearrange("p b d h -> p (b d h)"), True, True, ["sC_dta"], [kpc])
                    p.copy('dve', tot[:].rearrange("p b d h -> p (b d h)"), pc[:, :NB * 8], [kpc], ["sC_tot"])
                    p.act(ea[:].rearrange("p b d h -> p (b d h)"), acs[:].rearrange("p b d h -> p (b d h)"), AF.Exp, ["sC_acs"], ["sC_ea"])
                    p.act(etot[:].rearrange("p b d h -> p (b d h)"), tot[:].rearrange("p b d h -> p (b d h)"), AF.Exp, ["sC_tot"], ["sC_etot"])
                    p.tt('dve', edec[:].rearrange("p b d h -> p (b d h)"), tot[:].rearrange("p b d h -> p (b d h)"),
                         acs[:].rearrange("p b d h -> p (b d h)"), ALU.subtract, ["sC_tot", "sC_acs"], ["sC_edec"])
                    p.act(edec[:].rearrange("p b d h -> p (b d h)"), edec[:].rearrange("p b d h -> p (b d h)"), AF.Exp, ["sC_edec"], ["sC_edec"])
                    for d in range(2):
                        order = range(NB) if d == 0 else range(NB - 1, -1, -1)
                        mk_in, mk_l, mk_r = (("LE", "GT", "LE") if d == 0 else ("GE", "LT", "GE"))
                        for ci, blk in enumerate(order):
                            tsl = slice(blk * 128, (blk + 1) * 128)
                            first = ci == 0
                            pc, kpc = pC.next()
                            p.mm(pc[:, :128], BT[:, tsl], CT[:, tsl], True, True, ["sC_BT", "sC_CT"], [kpc])
                            cm_, kcm = cbm.next()
                            p.tt('dve', cm_[:, 0, :], pc[:, :128], c.mk[mk_in][:], ALU.mult, [kpc, "mk"], [kcm])
                            rh, krh = rhsr.next()
                            p.tt('dve', rh[:], bc(c.mk[mk_r][:], [128, 4, 128], 1), bc(dta[:, blk, d, :], [128, 4, 128], 2), ALU.mult,
                                 ["mk", "sC_dta"], [krh])
                            pa, kpa = pA.next()
                            p.mm(pa[:], c.mk[mk_l][:], rh[:].rearrange("p h l -> p (h l)"), True, True, [krh, "mk"], [kpa])
                            dc_, kdc = decr.next()
                            p.act(dc_[:].rearrange("p h l -> p (h l)"), pa[:], AF.Exp, [kpa], [kdc])
                            mt, kmt = mtr.next()
                            p.tt('dve', mt[:], dc_[:], bc(cm_[:, 0, :], [128, 4, 128], 1), ALU.mult, [kdc, kcm], [kmt])
                            py, kpy = pY.next()
                            for h in range(4):
                                p.mm(py[:, h * 64:(h + 1) * 64], mt[:, h, :], xdt[:, blk, d, h * 64:(h + 1) * 64], True, True,
                                     [kmt, "sC_xdt"], [kpy])
                            if not first:
                                p.mm(py[:, 256:512], CT[:, tsl], Hb[:], True, True, ["sC_CT", "sC_Hb"], [kpy])
                                tm, ktm = tmpr.next()
                                p.tt('dve', tm[:].rearrange("p (h q) -> p h q", h=4), py[:, 256:512].rearrange("p (h q) -> p h q", h=4),
                                     bc(ea[:, blk, d, :], [128, 4, 64], 2), ALU.mult, [kpy, "sC_ea"], [ktm])
                                if d == 0:
                                    p.tt('dve', Y[:, blk, :], tm[:], py[:, 0:256], ALU.add, [ktm, kpy], [("sC_Y", blk)])
                                else:
                                    p.tt('dve', tm[:], tm[:], py[:, 0:256], ALU.add, [ktm, kpy], [ktm])
                                    p.tt('dve', Y[:, blk, :], Y[:, blk, :], tm[:], ALU.add, [ktm, ("sC_Y", blk)], [("sC_Y", blk)])
                            else:
                                if d == 0:
                                    p.copy('dve', Y[:, blk, :], py[:, 0:256], [kpy], [("sC_Y", blk)])
                                else:
                                    p.tt('dve', Y[:, blk, :], Y[:, blk, :], py[:, 0:256], ALU.add, [kpy, ("sC_Y", blk)], [("sC_Y", blk)])
                            if ci < NB - 1:
                                xd, kxd = xdr.next()
                                p.tt('dve', xd[:].rearrange("p (h q) -> p h q", h=4),
                                     xdt[:, blk, d, :].rearrange("p (h q) -> p h q", h=4),
                                     bc(edec[:, blk, d, :], [128, 4, 64], 2), ALU.mult, ["sC_xdt", "sC_edec"], [kxd])
                                ph, kph = pH.next()
                                p.mm(ph[:, :256], Btok[:, blk, :], xd[:], True, True, ["sC_Btok", kxd], [kph])
                                if first:
                                    p.copy('dve', H[:], ph[:, :256], [kph], ["sC_H"])
                                else:
                                    p.tt('dve', H[:].rearrange("p (h q) -> p h q", h=4), H[:].rearrange("p (h q) -> p h q", h=4),
                                         bc(etot[:, blk, d, :], [128, 4, 64], 2), ALU.mult, ["sC_H", "sC_etot"], ["sC_H"])
                                    p.tt('dve', H[:], H[:], ph[:, :256], ALU.add, ["sC_H", kph], ["sC_H"])
                                p.copy('act', Hb[:], H[:], ["sC_H"], ["sC_Hb"])
                    for blk in range(NB):
                        tm, ktm = tmpr.next()
                        p.tt('dve', tm[:].rearrange("p (h q) -> p h q", h=4), xtok[:, blk, :].rearrange("p (h q) -> p h q", h=4),
                             bc(dsk[:, gq * 4:gq * 4 + 4], [128, 4, 64], 2), ALU.mult, ["sC_xtok", "s_dsk"], [ktm])
                        p.tt('dve', Y[:, blk, :], Y[:, blk, :], tm[:], ALU.add, [ktm, ("sC_Y", blk)], [("sC_Y", blk)])
                    p.dma('sp', c.Ys[:, gq * 256:(gq + 1) * 256].rearrange("(b p) q -> p b q", p=128), Y[:],
                          [("sC_Y", blk) for blk in range(NB)], [("Ys", gq)])
            p.barrier()
            with contextlib.ExitStack() as es2:
                wo = sb(c, es2, "sD_wo", [128, 16, D], BF16)
                yr = ring_sb(c, es2, "sD_y", 2, [128, 2048], F32)
                zr = ring_sb(c, es2, "sD_z", 2, [128, 2048], F32)
                junk = sb(c, es2, "sD_junk", [128, 2048], BF16)
                yn = ring_sb(c, es2, "sD_yn", 2, [128, 2048], BF16)
                st = ring_sb(c, es2, "sD_st", 2, [128, 4], F32)
                ynT = sb(c, es2, "sD_ynT", [128, 16, 512], BF16)
                xr = ring_sb(c, es2, "sD_x", 1, [128, KC, 512], F32)
                xo = sb(c, es2, "sD_xo", [128, KC, 512], F32)
                pT = ring_ps(c, es2, "sD_pT", 2, [128, 1024], BF16)
                pyr = ring_ps(c, es2, "sD_py", 2, [128, 512])
                wl = WLoader(c, es2, "sD_wst", 1024)
                for cc in range(16):
                    wl.load(wo[:, cc, :], w_out[cc * 128:(cc + 1) * 128, :], "sD_wo")
                for gi in range(L // 512):
                    tok0 = s * L + gi * 512
                    xi, kx = xr.next()
                    p.dma('sp', xi[:], xt_view(c, tok0, 512), xt_keys(tok0, 512), [kx])
                    for qb in range(4):
                        blk = gi * 4 + qb
                        y, ky = yr.next()
                        z, kz = zr.next()
                        p.dma('sp', y[:], c.Ys[blk * 128:(blk + 1) * 128, :], [("Ys", q) for q in range(8)], [ky])
                        p.dma('sp', z[:], c.Zs[blk * 128:(blk + 1) * 128, :], [("Zs", blk, q) for q in range(4)], [kz])
                        p.tt('dve', y[:], y[:], z[:], ALU.mult, [ky, kz], [ky])
                        s_, ks = st.next()
                        p.op('act', lambda e, y=y, s_=s_: e.activation(junk[:], y[:], AF.Square, accum_out=s_[:, 0:1]),
                             [ky], ["sD_junk", ks])
                        p.ts('dve', s_[:, 1:2], s_[:, 0:1], 1.0 / 2048, 1e-6, ALU.mult, ALU.add, [ks], [ks])
                        p.tt('pool', s_[:, 2:3], s_[:, 1:2], c.mhalf[:, 0:1], ALU.pow, [ks], [ks])
                        yn_, kyn = yn.next()
                        p.stt(yn_[:], y[:], s_[:, 2:3], nw[:], ALU.mult, ALU.mult, [ky, ks, "s_nw"], [kyn])
                        for b2 in range(2):
                            pt, kpt = pT.next()
                            for q in range(8):
                                cc = b2 * 8 + q
                                p.tr(pt[:, q * 128:(q + 1) * 128], yn_[:, cc * 128:(cc + 1) * 128], c.identb[:], [kyn], [kpt])
                            p.copy('act' if b2 else 'dve', ynT[:, b2 * 8:(b2 + 1) * 8, qb * 128:(qb + 1) * 128],
                                   pt[:].rearrange("p (a b) -> p a b", a=8), [kpt], ["sD_ynT"])
                    for dc in range(KC):
                        py, kpy = pyr.next()
                        for cc in range(16):
                            p.mm(py[:], wo[:, cc, dc * 128:(dc + 1) * 128], ynT[:, cc, :], cc == 0, cc == 15, ["sD_wo", "sD_ynT"], [kpy])
                        p.tt('dve', xo[:, dc, :], py[:], xi[:, dc, :], ALU.add, [kpy, kx], ["sD_xo"])
                    p.dma('sp', xt_view(c, tok0, 512), xo[:], ["sD_xo"], xt_keys(tok0, 512))
            p.barrier()


def conv_chunk(c, wc_ap, kwc, xn, cw, cb, cc, rw, krw, ac, kac, pA, keypre):
    p = c.p
    for tg in range(L // 512):
        pa, kpa = pA.next()
        for kc in range(KC):
            p.mm(pa[:], wc_ap[:, kc, :], xn[:, kc, tg * 512:(tg + 1) * 512], kc == 0, kc == KC - 1, [kwc, keypre + "xn"], [kpa])
        p.copy('act', rw[:, 2 + tg * 512:2 + (tg + 1) * 512], pa[:], [kpa], [krw])
    p.ts('dve', ac[:], rw[:, 0:L], cw[:, 0, cc:cc + 1], cb[:, cc:cc + 1], ALU.mult, ALU.add, [krw, keypre + "cw", keypre + "cb"], [kac])
    for tap in range(1, 5):
        p.stt(ac[:], rw[:, tap:tap + L], cw[:, tap, cc:cc + 1], ac[:], ALU.mult, ALU.add, [krw, keypre + "cw", kac], [kac])


def neumann_inverse_T(c, units, nlev):
    p = c.p
    for u in units:
        p.tr(u['pN'][:, 384:512], u['Mt'][0][:], c.ident[:], [u['k'] + "Mt0"], [u['k'] + "pN"])
        p.copy('act', u['M'][0][:], u['pN'][:, 384:512], [u['k'] + "pN"], [u['k'] + "M0"])
        p.tt('dve', u['Y'][0][:], u['Mt'][0][:], c.ident[:], ALU.add, [u['k'] + "Mt0"], [u['k'] + "Y0"])
    for k in range(nlev):
        a, b = k % 2, (k + 1) % 2
        last = k == nlev - 1
        for u in units:
            kk = u['k']
            p.mm(u['pN'][:, 0:128], u['Mt'][a][:], u['M'][a][:], True, True, [kk + f"Mt{a}", kk + f"M{a}"], [kk + "pN"])
            if not last:
                p.mm(u['pN'][:, 128:256], u['M'][a][:], u['Mt'][a][:], True, True, [kk + f"Mt{a}", kk + f"M{a}"], [kk + "pN"])
        for u in units:
            kk = u['k']
            p.copy('act', u['M'][b][:], u['pN'][:, 0:128], [kk + "pN"], [kk + f"M{b}"])
            if not last:
                p.copy('dve', u['Mt'][b][:], u['pN'][:, 128:256], [kk + "pN"], [kk + f"Mt{b}"])
        for u in units:
            kk = u['k']
            p.mm(u['pN'][:, 256:384], u['M'][b][:], u['Y'][a][:], True, True, [kk + f"M{b}", kk + f"Y{a}"], [kk + "pN"])
        for u in units:
            kk = u['k']
            if last:
                p.tt('dve', u['XT'][:], u['Y'][a][:], u['pN'][:, 256:384], ALU.add, [kk + f"Y{a}", kk + "pN"], [kk + "XT"])
            else:
                p.tt('dve', u['Y'][b][:], u['Y'][a][:], u['pN'][:, 256:384], ALU.add, [kk + f"Y{a}", kk + "pN"], [kk + f"Y{b}"])


def stage_gdn(c, layer, j, nseq):
    p = c.p
    W = c.W
    w_in, conv_w, conv_b = W['gdn_w_in'][j], W['gdn_conv_w'][j], W['gdn_conv_b'][j]
    a_log, dt_bias, norm_w, w_out = W['gdn_a_log'][j], W['gdn_dt_bias'][j], W['gdn_norm'][j], W['gdn_w_out'][j]
    gvec = W['mix_norm'][layer]
    NB = L // 128
    with contextlib.ExitStack() as es:
        g = sb(c, es, "g_g", [128, KC], F32)
        cw = sb(c, es, "g_cw", [128, 5, 32], F32)
        cb = sb(c, es, "g_cb", [128, 32], F32)
        dtb = sb(c, es, "g_dtb", [128, 32], F32)
        aneg = sb(c, es, "g_aneg", [128, 32], F32)
        nw = sb(c, es, "g_nw", [128, 128], F32)
        xn = sb(c, es, "g_xn", [128, KC, L], BF16)
        bga = sb(c, es, "g_bga", [128, NB, 48], F32)
        load_vec(c, g[:], gvec, "g_g")
        for tap in range(5):
            p.dma('sp', cw[:, tap, :], conv_w[tap].rearrange("(cc p) -> p cc", p=128), [], ["g_cw"], allow_slow_non_contiguous=True)
        p.dma('sp', cb[:], conv_b.rearrange("(cc p) -> p cc", p=128), [], ["g_cb"], allow_slow_non_contiguous=True)
        p.dma('sp', dtb[:], dt_bias.rearrange("a b -> (a b)").partition_broadcast(128), [], ["g_dtb"])
        p.dma('sp', aneg[:], a_log.rearrange("a b -> (a b)").partition_broadcast(128), [], ["g_aneg"])
        p.dma('sp', nw[:], norm_w.partition_broadcast(128), [], ["g_nw"])
        p.act(aneg[:], aneg[:], AF.Exp, ["g_aneg"], ["g_aneg"])
        p.ts('dve', aneg[:], aneg[:], -1.0, None, ALU.mult, None, ["g_aneg"], ["g_aneg"])
        p.barrier()
        for s in range(nseq):
            with contextlib.ExitStack() as es2:
                NT = 256
                xr = ring_sb(c, es2, "gA_x", 2, [128, KC, NT], F32)
                sq = sb(c, es2, "gA_sq", [128, KC, NT], F32)
                var = sb(c, es2, "gA_var", [128, NT], F32)
                rstd = sb(c, es2, "gA_rstd", [128, NT], F32)
                pss = ps(c, es2, "gA_pss", [128, 512])
                for gi in range(L // NT):
                    load_xn(c, s * L + gi * NT, NT, xr, sq, pss, var, rstd, g, "g_g", xn[:, :, gi * NT:(gi + 1) * NT], "g_xn", "gA_")
            p.barrier()
            with contextlib.ExitStack() as es2:
                wzr = ring_sb(c, es2, "gB_wz", 2, [128, KC, 512], BF16)
                wdt = sb(c, es2, "gB_wdt", [128, KC, 48], BF16)
                zr = ring_sb(c, es2, "gB_z", 3, [128, 512], F32)
                pzr = ring_ps(c, es2, "gB_pz", 4, [128, 512])
                pdr = ring_ps(c, es2, "gB_pd", 2, [128, 512])
                wl = WLoader(c, es2, "gB_wst", 4096)
                wl.load(wdt[:], w_in[:, 6144:6192].rearrange("(kc p) c -> p kc c", p=128), "gB_wdt")
                for blk in range(NB):
                    pd, kpd = pdr.next()
                    for kc in range(KC):
                        p.mm(pd[:, :48], xn[:, kc, blk * 128:(blk + 1) * 128], wdt[:, kc, :], kc == 0, kc == KC - 1,
                             ["g_xn", "gB_wdt"], [kpd])
                    p.copy('dve', bga[:, blk, 0:16], pd[:, 0:16], [kpd], ["g_bga"])
                    p.tt('dve', bga[:, blk, 16:48], pd[:, 16:48], dtb[:], ALU.add, [kpd, "g_dtb"], ["g_bga"])
                p.act(bga[:, :, 0:16], bga[:, :, 0:16], AF.Exp, ["g_bga"], ["g_bga"], scale=-1.0)
                p.ts('dve', bga[:, :, 0:16], bga[:, :, 0:16], 1.0, None, ALU.add, None, ["g_bga"], ["g_bga"])
                p.op('dve', lambda e: e.reciprocal(bga[:, :, 0:16], bga[:, :, 0:16]), ["g_bga"], ["g_bga"])
                p.act(bga[:, :, 16:48], bga[:, :, 16:48], AF.Exp, ["g_bga"], ["g_bga"])
                p.act(bga[:, :, 16:48], bga[:, :, 16:48], AF.Ln, ["g_bga"], ["g_bga"], bias=1.0)
                p.tt('dve', bga[:, :, 16:48], bga[:, :, 16:48], bc(aneg[:], [128, NB, 32], 1), ALU.mult, ["g_bga", "g_aneg"], ["g_bga"])
                for zc in range(4):
                    wz, kwz = wzr.next()
                    wl.load(wz[:], w_in[:, 4096 + zc * 512:4096 + (zc + 1) * 512].rearrange("(kc p) c -> p kc c", p=128), kwz)
                    for blk in range(NB):
                        pz, kpz = pzr.next()
                        for kc in range(KC):
                            p.mm(pz[:], xn[:, kc, blk * 128:(blk + 1) * 128], wz[:, kc, :], kc == 0, kc == KC - 1,
                                 ["g_xn", kwz], [kpz])
                        z, kz = zr.next()
                        p.act(z[:], pz[:], AF.Silu, [kpz], [kz])
                        p.dma('sp', c.Zs[blk * 128:(blk + 1) * 128, zc * 512:(zc + 1) * 512], z[:], [kz], [("Zs", blk, zc)])
            p.barrier()
            with contextlib.ExitStack() as es2:
                wcr = ring_sb(c, es2, "gC_wc", 3, [128, KC, 128], BF16)
                wl = WLoader(c, es2, "gC_wst", 1024)
                raw = ring_sb(c, es2, "gC_raw", 2, [128, L + 4], F32)
                acc = ring_sb(c, es2, "gC_acc", 2, [128, L], F32)
                t32a = sb(c, es2, "gC_t32a", [128, L], F32)
                t32b = sb(c, es2, "gC_t32b", [128, L], F32)
                cvT = ring_sb(c, es2, "gC_cvT", 2, [128, L], BF16)
                QhT = sb(c, es2, "gC_QhT", [128, L], BF16)
                KhT = sb(c, es2, "gC_KhT", [128, L], BF16)
                Ktok = sb(c, es2, "gC_Ktok", [128, NB, 128], BF16)
                Vtok = sb(c, es2, "gC_Vtok", [128, NB, 256], BF16)
                KKT = sb(c, es2, "gC_KKT", [128, NB, 128], F32)
                QKT = sb(c, es2, "gC_QKT", [128, NB, 128], F32)
                gq = sb(c, es2, "gC_gq", [128, NB, 2, 2], F32)
                G = sb(c, es2, "gC_G", [128, NB, 2, 2], F32)
                tot = sb(c, es2, "gC_tot", [128, NB, 2, 2], F32)
                eG = sb(c, es2, "gC_eG", [128, NB, 2, 2], F32)
                neG = sb(c, es2, "gC_neG", [128, NB, 2, 2], F32)
                edec = sb(c, es2, "gC_edec", [128, NB, 2, 2], F32)
                etot = sb(c, es2, "gC_etot", [128, NB, 2, 2], F32)
                nbeta = sb(c, es2, "gC_nbeta", [128, NB, 2], F32)
                O = sb(c, es2, "gC_O", [128, NB, 256], F32)
                units = []
                for u in range(4):
                    ud = {'k': f"gU{u}_", 'd': u // 2, 'e': u % 2}
                    ud['M'] = [sb(c, es2, f"gC_M{u}{i}", [128, 128], F32) for i in range(2)]
                    ud['Mt'] = [sb(c, es2, f"gC_Mt{u}{i}", [128, 128], F32) for i in range(2)]
                    ud['Y'] = [sb(c, es2, f"gC_Y{u}{i}", [128, 128], F32) for i in range(2)]
                    ud['XT'] = sb(c, es2, f"gC_XT{u}", [128, 128], BF16)
                    ud['S'] = sb(c, es2, f"gC_S{u}", [128, 128], F32)
                    ud['Sb'] = sb(c, es2, f"gC_Sb{u}", [128, 128], BF16)
                    ud['AT'] = sb(c, es2, f"gC_AT{u}", [128, 128], BF16)
                    ud['R'] = sb(c, es2, f"gC_R{u}", [128, 128], BF16)
                    ud['Vn'] = sb(c, es2, f"gC_Vn{u}", [128, 128], BF16)
                    ud['Kd'] = sb(c, es2, f"gC_Kd{u}", [128, 128], BF16)
                    ud['tmp'] = sb(c, es2, f"gC_tmp{u}", [128, 128], F32)
                    ud['pN'] = ps(c, es2, f"gC_pN{u}", [128, 512])
                    PSUM_KEYS.add(ud['k'] + "pN")
                    units.append(ud)
                rhsd = [sb(c, es2, f"gC_rhs{d}", [128, 2, 128], F32) for d in range(2)]
                decd = [sb(c, es2, f"gC_dec{d}", [128, 2, 128], F32) for d in range(2)]
                decm = [sb(c, es2, f"gC_decm{d}", [128, 2, 128], F32) for d in range(2)]
                decs = [sb(c, es2, f"gC_decs{d}", [128, 2, 128], F32) for d in range(2)]
                pSeg = ps(c, es2, "gC_pSeg", [128, 512])
                PSUM_KEYS.add("gC_pSeg")
                PSUM_KEYS.add("gC_pM0")
                pA = ring_ps(c, es2, "gC_pA", 2, [128, 512])
                pT = ps(c, es2, "gC_pT", [128, 1024], BF16) if False else None
                for tl, ktl in zip(raw.tiles, ["gC_raw0", "gC_raw1"]):
                    p.memset('pool', tl[:, 0:2], 0.0, [ktl])
                    p.memset('pool', tl[:, L + 2:L + 4], 0.0, [ktl])
                slot_i = [0]

                pM = ring_ps(c, es2, "gC_pM", 1, [128, 512])

                def slot():
                    i = slot_i[0] % 3
                    slot_i[0] += 1
                    if i < 2:
                        return pA.tiles[i][:, 0:128], pA.name + str(i)
                    return pM.tiles[0][:, 0:128], "gC_pM0"

                def tslot():
                    return slot()

                for hk in range(8):
                    for ci, cc in enumerate([hk, 8 + hk, 16 + 2 * hk, 17 + 2 * hk]):
                        wc, kwc = wcr.next()
                        wl.load(wc[:], w_in[:, cc * 128:(cc + 1) * 128].rearrange("(kc p) c -> p kc c", p=128), kwc)
                        rw, krw = raw.next()
                        ac, kac = acc.next()
                        conv_chunk(c, wc, kwc, xn, cw, cb, cc, rw, krw, ac, kac, pA, "g_")
                        if ci < 2:
                            p.act(t32a[:], ac[:], AF.Silu, [kac], ["gC_t32a"])
                            p.act(t32b[:], t32a[:], AF.Square, ["gC_t32a"], ["gC_t32b"])
                            for tg in range(L // 512):
                                pa, kpa = pA.next()
                                p.mm(pa[:], c.ones[:], t32b[:, tg * 512:(tg + 1) * 512], True, True, ["gC_t32b"], [kpa])
                                p.ts('dve', ac[:, tg * 512:(tg + 1) * 512], pa[:], 1e-6, None, ALU.add, None, [kpa], [kac])
                            p.act(ac[:], ac[:], AF.Ln, [kac], [kac])
                            p.act(ac[:], ac[:], AF.Exp, [kac], [kac], scale=-0.5)
                            dstT, kd = (QhT, "gC_QhT") if ci == 0 else (KhT, "gC_KhT")
                            scl = 128.0 ** -0.5 if ci == 0 else 1.0
                            p.stt(dstT[:], t32a[:], scl, ac[:], ALU.mult, ALU.mult, ["gC_t32a", kac], [kd])
                            if ci == 1:
                                for blk in range(NB):
                                    sl, ksl = slot()
                                    p.mm(sl, KhT[:, blk * 128:(blk + 1) * 128], c.identb[:], True, True, ["gC_KhT"], [ksl])
                                    p.copy('act' if blk % 2 else 'dve', Ktok[:, blk, :], sl, [ksl], ["gC_Ktok"])
                        else:
                            cv, kcv = cvT.next()
                            p.act(cv[:], ac[:], AF.Silu, [kac], [kcv])
                            for blk in range(NB):
                                sl, ksl = slot()
                                p.mm(sl, cv[:, blk * 128:(blk + 1) * 128], c.identb[:], True, True, [kcv], [ksl])
                                p.copy('act' if blk % 2 else 'dve', Vtok[:, blk, (ci - 2) * 128:(ci - 1) * 128], sl, [ksl], ["gC_Vtok"])
                    for d in range(2):
                        p.copy('dve', gq[:, :, d, :], bga[:, :, 16 + d * 16 + 2 * hk:16 + d * 16 + 2 * hk + 2], ["g_bga"], ["gC_gq"])
                    p.ts('dve', nbeta[:], bga[:, :, 2 * hk:2 * hk + 2], -1.0, None, ALU.mult, None, ["g_bga"], ["gC_nbeta"])
                    pa, kpa = pA.next()
                    for d in range(2):
                        msk = c.mk["LE"] if d == 0 else c.mk["GE"]
                        for blk in range(NB):
                            p.mm(pa[:, (blk * 2 + d) * 2:(blk * 2 + d) * 2 + 2], msk[:], gq[:, blk, d, :], True, True, ["gC_gq", "mk"], [kpa])
                    p.copy('dve', G[:].rearrange("p b d h -> p (b d h)"), pa[:, :NB * 4], [kpa], ["gC_G"])
                    pa, kpa = pA.next()
                    p.mm(pa[:, :NB * 4], c.ones[:], gq[:].rearrange("p b d h -> p (b d h)"), True, True, ["gC_gq"], [kpa])
                    p.copy('dve', tot[:].rearrange("p b d h -> p (b d h)"), pa[:, :NB * 4], [kpa], ["gC_tot"])
                    fl = "p b d h -> p (b d h)"
                    p.act(eG[:].rearrange(fl), G[:].rearrange(fl), AF.Exp, ["gC_G"], ["gC_eG"])
                    p.ts('dve', neG[:].rearrange(fl), eG[:].rearrange(fl), -1.0, None, ALU.mult, None, ["gC_eG"], ["gC_neG"])
                    p.act(etot[:].rearrange(fl), tot[:].rearrange(fl), AF.Exp, ["gC_tot"], ["gC_etot"])
                    p.tt('dve', edec[:].rearrange(fl), tot[:].rearrange(fl), G[:].rearrange(fl), ALU.subtract, ["gC_tot", "gC_G"], ["gC_edec"])
                    p.act(edec[:].rearrange(fl), edec[:].rearrange(fl), AF.Exp, ["gC_edec"], ["gC_edec"])
                    for blk in range(NB):
                        tsl = slice(blk * 128, (blk + 1) * 128)
                        sl, ksl = slot()
                        p.mm(sl, KhT[:, tsl], KhT[:, tsl], True, True, ["gC_KhT"], [ksl])
                        p.copy('act', KKT[:, blk, :], sl, [ksl], ["gC_KKT"])
                        sl, ksl = slot()
                        p.mm(sl, KhT[:, tsl], QhT[:, tsl], True, True, ["gC_KhT", "gC_QhT"], [ksl])
                        p.copy('dve', QKT[:, blk, :], sl, [ksl], ["gC_QKT"])
                    for step in range(NB):
                        first = step == 0
                        lastc = step == NB - 1
                        blks = [step, NB - 1 - step]
                        for d in range(2):
                            blk = blks[d]
                            mk_l, mk_r, mk_i, mk_s = (("GT", "LE", "LE", "LT") if d == 0 else ("LT", "GE", "GE", "GT"))
                            p.tt('dve', rhsd[d][:], bc(c.mk[mk_r][:], [128, 2, 128], 1), bc(gq[:, blk, d, :], [128, 2, 128], 2), ALU.mult,
                                 ["mk", "gC_gq"], [f"gC_rhs{d}"])
                            p.mm(pSeg[:, d * 256:(d + 1) * 256], c.mk[mk_l][:], rhsd[d][:].rearrange("p h l -> p (h l)"), True, True,
                                 [f"gC_rhs{d}", "mk"], ["gC_pSeg"])
                            p.act(decd[d][:].rearrange("p h l -> p (h l)"), pSeg[:, d * 256:(d + 1) * 256], AF.Exp, ["gC_pSeg"], [f"gC_dec{d}"])
                            p.tt('dve', decm[d][:], decd[d][:], bc(c.mk[mk_i][:], [128, 2, 128], 1), ALU.mult, [f"gC_dec{d}", "mk"], [f"gC_decm{d}"])
                            p.tt('dve', decs[d][:], decd[d][:], bc(c.mk[mk_s][:], [128, 2, 128], 1), ALU.mult, [f"gC_dec{d}", "mk"], [f"gC_decs{d}"])
                        for u in units:
                            d, e, kk = u['d'], u['e'], u['k']
                            blk = blks[d]
                            p.tt('dve', u['AT'][:], QKT[:, blk, :], decm[d][:, e, :], ALU.mult, ["gC_QKT", f"gC_decm{d}"], [kk + "AT"])
                            p.stt(u['Mt'][0][:], KKT[:, blk, :], nbeta[:, blk, e:e + 1], decs[d][:, e, :], ALU.mult, ALU.mult,
                                  ["gC_KKT", "gC_nbeta", f"gC_decs{d}"], [kk + "Mt0"])
                        neumann_inverse_T(c, units, 6)
                        sls = {}
                        for u in units:
                            d, e, kk = u['d'], u['e'], u['k']
                            blk = blks[d]
                            tsl = slice(blk * 128, (blk + 1) * 128)
                            vt = Vtok[:, blk, e * 128:(e + 1) * 128]
                            if not first:
                                sl, ksl = slot()
                                p.mm(sl, KhT[:, tsl], u['Sb'][:], True, True, ["gC_KhT", kk + "Sb"], [ksl])
                                p.stt(u['R'][:], sl, neG[:, blk, d, e:e + 1], vt, ALU.mult, ALU.add, [ksl, "gC_neG", "gC_Vtok"], [kk + "R"])
                                rr = u['R'][:]
                                krr = kk + "R"
                            else:
                                rr = vt
                                krr = "gC_Vtok"
                            sl, ksl = slot()
                            p.mm(sl, u['XT'][:], rr, True, True, [kk + "XT", krr], [ksl])
                            p.ts('dve', u['Vn'][:], sl, bga[:, blk, 2 * hk + e:2 * hk + e + 1], None, ALU.mult, None, [ksl, "g_bga"], [kk + "Vn"])
                        for u in units:
                            d, e, kk = u['d'], u['e'], u['k']
                            blk = blks[d]
                            tsl = slice(blk * 128, (blk + 1) * 128)
                            okey = ("gC_O", blk, e)
                            sl2, ksl2 = slot()
                            p.mm(sl2, u['AT'][:], u['Vn'][:], True, True, [kk + "AT", kk + "Vn"], [ksl2])
                            oap = O[:, blk, e * 128:(e + 1) * 128]
                            if not first:
                                sl, ksl = slot()
                                p.mm(sl, QhT[:, tsl], u['Sb'][:], True, True, ["gC_QhT", kk + "Sb"], [ksl])
                                p.ts('dve', u['tmp'][:], sl, eG[:, blk, d, e:e + 1], None, ALU.mult, None, [ksl, "gC_eG"], [kk + "tmp"])
                                p.tt('dve', u['tmp'][:], u['tmp'][:], sl2, ALU.add, [kk + "tmp", ksl2], [kk + "tmp"])
                                src, ksrc = u['tmp'][:], kk + "tmp"
                            else:
                                src, ksrc = sl2, ksl2
                            first_visit = (d == 0 and blk < NB // 2) or (d == 1 and blk >= NB // 2)
                            if first_visit:
                                p.copy('dve', oap, src, [ksrc], [okey]) if not first else p.copy('dve', oap, src, [ksrc], [okey])
                            else:
                                p.tt('dve', oap, oap, src, ALU.add, [ksrc, okey], [okey]) if not first else p.tt('dve', oap, oap, src, ALU.add, [ksrc, okey], [okey])
                            if not lastc:
                                p.ts('dve', u['Kd'][:], Ktok[:, blk, :], edec[:, blk, d, e:e + 1], None, ALU.mult, None, ["gC_Ktok", "gC_edec"], [kk + "Kd"])
                                sl, ksl = slot()
                                p.mm(sl, u['Kd'][:], u['Vn'][:], True, True, [kk + "Kd", kk + "Vn"], [ksl])
                                if first:
                                    p.copy('dve', u['S'][:], sl, [ksl], [kk + "S"])
                                else:
                                    p.stt(u['S'][:], u['S'][:], etot[:, blk, d, e:e + 1], sl, ALU.mult, ALU.add, [kk + "S", "gC_etot", ksl], [kk + "S"])
                                p.copy('act', u['Sb'][:], u['S'][:], [kk + "S"], [kk + "Sb"])
                    p.dma('sp', c.Ys[:, hk * 256:(hk + 1) * 256].rearrange("(b p) q -> p b q", p=128), O[:],
                          [("gC_O", blk, e) for blk in range(NB) for e in range(2)], [("Ys", hk)])
            p.barrier()
            with contextlib.ExitStack() as es2:
                wo = sb(c, es2, "gD_wo", [128, 16, D], BF16)
                yr = ring_sb(c, es2, "gD_y", 2, [128, 2048], F32)
                zr = ring_sb(c, es2, "gD_z", 2, [128, 2048], F32)
                y2 = sb(c, es2, "gD_y2", [128, 2048], F32)
                yn = ring_sb(c, es2, "gD_yn", 2, [128, 2048], BF16)
                st = ring_sb(c, es2, "gD_st", 2, [128, 16], F32)
                ynT = sb(c, es2, "gD_ynT", [128, 16, 512], BF16)
                xr = ring_sb(c, es2, "gD_x", 1, [128, KC, 512], F32)
                xo = sb(c, es2, "gD_xo", [128, KC, 512], F32)
                pT = ring_ps(c, es2, "gD_pT", 2, [128, 1024], BF16)
                pyr = ring_ps(c, es2, "gD_py", 2, [128, 512])
                wl = WLoader(c, es2, "gD_wst", 1024)
                for cc in range(16):
                    wl.load(wo[:, cc, :], w_out[cc * 128:(cc + 1) * 128, :], "gD_wo")
                for gi in range(L // 512):
                    tok0 = s * L + gi * 512
                    xi, kx = xr.next()
                    p.dma('sp', xi[:], xt_view(c, tok0, 512), xt_keys(tok0, 512), [kx])
                    for qb in range(4):
                        blk = gi * 4 + qb
                        y, ky = yr.next()
                        z, kz = zr.next()
                        p.dma('sp', y[:], c.Ys[blk * 128:(blk + 1) * 128, :], [("Ys", q) for q in range(8)], [ky])
                        p.dma('sp', z[:], c.Zs[blk * 128:(blk + 1) * 128, :], [("Zs", blk, q) for q in range(4)], [kz])
                        p.act(y2[:], y[:], AF.Square, [ky], ["gD_y2"])
                        s_, ks = st.next()
                        p.op('dve', lambda e, s_=s_: e.reduce_sum(s_[:], y2[:].rearrange("p (h v) -> p h v", h=16), AX.X), ["gD_y2"], [ks])
                        p.ts('dve', s_[:], s_[:], 1.0 / 128, 1e-6, ALU.mult, ALU.add, [ks], [ks])
                        p.tt('pool', s_[:], s_[:], c.mhalf[:, 0:16], ALU.pow, [ks], [ks])
                        h3 = "p (h v) -> p h v"
                        p.tt('dve', y[:].rearrange(h3, h=16), y[:].rearrange(h3, h=16), bc(s_[:], [128, 16, 128], 2), ALU.mult, [ky, ks], [ky])
                        p.tt('dve', z[:].rearrange(h3, h=16), z[:].rearrange(h3, h=16), bc(nw[:], [128, 16, 128], 1), ALU.mult, [kz, "g_nw"], [kz])
                        yn_, kyn = yn.next()
                        p.tt('dve', yn_[:], y[:], z[:], ALU.mult, [ky, kz], [kyn])
                        for b2 in range(2):
                            pt, kpt = pT.next()
                            for q in range(8):
                                cc = b2 * 8 + q
                                p.tr(pt[:, q * 128:(q + 1) * 128], yn_[:, cc * 128:(cc + 1) * 128], c.identb[:], [kyn], [kpt])
                            p.copy('act' if b2 else 'dve', ynT[:, b2 * 8:(b2 + 1) * 8, qb * 128:(qb + 1) * 128],
                                   pt[:].rearrange("p (a b) -> p a b", a=8), [kpt], ["gD_ynT"])
                    for dc in range(KC):
                        py, kpy = pyr.next()
                        for cc in range(16):
                            p.mm(py[:], wo[:, cc, dc * 128:(dc + 1) * 128], ynT[:, cc, :], cc == 0, cc == 15, ["gD_wo", "gD_ynT"], [kpy])
                        p.tt('dve', xo[:, dc, :], py[:], xi[:, dc, :], ALU.add, [kpy, kx], ["gD_xo"])
                    p.dma('sp', xt_view(c, tok0, 512), xo[:], ["gD_xo"], xt_keys(tok0, 512))
            p.barrier()


def stage_rwkv(c, layer, j, nseq):
    p = c.p
    W = c.W
    gvec = W['mix_norm'][layer]
    x_mu, w_rkv, w0, w1, w2 = W['rwkv_x_mu'][j], W['rwkv_w_rkv'][j], W['rwkv_w0'][j], W['rwkv_w1'][j], W['rwkv_w2'][j]
    a0, a1, a2, g1, g2 = W['rwkv_a0'][j], W['rwkv_a1'][j], W['rwkv_a2'][j], W['rwkv_g1'][j], W['rwkv_g2'][j]
    k_k, k_a, r_k, lnx_w, lnx_b, w_out = (W['rwkv_k_k'][j], W['rwkv_k_a'][j], W['rwkv_r_k'][j], W['rwkv_lnx_w'][j],
                                            W['rwkv_lnx_b'][j], W['rwkv_w_out'][j])
    CH = 64
    NCH = L // CH
    RW = c.RW
    with contextlib.ExitStack() as es:
        g = sb(c, es, "r_g", [128, KC], F32)
        mu = sb(c, es, "r_mu", [128, 6, KC], F32)
        nw0 = sb(c, es, "r_nw0", [128, 2, KC], F32)
        na0 = sb(c, es, "r_na0", [128, KC], F32)
        kkv = sb(c, es, "r_kk", [128, KC], F32)
        kav = sb(c, es, "r_ka", [128, KC], F32)
        omka = sb(c, es, "r_omka", [128, KC], F32)
        rkv = sb(c, es, "r_rk", [128, KC], F32)
        lnw = sb(c, es, "r_lnw", [128, KC], F32)
        lnb = sb(c, es, "r_lnb", [128, KC], F32)
        onesbd = sb(c, es, "r_onesbd", [128, 128], F32)
        mS = [sb(c, es, f"r_mS{d}", [128, 64], F32) for d in range(2)]
        mSn = [sb(c, es, f"r_mSn{d}", [128, 64], F32) for d in range(2)]
        mI = [sb(c, es, f"r_mI{d}", [128, 64], F32) for d in range(2)]
        load_vec(c, g[:], gvec, "r_c")
        for s_ in range(6):
            load_vec(c, mu[:, s_, :], x_mu[s_], "r_c")
        for d in range(2):
            load_vec(c, nw0[:, d, :], w0[d], "r_c")
        for t_, src in [(na0, a0), (kkv, k_k), (kav, k_a), (rkv, r_k), (lnw, lnx_w), (lnb, lnx_b)]:
            load_vec(c, t_[:], src, "r_c")
        p.ts('dve', nw0[:], nw0[:], -1.0, None, ALU.mult, None, ["r_c"], ["r_c"])
        p.ts('dve', na0[:], na0[:], -1.0, None, ALU.mult, None, ["r_c"], ["r_c"])
        p.ts('dve', omka[:], kav[:], -1.0, 1.0, ALU.mult, ALU.add, ["r_c"], ["r_c"])
        p.memset('pool', onesbd[:], 0.0, ["r_c"])
        p.memset('pool', onesbd[0:64, 0:64], 1.0, ["r_c"])
        p.memset('pool', onesbd[64:128, 64:128], 1.0, ["r_c"])
        for d in range(2):
            ns, ni = (("LT", "LE") if d == 0 else ("GT", "GE"))
            for hs in range(2):
                sl_ = slice(hs * 64, (hs + 1) * 64)
                p.copy('pool', mS[d][sl_, :], c.mk[ns][sl_, hs * 64:(hs + 1) * 64], ["mk"], ["r_c"])
                p.copy('pool', mI[d][sl_, :], c.mk[ni][sl_, hs * 64:(hs + 1) * 64], ["mk"], ["r_c"])
            p.ts('dve', mSn[d][:], mS[d][:], -1.0, None, ALU.mult, None, ["r_c"], ["r_c"])
        p.barrier()
        for s in range(nseq):
            with contextlib.ExitStack() as es2:
                NT = 256
                u = sb(c, es2, "rA_u", [128, KC, L + 2], F32)
                xm = sb(c, es2, "rA_xm", [128, KC, L], BF16)
                pss = ps(c, es2, "rA_pss", [128, 512])
                pA = ring_ps(c, es2, "rA_pA", 3, [128, 512])
                with contextlib.ExitStack() as es3:
                    xr = ring_sb(c, es3, "rA_x", 2, [128, KC, NT], F32)
                    sq = sb(c, es3, "rA_sq", [128, KC, NT], F32)
                    var = sb(c, es3, "rA_var", [128, NT], F32)
                    rstd = sb(c, es3, "rA_rstd", [128, NT], F32)
                    p.memset('pool', u[:, :, 0:1], 0.0, ["rA_u"])
                    p.memset('pool', u[:, :, L + 1:L + 2], 0.0, ["rA_u"])
                    for gi in range(L // NT):
                        t0 = gi * NT
                        xi, kx = xr.next()
                        p.dma('sp', xi[:], xt_view(c, s * L + t0, NT), xt_keys(s * L + t0, NT), [kx])
                        rms_stats(c, xi, kx, sq, "rA_sq", pss, "rA_pss", var, "rA_var", rstd, "rA_rstd", NT)
                        for kc in range(KC):
                            p.stt(u[:, kc, 1 + t0:1 + t0 + NT], xi[:, kc, :], g[:, kc:kc + 1], rstd[:], ALU.mult, ALU.mult,
                                  [kx, "r_c", "rA_rstd"], ["rA_u"])

                p.barrier()
                t1 = sb(c, es2, "rA_t1", [128, L], F32)
                t2 = sb(c, es2, "rA_t2", [128, L], F32)
                ost = ring_sb(c, es2, "rA_ost", 2, [128, L], F32)
                wt = sb(c, es2, "rA_wt", [128, KC, 1024], BF16)
                wl1 = sb(c, es2, "rA_wl1", [128, KC, 128], BF16)
                wl2 = sb(c, es2, "rA_wl2", [128, 1024], BF16)
                lT = sb(c, es2, "rA_lT", [128, L], BF16)
                wl = WLoader(c, es2, "rA_wst", 1024)

                def build_xm(si):
                    for kc in range(KC):
                        p.tt('dve', t1[:], u[:, kc, 0:L], u[:, kc, 2:L + 2], ALU.add, ["rA_u"], ["rA_t1"])
                        p.stt(t2[:], t1[:], 0.5, u[:, kc, 1:L + 1], ALU.mult, ALU.subtract, ["rA_t1", "rA_u"], ["rA_t2"])
                        p.stt(xm[:, kc, :], t2[:], mu[:, si, kc:kc + 1], u[:, kc, 1:L + 1], ALU.mult, ALU.add,
                              ["rA_t2", "r_c", "rA_u"], ["rA_xm"])

                def sigmoid_from(o, pa, bias_ap, scale_out, ko, kpa):
                    if bias_ap is not None:
                        p.op('act', lambda e: e.activation(o, pa, AF.Exp, bias=bias_ap, scale=-1.0), [kpa, "r_c"], [ko])
                    else:
                        p.op('act', lambda e: e.activation(o, pa, AF.Exp, scale=-1.0), [kpa], [ko])
                    p.ts('dve', o, o, 1.0, None, ALU.add, None, [ko], [ko])
                    p.op('dve', lambda e: e.reciprocal(o, o), [ko], [ko])
                    if scale_out != 1.0:
                        p.ts('dve', o, o, scale_out, None, ALU.mult, None, [ko], [ko])

                for si in range(3):
                    build_xm(si)
                    for kc in range(KC):
                        wl.load(wt[:, kc, :], w_rkv[si][kc * 128:(kc + 1) * 128, :], "rA_wt")
                    for oc in range(KC):
                        o_, ko = ost.next()
                        for tg in range(L // 512):
                            pa, kpa = pA.next()
                            for kc in range(KC):
                                p.mm(pa[:], wt[:, kc, oc * 128:(oc + 1) * 128], xm[:, kc, tg * 512:(tg + 1) * 512], kc == 0, kc == KC - 1,
                                     ["rA_wt", "rA_xm"], [kpa])
                            p.copy('act' if tg % 2 else 'dve', o_[:, tg * 512:(tg + 1) * 512], pa[:], [kpa], [ko])
                        p.dma('sp', RW[si, oc * 128:(oc + 1) * 128, :], o_[:], [ko], [("RW", si, oc)])
                for si, nm in [(3, 'w0'), (3, 'w1'), (4, 'a'), (5, 'g')]:
                    if nm in ('w0', 'a', 'g'):
                        build_xm(si)
                    if nm[0] == 'w':
                        d = int(nm[1])
                        l1, l2, rank, dsti, bias_t = w1[d], w2[d], 64, 5 + d, nw0[:, d, :]
                    elif nm == 'a':
                        l1, l2, rank, dsti, bias_t = a1, a2, 64, 3, na0
                    else:
                        l1, l2, rank, dsti, bias_t = g1, g2, 128, 4, None
                    wl.load(wl1[:, :, :rank], l1.rearrange("(kc p) r -> p kc r", p=128), "rA_wl1")
                    wl.load(wl2[:rank, :], l2, "rA_wl2")
                    for tg in range(L // 512):
                        pa, kpa = pA.next()
                        for kc in range(KC):
                            p.mm(pa[:rank, :], wl1[:, kc, :rank], xm[:, kc, tg * 512:(tg + 1) * 512], kc == 0, kc == KC - 1,
                                 ["rA_wl1", "rA_xm"], [kpa])
                        dst_ = lT[:rank, tg * 512:(tg + 1) * 512]
                        if nm[0] == 'w':
                            p.act(dst_, pa[:rank, :], AF.Tanh, [kpa], ["rA_lT"])
                        elif nm == 'a':
                            p.copy('act', dst_, pa[:rank, :], [kpa], ["rA_lT"])
                        else:
                            p.op('act', lambda e, dst_=dst_, pa=pa: e.activation(t1[:, :512], pa[:, :], AF.Exp, scale=-1.0), [kpa], ["rA_t1"])
                            p.ts('dve', t1[:, :512], t1[:, :512], 1.0, None, ALU.add, None, ["rA_t1"], ["rA_t1"])
                            p.op('dve', lambda e: e.reciprocal(t1[:, :512], t1[:, :512]), ["rA_t1"], ["rA_t1"])
                            p.copy('dve', dst_, t1[:, :512], ["rA_t1"], ["rA_lT"])
                    for oc in range(KC):
                        o_, ko = ost.next()
                        for tg in range(L // 512):
                            pa, kpa = pA.next()
                            p.mm(pa[:], wl2[:rank, oc * 128:(oc + 1) * 128], lT[:rank, tg * 512:(tg + 1) * 512], True, True,
                                 ["rA_wl2", "rA_lT"], [kpa])
                            osl = o_[:, tg * 512:(tg + 1) * 512]
                            if nm[0] == 'w':
                                sigmoid_from(osl, pa[:], bias_t[:, oc:oc + 1], -0.6065306597126334, ko, kpa)
                            elif nm == 'a':
                                sigmoid_from(osl, pa[:], bias_t[:, oc:oc + 1], 1.0, ko, kpa)
                            else:
                                p.copy('act', osl, pa[:], [kpa], [ko])
                        p.dma('sp', RW[dsti, oc * 128:(oc + 1) * 128, :], o_[:], [ko], [("RW", dsti, oc)])
            p.barrier()
            with contextlib.ExitStack() as es2:
                yT = sb(c, es2, "rC_yT", [128, KC, L], BF16)
                pA = ring_ps(c, es2, "rC_pA", 3, [128, 512])
                es3 = contextlib.ExitStack()
                F = [sb(c, es3, f"rC_F{i}", [128, L], F32) for i in range(8)]
                kF = [f"rC_F{i}" for i in range(8)]
                AR = [sb(c, es3, f"rC_AR{d}", [128, NCH, 2, CH], BF16) for d in range(2)]
                Kt = [sb(c, es3, f"rC_Kt{d}", [128, L], BF16) for d in range(2)]
                Bt = [sb(c, es3, f"rC_Bt{d}", [128, L], BF16) for d in range(2)]
                Kd = sb(c, es3, "rC_Kd", [128, L], BF16)
                Bd = sb(c, es3, "rC_Bd", [128, L], BF16)
                Kdtok = [sb(c, es3, f"rC_Kdtok{d}", [128, NCH, CH], BF16) for d in range(2)]
                Bdtok = [sb(c, es3, f"rC_Bdtok{d}", [128, NCH, CH], BF16) for d in range(2)]
                PC = [sb(c, es3, f"rC_PC{d}", [128, NCH], F32) for d in range(2)]
                Vp = sb(c, es3, "rC_Vp", [128, NCH, CH], BF16)
                Ost = sb(c, es3, "rC_Ost", [128, NCH, CH], F32)
                Obf = sb(c, es3, "rC_Obf", [128, NCH, CH], BF16)
                st = sb(c, es3, "rC_st", [128, NCH, 4], F32)
                units = []
                for uu in range(2):
                    ud = {'k': f"rU{uu}_", 'd': uu}
                    ud['M'] = [sb(c, es3, f"rC_M{uu}{i}", [128, 128], F32) for i in range(2)]
                    ud['Mt'] = [sb(c, es3, f"rC_Mt{uu}{i}", [128, 128], F32) for i in range(2)]
                    ud['Y'] = [sb(c, es3, f"rC_Y{uu}{i}", [128, 128], F32) for i in range(2)]
                    ud['XT'] = sb(c, es3, f"rC_XT{uu}", [128, 128], BF16)
                    ud['AkT'] = sb(c, es3, f"rC_AkT{uu}", [128, 128], BF16)
                    ud['RkT'] = sb(c, es3, f"rC_RkT{uu}", [128, 128], BF16)
                    ud['RbT'] = sb(c, es3, f"rC_RbT{uu}", [128, 128], BF16)
                    ud['S'] = sb(c, es3, f"rC_S{uu}", [128, CH], F32)
                    ud['Sb'] = sb(c, es3, f"rC_Sb{uu}", [128, CH], BF16)
                    ud['R1'] = sb(c, es3, f"rC_R1{uu}", [128, CH], BF16)
                    ud['NU'] = sb(c, es3, f"rC_NU{uu}", [128, CH], BF16)
                    ud['pN'] = ps(c, es3, f"rC_pN{uu}", [128, 512])
                    PSUM_KEYS.add(ud['k'] + "pN")
                    ud['pX'] = ps(c, es3, f"rC_pX{uu}", [128, 512])
                    PSUM_KEYS.add(ud['k'] + "pX")
                    for nm_ in ['Mt', 'AkT', 'RkT', 'RbT']:
                        tl_ = ud[nm_][0] if nm_ == 'Mt' else ud[nm_]
                        p.memset('pool', tl_[:], 0.0, [ud['k'] + (nm_ + "0" if nm_ == 'Mt' else nm_)])
                    units.append(ud)
                pB = ring_ps(c, es3, "rC_pB", 1, [128, 1024], BF16)
                for hp in range(8):
                    rows = slice(hp * 128, (hp + 1) * 128)
                    cs3 = "p (n q) -> p n q"

                    def ld(dstF, idx):
                        p.dma('sp', F[dstF][:], RW[idx, rows, :], [("RW", idx, hp)], [kF[dstF]])

                    def onesbd_bcast(srcF, dstF, add_eps):
                        for tg in range(L // 512):
                            pa, kpa = pA.next()
                            p.mm(pa[:], onesbd[:], F[srcF][:, tg * 512:(tg + 1) * 512], True, True, [kF[srcF], "r_c"], [kpa])
                            if add_eps is not None:
                                p.ts('dve', F[dstF][:, tg * 512:(tg + 1) * 512], pa[:], add_eps, None, ALU.add, None, [kpa], [kF[dstF]])
                            else:
                                p.copy('dve', F[dstF][:, tg * 512:(tg + 1) * 512], pa[:], [kpa], [kF[dstF]])

                    ld(0, 1)
                    ld(1, 3)
                    p.ts('dve', F[6][:], F[0][:], kkv[:, hp:hp + 1], None, ALU.mult, None, [kF[0], "r_c"], [kF[6]])
                    p.act(F[7][:], F[6][:], AF.Square, [kF[6]], [kF[7]])
                    onesbd_bcast(7, 7, 1e-6) if False else None
                    onesbd_bcast(7, 2, 1e-6)
                    p.act(F[2][:], F[2][:], AF.Ln, [kF[2]], [kF[2]])
                    p.act(F[2][:], F[2][:], AF.Exp, [kF[2]], [kF[2]], scale=-0.5)
                    p.tt('dve', F[2][:], F[6][:], F[2][:], ALU.mult, [kF[6], kF[2]], [kF[2]])
                    p.ts('dve', F[6][:], F[1][:], kav[:, hp:hp + 1], omka[:, hp:hp + 1], ALU.mult, ALU.add, [kF[1], "r_c"], [kF[6]])
                    p.tt('dve', F[3][:], F[0][:], F[6][:], ALU.mult, [kF[0], kF[6]], [kF[3]])
                    p.tt('dve', F[4][:], F[2][:], F[1][:], ALU.mult, [kF[2], kF[1]], [kF[4]])
                    ld(5, 0)
                    ld(0, 2)
                    p.stt(F[6][:], F[5][:], rkv[:, hp:hp + 1], F[3][:], ALU.mult, ALU.mult, [kF[5], "r_c", kF[3]], [kF[6]])
                    onesbd_bcast(6, 7, None)
                    p.tt('dve', F[1][:], F[7][:], F[0][:], ALU.mult, [kF[7], kF[0]], [kF[1]])
                    p.copy('act', Kd[:], F[0][:], [kF[0]], ["rC_Kd"])

                    def to_stacked(srcT, ksrc, dst, kdst):
                        for c8 in range(NCH // 8):
                            pb, kpb = pB.next()
                            for hs in range(2):
                                sl_ = slice(hs * 64, (hs + 1) * 64)
                                for q in range(8):
                                    ch = c8 * 8 + q
                                    p.mm64(pA.tiles[0][sl_, q * 64:(q + 1) * 64], srcT[sl_, ch * CH:(ch + 1) * CH], c.identb[sl_, sl_], True, True,
                                         [ksrc], ["rC_pA0"])
                            p.copy('act' if c8 % 2 else 'dve', dst[:, c8 * 8:(c8 + 1) * 8, :],
                                   pA.tiles[0][:, :].rearrange("p (a b) -> p a b", a=8), ["rC_pA0"], [kdst])

                    to_stacked(Kd, "rC_Kd", Vp, "rC_Vp")
                    for d in range(2):
                        ld(0, 5 + d)
                        p.op('dve', lambda e: e.tensor_tensor_scan(F[6][:], c.ones[:, 0:1].broadcast_to([128, L]), F[0][:], 0.0, ALU.mult, ALU.add),
                             [kF[0]], [kF[6]])
                        cs = F[6][:].rearrange(cs3, q=CH)
                        lw = F[0][:].rearrange(cs3, q=CH)
                        li = F[7][:].rearrange(cs3, q=CH)
                        if d == 0:
                            p.tt('dve', st[:, :, 0:1], cs[:, :, 0:1], lw[:, :, 0:1], ALU.subtract, [kF[6], kF[0]], ["rC_st"])
                            p.tt('dve', li, cs, bc(st[:, :, 0], [128, NCH, CH], 2) if False else st[:, :, 0:1].broadcast_to([128, NCH, CH]),
                                 ALU.subtract, [kF[6], "rC_st"], [kF[7]])
                            p.tt('dve', F[6][:], F[7][:], F[0][:], ALU.subtract, [kF[7], kF[0]], [kF[6]])
                            last = CH - 1
                        else:
                            p.copy('dve', st[:, :, 0:1], cs[:, :, CH - 1:CH], [kF[6]], ["rC_st"])
                            p.tt('dve', cs, st[:, :, 0:1].broadcast_to([128, NCH, CH]), cs, ALU.subtract, [kF[6], "rC_st"], [kF[6]])
                            p.tt('dve', F[7][:], F[6][:], F[0][:], ALU.add, [kF[6], kF[0]], [kF[7]])
                            last = 0
                        p.act(F[6][:], F[6][:], AF.Exp, [kF[6]], [kF[6]])
                        p.tt('dve', AR[d][:, :, 0, :], F[2][:].rearrange(cs3, q=CH), F[6][:].rearrange(cs3, q=CH), ALU.mult,
                             [kF[2], kF[6]], [f"rC_AR{d}"])
                        p.act(F[0][:], F[7][:], AF.Exp, [kF[7]], [kF[0]])
                        p.tt('dve', AR[d][:, :, 1, :], F[5][:].rearrange(cs3, q=CH), F[0][:].rearrange(cs3, q=CH), ALU.mult,
                             [kF[5], kF[0]], [f"rC_AR{d}"])
                        p.copy('dve', PC[d][:].unsqueeze(2), F[0][:].rearrange(cs3, q=CH)[:, :, last:last + 1], [kF[0]], [f"rC_PC{d}"])
                        p.act(F[6][:], F[7][:], AF.Exp, [kF[7]], [kF[6]], scale=-1.0)
                        p.tt('dve', Kt[d][:], F[3][:], F[6][:], ALU.mult, [kF[3], kF[6]], [f"rC_Kt{d}"])
                        p.tt('dve', Bt[d][:], F[4][:], F[6][:], ALU.mult, [kF[4], kF[6]], [f"rC_Bt{d}"])
                        p.tt('dve', F[0][:].rearrange(cs3, q=CH), li[:, :, last:last + 1].broadcast_to([128, NCH, CH]), li, ALU.subtract,
                             [kF[7]], [kF[0]])
                        p.act(F[0][:], F[0][:], AF.Exp, [kF[0]], [kF[0]])
                        p.tt('dve', Kd[:], F[3][:], F[0][:], ALU.mult, [kF[3], kF[0]], ["rC_Kd"])
                        p.tt('dve', Bd[:], F[4][:], F[0][:], ALU.mult, [kF[4], kF[0]], ["rC_Bd"])
                        to_stacked(Kd, "rC_Kd", Kdtok[d], f"rC_Kdtok{d}")
                        to_stacked(Bd, "rC_Bd", Bdtok[d], f"rC_Bdtok{d}")
                    for step in range(NCH):
                        first = step == 0
                        lastc = step == NCH - 1
                        chs = [step, NCH - 1 - step]
                        for u_ in units:
                            d, kk_ = u_['d'], u_['k']
                            ch = chs[d]
                            tsl = slice(ch * CH, (ch + 1) * CH)
                            pX = u_['pX']
                            for hs in range(2):
                                sl_ = slice(hs * 64, (hs + 1) * 64)
                                p.mm64(pX[sl_, 0:128], Kt[d][sl_, tsl], AR[d][sl_, ch, :, :].rearrange("p a q -> p (a q)"), True, True,
                                     [f"rC_Kt{d}", f"rC_AR{d}"], [kk_ + "pX"])
                                p.mm64(pX[sl_, 128:256], Bt[d][sl_, tsl], AR[d][sl_, ch, :, :].rearrange("p a q -> p (a q)"), True, True,
                                     [f"rC_Bt{d}", f"rC_AR{d}"], [kk_ + "pX"])
                            for hs in range(2):
                                sl_ = slice(hs * 64, (hs + 1) * 64)
                                cs_ = slice(hs * 64, (hs + 1) * 64)
                                p.tt('dve', u_['AkT'][sl_, cs_], pX[sl_, 0:64], mS[d][sl_, :], ALU.mult, [kk_ + "pX", "r_c"], [kk_ + "AkT"])
                                p.tt('dve', u_['RkT'][sl_, cs_], pX[sl_, 64:128], mI[d][sl_, :], ALU.mult, [kk_ + "pX", "r_c"], [kk_ + "RkT"])
                                p.tt('dve', u_['Mt'][0][sl_, cs_], pX[sl_, 128:192], mSn[d][sl_, :], ALU.mult, [kk_ + "pX", "r_c"], [kk_ + "Mt0"])
                                p.tt('dve', u_['RbT'][sl_, cs_], pX[sl_, 192:256], mI[d][sl_, :], ALU.mult, [kk_ + "pX", "r_c"], [kk_ + "RbT"])
                        neumann_inverse_T(c, units, 5)
                        for u_ in units:
                            d, kk_ = u_['d'], u_['k']
                            ch = chs[d]
                            pX = u_['pX']
                            if not first:
                                for hs in range(2):
                                    sl_ = slice(hs * 64, (hs + 1) * 64)
                                    p.mm64(pX[sl_, 256:320], AR[d][sl_, ch, 0, :], u_['Sb'][sl_, :], True, False, [f"rC_AR{d}", kk_ + "Sb"], [kk_ + "pX"])
                            p.mm(pX[:, 256:320], u_['AkT'][:], Vp[:, ch, :], first, True, [kk_ + "AkT", "rC_Vp"], [kk_ + "pX"])
                            p.copy('act', u_['R1'][:], pX[:, 256:320], [kk_ + "pX"], [kk_ + "R1"])
                        for u_ in units:
                            d, kk_ = u_['d'], u_['k']
                            pX = u_['pX']
                            p.mm(pX[:, 320:384], u_['XT'][:], u_['R1'][:], True, True, [kk_ + "XT", kk_ + "R1"], [kk_ + "pX"])
                            p.op('act', lambda e, u_=u_, pX=pX: e.activation(u_['NU'][:], pX[:, 320:384], AF.Copy, scale=-1.0), [kk_ + "pX"], [kk_ + "NU"])
                        for u_ in units:
                            d, kk_ = u_['d'], u_['k']
                            ch = chs[d]
                            pX = u_['pX']
                            if not first:
                                for hs in range(2):
                                    sl_ = slice(hs * 64, (hs + 1) * 64)
                                    p.mm64(pX[sl_, 384:448], AR[d][sl_, ch, 1, :], u_['Sb'][sl_, :], True, False, [f"rC_AR{d}", kk_ + "Sb"], [kk_ + "pX"])
                            p.mm(pX[:, 384:448], u_['RkT'][:], Vp[:, ch, :], first, False, [kk_ + "RkT", "rC_Vp"], [kk_ + "pX"])
                            p.mm(pX[:, 384:448], u_['RbT'][:], u_['NU'][:], False, True, [kk_ + "RbT", kk_ + "NU"], [kk_ + "pX"])
                            okey = ("rC_O", ch)
                            first_visit = (d == 0 and ch < NCH // 2) or (d == 1 and ch >= NCH // 2)
                            if first_visit:
                                p.copy('dve', Ost[:, ch, :], pX[:, 384:448], [kk_ + "pX"], [okey])
                            else:
                                p.tt('dve', Ost[:, ch, :], Ost[:, ch, :], pX[:, 384:448], ALU.add, [kk_ + "pX", okey], [okey])
                            if not lastc:
                                for hs in range(2):
                                    sl_ = slice(hs * 64, (hs + 1) * 64)
                                    p.mm64(pX[sl_, 448:512], Kdtok[d][sl_, ch, :], Vp[sl_, ch, :], True, False, [f"rC_Kdtok{d}", "rC_Vp"], [kk_ + "pX"])
                                    p.mm64(pX[sl_, 448:512], Bdtok[d][sl_, ch, :], u_['NU'][sl_, :], False, True, [f"rC_Bdtok{d}", kk_ + "NU"], [kk_ + "pX"])
                                if first:
                                    p.copy('dve', u_['S'][:], pX[:, 448:512], [kk_ + "pX"], [kk_ + "S"])
                                else:
                                    p.stt(u_['S'][:], u_['S'][:], PC[d][:, ch:ch + 1], pX[:, 448:512], ALU.mult, ALU.add,
                                          [kk_ + "S", f"rC_PC{d}", kk_ + "pX"], [kk_ + "S"])
                                p.copy('act', u_['Sb'][:], u_['S'][:], [kk_ + "S"], [kk_ + "Sb"])
                    okeys = [("rC_O", ch) for ch in range(NCH)]
                    p.op('dve', lambda e: e.reduce_sum(st[:, :, 0], Ost[:], AX.X), okeys, ["rC_st"])
                    p.ts('dve', st[:, :, 0], st[:, :, 0], 1.0 / CH, None, ALU.mult, None, ["rC_st"], ["rC_st"])
                    p.tt('dve', Ost[:], Ost[:], st[:, :, 0:1].broadcast_to([128, NCH, CH]), ALU.subtract, okeys + ["rC_st"], ["rC_Oc"])
                    F6v = F[6][:].rearrange(cs3, q=CH)
                    p.act(F6v, Ost[:], AF.Square, ["rC_Oc"], [kF[6]])
                    p.op('dve', lambda e: e.reduce_sum(st[:, :, 1], F6v, AX.X), [kF[6]], ["rC_st"])
                    p.ts('dve', st[:, :, 1], st[:, :, 1], 1.0 / CH, 64e-5, ALU.mult, ALU.add, ["rC_st"], ["rC_st"])
                    p.tt('pool', st[:, :, 2], st[:, :, 1], c.mhalf[:, 0:NCH], ALU.pow, ["rC_st"], ["rC_st"])
                    p.tt('dve', Obf[:], Ost[:], st[:, :, 2:3].broadcast_to([128, NCH, CH]), ALU.mult, ["rC_Oc", "rC_st"], ["rC_Obf"])
                    ld(0, 4)
                    for c8 in range(NCH // 8):
                        for hs in range(2):
                            sl_ = slice(hs * 64, (hs + 1) * 64)
                            for q in range(8):
                                ch = c8 * 8 + q
                                p.mm64(pA.tiles[1][sl_, q * 64:(q + 1) * 64], Obf[sl_, ch, :], c.identb[sl_, sl_], True, True, ["rC_Obf"], ["rC_pA1"])
                        fs = slice(c8 * 512, (c8 + 1) * 512)
                        p.ts('dve', F[6][:, fs], pA.tiles[1][:, :], lnw[:, hp:hp + 1], lnb[:, hp:hp + 1], ALU.mult, ALU.add, ["rC_pA1", "r_c"], [kF[6]])
                    p.tt('dve', F[6][:], F[6][:], F[1][:], ALU.add, [kF[6], kF[1]], [kF[6]])
                    p.tt('dve', yT[:, hp, :], F[6][:], F[0][:], ALU.mult, [kF[6], kF[0]], [("rC_yT", hp)])

                p.barrier()
                es3.close()
                wl = WLoader(c, es2, "rD_wst", 1024)
                wo = sb(c, es2, "rD_wo", [128, KC, D], BF16)
                xr = ring_sb(c, es2, "rD_x", 1, [128, KC, 512], F32)
                xo = sb(c, es2, "rD_xo", [128, KC, 512], F32)
                for kc in range(KC):
                    wl.load(wo[:, kc, :], w_out[kc * 128:(kc + 1) * 128, :], "rD_wo")
                for gi in range(L // 512):
                    tok0 = s * L + gi * 512
                    xi, kx = xr.next()
                    p.dma('sp', xi[:], xt_view(c, tok0, 512), xt_keys(tok0, 512), [kx])
                    for dc in range(KC):
                        pa, kpa = pA.next()
                        for kc in range(KC):
                            p.mm(pa[:], wo[:, kc, dc * 128:(dc + 1) * 128], yT[:, kc, gi * 512:(gi + 1) * 512], kc == 0, kc == KC - 1,
                                 ["rD_wo"] + [("rC_yT", kc)], [kpa])
                        p.tt('dve', xo[:, dc, :], pa[:], xi[:, dc, :], ALU.add, [kpa, kx], ["rD_xo"])
                    p.dma('sp', xt_view(c, tok0, 512), xo[:], ["rD_xo"], xt_keys(tok0, 512))
            p.barrier()


def build(nseq_prompt=1, nseq_sample=4, cfg=None):
    cfg = cfg or {}
    depth = cfg.get('depth', 4)
    nseq = nseq_prompt + nseq_sample
    ntok = nseq * L
    nc = bass.Bass("TRN2", target_bir_lowering=False)
    c = Ctx()
    c.nc = nc
    c.cfg = cfg
    din = {}

    def inp(name, shape):
        din[name] = nc.dram_tensor(name, list(shape), F32, kind="ExternalInput").ap()
        return din[name]

    c.xp = inp("x_prompt", [max(nseq_prompt, 1), L, D])
    c.xs = inp("x_sample", [max(nseq_sample, 1), L, D])
    W = {}
    for name, shape in WEIGHT_SHAPES.items():
        W[name] = inp(name, shape)
    c.W = W
    c_ident = inp("c_ident", [128, 128])
    c.c_cos = inp("c_cos", [128, L])
    c.c_sin = inp("c_sin", [128, L])
    c.yp = nc.dram_tensor("y_prompt", [max(nseq_prompt, 1), L, D], F32, kind="ExternalOutput").ap()
    c.ys = nc.dram_tensor("y_sample", [max(nseq_sample, 1), L, D], F32, kind="ExternalOutput").ap()
    c.XT = nc.dram_tensor("XT", [D, ntok], F32).ap()
    c.Zs = nc.dram_tensor("Zs", [L, 2048], F32).ap()
    c.Ys = nc.dram_tensor("Ys", [L, 2048], F32).ap()
    c.RW = nc.dram_tensor("RW", [7, D, L], F32).ap()

    with contextlib.ExitStack() as es:
        p = Prog(nc, es)
        c.p = p
        c.ident = sb(c, es, "ident", [128, 128], F32)
        c.ones = sb(c, es, "ones", [128, 128], F32)
        c.mhalf = sb(c, es, "mhalf", [128, 512], F32)
        c.identb = sb(c, es, "identb", [128, 128], BF16)
        p.dma('sp', c.ident[:], c_ident[:, :], [], ["ident"])
        p.copy('dve', c.identb[:], c.ident[:], ["ident"], ["identb"])
        p.memset('pool', c.ones[:], 1.0, ["ones"])
        p.memset('pool', c.mhalf[:], -0.5, ["mhalf"])
        make_masks(c, es)
        p.barrier()

        srcs = [(c.xp, b) for b in range(nseq_prompt)] + [(c.xs, b) for b in range(nseq_sample)]
        dsts = [(c.yp, b) for b in range(nseq_prompt)] + [(c.ys, b) for b in range(nseq_sample)]
        stage_in(c, srcs)
        for i in range(depth):
            if cfg.get('ffn', True) and cfg.get('ffn1', True):
                stage_ffn(c, W['ffn1_norm'][i], W['ffn1_w_gu'][i], W['ffn1_w_down'][i], ntok)
            if i in cfg.get('mixers', [0, 1, 2, 3]):
                m, jj = i % 4, i // 4
                if m == 0:
                    stage_ssd(c, i, jj, nseq)
                if m == 1:
                    stage_gdn(c, i, jj, nseq)
                if m == 2:
                    stage_attn(c, i, jj, nseq)
                if m == 3:
                    stage_rwkv(c, i, jj, nseq)
            if cfg.get('ffn', True) and cfg.get('ffn2', True):
                stage_ffn(c, W['ffn2_norm'][i], W['ffn2_w_gu'][i], W['ffn2_w_down'][i], ntok)
        stage_out(c, dsts, W['final_norm'])
        p.emit()
    return nc


WEIGHT_SHAPES = {
    'ffn1_norm': (4, 1024), 'ffn1_w_gu': (4, 1024, 5632), 'ffn1_w_down': (4, 2816, 1024),
    'mix_norm': (4, 1024), 'ffn2_norm': (4, 1024), 'ffn2_w_gu': (4, 1024, 5632), 'ffn2_w_down': (4, 2816, 1024),
    'ssd_w_in': (1, 1024, 6208), 'ssd_conv_w': (1, 5, 4096), 'ssd_conv_b': (1, 4096), 'ssd_a_log': (1, 2, 32),
    'ssd_dt_bias': (1, 2, 32), 'ssd_d': (1, 32), 'ssd_norm': (1, 2048), 'ssd_w_out': (1, 2048, 1024),
    'gdn_w_in': (1, 1024, 6192), 'gdn_conv_w': (1, 5, 4096), 'gdn_conv_b': (1, 4096), 'gdn_a_log': (1, 2, 16),
    'gdn_dt_bias': (1, 2, 16), 'gdn_norm': (1, 128), 'gdn_w_out': (1, 2048, 1024),
    'att_w_qkv': (1, 1024, 1536), 'att_sinks': (1, 16), 'att_w_out': (1, 1024, 1024),
    'rwkv_x_mu': (1, 6, 1024), 'rwkv_w_rkv': (1, 3, 1024, 1024), 'rwkv_w0': (1, 2, 1024),
    'rwkv_w1': (1, 2, 1024, 64), 'rwkv_w2': (1, 2, 64, 1024), 'rwkv_a0': (1, 1024), 'rwkv_a1': (1, 1024, 64),
    'rwkv_a2': (1, 64, 1024), 'rwkv_g1': (1, 1024, 128), 'rwkv_g2': (1, 128, 1024), 'rwkv_k_k': (1, 1024),
    'rwkv_k_a': (1, 1024), 'rwkv_r_k': (1, 1024), 'rwkv_lnx_w': (1, 1024), 'rwkv_lnx_b': (1, 1024),
    'rwkv_w_out': (1, 1024, 1024), 'final_norm': (1024,),
}


def consts():
    r = np.arange(128) % 64
    i = (r % 32).astype(np.float64)
    inv_freq = 10000.0 ** (-i / 32.0)
    ang = (np.arange(L, dtype=np.float64)[None, :] * inv_freq[:, None]).astype(np.float32).astype(np.float64)
    sgn = np.where(r < 32, -1.0, 1.0)[:, None]
    return {"c_ident": np.eye(128, dtype=np.float32),
            "c_cos": np.cos(ang).astype(np.float32),
            "c_sin": (np.sin(ang) * sgn).astype(np.float32)}


def kernel(**inputs):
    nc = build(1, 4)
    xp = np.ascontiguousarray(inputs['x_prompt'], dtype=np.float32)
    xs = np.ascontiguousarray(inputs['x_sample'], dtype=np.float32)
    shared = {k: np.ascontiguousarray(inputs[k], dtype=np.float32) for k in WEIGHT_SHAPES}
    shared.update(consts())
    in_maps = []
    for c in range(NCORES):
        m = dict(shared)
        m['x_prompt'] = xp[c:c + 1]
        m['x_sample'] = xs[4 * c:4 * c + 4]
        in_maps.append(m)
    res = run_bass_kernel_spmd(nc, in_maps, core_ids=list(range(NCORES)))
    yp = np.concatenate([r['y_prompt'] for r in res.results], axis=0)
    ys = np.concatenate([r['y_sample'] for r in res.results], axis=0)
    return yp.astype(np.float32), ys.astype(np.float32)
```

```python
import contextlib
import numpy as np
import concourse.bass as bass
import concourse.mybir as mybir
from concourse.alu_op_type import AluOpType as ALU
from concourse.bass_utils import run_bass_kernel_spmd

F32 = mybir.dt.float32
BF16 = mybir.dt.bfloat16
AF = mybir.ActivationFunctionType
AX = mybir.AxisListType

NCORES = 8
D = 1024
L = 2048
DFF = 2816
KC = D // 128
FC = DFF // 128
ENG = ['pe', 'dve', 'act', 'pool', 'sp']
SAME_ENG_SYNC = True


PSUM_KEYS = set()


class Prog:
    def __init__(self, nc, es):
        self.nc = nc
        self.es = es
        self.engs = {'pe': nc.tensor, 'dve': nc.vector, 'act': nc.scalar, 'pool': nc.gpsimd, 'sp': nc.sync}
        self.q = {e: [] for e in ENG}
        self.sem = {e: es.enter_context(nc.semaphore("s_" + e)) for e in ENG}
        self.cnt = {e: 0 for e in ENG}
        self.seen = {e: {} for e in ENG}
        self.lastw = {}
        self.rds = {}
        self.ndsem = {'sp': 6, 'pool': 2, 'act': 2}
        self.dsem = {}
        self.dval = {}
        self.drr = {}
        for qn, n in self.ndsem.items():
            self.dsem[qn] = [es.enter_context(nc.semaphore(f"d_{qn}{i}")) for i in range(n)]
            for i in range(n):
                self.dval[(qn, i)] = 0
            self.drr[qn] = 0
        self.ninstr = 0

    def _semh(self, s):
        if isinstance(s, tuple):
            return self.dsem[s[0]][s[1]]
        return self.sem[s]

    def _wait(self, eng, tok):
        s, v = tok
        if s == eng and (eng == 'pe' or not SAME_ENG_SYNC):
            return
        if self.seen[eng].get(s, 0) >= v:
            return
        self.seen[eng][s] = v
        self.q[eng].append(('w', self._semh(s), v))

    def _deps(self, eng, reads, writes, is_dma=False):
        for k in reads:
            for tok in self.lastw.get(k, ()):
                self._wait(eng, tok)
        for k in writes:
            for tok in self.lastw.get(k, ()):
                if is_dma and isinstance(tok[0], tuple):
                    continue
                self._wait(eng, tok)
            for s, v in self.rds.get(k, {}).items():
                self._wait(eng, (s, v))

    def _record(self, tok, reads, writes, is_dma=False):
        s, v = tok
        for k in reads:
            d = self.rds.setdefault(k, {})
            if d.get(s, 0) < v:
                d[s] = v
        for k in writes:
            if is_dma and not self.rds.get(k) and k in self.lastw and all(isinstance(t[0], tuple) for t in self.lastw[k]):
                self.lastw[k] = [t for t in self.lastw[k] if t[0] != s] + [tok]
            else:
                self.lastw[k] = [tok]
            self.rds[k] = {}

    def op(self, eng, fn, reads=(), writes=()):
        xr = [k for k in reads if k in PSUM_KEYS]
        if xr:
            reads = [k for k in reads if k not in PSUM_KEYS]
            writes = list(writes) + [k for k in xr if k not in writes]
        self._deps(eng, reads, writes)
        self.cnt[eng] += 1
        tok = (eng, self.cnt[eng])
        self.q[eng].append(('o', fn, self.sem[eng], 1))
        self._record(tok, reads, writes)
        self.ninstr += 1

    def dma(self, qn, out, in_, reads=(), writes=(), **kw):
        self._deps(qn, reads, writes, is_dma=True)
        i = self.drr[qn]
        self.drr[qn] = (i + 1) % self.ndsem[qn]
        prev = self.dval[(qn, i)]
        if prev > 0:
            self._wait(qn, ((qn, i), prev))
        self.dval[(qn, i)] = prev + 16
        tok = ((qn, i), prev + 16)
        self.q[qn].append(('o', lambda e: e.dma_start(out=out, in_=in_, **kw), self.dsem[qn][i], 16))
        self._record(tok, reads, writes, is_dma=True)
        self.ninstr += 1

    def barrier(self):
        for e in ENG:
            for e2 in ENG:
                if e2 != e and self.cnt[e2] > 0:
                    self._wait(e, (e2, self.cnt[e2]))
            for k, v in self.dval.items():
                if v > 0:
                    self._wait(e, (k, v))
        self.lastw = {}
        self.rds = {}

    def emit(self):
        nc = self.nc
        with nc.Block() as block:
            def run(e, name):
                for it in self.q[name]:
                    if it[0] == 'w':
                        e.wait_ge(it[1], it[2])
                    else:
                        it[1](e).then_inc(it[2], it[3])

            @block.tensor
            def _(e):
                run(e, 'pe')

            @block.vector
            def _(e):
                run(e, 'dve')

            @block.scalar
            def _(e):
                run(e, 'act')

            @block.gpsimd
            def _(e):
                run(e, 'pool')

            @block.sync
            def _(e):
                run(e, 'sp')

    def mm(self, out, lhsT, rhs, start, stop, reads, writes):
        self.op('pe', lambda e: e.matmul(out, lhsT, rhs, start=start, stop=stop), reads, writes)

    def mm64(self, out, lhsT, rhs, start, stop, reads, writes):
        grp = int(lhsT.base_partition) if hasattr(lhsT, 'base_partition') and not callable(lhsT.base_partition) else int(lhsT.base_partition())
        last = getattr(self, '_last64', None)
        if self.cnt['pe'] > 0 and last is not None and last[0] == self.cnt['pe'] and last[1] != grp:
            self.q['pe'].append(('w', self.sem['pe'], self.cnt['pe']))
        self.mm(out, lhsT, rhs, start, stop, reads, writes)
        self._last64 = (self.cnt['pe'], grp)

    def tr(self, out, in_, ident, reads, writes):
        self.op('pe', lambda e: e.transpose(out, in_, ident), reads, writes)

    def act(self, out, in_, func, reads, writes, bias=None, scale=None):
        kw = {}
        if bias is not None:
            kw['bias'] = bias
        if scale is not None:
            kw['scale'] = scale
        self.op('act', lambda e: e.activation(out, in_, func, **kw), reads, writes)

    def tt(self, eng, out, in0, in1, op, reads, writes):
        self.op(eng, lambda e: e.tensor_tensor(out, in0, in1, op), reads, writes)

    def ts(self, eng, out, in0, s1, s2, op0, op1, reads, writes):
        if op1 is None:
            self.op(eng, lambda e: e.tensor_scalar(out, in0, s1, None, op0), reads, writes)
        else:
            self.op(eng, lambda e: e.tensor_scalar(out, in0, s1, s2, op0, op1), reads, writes)

    def stt(self, out, in0, scalar, in1, op0, op1, reads, writes):
        self.op('dve', lambda e: e.scalar_tensor_tensor(out, in0, scalar, in1, op0, op1), reads, writes)

    def copy(self, eng, out, in_, reads, writes):
        if eng == 'act':
            self.op('act', lambda e: e.copy(out, in_), reads, writes)
        else:
            self.op(eng, lambda e: e.tensor_copy(out, in_), reads, writes)

    def memset(self, eng, ap, val, writes):
        self.op(eng, lambda e: e.memset(ap, val), (), writes)


class Ctx:
    pass


_UID = [0]


def sb(c, es, name, shape, dt):
    _UID[0] += 1
    return es.enter_context(c.nc.sbuf_tensor(f"{name}_{_UID[0]}", shape, dt))


def ps(c, es, name, shape, dt=F32):
    _UID[0] += 1
    return es.enter_context(c.nc.psum_tensor(f"{name}_{_UID[0]}", shape, dt))


def stage_in(c, srcs):
    p = c.p
    with contextlib.ExitStack() as es:
        xin = [sb(c, es, f"in_x{i}", [128, D], F32) for i in range(2)]
        xt = [sb(c, es, f"in_xt{i}", [128, KC, 512], F32) for i in range(2)]
        pt = [ps(c, es, f"in_ps{i}", [128, 512]) for i in range(4)]
        n = 0
        for s, (src, b) in enumerate(srcs):
            for g in range(L // 512):
                xo = xt[g % 2]
                ko = f"in_xt{g % 2}"
                for j in range(4):
                    t0 = g * 512 + j * 128
                    xi = xin[n % 2]
                    ki = f"in_x{n % 2}"
                    p.dma('sp', xi[:], src[b, t0:t0 + 128, :], [], [ki])
                    for half in range(2):
                        pp = pt[(2 * n + half) % 4]
                        kp = f"in_ps{(2 * n + half) % 4}"
                        for q4 in range(4):
                            kc = half * 4 + q4
                            p.tr(pp[:, q4 * 128:(q4 + 1) * 128], xi[:, kc * 128:(kc + 1) * 128], c.ident[:],
                                 [ki], [kp])
                        eng = 'act' if half == 0 else 'dve'
                        p.copy(eng, xo[:, half * 4:half * 4 + 4, j * 128:(j + 1) * 128],
                               pp[:].rearrange("p (a b) -> p a b", a=4), [kp], [ko])
                    n += 1
                tok0 = s * L + g * 512
                p.dma('sp', c.XT[:, tok0:tok0 + 512].rearrange("(kc p) t -> p kc t", p=128), xo[:],
                      [ko], [("XT", tok0 // 256), ("XT", tok0 // 256 + 1)])
    p.barrier()


def load_vec(c, dst, src_ap, key, q='sp'):
    c.p.dma(q, dst, src_ap.rearrange("(j p) -> p j", p=128), [], [key], allow_slow_non_contiguous=True)


def rms_stats(c, x, kx, sq, ksq, pss, kps, var, kvar, rstd, krstd, nt, nch=KC, dim=D, eps=1e-6):
    p = c.p
    p.act(sq[:, :nch, :nt], x[:, :nch, :nt], AF.Square, [kx], [ksq])
    for kc in range(nch):
        p.mm(pss[:, :nt], c.ones[:], sq[:, kc, :nt], kc == 0, kc == nch - 1, [ksq], [kps])
    p.ts('dve', var[:, :nt], pss[:, :nt], 1.0 / dim, eps, ALU.mult, ALU.add, [kps], [kvar])
    p.tt('pool', rstd[:, :nt], var[:, :nt], c.mhalf[:, :nt], ALU.pow, [kvar], [krstd])


def stage_out(c, dsts, gvec):
    p = c.p
    NT = 256
    with contextlib.ExitStack() as es:
        g = sb(c, es, "o_g", [128, KC], F32)
        x = [sb(c, es, f"o_x{i}", [128, KC, NT], F32) for i in range(2)]
        sq = sb(c, es, "o_sq", [128, KC, NT], F32)
        var = sb(c, es, "o_var", [128, NT], F32)
        rstd = sb(c, es, "o_rstd", [128, NT], F32)
        xn = sb(c, es, "o_xn", [128, KC, NT], F32)
        yo = [sb(c, es, f"o_y{i}", [128, D], F32) for i in range(2)]
        pss = ps(c, es, "o_pss", [128, 512])
        pt = [ps(c, es, f"o_pt{i}", [128, 512]) for i in range(4)]
        load_vec(c, g[:], gvec, "o_g")
        n = 0
        m = 0
        for s, (dst, b) in enumerate(dsts):
            for gi in range(L // NT):
                tok0 = s * L + gi * NT
                xi = x[gi % 2]
                kx = f"o_x{gi % 2}"
                p.dma('sp', xi[:], c.XT[:, tok0:tok0 + NT].rearrange("(kc p) t -> p kc t", p=128),
                      [("XT", tok0 // 256)], [kx])
                rms_stats(c, xi, kx, sq, "o_sq", pss, "o_pss", var, "o_var", rstd, "o_rstd", NT)
                for kc in range(KC):
                    p.stt(xn[:, kc, :], xi[:, kc, :], g[:, kc:kc + 1], rstd[:], ALU.mult, ALU.mult,
                          [kx, "o_g", "o_rstd"], ["o_xn"])
                for j in range(NT // 128):
                    y = yo[m % 2]
                    ky = f"o_y{m % 2}"
                    for half in range(2):
                        pp = pt[n % 4]
                        kp = f"o_pt{n % 4}"
                        n += 1
                        for q4 in range(4):
                            kc = half * 4 + q4
                            p.tr(pp[:, q4 * 128:(q4 + 1) * 128], xn[:, kc, j * 128:(j + 1) * 128], c.ident[:],
                                 ["o_xn"], [kp])
                        eng = 'act' if half == 0 else 'dve'
                        p.copy(eng, y[:, half * 512:(half + 1) * 512], pp[:], [kp], [ky])
                    t0 = gi * NT + j * 128
                    p.dma('sp', dst[b, t0:t0 + 128, :], y[:], [ky], [])
                    m += 1
    p.barrier()


def stage_ffn(c, gvec, w_gu, w_down, ntok):
    p = c.p
    NT = 256
    with contextlib.ExitStack() as es:
        wgu = sb(c, es, "f_wgu", [128, KC, 2 * DFF], BF16)
        wd = sb(c, es, "f_wd", [128, FC, D], BF16)
        g = sb(c, es, "f_g", [128, KC], F32)
        x = [sb(c, es, f"f_x{i}", [128, KC, NT], F32) for i in range(2)]
        sq = sb(c, es, "f_sq", [128, KC, NT], F32)
        xo = sb(c, es, "f_xo", [128, KC, NT], F32)
        var = sb(c, es, "f_var", [128, NT], F32)
        rstd = sb(c, es, "f_rstd", [128, NT], F32)
        xn = [sb(c, es, f"f_xn{i}", [128, KC, NT], BF16) for i in range(2)]
        h = sb(c, es, "f_h", [128, FC, NT], BF16)
        sg = [sb(c, es, f"f_sg{i}", [128, NT], F32) for i in range(2)]
        pss = ps(c, es, "f_pss", [128, 512])
        pg = [ps(c, es, f"f_pg{i}", [128, 512]) for i in range(2)]
        pu = [ps(c, es, f"f_pu{i}", [128, 512]) for i in range(2)]
        py = [ps(c, es, f"f_py{i}", [128, 512]) for i in range(2)]
        load_vec(c, g[:], gvec, "f_g")
        wl = WLoader(c, es, "f_wst", 1408)
        for kc in range(KC):
            for hh in range(4):
                wl.load(wgu[:, kc, hh * 1408:(hh + 1) * 1408],
                        w_gu[kc * 128:(kc + 1) * 128, hh * 1408:(hh + 1) * 1408], "f_wgu")
        for j in range(FC):
            wl.load(wd[:, j, :], w_down[j * 128:(j + 1) * 128, :], "f_wd")
        ngrp = c.cfg.get('ffn_groups', ntok // NT)

        def prep(gi):
            xi, kx = x[gi % 2], f"f_x{gi % 2}"
            xg, kxn = xn[gi % 2], f"f_xn{gi % 2}"
            p.dma('sp', xi[:], c.XT[:, gi * NT:(gi + 1) * NT].rearrange("(kc p) t -> p kc t", p=128), [("XT", gi)], [kx])
            rms_stats(c, xi, kx, sq, "f_sq", pss, "f_pss", var, "f_var", rstd, "f_rstd", NT)
            for kc in range(KC):
                p.stt(xg[:, kc, :], xi[:, kc, :], g[:, kc:kc + 1], rstd[:], ALU.mult, ALU.mult, [kx, "f_g", "f_rstd"], [kxn])

        if ngrp > 0:
            prep(0)
        for gi in range(ngrp):
            tok0 = gi * NT
            xi, kx = x[gi % 2], f"f_x{gi % 2}"
            xg, kxn = xn[gi % 2], f"f_xn{gi % 2}"
            for j in range(FC):
                pgj, puj, sgj = pg[j % 2], pu[j % 2], sg[j % 2]
                kg, ku, ks = f"f_pg{j % 2}", f"f_pu{j % 2}", f"f_sg{j % 2}"
                for kc in range(KC):
                    p.mm(pgj[:, :NT], wgu[:, kc, j * 128:(j + 1) * 128], xg[:, kc, :], kc == 0, kc == KC - 1, ["f_wgu", kxn], [kg])
                for kc in range(KC):
                    p.mm(puj[:, :NT], wgu[:, kc, DFF + j * 128:DFF + (j + 1) * 128], xg[:, kc, :], kc == 0, kc == KC - 1, ["f_wgu", kxn], [ku])
                p.act(sgj[:], pgj[:, :NT], AF.Silu, [kg], [ks])
                p.tt('dve', h[:, j, :], sgj[:], puj[:, :NT], ALU.mult, [ks, ku], [("f_h", j)])
            if gi + 1 < ngrp:
                prep(gi + 1)
            for dc in range(KC):
                pyj = py[dc % 2]
                ky = f"f_py{dc % 2}"
                for j in range(FC):
                    p.mm(pyj[:, :NT], wd[:, j, dc * 128:(dc + 1) * 128], h[:, j, :], j == 0, j == FC - 1, ["f_wd", ("f_h", j)], [ky])
                p.stt(xo[:, dc, :], pyj[:, :NT], 0.5, xi[:, dc, :], ALU.mult, ALU.add, [ky, kx], ["f_xo"])
            p.dma('sp', c.XT[:, tok0:tok0 + NT].rearrange("(kc p) t -> p kc t", p=128), xo[:], ["f_xo"], [("XT", gi)])
    p.barrier()


class Ring:
    def __init__(self, tiles, name):
        self.tiles = tiles
        self.name = name
        self.i = 0

    def next(self):
        t = self.tiles[self.i % len(self.tiles)]
        k = f"{self.name}{self.i % len(self.tiles)}"
        self.i += 1
        return t, k


def ring_sb(c, es, name, n, shape, dt):
    return Ring([sb(c, es, f"{name}{i}", shape, dt) for i in range(n)], name)


def ring_ps(c, es, name, n, shape, dt=F32):
    for i in range(n):
        PSUM_KEYS.add(f"{name}{i}")
    return Ring([ps(c, es, f"{name}{i}", shape, dt) for i in range(n)], name)


class WLoader:
    def __init__(self, c, es, name, width, n=2):
        self.c = c
        self.ring = ring_sb(c, es, name, n, [128, width], F32)
        self.width = width
        self.k = 0

    def load(self, dst, src, key, shape=None):
        p = self.c.p
        st, kst = self.ring.next()
        n = 1
        for d_ in dst.shape[1:]:
            n *= d_
        sv = st[:dst.shape[0], :n]
        if len(dst.shape) == 3:
            sv = sv.rearrange("p (a b) -> p a b", a=dst.shape[1])
        elif len(dst.shape) == 4:
            sv = sv.rearrange("p (a b c) -> p a b c", a=dst.shape[1], b=dst.shape[2])
        p.dma('sp', sv, src, [], [kst])
        eng = 'dve' if self.k % 2 == 0 else 'act'
        if getattr(self, 'dbg', 0) == 3:
            eng = 'pool'
        if getattr(self, 'dbg', 0) == 4:
            eng = 'act'
        self.k += 1
        if getattr(self, 'dbg', 0) != 2:
            p.copy(eng, dst, sv, [kst], [key])


def xt_view(c, tok0, nt):
    return c.XT[:, tok0:tok0 + nt].rearrange("(kc p) t -> p kc t", p=128)


def xt_keys(tok0, nt):
    return [("XT", k) for k in range(tok0 // 256, (tok0 + nt + 255) // 256)]


def load_xn(c, tok0, NT, xr, sq, pss, var, rstd, g, kg, xn, kxn, pref):
    p = c.p
    xi, kx = xr.next()
    p.dma('sp', xi[:, :, :NT], xt_view(c, tok0, NT), xt_keys(tok0, NT), [kx])
    rms_stats(c, xi, kx, sq, pref + "sq", pss, pref + "pss", var, pref + "var", rstd, pref + "rstd", NT)
    for kc in range(KC):
        p.stt(xn[:, kc, :NT], xi[:, kc, :NT], g[:, kc:kc + 1], rstd[:, :NT], ALU.mult, ALU.mult,
              [kx, kg, pref + "rstd"], [kxn])
    return xi, kx


def stage_attn(c, layer, j, nseq):
    p = c.p
    W = c.W
    wqkv, wout, sinks_d, gvec = W['att_w_qkv'][j], W['att_w_out'][j], W['att_sinks'][j], W['mix_norm'][layer]
    NEG = -30000.0
    with contextlib.ExitStack() as es:
        wq = sb(c, es, "a_wq", [128, KC, 1024], BF16)
        wqs = sb(c, es, "a_wqs", [128, KC, 1024], BF16)
        wk = sb(c, es, "a_wk", [128, KC, 512], BF16)
        wks = sb(c, es, "a_wks", [128, KC, 512], BF16)
        wv = sb(c, es, "a_wv", [128, KC, 256], BF16)
        wo = sb(c, es, "a_wo", [128, KC, 1024], BF16)
        cos = sb(c, es, "a_cos", [128, L], F32)
        sin = sb(c, es, "a_sin", [128, L], F32)
        g = sb(c, es, "a_g", [128, KC], F32)
        snk = sb(c, es, "a_snk", [128, 16], F32)
        nsnk = sb(c, es, "a_nsnk", [128, 16], F32)
        mask = sb(c, es, "a_mask", [128, 384], F32)
        qT = sb(c, es, "a_qT", [128, KC, L], BF16)
        kT = sb(c, es, "a_kT", [128, 4, L], BF16)
        vtok = sb(c, es, "a_v", [128, 16, 256], BF16)
        load_vec(c, g[:], gvec, "a_g")
        p.dma('sp', cos[:], c.c_cos[:, :], [], ["a_cos"])
        p.dma('sp', sin[:], c.c_sin[:, :], [], ["a_sin"])
        p.dma('sp', snk[:], sinks_d.partition_broadcast(128), [], ["a_snk"])
        p.ts('dve', nsnk[:], snk[:], -1.0, None, ALU.mult, None, ["a_snk"], ["a_nsnk"])
        p.memset('pool', mask[:], 0.0, ["a_mask"])
        p.op('pool', lambda e: e.affine_select(mask[:], mask[:], [[1, 384]], ALU.is_ge, NEG, base=0,
                                               channel_multiplier=-1), ["a_mask"], ["a_mask"])
        p.op('pool', lambda e: e.affine_select(mask[:], mask[:], [[-1, 384]], ALU.is_ge, NEG, base=256,
                                               channel_multiplier=1), ["a_mask"], ["a_mask"])
        wl = WLoader(c, es, "a_wst", 1024)
        for kc in range(KC):
            rows = slice(kc * 128, (kc + 1) * 128)
            wl.load(wq[:, kc, :], wqkv[rows, 0:1024], "a_w")
            src = wqkv[rows, 0:1024].rearrange("p (h r d) -> p h r d", h=16, r=2)
            dst = wqs[:, kc, :].rearrange("p (h r d) -> p h r d", h=16, r=2)
            wl.load(dst[:, :, 0, :], src[:, :, 1, :], "a_w")
            wl.load(dst[:, :, 1, :], src[:, :, 0, :], "a_w")
            srck = wqkv[rows, 1024:1280].rearrange("p (g r d) -> p g r d", g=4, r=2)
            dk = wk[:, kc, :].rearrange("p (g c r d) -> p g c r d", g=4, c=2, r=2)
            dks = wks[:, kc, :].rearrange("p (g c r d) -> p g c r d", g=4, c=2, r=2)
            for cpy in range(2):
                for r in range(2):
                    wl.load(dk[:, :, cpy, r, :], srck[:, :, r, :], "a_w")
                    wl.load(dks[:, :, cpy, r, :], srck[:, :, 1 - r, :], "a_w")
            wl.load(wv[:, kc, :], wqkv[rows, 1280:1536], "a_w")
            wl.load(wo[:, kc, :], wout[rows, :], "a_w")
        p.barrier()
        for s in range(nseq):
            with contextlib.ExitStack() as es2:
                NT = 256
                xr = ring_sb(c, es2, "aA_x", 2, [128, KC, NT], F32)
                sq = sb(c, es2, "aA_sq", [128, KC, NT], F32)
                var = sb(c, es2, "aA_var", [128, NT], F32)
                rstd = sb(c, es2, "aA_rstd", [128, NT], F32)
                xn = sb(c, es2, "aA_xn", [128, KC, NT], BF16)
                t1r = ring_sb(c, es2, "aA_t1", 2, [128, NT], F32)
                t2r = ring_sb(c, es2, "aA_t2", 2, [128, NT], F32)
                pss = ps(c, es2, "aA_pss", [128, 512])
                p1r = ring_ps(c, es2, "aA_p1", 2, [128, 512])
                p2r = ring_ps(c, es2, "aA_p2", 2, [128, 512])
                pvr = ring_ps(c, es2, "aA_pv", 2, [128, 512])
                for gi in range(L // NT):
                    t0 = gi * NT
                    load_xn(c, s * L + t0, NT, xr, sq, pss, var, rstd, g, "a_g", xn, "aA_xn", "aA_")
                    for oc in range(12):
                        if oc < 8:
                            wa, wb, dstT = wq[:, :, oc * 128:(oc + 1) * 128], wqs[:, :, oc * 128:(oc + 1) * 128], qT[:, oc, t0:t0 + NT]
                            kd = ("a_qT", oc)
                        else:
                            gg = oc - 8
                            wa, wb, dstT = wk[:, :, gg * 128:(gg + 1) * 128], wks[:, :, gg * 128:(gg + 1) * 128], kT[:, gg, t0:t0 + NT]
                            kd = ("a_kT", gg)
                        p1, k1 = p1r.next()
                        p2, k2 = p2r.next()
                        for kc in range(KC):
                            p.mm(p1[:, :NT], wa[:, kc, :], xn[:, kc, :], kc == 0, kc == KC - 1, ["a_w", "aA_xn"], [k1])
                        for kc in range(KC):
                            p.mm(p2[:, :NT], wb[:, kc, :], xn[:, kc, :], kc == 0, kc == KC - 1, ["a_w", "aA_xn"], [k2])
                        t1, kt1 = t1r.next()
                        t2, kt2 = t2r.next()
                        p.tt('dve', t1[:], p1[:, :NT], cos[:, t0:t0 + NT], ALU.mult, [k1, "a_cos"], [kt1])
                        p.tt('dve', t2[:], p2[:, :NT], sin[:, t0:t0 + NT], ALU.mult, [k2, "a_sin"], [kt2])
                        p.tt('dve', dstT, t1[:], t2[:], ALU.add, [kt1, kt2], [kd])
                    for tb in range(NT // 128):
                        pv, kv = pvr.next()
                        for kc in range(KC):
                            p.mm(pv[:, :256], xn[:, kc, tb * 128:(tb + 1) * 128], wv[:, kc, :], kc == 0, kc == KC - 1,
                                 ["a_w", "aA_xn"], [kv])
                        p.copy('act', vtok[:, (t0 // 128) + tb, :], pv[:, :256], [kv], [("a_v", (t0 // 128) + tb)])
            p.barrier()
            with contextlib.ExitStack() as es2:
                smr = ring_sb(c, es2, "aB_sm", 2, [128, 384], F32)
                er = ring_sb(c, es2, "aB_e", 2, [128, 384], F32)
                enr = ring_sb(c, es2, "aB_en", 2, [128, 384], BF16)
                eTr = ring_sb(c, es2, "aB_eT", 2, [128, 384], BF16)
                str_ = ring_sb(c, es2, "aB_st", 4, [128, 8], F32)
                oT = sb(c, es2, "aB_oT", [128, KC, 512], BF16)
                xr = ring_sb(c, es2, "aB_x", 1, [128, KC, 512], F32)
                xo = sb(c, es2, "aB_xo", [128, KC, 512], F32)
                spr = ring_ps(c, es2, "aB_sp", 2, [128, 512])
                tpr = ring_ps(c, es2, "aB_tp", 2, [128, 512], BF16)
                opr = ring_ps(c, es2, "aB_op", 2, [128, 512])
                ypr = ring_ps(c, es2, "aB_yp", 2, [128, 512])
                for gi in range(L // 512):
                    tok0 = s * L + gi * 512
                    xi, kx = xr.next()
                    p.dma('sp', xi[:], xt_view(c, tok0, 512), xt_keys(tok0, 512), [kx])
                    for qc in range(KC):
                        op_, kop = opr.next()
                        for qb in range(4):
                            jb = gi * 4 + qb
                            kb0, kb1 = max(jb - 1, 0), min(jb + 1, 15)
                            nk = kb1 - kb0 + 1
                            mo = 128 if jb == 0 else 0
                            nkw = nk * 128
                            for hp in range(2):
                                h = qc * 2 + hp
                                gk = h // 4
                                b0 = hp * 64
                                sp_, ksp = spr.next()
                                p.mm(sp_[:, :nkw], qT[b0:b0 + 64, qc, jb * 128:(jb + 1) * 128],
                                     kT[b0:b0 + 64, gk, kb0 * 128:(kb1 + 1) * 128], True, True,
                                     [("a_qT", qc), ("a_kT", gk)], [ksp])
                                sm, ksm = smr.next()
                                p.tt('dve', sm[:, :nkw], sp_[:, :nkw], mask[:, mo:mo + nkw], ALU.add, [ksp, "a_mask"], [ksm])
                                st, kst = str_.next()
                                p.op('dve', lambda e, st=st, sm=sm, nkw=nkw: e.reduce_max(st[:, 0:1], sm[:, :nkw], AX.X), [ksm], [kst])
                                p.ts('dve', st[:, 1:2], st[:, 0:1], -0.125, nsnk[:, h:h + 1], ALU.mult, ALU.min, ["a_nsnk", kst], [kst])
                                e_, ke = er.next()
                                p.op('act', lambda e, e_=e_, sm=sm, st=st, nkw=nkw: e.activation(
                                    e_[:, :nkw], sm[:, :nkw], AF.Exp, bias=st[:, 1:2], scale=0.125, accum_out=st[:, 2:3]),
                                    [ksm, kst], [ke, kst])
                                p.op('act', lambda e, st=st, h=h: e.activation(st[:, 3:4], st[:, 1:2], AF.Exp, bias=snk[:, h:h + 1]),
                                     [kst, "a_snk"], [kst])
                                p.tt('dve', st[:, 4:5], st[:, 2:3], st[:, 3:4], ALU.add, [kst], [kst])
                                p.op('dve', lambda e, st=st: e.reciprocal(st[:, 5:6], st[:, 4:5]), [kst], [kst])
                                en, ken = enr.next()
                                p.ts('dve', en[:, :nkw], e_[:, :nkw], st[:, 5:6], None, ALU.mult, None, [ke, kst], [ken])
                                tp, ktp = tpr.next()
                                for kb in range(nk):
                                    p.tr(tp[:, kb * 128:(kb + 1) * 128], en[:, kb * 128:(kb + 1) * 128], c.identb[:], [ken], [ktp])
                                eT, keT = eTr.next()
                                p.copy('act', eT[:, :nkw], tp[:, :nkw], [ktp], [keT])
                                for kb in range(nk):
                                    p.mm(op_[b0:b0 + 64, qb * 128:(qb + 1) * 128], vtok[:, kb0 + kb, gk * 64:(gk + 1) * 64],
                                         eT[:, kb * 128:(kb + 1) * 128], kb == 0, kb == nk - 1,
                                         [("a_v", kb0 + kb), keT], [kop])
                        p.copy('act', oT[:, qc, :], op_[:], [kop], [("aB_oT", qc)])
                    for dc in range(KC):
                        yp, kyp = ypr.next()
                        for qc in range(KC):
                            p.mm(yp[:], wo[:, qc, dc * 128:(dc + 1) * 128], oT[:, qc, :], qc == 0, qc == KC - 1,
                                 ["a_w", ("aB_oT", qc)], [kyp])
                        p.tt('dve', xo[:, dc, :], yp[:], xi[:, dc, :], ALU.add, [kyp, kx], ["aB_xo"])
                    p.dma('sp', xt_view(c, tok0, 512), xo[:], ["aB_xo"], xt_keys(tok0, 512))
            p.barrier()


def make_masks(c, es):
    p = c.p
    c.mk = {}
    for name, pat, base, cm, op in [("LE", 1, 0, -1, ALU.is_ge), ("GE", -1, 0, 1, ALU.is_ge),
                                    ("GT", -1, 0, 1, ALU.is_gt), ("LT", 1, 0, -1, ALU.is_gt)]:
        t = sb(c, es, "mk" + name, [128, 128], F32)
        p.memset('pool', t[:], 1.0, ["mk" + name])
        p.op('pool', lambda e, t=t, pat=pat, base=base, cm=cm, op=op: e.affine_select(
            t[:], t[:], [[pat, 128]], op, 0.0, base=base, channel_multiplier=cm), ["mk" + name], ["mk" + name])
        c.mk[name] = t


def bc(ap, shape, axis):
    return ap.unsqueeze(axis).broadcast_to(shape)


def stage_ssd(c, layer, j, nseq):
    p = c.p
    W = c.W
    w_in, conv_w, conv_b = W['ssd_w_in'][j], W['ssd_conv_w'][j], W['ssd_conv_b'][j]
    a_log, dt_bias, d_skip, norm_w, w_out = W['ssd_a_log'][j], W['ssd_dt_bias'][j], W['ssd_d'][j], W['ssd_norm'][j], W['ssd_w_out'][j]
    gvec = W['mix_norm'][layer]
    NB = L // 128
    with contextlib.ExitStack() as es:
        g = sb(c, es, "s_g", [128, KC], F32)
        cw = sb(c, es, "s_cw", [128, 5, 32], F32)
        cb = sb(c, es, "s_cb", [128, 32], F32)
        dtb = sb(c, es, "s_dtb", [128, 64], F32)
        aneg = sb(c, es, "s_aneg", [128, 64], F32)
        dsk = sb(c, es, "s_dsk", [128, 32], F32)
        nw = sb(c, es, "s_nw", [128, 2048], F32)
        xn = sb(c, es, "s_xn", [128, KC, L], BF16)
        dt = sb(c, es, "s_dt", [128, NB, 64], F32)
        load_vec(c, g[:], gvec, "s_g")
        for tap in range(5):
            p.dma('sp', cw[:, tap, :], conv_w[tap].rearrange("(cc p) -> p cc", p=128), [], ["s_cw"], allow_slow_non_contiguous=True)
        p.dma('sp', cb[:], conv_b.rearrange("(cc p) -> p cc", p=128), [], ["s_cb"], allow_slow_non_contiguous=True)
        p.dma('sp', dtb[:], dt_bias.rearrange("a b -> (a b)").partition_broadcast(128), [], ["s_dtb"])
        p.dma('sp', aneg[:], a_log.rearrange("a b -> (a b)").partition_broadcast(128), [], ["s_aneg"])
        p.dma('sp', dsk[:], d_skip.partition_broadcast(128), [], ["s_dsk"])
        p.dma('sp', nw[:], norm_w.partition_broadcast(128), [], ["s_nw"])
        p.act(aneg[:], aneg[:], AF.Exp, ["s_aneg"], ["s_aneg"])
        p.ts('dve', aneg[:], aneg[:], -1.0, None, ALU.mult, None, ["s_aneg"], ["s_aneg"])
        p.barrier()
        for s in range(nseq):
            with contextlib.ExitStack() as es2:
                NT = 256
                xr = ring_sb(c, es2, "sA_x", 2, [128, KC, NT], F32)
                sq = sb(c, es2, "sA_sq", [128, KC, NT], F32)
                var = sb(c, es2, "sA_var", [128, NT], F32)
                rstd = sb(c, es2, "sA_rstd", [128, NT], F32)
                pss = ps(c, es2, "sA_pss", [128, 512])
                for gi in range(L // NT):
                    load_xn(c, s * L + gi * NT, NT, xr, sq, pss, var, rstd, g, "s_g", xn[:, :, gi * NT:(gi + 1) * NT], "s_xn", "sA_")
            p.barrier()
            with contextlib.ExitStack() as es2:
                wzr = ring_sb(c, es2, "sB_wz", 2, [128, KC, 512], BF16)
                wdt = sb(c, es2, "sB_wdt", [128, KC, 64], BF16)
                zr = ring_sb(c, es2, "sB_z", 3, [128, 512], F32)
                pzr = ring_ps(c, es2, "sB_pz", 4, [128, 512])
                pdr = ring_ps(c, es2, "sB_pd", 2, [128, 512])
                wl = WLoader(c, es2, "sB_wst", 4096)
                wl.load(wdt[:], w_in[:, 6144:6208].rearrange("(kc p) c -> p kc c", p=128), "sB_wdt")
                for blk in range(NB):
                    pd, kpd = pdr.next()
                    for kc in range(KC):
                        p.mm(pd[:, :64], xn[:, kc, blk * 128:(blk + 1) * 128], wdt[:, kc, :], kc == 0, kc == KC - 1,
                             ["s_xn", "sB_wdt"], [kpd])
                    p.tt('dve', dt[:, blk, :], pd[:, :64], dtb[:], ALU.add, [kpd, "s_dtb"], ["s_dt"])
                p.act(dt[:], dt[:], AF.Exp, ["s_dt"], ["s_dt"])
                p.act(dt[:], dt[:], AF.Ln, ["s_dt"], ["s_dt"], bias=1.0)
                for zc in range(4):
                    wz, kwz = wzr.next()
                    wl.load(wz[:], w_in[:, zc * 512:(zc + 1) * 512].rearrange("(kc p) c -> p kc c", p=128), kwz)
                    for blk in range(NB):
                        pz, kpz = pzr.next()
                        for kc in range(KC):
                            p.mm(pz[:], xn[:, kc, blk * 128:(blk + 1) * 128], wz[:, kc, :], kc == 0, kc == KC - 1,
                                 ["s_xn", kwz], [kpz])
                        z, kz = zr.next()
                        p.act(z[:], pz[:], AF.Silu, [kpz], [kz])
                        p.dma('sp', c.Zs[blk * 128:(blk + 1) * 128, zc * 512:(zc + 1) * 512], z[:], [kz], [("Zs", blk, zc)])
            p.barrier()
            with contextlib.ExitStack() as es2:
                wcr = ring_sb(c, es2, "sC_wc", 3, [128, KC, 128], BF16)
                wl = WLoader(c, es2, "sC_wst", 1024)
                raw = ring_sb(c, es2, "sC_raw", 2, [128, L + 4], F32)
                acc = ring_sb(c, es2, "sC_acc", 2, [128, L], F32)
                cvT = ring_sb(c, es2, "sC_cvT", 2, [128, L], BF16)
                BT = sb(c, es2, "sC_BT", [128, L], BF16)
                CT = sb(c, es2, "sC_CT", [128, L], BF16)
                Btok = sb(c, es2, "sC_Btok", [128, NB, 128], BF16)
                xtok = sb(c, es2, "sC_xtok", [128, NB, 256], BF16)
                xdt = sb(c, es2, "sC_xdt", [128, NB, 2, 256], BF16)
                dta = sb(c, es2, "sC_dta", [128, NB, 2, 4], F32)
                acs = sb(c, es2, "sC_acs", [128, NB, 2, 4], F32)
                tot = sb(c, es2, "sC_tot", [128, NB, 2, 4], F32)
                ea = sb(c, es2, "sC_ea", [128, NB, 2, 4], F32)
                edec = sb(c, es2, "sC_edec", [128, NB, 2, 4], F32)
                etot = sb(c, es2, "sC_etot", [128, NB, 2, 4], F32)
                Y = sb(c, es2, "sC_Y", [128, NB, 256], F32)
                H = sb(c, es2, "sC_H", [128, 256], F32)
                Hb = sb(c, es2, "sC_Hb", [128, 256], BF16)
                cbm = ring_sb(c, es2, "sC_cbm", 2, [128, 2, 128], F32)
                rhsr = ring_sb(c, es2, "sC_rhs", 2, [128, 4, 128], F32)
                decr = ring_sb(c, es2, "sC_dec", 2, [128, 4, 128], F32)
                mtr = ring_sb(c, es2, "sC_mt", 2, [128, 4, 128], BF16)
                tmpr = ring_sb(c, es2, "sC_tmp", 2, [128, 256], F32)
                xdr = ring_sb(c, es2, "sC_xd", 2, [128, 256], BF16)
                pA = ring_ps(c, es2, "sC_pA", 2, [128, 512])
                pT = ring_ps(c, es2, "sC_pT", 2, [128, 1024], BF16)
                pC = ring_ps(c, es2, "sC_pC", 1, [128, 512])
                pY = ring_ps(c, es2, "sC_pY", 2, [128, 512])
                pH = ring_ps(c, es2, "sC_pH", 1, [128, 512])
                for tl, ktl in zip(raw.tiles, ["sC_raw0", "sC_raw1"]):
                    p.memset('pool', tl[:, 0:2], 0.0, [ktl])
                    p.memset('pool', tl[:, L + 2:L + 4], 0.0, [ktl])
                for gq in range(8):
                    for ci, cc in enumerate([2 * gq, 2 * gq + 1, 16 + gq, 24 + gq]):
                        wc, kwc = wcr.next()
                        col0 = 2048 + cc * 128
                        wl.load(wc[:], w_in[:, col0:col0 + 128].rearrange("(kc p) c -> p kc c", p=128), kwc)
                        rw, krw = raw.next()
                        for tg in range(L // 512):
                            pa, kpa = pA.next()
                            for kc in range(KC):
                                p.mm(pa[:], wc[:, kc, :], xn[:, kc, tg * 512:(tg + 1) * 512], kc == 0, kc == KC - 1,
                                     [kwc, "s_xn"], [kpa])
                            p.copy('act', rw[:, 2 + tg * 512:2 + (tg + 1) * 512], pa[:], [kpa], [krw])
                        ac, kac = acc.next()
                        p.ts('dve', ac[:], rw[:, 0:L], cw[:, 0, cc:cc + 1], cb[:, cc:cc + 1], ALU.mult, ALU.add,
                             [krw, "s_cw", "s_cb"], [kac])
                        for tap in range(1, 5):
                            p.stt(ac[:], rw[:, tap:tap + L], cw[:, tap, cc:cc + 1], ac[:], ALU.mult, ALU.add,
                                  [krw, "s_cw", kac], [kac])
                        if ci < 2:
                            cv, kcv = cvT.next()
                            p.act(cv[:], ac[:], AF.Silu, [kac], [kcv])
                            for b4 in range(NB // 8):
                                pt, kpt = pT.next()
                                for q in range(8):
                                    blk = b4 * 8 + q
                                    p.tr(pt[:, q * 128:(q + 1) * 128], cv[:, blk * 128:(blk + 1) * 128], c.identb[:], [kcv], [kpt])
                                p.copy('act' if b4 % 2 else 'dve', xtok[:, b4 * 8:(b4 + 1) * 8, ci * 128:(ci + 1) * 128],
                                       pt[:].rearrange("p (a b) -> p a b", a=8), [kpt], ["sC_xtok"])
                        elif ci == 2:
                            p.act(BT[:], ac[:], AF.Silu, [kac], ["sC_BT"])
                            for b4 in range(NB // 8):
                                pt, kpt = pT.next()
                                for q in range(8):
                                    blk = b4 * 8 + q
                                    p.tr(pt[:, q * 128:(q + 1) * 128], BT[:, blk * 128:(blk + 1) * 128], c.identb[:], ["sC_BT"], [kpt])
                                p.copy('act' if b4 % 2 else 'dve', Btok[:, b4 * 8:(b4 + 1) * 8, :],
                                       pt[:].rearrange("p (a b) -> p a b", a=8), [kpt], ["sC_Btok"])
                        else:
                            p.act(CT[:], ac[:], AF.Silu, [kac], ["sC_CT"])
                    for d in range(2):
                        c0 = d * 32 + gq * 4
                        p.tt('dve', dta[:, :, d, :], dt[:, :, c0:c0 + 4], bc(aneg[:, c0:c0 + 4], [128, NB, 4], 1), ALU.mult,
                             ["s_dt", "s_aneg"], ["sC_dta"])
                        p.tt('dve', xdt[:, :, d, :].rearrange("p b (h q) -> p b h q", h=4),
                             xtok[:].rearrange("p b (h q) -> p b h q", h=4),
                             bc(dt[:, :, c0:c0 + 4], [128, NB, 4, 64], 3), ALU.mult, ["sC_xtok", "s_dt"], ["sC_xdt"])
                    pc, kpc = pC.next()
                    for d in range(2):
                        msk = c.mk["LE"] if d == 0 else c.mk["GE"]
                        for blk in range(NB):
                            p.mm(pc[:, (blk * 2 + d) * 4:(blk * 2 + d) * 4 + 4], msk[:], dta[:, blk, d, :], True, True,
                                 ["sC_dta", "mk"], [kpc])
                    p.copy('dve', acs[:].rearrange("p b d h -> p (b d h)"), pc[:, :NB * 8], [kpc], ["sC_acs"])
                    pc, kpc = pC.next()
                    p.mm(pc[:, :NB * 8], c.ones[:], dta[:].rearrange("p b d h -> p (b d h)"), True, True, ["sC_dta"], [kpc])
                    p.copy('dve', tot[:].rearrange("p b d h -> p (b d h)"), pc[:, :NB * 8], [kpc], ["sC_tot"])
                    p.act(ea[:].rearrange("p b d h -> p (b d h)"), acs[:].rearrange("p b d h -> p (b d h)"), AF.Exp, ["sC_acs"], ["sC_ea"])
                    p.act(etot[:].rearrange("p b d h -> p (b d h)"), tot[:].rearrange("p b d h -> p (b d h)"), AF.Exp, ["sC_tot"], ["sC_etot"])
                    p.tt('dve', edec[:].rearrange("p b d h -> p (b d h)"), tot[:].rearrange("p b d h -> p (b d h)"),
                         acs[:].rearrange("p b d h -> p (b d h)"), ALU.subtract, ["sC_tot", "sC_acs"], ["sC_edec"])
                    p.act(edec[:].rearrange("p b d h -> p (b d h)"), edec[:].rearrange("p b d h -> p (b d h)"), AF.Exp, ["sC_edec"], ["sC_edec"])
                    for d in range(2):
                        order = range(NB) if d == 0 else range(NB - 1, -1, -1)
                        mk_in, mk_l, mk_r = (("LE", "GT", "LE") if d == 0 else ("GE", "LT", "GE"))
                        for ci, blk in enumerate(order):
                            tsl = slice(blk * 128, (blk + 1) * 128)
                            first = ci == 0
                            pc, kpc = pC.next()
                            p.mm(pc[:, :128], BT[:, tsl], CT[:, tsl], True, True, ["sC_BT", "sC_CT"], [kpc])
                            cm_, kcm = cbm.next()
                            p.tt('dve', cm_[:, 0, :], pc[:, :128], c.mk[mk_in][:], ALU.mult, [kpc, "mk"], [kcm])
                            rh, krh = rhsr.next()
                            p.tt('dve', rh[:], bc(c.mk[mk_r][:], [128, 4, 128], 1), bc(dta[:, blk, d, :], [128, 4, 128], 2), ALU.mult,
                                 ["mk", "sC_dta"], [krh])
                            pa, kpa = pA.next()
                            p.mm(pa[:], c.mk[mk_l][:], rh[:].rearrange("p h l -> p (h l)"), True, True, [krh, "mk"], [kpa])
                            dc_, kdc = decr.next()
                            p.act(dc_[:].rearrange("p h l -> p (h l)"), pa[:], AF.Exp, [kpa], [kdc])
                            mt, kmt = mtr.next()
                            p.tt('dve', mt[:], dc_[:], bc(cm_[:, 0, :], [128, 4, 128], 1), ALU.mult, [kdc, kcm], [kmt])
                            py, kpy = pY.next()
                            for h in range(4):
                                p.mm(py[:, h * 64:(h + 1) * 64], mt[:, h, :], xdt[:, blk, d, h * 64:(h + 1) * 64], True, True,
                                     [kmt, "sC_xdt"], [kpy])
                            if not first:
                                p.mm(py[:, 256:512], CT[:, tsl], Hb[:], True, True, ["sC_CT", "sC_Hb"], [kpy])
                                tm, ktm = tmpr.next()
                                p.tt('dve', tm[:].rearrange("p (h q) -> p h q", h=4), py[:, 256:512].rearrange("p (h q) -> p h q", h=4),
                                     bc(ea[:, blk, d, :], [128, 4, 64], 2), ALU.mult, [kpy, "sC_ea"], [ktm])
                                if d == 0:
                                    p.tt('dve', Y[:, blk, :], tm[:], py[:, 0:256], ALU.add, [ktm, kpy], [("sC_Y", blk)])
                                else:
                                    p.tt('dve', tm[:], tm[:], py[:, 0:256], ALU.add, [ktm, kpy], [ktm])
                                    p.tt('dve', Y[:, blk, :], Y[:, blk, :], tm[:], ALU.add, [ktm, ("sC_Y", blk)], [("sC_Y", blk)])
                            else:
                                if d == 0:
                                    p.copy('dve', Y[:, blk, :], py[:, 0:256], [kpy], [("sC_Y", blk)])
                                else:
                                    p.tt('dve', Y[:, blk, :], Y[:, blk, :], py[:, 0:256], ALU.add, [kpy, ("sC_Y", blk)], [("sC_Y", blk)])
                            if ci < NB - 1:
                                xd, kxd = xdr.next()
                                p.tt('dve', xd[:].rearrange("p (h q) -> p h q", h=4),
                                     xdt[:, blk, d, :].rearrange("p (h q) -> p h q", h=4),
                                     bc(edec[:, blk, d, :], [128, 4, 64], 2), ALU.mult, ["sC_xdt", "sC_edec"], [kxd])
                                ph, kph = pH.next()
                                p.mm(ph[:, :256], Btok[:, blk, :], xd[:], True, True, ["sC_Btok", kxd], [kph])
                                if first:
                                    p.copy('dve', H[:], ph[:, :256], [kph], ["sC_H"])
                                else:
                                    p.tt('dve', H[:].rearrange("p (h q) -> p h q", h=4), H[:].rearrange("p (h q) -> p h q", h=4),
                                         bc(etot[:, blk, d, :], [128, 4, 64], 2), ALU.mult, ["sC_H", "sC_etot"], ["sC_H"])
                                    p.tt('dve', H[:], H[:], ph[:, :256], ALU.add, ["sC_H", kph], ["sC_H"])
                                p.copy('act', Hb[:], H[:], ["sC_H"], ["sC_Hb"])
                    for blk in range(NB):
                        tm, ktm = tmpr.next()
                        p.tt('dve', tm[:].rearrange("p (h q) -> p h q", h=4), xtok[:, blk, :].rearrange("p (h q) -> p h q", h=4),
                             bc(dsk[:, gq * 4:gq * 4 + 4], [128, 4, 64], 2), ALU.mult, ["sC_xtok", "s_dsk"], [ktm])
                        p.tt('dve', Y[:, blk, :], Y[:, blk, :], tm[:], ALU.add, [ktm, ("sC_Y", blk)], [("sC_Y", blk)])
                    p.dma('sp', c.Ys[:, gq * 256:(gq + 1) * 256].rearrange("(b p) q -> p b q", p=128), Y[:],
                          [("sC_Y", blk) for blk in range(NB)], [("Ys", gq)])
            p.barrier()
            with contextlib.ExitStack() as es2:
                wo = sb(c, es2, "sD_wo", [128, 16, D], BF16)
                yr = ring_sb(c, es2, "sD_y", 2, [128, 2048], F32)
                zr = ring_sb(c, es2, "sD_z", 2, [128, 2048], F32)
                junk = sb(c, es2, "sD_junk", [128, 2048], BF16)
                yn = ring_sb(c, es2, "sD_yn", 2, [128, 2048], BF16)
                st = ring_sb(c, es2, "sD_st", 2, [128, 4], F32)
                ynT = sb(c, es2, "sD_ynT", [128, 16, 512], BF16)
                xr = ring_sb(c, es2, "sD_x", 1, [128, KC, 512], F32)
                xo = sb(c, es2, "sD_xo", [128, KC, 512], F32)
                pT = ring_ps(c, es2, "sD_pT", 2, [128, 1024], BF16)
                pyr = ring_ps(c, es2, "sD_py", 2, [128, 512])
                wl = WLoader(c, es2, "sD_wst", 1024)
                for cc in range(16):
                    wl.load(wo[:, cc, :], w_out[cc * 128:(cc + 1) * 128, :], "sD_wo")
                for gi in range(L // 512):
                    tok0 = s * L + gi * 512
                    xi, kx = xr.next()
                    p.dma('sp', xi[:], xt_view(c, tok0, 512), xt_keys(tok0, 512), [kx])
                    for qb in range(4):
                        blk = gi * 4 + qb
                        y, ky = yr.next()
                        z, kz = zr.next()
                        p.dma('sp', y[:], c.Ys[blk * 128:(blk + 1) * 128, :], [("Ys", q) for q in range(8)], [ky])
                        p.dma('sp', z[:], c.Zs[blk * 128:(blk + 1) * 128, :], [("Zs", blk, q) for q in range(4)], [kz])
                        p.tt('dve', y[:], y[:], z[:], ALU.mult, [ky, kz], [ky])
                        s_, ks = st.next()
                        p.op('act', lambda e, y=y, s_=s_: e.activation(junk[:], y[:], AF.Square, accum_out=s_[:, 0:1]),
                             [ky], ["sD_junk", ks])
                        p.ts('dve', s_[:, 1:2], s_[:, 0:1], 1.0 / 2048, 1e-6, ALU.mult, ALU.add, [ks], [ks])
                        p.tt('pool', s_[:, 2:3], s_[:, 1:2], c.mhalf[:, 0:1], ALU.pow, [ks], [ks])
                        yn_, kyn = yn.next()
                        p.stt(yn_[:], y[:], s_[:, 2:3], nw[:], ALU.mult, ALU.mult, [ky, ks, "s_nw"], [kyn])
                        for b2 in range(2):
                            pt, kpt = pT.next()
                            for q in range(8):
                                cc = b2 * 8 + q
                                p.tr(pt[:, q * 128:(q + 1) * 128], yn_[:, cc * 128:(cc + 1) * 128], c.identb[:], [kyn], [kpt])
                            p.copy('act' if b2 else 'dve', ynT[:, b2 * 8:(b2 + 1) * 8, qb * 128:(qb + 1) * 128],
                                   pt[:].rearrange("p (a b) -> p a b", a=8), [kpt], ["sD_ynT"])
                    for dc in range(KC):
                        py, kpy = pyr.next()
                        for cc in range(16):
                            p.mm(py[:], wo[:, cc, dc * 128:(dc + 1) * 128], ynT[:, cc, :], cc == 0, cc == 15, ["sD_wo", "sD_ynT"], [kpy])
                        p.tt('dve', xo[:, dc, :], py[:], xi[:, dc, :], ALU.add, [kpy, kx], ["sD_xo"])
                    p.dma('sp', xt_view(c, tok0, 512), xo[:], ["sD_xo"], xt_keys(tok0, 512))
            p.barrier()


def conv_chunk(c, wc_ap, kwc, xn, cw, cb, cc, rw, krw, ac, kac, pA, keypre):
    p = c.p
    for tg in range(L // 512):
        pa, kpa = pA.next()
        for kc in range(KC):
            p.mm(pa[:], wc_ap[:, kc, :], xn[:, kc, tg * 512:(tg + 1) * 512], kc == 0, kc == KC - 1, [kwc, keypre + "xn"], [kpa])
        p.copy('act', rw[:, 2 + tg * 512:2 + (tg + 1) * 512], pa[:], [kpa], [krw])
    p.ts('dve', ac[:], rw[:, 0:L], cw[:, 0, cc:cc + 1], cb[:, cc:cc + 1], ALU.mult, ALU.add, [krw, keypre + "cw", keypre + "cb"], [kac])
    for tap in range(1, 5):
        p.stt(ac[:], rw[:, tap:tap + L], cw[:, tap, cc:cc + 1], ac[:], ALU.mult, ALU.add, [krw, keypre + "cw", kac], [kac])


def neumann_inverse_T(c, units, nlev):
    p = c.p
    for u in units:
        p.tr(u['pN'][:, 384:512], u['Mt'][0][:], c.ident[:], [u['k'] + "Mt0"], [u['k'] + "pN"])
        p.copy('act', u['M'][0][:], u['pN'][:, 384:512], [u['k'] + "pN"], [u['k'] + "M0"])
        p.tt('dve', u['Y'][0][:], u['Mt'][0][:], c.ident[:], ALU.add, [u['k'] + "Mt0"], [u['k'] + "Y0"])
    for r in range(nlev + 1):
        a, b = r % 2, (r + 1) % 2
        sq = r < nlev
        sqt = r < nlev - 1
        yp = r >= 1
        for u in units:
            kk = u['k']
            if sq:
                p.mm(u['pN'][:, 0:128], u['Mt'][a][:], u['M'][a][:], True, True, [kk + f"Mt{a}", kk + f"M{a}"], [kk + "pN"])
            if sqt:
                p.mm(u['pN'][:, 128:256], u['M'][a][:], u['Mt'][a][:], True, True, [kk + f"Mt{a}", kk + f"M{a}"], [kk + "pN"])
            if yp:
                p.mm(u['pN'][:, 256:384], u['M'][a][:], u['Y'][b][:], True, True, [kk + f"M{a}", kk + f"Y{b}"], [kk + "pN"])
        for u in units:
            kk = u['k']
            if sq:
                p.copy('act', u['M'][b][:], u['pN'][:, 0:128], [kk + "pN"], [kk + f"M{b}"])
            if sqt:
                p.copy('dve', u['Mt'][b][:], u['pN'][:, 128:256], [kk + "pN"], [kk + f"Mt{b}"])
            if yp:
                if r == nlev:
                    p.tt('dve', u['XT'][:], u['Y'][b][:], u['pN'][:, 256:384], ALU.add, [kk + f"Y{b}", kk + "pN"], [kk + "XT"])
                else:
                    p.tt('dve', u['Y'][a][:], u['Y'][b][:], u['pN'][:, 256:384], ALU.add, [kk + f"Y{b}", kk + "pN"], [kk + f"Y{a}"])


def stage_gdn(c, layer, j, nseq):
    p = c.p
    W = c.W
    w_in, conv_w, conv_b = W['gdn_w_in'][j], W['gdn_conv_w'][j], W['gdn_conv_b'][j]
    a_log, dt_bias, norm_w, w_out = W['gdn_a_log'][j], W['gdn_dt_bias'][j], W['gdn_norm'][j], W['gdn_w_out'][j]
    gvec = W['mix_norm'][layer]
    NB = L // 128
    with contextlib.ExitStack() as es:
        g = sb(c, es, "g_g", [128, KC], F32)
        cw = sb(c, es, "g_cw", [128, 5, 32], F32)
        cb = sb(c, es, "g_cb", [128, 32], F32)
        dtb = sb(c, es, "g_dtb", [128, 32], F32)
        aneg = sb(c, es, "g_aneg", [128, 32], F32)
        nw = sb(c, es, "g_nw", [128, 128], F32)
        xn = sb(c, es, "g_xn", [128, KC, L], BF16)
        bga = sb(c, es, "g_bga", [128, NB, 48], F32)
        load_vec(c, g[:], gvec, "g_g")
        for tap in range(5):
            p.dma('sp', cw[:, tap, :], conv_w[tap].rearrange("(cc p) -> p cc", p=128), [], ["g_cw"], allow_slow_non_contiguous=True)
        p.dma('sp', cb[:], conv_b.rearrange("(cc p) -> p cc", p=128), [], ["g_cb"], allow_slow_non_contiguous=True)
        p.dma('sp', dtb[:], dt_bias.rearrange("a b -> (a b)").partition_broadcast(128), [], ["g_dtb"])
        p.dma('sp', aneg[:], a_log.rearrange("a b -> (a b)").partition_broadcast(128), [], ["g_aneg"])
        p.dma('sp', nw[:], norm_w.partition_broadcast(128), [], ["g_nw"])
        p.act(aneg[:], aneg[:], AF.Exp, ["g_aneg"], ["g_aneg"])
        p.ts('dve', aneg[:], aneg[:], -1.0, None, ALU.mult, None, ["g_aneg"], ["g_aneg"])
        p.barrier()
        for s in range(nseq):
            with contextlib.ExitStack() as es2:
                NT = 256
                xr = ring_sb(c, es2, "gA_x", 2, [128, KC, NT], F32)
                sq = sb(c, es2, "gA_sq", [128, KC, NT], F32)
                var = sb(c, es2, "gA_var", [128, NT], F32)
                rstd = sb(c, es2, "gA_rstd", [128, NT], F32)
                pss = ps(c, es2, "gA_pss", [128, 512])
                for gi in range(L // NT):
                    load_xn(c, s * L + gi * NT, NT, xr, sq, pss, var, rstd, g, "g_g", xn[:, :, gi * NT:(gi + 1) * NT], "g_xn", "gA_")
            p.barrier()
            with contextlib.ExitStack() as es2:
                wzr = ring_sb(c, es2, "gB_wz", 2, [128, KC, 512], BF16)
                wdt = sb(c, es2, "gB_wdt", [128, KC, 48], BF16)
                zr = ring_sb(c, es2, "gB_z", 3, [128, 512], F32)
                pzr = ring_ps(c, es2, "gB_pz", 4, [128, 512])
                pdr = ring_ps(c, es2, "gB_pd", 2, [128, 512])
                wl = WLoader(c, es2, "gB_wst", 4096)
                wl.load(wdt[:], w_in[:, 6144:6192].rearrange("(kc p) c -> p kc c", p=128), "gB_wdt")
                for blk in range(NB):
                    pd, kpd = pdr.next()
                    for kc in range(KC):
                        p.mm(pd[:, :48], xn[:, kc, blk * 128:(blk + 1) * 128], wdt[:, kc, :], kc == 0, kc == KC - 1,
                             ["g_xn", "gB_wdt"], [kpd])
                    p.copy('dve', bga[:, blk, 0:16], pd[:, 0:16], [kpd], ["g_bga"])
                    p.tt('dve', bga[:, blk, 16:48], pd[:, 16:48], dtb[:], ALU.add, [kpd, "g_dtb"], ["g_bga"])
                p.act(bga[:, :, 0:16], bga[:, :, 0:16], AF.Exp, ["g_bga"], ["g_bga"], scale=-1.0)
                p.ts('dve', bga[:, :, 0:16], bga[:, :, 0:16], 1.0, None, ALU.add, None, ["g_bga"], ["g_bga"])
                p.op('dve', lambda e: e.reciprocal(bga[:, :, 0:16], bga[:, :, 0:16]), ["g_bga"], ["g_bga"])
                p.act(bga[:, :, 16:48], bga[:, :, 16:48], AF.Exp, ["g_bga"], ["g_bga"])
                p.act(bga[:, :, 16:48], bga[:, :, 16:48], AF.Ln, ["g_bga"], ["g_bga"], bias=1.0)
                p.tt('dve', bga[:, :, 16:48], bga[:, :, 16:48], bc(aneg[:], [128, NB, 32], 1), ALU.mult, ["g_bga", "g_aneg"], ["g_bga"])
                for zc in range(4):
                    wz, kwz = wzr.next()
                    wl.load(wz[:], w_in[:, 4096 + zc * 512:4096 + (zc + 1) * 512].rearrange("(kc p) c -> p kc c", p=128), kwz)
                    for blk in range(NB):
                        pz, kpz = pzr.next()
                        for kc in range(KC):
                            p.mm(pz[:], xn[:, kc, blk * 128:(blk + 1) * 128], wz[:, kc, :], kc == 0, kc == KC - 1,
                                 ["g_xn", kwz], [kpz])
                        z, kz = zr.next()
                        p.act(z[:], pz[:], AF.Silu, [kpz], [kz])
                        p.dma('sp', c.Zs[blk * 128:(blk + 1) * 128, zc * 512:(zc + 1) * 512], z[:], [kz], [("Zs", blk, zc)])
            p.barrier()
            with contextlib.ExitStack() as es2:
                wcr = ring_sb(c, es2, "gC_wc", 3, [128, KC, 128], BF16)
                wl = WLoader(c, es2, "gC_wst", 1024)
                raw = ring_sb(c, es2, "gC_raw", 2, [128, L + 4], F32)
                acc = ring_sb(c, es2, "gC_acc", 2, [128, L], F32)
                t32a = sb(c, es2, "gC_t32a", [128, L], F32)
                t32b = sb(c, es2, "gC_t32b", [128, L], F32)
                cvT = ring_sb(c, es2, "gC_cvT", 2, [128, L], BF16)
                QhT = sb(c, es2, "gC_QhT", [128, L], BF16)
                KhT = sb(c, es2, "gC_KhT", [128, L], BF16)
                Ktok = sb(c, es2, "gC_Ktok", [128, NB, 128], BF16)
                Vtok = sb(c, es2, "gC_Vtok", [128, NB, 256], BF16)
                KKT = sb(c, es2, "gC_KKT", [128, NB, 128], F32)
                QKT = sb(c, es2, "gC_QKT", [128, NB, 128], F32)
                gq = sb(c, es2, "gC_gq", [128, NB, 2, 2], F32)
                G = sb(c, es2, "gC_G", [128, NB, 2, 2], F32)
                tot = sb(c, es2, "gC_tot", [128, NB, 2, 2], F32)
                eG = sb(c, es2, "gC_eG", [128, NB, 2, 2], F32)
                neG = sb(c, es2, "gC_neG", [128, NB, 2, 2], F32)
                edec = sb(c, es2, "gC_edec", [128, NB, 2, 2], F32)
                etot = sb(c, es2, "gC_etot", [128, NB, 2, 2], F32)
                nbeta = sb(c, es2, "gC_nbeta", [128, NB, 2], F32)
                O = sb(c, es2, "gC_O", [128, NB, 256], F32)
                units = []
                for u in range(4):
                    ud = {'k': f"gU{u}_", 'd': u // 2, 'e': u % 2}
                    ud['M'] = [sb(c, es2, f"gC_M{u}{i}", [128, 128], F32) for i in range(2)]
                    ud['Mt'] = [sb(c, es2, f"gC_Mt{u}{i}", [128, 128], F32) for i in range(2)]
                    ud['Y'] = [sb(c, es2, f"gC_Y{u}{i}", [128, 128], F32) for i in range(2)]
                    ud['XT'] = sb(c, es2, f"gC_XT{u}", [128, 128], BF16)
                    ud['S'] = sb(c, es2, f"gC_S{u}", [128, 128], F32)
                    ud['Sb'] = sb(c, es2, f"gC_Sb{u}", [128, 128], BF16)
                    ud['AT'] = sb(c, es2, f"gC_AT{u}", [128, 128], BF16)
                    ud['R'] = sb(c, es2, f"gC_R{u}", [128, 128], BF16)
                    ud['Vn'] = sb(c, es2, f"gC_Vn{u}", [128, 128], BF16)
                    ud['Kd'] = sb(c, es2, f"gC_Kd{u}", [128, 128], BF16)
                    ud['tmp'] = sb(c, es2, f"gC_tmp{u}", [128, 128], F32)
                    ud['pN'] = ps(c, es2, f"gC_pN{u}", [128, 512])
                    PSUM_KEYS.add(ud['k'] + "pN")
                    units.append(ud)
                rhsd = [sb(c, es2, f"gC_rhs{d}", [128, 2, 128], F32) for d in range(2)]
                decd = [sb(c, es2, f"gC_dec{d}", [128, 2, 128], F32) for d in range(2)]
                decm = [sb(c, es2, f"gC_decm{d}", [128, 2, 128], F32) for d in range(2)]
                decs = [sb(c, es2, f"gC_decs{d}", [128, 2, 128], F32) for d in range(2)]
                pSeg = ps(c, es2, "gC_pSeg", [128, 512])
                PSUM_KEYS.add("gC_pSeg")
                PSUM_KEYS.add("gC_pM0")
                pA = ring_ps(c, es2, "gC_pA", 2, [128, 512])
                pT = ps(c, es2, "gC_pT", [128, 1024], BF16) if False else None
                for tl, ktl in zip(raw.tiles, ["gC_raw0", "gC_raw1"]):
                    p.memset('pool', tl[:, 0:2], 0.0, [ktl])
                    p.memset('pool', tl[:, L + 2:L + 4], 0.0, [ktl])
                slot_i = [0]

                pM = ring_ps(c, es2, "gC_pM", 1, [128, 512])

                def slot():
                    i = slot_i[0] % 3
                    slot_i[0] += 1
                    if i < 2:
                        return pA.tiles[i][:, 0:128], pA.name + str(i)
                    return pM.tiles[0][:, 0:128], "gC_pM0"

                def tslot():
                    return slot()

                for hk in range(8):
                    for ci, cc in enumerate([hk, 8 + hk, 16 + 2 * hk, 17 + 2 * hk]):
                        wc, kwc = wcr.next()
                        wl.load(wc[:], w_in[:, cc * 128:(cc + 1) * 128].rearrange("(kc p) c -> p kc c", p=128), kwc)
                        rw, krw = raw.next()
                        ac, kac = acc.next()
                        conv_chunk(c, wc, kwc, xn, cw, cb, cc, rw, krw, ac, kac, pA, "g_")
                        if ci < 2:
                            p.act(t32a[:], ac[:], AF.Silu, [kac], ["gC_t32a"])
                            p.act(t32b[:], t32a[:], AF.Square, ["gC_t32a"], ["gC_t32b"])
                            for tg in range(L // 512):
                                pa, kpa = pA.next()
                                p.mm(pa[:], c.ones[:], t32b[:, tg * 512:(tg + 1) * 512], True, True, ["gC_t32b"], [kpa])
                                p.ts('dve', ac[:, tg * 512:(tg + 1) * 512], pa[:], 1e-6, None, ALU.add, None, [kpa], [kac])
                            p.act(ac[:], ac[:], AF.Ln, [kac], [kac])
                            p.act(ac[:], ac[:], AF.Exp, [kac], [kac], scale=-0.5)
                            dstT, kd = (QhT, "gC_QhT") if ci == 0 else (KhT, "gC_KhT")
                            scl = 128.0 ** -0.5 if ci == 0 else 1.0
                            p.stt(dstT[:], t32a[:], scl, ac[:], ALU.mult, ALU.mult, ["gC_t32a", kac], [kd])
                            if ci == 1:
                                for blk in range(NB):
                                    sl, ksl = slot()
                                    p.mm(sl, KhT[:, blk * 128:(blk + 1) * 128], c.identb[:], True, True, ["gC_KhT"], [ksl])
                                    p.copy('act' if blk % 2 else 'dve', Ktok[:, blk, :], sl, [ksl], ["gC_Ktok"])
                        else:
                            cv, kcv = cvT.next()
                            p.act(cv[:], ac[:], AF.Silu, [kac], [kcv])
                            for blk in range(NB):
                                sl, ksl = slot()
                                p.mm(sl, cv[:, blk * 128:(blk + 1) * 128], c.identb[:], True, True, [kcv], [ksl])
                                p.copy('act' if blk % 2 else 'dve', Vtok[:, blk, (ci - 2) * 128:(ci - 1) * 128], sl, [ksl], ["gC_Vtok"])
                    for d in range(2):
                        p.copy('dve', gq[:, :, d, :], bga[:, :, 16 + d * 16 + 2 * hk:16 + d * 16 + 2 * hk + 2], ["g_bga"], ["gC_gq"])
                    p.ts('dve', nbeta[:], bga[:, :, 2 * hk:2 * hk + 2], -1.0, None, ALU.mult, None, ["g_bga"], ["gC_nbeta"])
                    pa, kpa = pA.next()
                    for d in range(2):
                        msk = c.mk["LE"] if d == 0 else c.mk["GE"]
                        for blk in range(NB):
                            p.mm(pa[:, (blk * 2 + d) * 2:(blk * 2 + d) * 2 + 2], msk[:], gq[:, blk, d, :], True, True, ["gC_gq", "mk"], [kpa])
                    p.copy('dve', G[:].rearrange("p b d h -> p (b d h)"), pa[:, :NB * 4], [kpa], ["gC_G"])
                    pa, kpa = pA.next()
                    p.mm(pa[:, :NB * 4], c.ones[:], gq[:].rearrange("p b d h -> p (b d h)"), True, True, ["gC_gq"], [kpa])
                    p.copy('dve', tot[:].rearrange("p b d h -> p (b d h)"), pa[:, :NB * 4], [kpa], ["gC_tot"])
                    fl = "p b d h -> p (b d h)"
                    p.act(eG[:].rearrange(fl), G[:].rearrange(fl), AF.Exp, ["gC_G"], ["gC_eG"])
                    p.ts('dve', neG[:].rearrange(fl), eG[:].rearrange(fl), -1.0, None, ALU.mult, None, ["gC_eG"], ["gC_neG"])
                    p.act(etot[:].rearrange(fl), tot[:].rearrange(fl), AF.Exp, ["gC_tot"], ["gC_etot"])
                    p.tt('dve', edec[:].rearrange(fl), tot[:].rearrange(fl), G[:].rearrange(fl), ALU.subtract, ["gC_tot", "gC_G"], ["gC_edec"])
                    p.act(edec[:].rearrange(fl), edec[:].rearrange(fl), AF.Exp, ["gC_edec"], ["gC_edec"])
                    for blk in range(NB):
                        tsl = slice(blk * 128, (blk + 1) * 128)
                        sl, ksl = slot()
                        p.mm(sl, KhT[:, tsl], KhT[:, tsl], True, True, ["gC_KhT"], [ksl])
                        p.copy('act', KKT[:, blk, :], sl, [ksl], ["gC_KKT"])
                        sl, ksl = slot()
                        p.mm(sl, KhT[:, tsl], QhT[:, tsl], True, True, ["gC_KhT", "gC_QhT"], [ksl])
                        p.copy('dve', QKT[:, blk, :], sl, [ksl], ["gC_QKT"])
                    for step in range(NB):
                        first = step == 0
                        lastc = step == NB - 1
                        blks = [step, NB - 1 - step]
                        for d in range(2):
                            blk = blks[d]
                            mk_l, mk_r, mk_i, mk_s = (("GT", "LE", "LE", "LT") if d == 0 else ("LT", "GE", "GE", "GT"))
                            p.tt('dve', rhsd[d][:], bc(c.mk[mk_r][:], [128, 2, 128], 1), bc(gq[:, blk, d, :], [128, 2, 128], 2), ALU.mult,
                                 ["mk", "gC_gq"], [f"gC_rhs{d}"])
                            p.mm(pSeg[:, d * 256:(d + 1) * 256], c.mk[mk_l][:], rhsd[d][:].rearrange("p h l -> p (h l)"), True, True,
                                 [f"gC_rhs{d}", "mk"], ["gC_pSeg"])
                            p.act(decd[d][:].rearrange("p h l -> p (h l)"), pSeg[:, d * 256:(d + 1) * 256], AF.Exp, ["gC_pSeg"], [f"gC_dec{d}"])
                            p.tt('dve', decm[d][:], decd[d][:], bc(c.mk[mk_i][:], [128, 2, 128], 1), ALU.mult, [f"gC_dec{d}", "mk"], [f"gC_decm{d}"])
                            p.tt('dve', decs[d][:], decd[d][:], bc(c.mk[mk_s][:], [128, 2, 128], 1), ALU.mult, [f"gC_dec{d}", "mk"], [f"gC_decs{d}"])
                        for u in units:
                            d, e, kk = u['d'], u['e'], u['k']
                            blk = blks[d]
                            p.tt('dve', u['AT'][:], QKT[:, blk, :], decm[d][:, e, :], ALU.mult, ["gC_QKT", f"gC_decm{d}"], [kk + "AT"])
                            p.stt(u['Mt'][0][:], KKT[:, blk, :], nbeta[:, blk, e:e + 1], decs[d][:, e, :], ALU.mult, ALU.mult,
                                  ["gC_KKT", "gC_nbeta", f"gC_decs{d}"], [kk + "Mt0"])
                        neumann_inverse_T(c, units, 6)
                        sls = {}
                        for u in units:
                            d, e, kk = u['d'], u['e'], u['k']
                            blk = blks[d]
                            tsl = slice(blk * 128, (blk + 1) * 128)
                            vt = Vtok[:, blk, e * 128:(e + 1) * 128]
                            if not first:
                                sl, ksl = slot()
                                p.mm(sl, KhT[:, tsl], u['Sb'][:], True, True, ["gC_KhT", kk + "Sb"], [ksl])
                                p.stt(u['R'][:], sl, neG[:, blk, d, e:e + 1], vt, ALU.mult, ALU.add, [ksl, "gC_neG", "gC_Vtok"], [kk + "R"])
                                rr = u['R'][:]
                                krr = kk + "R"
                            else:
                                rr = vt
                                krr = "gC_Vtok"
                            sl, ksl = slot()
                            p.mm(sl, u['XT'][:], rr, True, True, [kk + "XT", krr], [ksl])
                            p.ts('dve', u['Vn'][:], sl, bga[:, blk, 2 * hk + e:2 * hk + e + 1], None, ALU.mult, None, [ksl, "g_bga"], [kk + "Vn"])
                        for u in units:
                            d, e, kk = u['d'], u['e'], u['k']
                            blk = blks[d]
                            tsl = slice(blk * 128, (blk + 1) * 128)
                            okey = ("gC_O", blk, e)
                            sl2, ksl2 = slot()
                            p.mm(sl2, u['AT'][:], u['Vn'][:], True, True, [kk + "AT", kk + "Vn"], [ksl2])
                            oap = O[:, blk, e * 128:(e + 1) * 128]
                            if not first:
                                sl, ksl = slot()
                                p.mm(sl, QhT[:, tsl], u['Sb'][:], True, True, ["gC_QhT", kk + "Sb"], [ksl])
                                p.ts('dve', u['tmp'][:], sl, eG[:, blk, d, e:e + 1], None, ALU.mult, None, [ksl, "gC_eG"], [kk + "tmp"])
                                p.tt('dve', u['tmp'][:], u['tmp'][:], sl2, ALU.add, [kk + "tmp", ksl2], [kk + "tmp"])
                                src, ksrc = u['tmp'][:], kk + "tmp"
                            else:
                                src, ksrc = sl2, ksl2
                            first_visit = (d == 0 and blk < NB // 2) or (d == 1 and blk >= NB // 2)
                            if first_visit:
                                p.copy('dve', oap, src, [ksrc], [okey]) if not first else p.copy('dve', oap, src, [ksrc], [okey])
                            else:
                                p.tt('dve', oap, oap, src, ALU.add, [ksrc, okey], [okey]) if not first else p.tt('dve', oap, oap, src, ALU.add, [ksrc, okey], [okey])
                            if not lastc:
                                p.ts('dve', u['Kd'][:], Ktok[:, blk, :], edec[:, blk, d, e:e + 1], None, ALU.mult, None, ["gC_Ktok", "gC_edec"], [kk + "Kd"])
                                sl, ksl = slot()
                                p.mm(sl, u['Kd'][:], u['Vn'][:], True, True, [kk + "Kd", kk + "Vn"], [ksl])
                                if first:
                                    p.copy('dve', u['S'][:], sl, [ksl], [kk + "S"])
                                else:
                                    p.stt(u['S'][:], u['S'][:], etot[:, blk, d, e:e + 1], sl, ALU.mult, ALU.add, [kk + "S", "gC_etot", ksl], [kk + "S"])
                                p.copy('act', u['Sb'][:], u['S'][:], [kk + "S"], [kk + "Sb"])
                    p.dma('sp', c.Ys[:, hk * 256:(hk + 1) * 256].rearrange("(b p) q -> p b q", p=128), O[:],
                          [("gC_O", blk, e) for blk in range(NB) for e in range(2)], [("Ys", hk)])
            p.barrier()
            with contextlib.ExitStack() as es2:
                wo = sb(c, es2, "gD_wo", [128, 16, D], BF16)
                yr = ring_sb(c, es2, "gD_y", 2, [128, 2048], F32)
                zr = ring_sb(c, es2, "gD_z", 2, [128, 2048], F32)
                y2 = sb(c, es2, "gD_y2", [128, 2048], F32)
                yn = ring_sb(c, es2, "gD_yn", 2, [128, 2048], BF16)
                st = ring_sb(c, es2, "gD_st", 2, [128, 16], F32)
                ynT = sb(c, es2, "gD_ynT", [128, 16, 512], BF16)
                xr = ring_sb(c, es2, "gD_x", 1, [128, KC, 512], F32)
                xo = sb(c, es2, "gD_xo", [128, KC, 512], F32)
                pT = ring_ps(c, es2, "gD_pT", 2, [128, 1024], BF16)
                pyr = ring_ps(c, es2, "gD_py", 2, [128, 512])
                wl = WLoader(c, es2, "gD_wst", 1024)
                for cc in range(16):
                    wl.load(wo[:, cc, :], w_out[cc * 128:(cc + 1) * 128, :], "gD_wo")
                for gi in range(L // 512):
                    tok0 = s * L + gi * 512
                    xi, kx = xr.next()
                    p.dma('sp', xi[:], xt_view(c, tok0, 512), xt_keys(tok0, 512), [kx])
                    for qb in range(4):
                        blk = gi * 4 + qb
                        y, ky = yr.next()
                        z, kz = zr.next()
                        p.dma('sp', y[:], c.Ys[blk * 128:(blk + 1) * 128, :], [("Ys", q) for q in range(8)], [ky])
                        p.dma('sp', z[:], c.Zs[blk * 128:(blk + 1) * 128, :], [("Zs", blk, q) for q in range(4)], [kz])
                        p.act(y2[:], y[:], AF.Square, [ky], ["gD_y2"])
                        s_, ks = st.next()
                        p.op('dve', lambda e, s_=s_: e.reduce_sum(s_[:], y2[:].rearrange("p (h v) -> p h v", h=16), AX.X), ["gD_y2"], [ks])
                        p.ts('dve', s_[:], s_[:], 1.0 / 128, 1e-6, ALU.mult, ALU.add, [ks], [ks])
                        p.tt('pool', s_[:], s_[:], c.mhalf[:, 0:16], ALU.pow, [ks], [ks])
                        h3 = "p (h v) -> p h v"
                        p.tt('dve', y[:].rearrange(h3, h=16), y[:].rearrange(h3, h=16), bc(s_[:], [128, 16, 128], 2), ALU.mult, [ky, ks], [ky])
                        p.tt('dve', z[:].rearrange(h3, h=16), z[:].rearrange(h3, h=16), bc(nw[:], [128, 16, 128], 1), ALU.mult, [kz, "g_nw"], [kz])
                        yn_, kyn = yn.next()
                        p.tt('dve', yn_[:], y[:], z[:], ALU.mult, [ky, kz], [kyn])
                        for b2 in range(2):
                            pt, kpt = pT.next()
                            for q in range(8):
                                cc = b2 * 8 + q
                                p.tr(pt[:, q * 128:(q + 1) * 128], yn_[:, cc * 128:(cc + 1) * 128], c.identb[:], [kyn], [kpt])
                            p.copy('act' if b2 else 'dve', ynT[:, b2 * 8:(b2 + 1) * 8, qb * 128:(qb + 1) * 128],
                                   pt[:].rearrange("p (a b) -> p a b", a=8), [kpt], ["gD_ynT"])
                    for dc in range(KC):
                        py, kpy = pyr.next()
                        for cc in range(16):
                            p.mm(py[:], wo[:, cc, dc * 128:(dc + 1) * 128], ynT[:, cc, :], cc == 0, cc == 15, ["gD_wo", "gD_ynT"], [kpy])
                        p.tt('dve', xo[:, dc, :], py[:], xi[:, dc, :], ALU.add, [kpy, kx], ["gD_xo"])
                    p.dma('sp', xt_view(c, tok0, 512), xo[:], ["gD_xo"], xt_keys(tok0, 512))
            p.barrier()


def stage_rwkv(c, layer, j, nseq):
    p = c.p
    W = c.W
    gvec = W['mix_norm'][layer]
    x_mu, w_rkv, w0, w1, w2 = W['rwkv_x_mu'][j], W['rwkv_w_rkv'][j], W['rwkv_w0'][j], W['rwkv_w1'][j], W['rwkv_w2'][j]
    a0, a1, a2, g1, g2 = W['rwkv_a0'][j], W['rwkv_a1'][j], W['rwkv_a2'][j], W['rwkv_g1'][j], W['rwkv_g2'][j]
    k_k, k_a, r_k, lnx_w, lnx_b, w_out = (W['rwkv_k_k'][j], W['rwkv_k_a'][j], W['rwkv_r_k'][j], W['rwkv_lnx_w'][j],
                                            W['rwkv_lnx_b'][j], W['rwkv_w_out'][j])
    CH = 64
    NCH = L // CH
    RW = c.RW
    with contextlib.ExitStack() as es:
        g = sb(c, es, "r_g", [128, KC], F32)
        mu = sb(c, es, "r_mu", [128, 6, KC], F32)
        nw0 = sb(c, es, "r_nw0", [128, 2, KC], F32)
        na0 = sb(c, es, "r_na0", [128, KC], F32)
        kkv = sb(c, es, "r_kk", [128, KC], F32)
        kav = sb(c, es, "r_ka", [128, KC], F32)
        omka = sb(c, es, "r_omka", [128, KC], F32)
        rkv = sb(c, es, "r_rk", [128, KC], F32)
        lnw = sb(c, es, "r_lnw", [128, KC], F32)
        lnb = sb(c, es, "r_lnb", [128, KC], F32)
        onesbd = sb(c, es, "r_onesbd", [128, 128], F32)
        mS = [sb(c, es, f"r_mS{d}", [128, 64], F32) for d in range(2)]
        mSn = [sb(c, es, f"r_mSn{d}", [128, 64], F32) for d in range(2)]
        mI = [sb(c, es, f"r_mI{d}", [128, 64], F32) for d in range(2)]
        load_vec(c, g[:], gvec, "r_c")
        for s_ in range(6):
            load_vec(c, mu[:, s_, :], x_mu[s_], "r_c")
        for d in range(2):
            load_vec(c, nw0[:, d, :], w0[d], "r_c")
        for t_, src in [(na0, a0), (kkv, k_k), (kav, k_a), (rkv, r_k), (lnw, lnx_w), (lnb, lnx_b)]:
            load_vec(c, t_[:], src, "r_c")
        p.ts('dve', nw0[:], nw0[:], -1.0, None, ALU.mult, None, ["r_c"], ["r_c"])
        p.ts('dve', na0[:], na0[:], -1.0, None, ALU.mult, None, ["r_c"], ["r_c"])
        p.ts('dve', omka[:], kav[:], -1.0, 1.0, ALU.mult, ALU.add, ["r_c"], ["r_c"])
        p.memset('pool', onesbd[:], 0.0, ["r_c"])
        p.memset('pool', onesbd[0:64, 0:64], 1.0, ["r_c"])
        p.memset('pool', onesbd[64:128, 64:128], 1.0, ["r_c"])
        for d in range(2):
            ns, ni = (("LT", "LE") if d == 0 else ("GT", "GE"))
            for hs in range(2):
                sl_ = slice(hs * 64, (hs + 1) * 64)
                p.copy('pool', mS[d][sl_, :], c.mk[ns][sl_, hs * 64:(hs + 1) * 64], ["mk"], ["r_c"])
                p.copy('pool', mI[d][sl_, :], c.mk[ni][sl_, hs * 64:(hs + 1) * 64], ["mk"], ["r_c"])
            p.ts('dve', mSn[d][:], mS[d][:], -1.0, None, ALU.mult, None, ["r_c"], ["r_c"])
        p.barrier()
        for s in range(nseq):
            with contextlib.ExitStack() as es2:
                NT = 256
                u = sb(c, es2, "rA_u", [128, KC, L + 2], F32)
                xm = sb(c, es2, "rA_xm", [128, KC, L], BF16)
                pss = ps(c, es2, "rA_pss", [128, 512])
                pA = ring_ps(c, es2, "rA_pA", 3, [128, 512])
                with contextlib.ExitStack() as es3:
                    xr = ring_sb(c, es3, "rA_x", 2, [128, KC, NT], F32)
                    sq = sb(c, es3, "rA_sq", [128, KC, NT], F32)
                    var = sb(c, es3, "rA_var", [128, NT], F32)
                    rstd = sb(c, es3, "rA_rstd", [128, NT], F32)
                    p.memset('pool', u[:, :, 0:1], 0.0, ["rA_u"])
                    p.memset('pool', u[:, :, L + 1:L + 2], 0.0, ["rA_u"])
                    for gi in range(L // NT):
                        t0 = gi * NT
                        xi, kx = xr.next()
                        p.dma('sp', xi[:], xt_view(c, s * L + t0, NT), xt_keys(s * L + t0, NT), [kx])
                        rms_stats(c, xi, kx, sq, "rA_sq", pss, "rA_pss", var, "rA_var", rstd, "rA_rstd", NT)
                        for kc in range(KC):
                            p.stt(u[:, kc, 1 + t0:1 + t0 + NT], xi[:, kc, :], g[:, kc:kc + 1], rstd[:], ALU.mult, ALU.mult,
                                  [kx, "r_c", "rA_rstd"], ["rA_u"])

                p.barrier()
                t1 = sb(c, es2, "rA_t1", [128, L], F32)
                t2 = sb(c, es2, "rA_t2", [128, L], F32)
                ost = ring_sb(c, es2, "rA_ost", 2, [128, L], F32)
                wt = sb(c, es2, "rA_wt", [128, KC, 1024], BF16)
                wl1 = sb(c, es2, "rA_wl1", [128, KC, 128], BF16)
                wl2 = sb(c, es2, "rA_wl2", [128, 1024], BF16)
                lT = sb(c, es2, "rA_lT", [128, L], BF16)
                wl = WLoader(c, es2, "rA_wst", 1024)

                def build_xm(si):
                    for kc in range(KC):
                        p.tt('dve', t1[:], u[:, kc, 0:L], u[:, kc, 2:L + 2], ALU.add, ["rA_u"], ["rA_t1"])
                        p.stt(t2[:], t1[:], 0.5, u[:, kc, 1:L + 1], ALU.mult, ALU.subtract, ["rA_t1", "rA_u"], ["rA_t2"])
                        p.stt(xm[:, kc, :], t2[:], mu[:, si, kc:kc + 1], u[:, kc, 1:L + 1], ALU.mult, ALU.add,
                              ["rA_t2", "r_c", "rA_u"], ["rA_xm"])

                def sigmoid_from(o, pa, bias_ap, scale_out, ko, kpa):
                    if bias_ap is not None:
                        p.op('act', lambda e: e.activation(o, pa, AF.Exp, bias=bias_ap, scale=-1.0), [kpa, "r_c"], [ko])
                    else:
                        p.op('act', lambda e: e.activation(o, pa, AF.Exp, scale=-1.0), [kpa], [ko])
                    p.ts('dve', o, o, 1.0, None, ALU.add, None, [ko], [ko])
                    p.op('dve', lambda e: e.reciprocal(o, o), [ko], [ko])
                    if scale_out != 1.0:
                        p.ts('dve', o, o, scale_out, None, ALU.mult, None, [ko], [ko])

                for si in range(3):
                    build_xm(si)
                    for kc in range(KC):
                        wl.load(wt[:, kc, :], w_rkv[si][kc * 128:(kc + 1) * 128, :], "rA_wt")
                    for oc in range(KC):
                        o_, ko = ost.next()
                        for tg in range(L // 512):
                            pa, kpa = pA.next()
                            for kc in range(KC):
                                p.mm(pa[:], wt[:, kc, oc * 128:(oc + 1) * 128], xm[:, kc, tg * 512:(tg + 1) * 512], kc == 0, kc == KC - 1,
                                     ["rA_wt", "rA_xm"], [kpa])
                            p.copy('act' if tg % 2 else 'dve', o_[:, tg * 512:(tg + 1) * 512], pa[:], [kpa], [ko])
                        p.dma('sp', RW[si, oc * 128:(oc + 1) * 128, :], o_[:], [ko], [("RW", si, oc)])
                for si, nm in [(3, 'w0'), (3, 'w1'), (4, 'a'), (5, 'g')]:
                    if nm in ('w0', 'a', 'g'):
                        build_xm(si)
                    if nm[0] == 'w':
                        d = int(nm[1])
                        l1, l2, rank, dsti, bias_t = w1[d], w2[d], 64, 5 + d, nw0[:, d, :]
                    elif nm == 'a':
                        l1, l2, rank, dsti, bias_t = a1, a2, 64, 3, na0
                    else:
                        l1, l2, rank, dsti, bias_t = g1, g2, 128, 4, None
                    wl.load(wl1[:, :, :rank], l1.rearrange("(kc p) r -> p kc r", p=128), "rA_wl1")
                    wl.load(wl2[:rank, :], l2, "rA_wl2")
                    for tg in range(L // 512):
                        pa, kpa = pA.next()
                        for kc in range(KC):
                            p.mm(pa[:rank, :], wl1[:, kc, :rank], xm[:, kc, tg * 512:(tg + 1) * 512], kc == 0, kc == KC - 1,
                                 ["rA_wl1", "rA_xm"], [kpa])
                        dst_ = lT[:rank, tg * 512:(tg + 1) * 512]
                        if nm[0] == 'w':
                            p.act(dst_, pa[:rank, :], AF.Tanh, [kpa], ["rA_lT"])
                        elif nm == 'a':
                            p.copy('act', dst_, pa[:rank, :], [kpa], ["rA_lT"])
                        else:
                            p.op('act', lambda e, dst_=dst_, pa=pa: e.activation(t1[:, :512], pa[:, :], AF.Exp, scale=-1.0), [kpa], ["rA_t1"])
                            p.ts('dve', t1[:, :512], t1[:, :512], 1.0, None, ALU.add, None, ["rA_t1"], ["rA_t1"])
                            p.op('dve', lambda e: e.reciprocal(t1[:, :512], t1[:, :512]), ["rA_t1"], ["rA_t1"])
                            p.copy('dve', dst_, t1[:, :512], ["rA_t1"], ["rA_lT"])
                    for oc in range(KC):
                        o_, ko = ost.next()
                        for tg in range(L // 512):
                            pa, kpa = pA.next()
                            p.mm(pa[:], wl2[:rank, oc * 128:(oc + 1) * 128], lT[:rank, tg * 512:(tg + 1) * 512], True, True,
                                 ["rA_wl2", "rA_lT"], [kpa])
                            osl = o_[:, tg * 512:(tg + 1) * 512]
                            if nm[0] == 'w':
                                sigmoid_from(osl, pa[:], bias_t[:, oc:oc + 1], -0.6065306597126334, ko, kpa)
                            elif nm == 'a':
                                sigmoid_from(osl, pa[:], bias_t[:, oc:oc + 1], 1.0, ko, kpa)
                            else:
                                p.copy('act', osl, pa[:], [kpa], [ko])
                        p.dma('sp', RW[dsti, oc * 128:(oc + 1) * 128, :], o_[:], [ko], [("RW", dsti, oc)])
            p.barrier()
            with contextlib.ExitStack() as es2:
                yT = sb(c, es2, "rC_yT", [128, KC, L], BF16)
                pA = ring_ps(c, es2, "rC_pA", 3, [128, 512])
                es3 = contextlib.ExitStack()
                F = [sb(c, es3, f"rC_F{i}", [128, L], F32) for i in range(8)]
                kF = [f"rC_F{i}" for i in range(8)]
                AR = [sb(c, es3, f"rC_AR{d}", [128, NCH, 2, CH], BF16) for d in range(2)]
                Kt = [sb(c, es3, f"rC_Kt{d}", [128, L], BF16) for d in range(2)]
                Bt = [sb(c, es3, f"rC_Bt{d}", [128, L], BF16) for d in range(2)]
                Kd = sb(c, es3, "rC_Kd", [128, L], BF16)
                Bd = sb(c, es3, "rC_Bd", [128, L], BF16)
                Kdtok = [sb(c, es3, f"rC_Kdtok{d}", [128, NCH, CH], BF16) for d in range(2)]
                Bdtok = [sb(c, es3, f"rC_Bdtok{d}", [128, NCH, CH], BF16) for d in range(2)]
                PC = [sb(c, es3, f"rC_PC{d}", [128, NCH], F32) for d in range(2)]
                Vp = sb(c, es3, "rC_Vp", [128, NCH, CH], BF16)
                Ost = sb(c, es3, "rC_Ost", [128, NCH, CH], F32)
                Obf = sb(c, es3, "rC_Obf", [128, NCH, CH], BF16)
                st = sb(c, es3, "rC_st", [128, NCH, 4], F32)
                units = []
                for uu in range(2):
                    ud = {'k': f"rU{uu}_", 'd': uu}
                    ud['M'] = [sb(c, es3, f"rC_M{uu}{i}", [128, 128], F32) for i in range(2)]
                    ud['Mt'] = [sb(c, es3, f"rC_Mt{uu}{i}", [128, 128], F32) for i in range(2)]
                    ud['Y'] = [sb(c, es3, f"rC_Y{uu}{i}", [128, 128], F32) for i in range(2)]
                    ud['XT'] = sb(c, es3, f"rC_XT{uu}", [128, 128], BF16)
                    ud['AkT'] = sb(c, es3, f"rC_AkT{uu}", [128, 128], BF16)
                    ud['RkT'] = sb(c, es3, f"rC_RkT{uu}", [128, 128], BF16)
                    ud['RbT'] = sb(c, es3, f"rC_RbT{uu}", [128, 128], BF16)
                    ud['S'] = sb(c, es3, f"rC_S{uu}", [128, CH], F32)
                    ud['Sb'] = sb(c, es3, f"rC_Sb{uu}", [128, CH], BF16)
                    ud['R1'] = sb(c, es3, f"rC_R1{uu}", [128, CH], BF16)
                    ud['NU'] = sb(c, es3, f"rC_NU{uu}", [128, CH], BF16)
                    ud['pN'] = ps(c, es3, f"rC_pN{uu}", [128, 512])
                    PSUM_KEYS.add(ud['k'] + "pN")
                    ud['pX'] = ps(c, es3, f"rC_pX{uu}", [128, 512])
                    PSUM_KEYS.add(ud['k'] + "pX")
                    for nm_ in ['Mt', 'AkT', 'RkT', 'RbT']:
                        tl_ = ud[nm_][0] if nm_ == 'Mt' else ud[nm_]
                        p.memset('pool', tl_[:], 0.0, [ud['k'] + (nm_ + "0" if nm_ == 'Mt' else nm_)])
                    units.append(ud)
                pB = ring_ps(c, es3, "rC_pB", 1, [128, 1024], BF16)
                for hp in range(8):
                    rows = slice(hp * 128, (hp + 1) * 128)
                    cs3 = "p (n q) -> p n q"

                    def ld(dstF, idx):
                        p.dma('sp', F[dstF][:], RW[idx, rows, :], [("RW", idx, hp)], [kF[dstF]])

                    def onesbd_bcast(srcF, dstF, add_eps):
                        for tg in range(L // 512):
                            pa, kpa = pA.next()
                            p.mm(pa[:], onesbd[:], F[srcF][:, tg * 512:(tg + 1) * 512], True, True, [kF[srcF], "r_c"], [kpa])
                            if add_eps is not None:
                                p.ts('dve', F[dstF][:, tg * 512:(tg + 1) * 512], pa[:], add_eps, None, ALU.add, None, [kpa], [kF[dstF]])
                            else:
                                p.copy('dve', F[dstF][:, tg * 512:(tg + 1) * 512], pa[:], [kpa], [kF[dstF]])

                    ld(0, 1)
                    ld(1, 3)
                    p.ts('dve', F[6][:], F[0][:], kkv[:, hp:hp + 1], None, ALU.mult, None, [kF[0], "r_c"], [kF[6]])
                    p.act(F[7][:], F[6][:], AF.Square, [kF[6]], [kF[7]])
                    onesbd_bcast(7, 7, 1e-6) if False else None
                    onesbd_bcast(7, 2, 1e-6)
                    p.act(F[2][:], F[2][:], AF.Ln, [kF[2]], [kF[2]])
                    p.act(F[2][:], F[2][:], AF.Exp, [kF[2]], [kF[2]], scale=-0.5)
                    p.tt('dve', F[2][:], F[6][:], F[2][:], ALU.mult, [kF[6], kF[2]], [kF[2]])
                    p.ts('dve', F[6][:], F[1][:], kav[:, hp:hp + 1], omka[:, hp:hp + 1], ALU.mult, ALU.add, [kF[1], "r_c"], [kF[6]])
                    p.tt('dve', F[3][:], F[0][:], F[6][:], ALU.mult, [kF[0], kF[6]], [kF[3]])
                    p.tt('dve', F[4][:], F[2][:], F[1][:], ALU.mult, [kF[2], kF[1]], [kF[4]])
                    ld(5, 0)
                    ld(0, 2)
                    p.stt(F[6][:], F[5][:], rkv[:, hp:hp + 1], F[3][:], ALU.mult, ALU.mult, [kF[5], "r_c", kF[3]], [kF[6]])
                    onesbd_bcast(6, 7, None)
                    p.tt('dve', F[1][:], F[7][:], F[0][:], ALU.mult, [kF[7], kF[0]], [kF[1]])
                    p.copy('act', Kd[:], F[0][:], [kF[0]], ["rC_Kd"])

                    def to_stacked(srcT, ksrc, dst, kdst):
                        for c8 in range(NCH // 8):
                            pb, kpb = pB.next()
                            for hs in range(2):
                                sl_ = slice(hs * 64, (hs + 1) * 64)
                                for q in range(8):
                                    ch = c8 * 8 + q
                                    p.mm64(pA.tiles[0][sl_, q * 64:(q + 1) * 64], srcT[sl_, ch * CH:(ch + 1) * CH], c.identb[sl_, sl_], True, True,
                                         [ksrc], ["rC_pA0"])
                            p.copy('act' if c8 % 2 else 'dve', dst[:, c8 * 8:(c8 + 1) * 8, :],
                                   pA.tiles[0][:, :].rearrange("p (a b) -> p a b", a=8), ["rC_pA0"], [kdst])

                    to_stacked(Kd, "rC_Kd", Vp, "rC_Vp")
                    for d in range(2):
                        ld(0, 5 + d)
                        p.op('dve', lambda e: e.tensor_tensor_scan(F[6][:], c.ones[:, 0:1].broadcast_to([128, L]), F[0][:], 0.0, ALU.mult, ALU.add),
                             [kF[0]], [kF[6]])
                        cs = F[6][:].rearrange(cs3, q=CH)
                        lw = F[0][:].rearrange(cs3, q=CH)
                        li = F[7][:].rearrange(cs3, q=CH)
                        if d == 0:
                            p.tt('dve', st[:, :, 0:1], cs[:, :, 0:1], lw[:, :, 0:1], ALU.subtract, [kF[6], kF[0]], ["rC_st"])
                            p.tt('dve', li, cs, bc(st[:, :, 0], [128, NCH, CH], 2) if False else st[:, :, 0:1].broadcast_to([128, NCH, CH]),
                                 ALU.subtract, [kF[6], "rC_st"], [kF[7]])
                            p.tt('dve', F[6][:], F[7][:], F[0][:], ALU.subtract, [kF[7], kF[0]], [kF[6]])
                            last = CH - 1
                        else:
                            p.copy('dve', st[:, :, 0:1], cs[:, :, CH - 1:CH], [kF[6]], ["rC_st"])
                            p.tt('dve', cs, st[:, :, 0:1].broadcast_to([128, NCH, CH]), cs, ALU.subtract, [kF[6], "rC_st"], [kF[6]])
                            p.tt('dve', F[7][:], F[6][:], F[0][:], ALU.add, [kF[6], kF[0]], [kF[7]])
                            last = 0
                        p.act(F[6][:], F[6][:], AF.Exp, [kF[6]], [kF[6]])
                        p.tt('dve', AR[d][:, :, 0, :], F[2][:].rearrange(cs3, q=CH), F[6][:].rearrange(cs3, q=CH), ALU.mult,
                             [kF[2], kF[6]], [f"rC_AR{d}"])
                        p.act(F[0][:], F[7][:], AF.Exp, [kF[7]], [kF[0]])
                        p.tt('dve', AR[d][:, :, 1, :], F[5][:].rearrange(cs3, q=CH), F[0][:].rearrange(cs3, q=CH), ALU.mult,
                             [kF[5], kF[0]], [f"rC_AR{d}"])
                        p.copy('dve', PC[d][:].unsqueeze(2), F[0][:].rearrange(cs3, q=CH)[:, :, last:last + 1], [kF[0]], [f"rC_PC{d}"])
                        p.act(F[6][:], F[7][:], AF.Exp, [kF[7]], [kF[6]], scale=-1.0)
                        p.tt('dve', Kt[d][:], F[3][:], F[6][:], ALU.mult, [kF[3], kF[6]], [f"rC_Kt{d}"])
                        p.tt('dve', Bt[d][:], F[4][:], F[6][:], ALU.mult, [kF[4], kF[6]], [f"rC_Bt{d}"])
                        p.tt('dve', F[0][:].rearrange(cs3, q=CH), li[:, :, last:last + 1].broadcast_to([128, NCH, CH]), li, ALU.subtract,
                             [kF[7]], [kF[0]])
                        p.act(F[0][:], F[0][:], AF.Exp, [kF[0]], [kF[0]])
                        p.tt('dve', Kd[:], F[3][:], F[0][:], ALU.mult, [kF[3], kF[0]], ["rC_Kd"])
                        p.tt('dve', Bd[:], F[4][:], F[0][:], ALU.mult, [kF[4], kF[0]], ["rC_Bd"])
                        to_stacked(Kd, "rC_Kd", Kdtok[d], f"rC_Kdtok{d}")
                        to_stacked(Bd, "rC_Bd", Bdtok[d], f"rC_Bdtok{d}")
                    for step in range(NCH):
                        first = step == 0
                        lastc = step == NCH - 1
                        chs = [step, NCH - 1 - step]
                        for u_ in units:
                            d, kk_ = u_['d'], u_['k']
                            ch = chs[d]
                            tsl = slice(ch * CH, (ch + 1) * CH)
                            pX = u_['pX']
                            for hs in range(2):
                                sl_ = slice(hs * 64, (hs + 1) * 64)
                                p.mm64(pX[sl_, 0:128], Kt[d][sl_, tsl], AR[d][sl_, ch, :, :].rearrange("p a q -> p (a q)"), True, True,
                                     [f"rC_Kt{d}", f"rC_AR{d}"], [kk_ + "pX"])
                                p.mm64(pX[sl_, 128:256], Bt[d][sl_, tsl], AR[d][sl_, ch, :, :].rearrange("p a q -> p (a q)"), True, True,
                                     [f"rC_Bt{d}", f"rC_AR{d}"], [kk_ + "pX"])
                            for hs in range(2):
                                sl_ = slice(hs * 64, (hs + 1) * 64)
                                cs_ = slice(hs * 64, (hs + 1) * 64)
                                p.tt('dve', u_['AkT'][sl_, cs_], pX[sl_, 0:64], mS[d][sl_, :], ALU.mult, [kk_ + "pX", "r_c"], [kk_ + "AkT"])
                                p.tt('dve', u_['RkT'][sl_, cs_], pX[sl_, 64:128], mI[d][sl_, :], ALU.mult, [kk_ + "pX", "r_c"], [kk_ + "RkT"])
                                p.tt('dve', u_['Mt'][0][sl_, cs_], pX[sl_, 128:192], mSn[d][sl_, :], ALU.mult, [kk_ + "pX", "r_c"], [kk_ + "Mt0"])
                                p.tt('dve', u_['RbT'][sl_, cs_], pX[sl_, 192:256], mI[d][sl_, :], ALU.mult, [kk_ + "pX", "r_c"], [kk_ + "RbT"])
                        neumann_inverse_T(c, units, 5)
                        for u_ in units:
                            d, kk_ = u_['d'], u_['k']
                            ch = chs[d]
                            pX = u_['pX']
                            if not first:
                                for hs in range(2):
                                    sl_ = slice(hs * 64, (hs + 1) * 64)
                                    p.mm64(pX[sl_, 256:320], AR[d][sl_, ch, 0, :], u_['Sb'][sl_, :], True, False, [f"rC_AR{d}", kk_ + "Sb"], [kk_ + "pX"])
                            p.mm(pX[:, 256:320], u_['AkT'][:], Vp[:, ch, :], first, True, [kk_ + "AkT", "rC_Vp"], [kk_ + "pX"])
                            p.copy('act', u_['R1'][:], pX[:, 256:320], [kk_ + "pX"], [kk_ + "R1"])
                        for u_ in units:
                            d, kk_ = u_['d'], u_['k']
                            pX = u_['pX']
                            p.mm(pX[:, 320:384], u_['XT'][:], u_['R1'][:], True, True, [kk_ + "XT", kk_ + "R1"], [kk_ + "pX"])
                            p.op('act', lambda e, u_=u_, pX=pX: e.activation(u_['NU'][:], pX[:, 320:384], AF.Copy, scale=-1.0), [kk_ + "pX"], [kk_ + "NU"])
                        for u_ in units:
                            d, kk_ = u_['d'], u_['k']
                            ch = chs[d]
                            pX = u_['pX']
                            if not first:
                                for hs in range(2):
                                    sl_ = slice(hs * 64, (hs + 1) * 64)
                                    p.mm64(pX[sl_, 384:448], AR[d][sl_, ch, 1, :], u_['Sb'][sl_, :], True, False, [f"rC_AR{d}", kk_ + "Sb"], [kk_ + "pX"])
                            p.mm(pX[:, 384:448], u_['RkT'][:], Vp[:, ch, :], first, False, [kk_ + "RkT", "rC_Vp"], [kk_ + "pX"])
                            p.mm(pX[:, 384:448], u_['RbT'][:], u_['NU'][:], False, True, [kk_ + "RbT", kk_ + "NU"], [kk_ + "pX"])
                            okey = ("rC_O", ch)
                            first_visit = (d == 0 and ch < NCH // 2) or (d == 1 and ch >= NCH // 2)
                            if first_visit:
                                p.copy('dve', Ost[:, ch, :], pX[:, 384:448], [kk_ + "pX"], [okey])
                            else:
                                p.tt('dve', Ost[:, ch, :], Ost[:, ch, :], pX[:, 384:448], ALU.add, [kk_ + "pX", okey], [okey])
                            if not lastc:
                                for hs in range(2):
                                    sl_ = slice(hs * 64, (hs + 1) * 64)
                                    p.mm64(pX[sl_, 448:512], Kdtok[d][sl_, ch, :], Vp[sl_, ch, :], True, False, [f"rC_Kdtok{d}", "rC_Vp"], [kk_ + "pX"])
                                    p.mm64(pX[sl_, 448:512], Bdtok[d][sl_, ch, :], u_['NU'][sl_, :], False, True, [f"rC_Bdtok{d}", kk_ + "NU"], [kk_ + "pX"])
                                if first:
                                    p.copy('dve', u_['S'][:], pX[:, 448:512], [kk_ + "pX"], [kk_ + "S"])
                                else:
                                    p.stt(u_['S'][:], u_['S'][:], PC[d][:, ch:ch + 1], pX[:, 448:512], ALU.mult, ALU.add,
                                          [kk_ + "S", f"rC_PC{d}", kk_ + "pX"], [kk_ + "S"])
                                p.copy('act', u_['Sb'][:], u_['S'][:], [kk_ + "S"], [kk_ + "Sb"])
                    okeys = [("rC_O", ch) for ch in range(NCH)]
                    p.op('dve', lambda e: e.reduce_sum(st[:, :, 0], Ost[:], AX.X), okeys, ["rC_st"])
                    p.ts('dve', st[:, :, 0], st[:, :, 0], 1.0 / CH, None, ALU.mult, None, ["rC_st"], ["rC_st"])
                    p.tt('dve', Ost[:], Ost[:], st[:, :, 0:1].broadcast_to([128, NCH, CH]), ALU.subtract, okeys + ["rC_st"], ["rC_Oc"])
                    F6v = F[6][:].rearrange(cs3, q=CH)
                    p.act(F6v, Ost[:], AF.Square, ["rC_Oc"], [kF[6]])
                    p.op('dve', lambda e: e.reduce_sum(st[:, :, 1], F6v, AX.X), [kF[6]], ["rC_st"])
                    p.ts('dve', st[:, :, 1], st[:, :, 1], 1.0 / CH, 64e-5, ALU.mult, ALU.add, ["rC_st"], ["rC_st"])
                    p.tt('pool', st[:, :, 2], st[:, :, 1], c.mhalf[:, 0:NCH], ALU.pow, ["rC_st"], ["rC_st"])
                    p.tt('dve', Obf[:], Ost[:], st[:, :, 2:3].broadcast_to([128, NCH, CH]), ALU.mult, ["rC_Oc", "rC_st"], ["rC_Obf"])
                    ld(0, 4)
                    for c8 in range(NCH // 8):
                        for hs in range(2):
                            sl_ = slice(hs * 64, (hs + 1) * 64)
                            for q in range(8):
                                ch = c8 * 8 + q
                                p.mm64(pA.tiles[1][sl_, q * 64:(q + 1) * 64], Obf[sl_, ch, :], c.identb[sl_, sl_], True, True, ["rC_Obf"], ["rC_pA1"])
                        fs = slice(c8 * 512, (c8 + 1) * 512)
                        p.ts('dve', F[6][:, fs], pA.tiles[1][:, :], lnw[:, hp:hp + 1], lnb[:, hp:hp + 1], ALU.mult, ALU.add, ["rC_pA1", "r_c"], [kF[6]])
                    p.tt('dve', F[6][:], F[6][:], F[1][:], ALU.add, [kF[6], kF[1]], [kF[6]])
                    p.tt('dve', yT[:, hp, :], F[6][:], F[0][:], ALU.mult, [kF[6], kF[0]], [("rC_yT", hp)])

                p.barrier()
                es3.close()
                wl = WLoader(c, es2, "rD_wst", 1024)
                wo = sb(c, es2, "rD_wo", [128, KC, D], BF16)
                xr = ring_sb(c, es2, "rD_x", 1, [128, KC, 512], F32)
                xo = sb(c, es2, "rD_xo", [128, KC, 512], F32)
                for kc in range(KC):
                    wl.load(wo[:, kc, :], w_out[kc * 128:(kc + 1) * 128, :], "rD_wo")
                for gi in range(L // 512):
                    tok0 = s * L + gi * 512
                    xi, kx = xr.next()
                    p.dma('sp', xi[:], xt_view(c, tok0, 512), xt_keys(tok0, 512), [kx])
                    for dc in range(KC):
                        pa, kpa = pA.next()
                        for kc in range(KC):
                            p.mm(pa[:], wo[:, kc, dc * 128:(dc + 1) * 128], yT[:, kc, gi * 512:(gi + 1) * 512], kc == 0, kc == KC - 1,
                                 ["rD_wo"] + [("rC_yT", kc)], [kpa])
                        p.tt('dve', xo[:, dc, :], pa[:], xi[:, dc, :], ALU.add, [kpa, kx], ["rD_xo"])
                    p.dma('sp', xt_view(c, tok0, 512), xo[:], ["rD_xo"], xt_keys(tok0, 512))
            p.barrier()


def build(nseq_prompt=1, nseq_sample=4, cfg=None):
    cfg = cfg or {}
    depth = cfg.get('depth', 4)
    nseq = nseq_prompt + nseq_sample
    ntok = nseq * L
    nc = bass.Bass("TRN2", target_bir_lowering=False)
    c = Ctx()
    c.nc = nc
    c.cfg = cfg
    din = {}

    def inp(name, shape):
        din[name] = nc.dram_tensor(name, list(shape), F32, kind="ExternalInput").ap()
        return din[name]

    c.xp = inp("x_prompt", [max(nseq_prompt, 1), L, D])
    c.xs = inp("x_sample", [max(nseq_sample, 1), L, D])
    W = {}
    for name, shape in WEIGHT_SHAPES.items():
        W[name] = inp(name, shape)
    c.W = W
    c_ident = inp("c_ident", [128, 128])
    c.c_cos = inp("c_cos", [128, L])
    c.c_sin = inp("c_sin", [128, L])
    c.yp = nc.dram_tensor("y_prompt", [max(nseq_prompt, 1), L, D], F32, kind="ExternalOutput").ap()
    c.ys = nc.dram_tensor("y_sample", [max(nseq_sample, 1), L, D], F32, kind="ExternalOutput").ap()
    c.XT = nc.dram_tensor("XT", [D, ntok], F32).ap()
    c.Zs = nc.dram_tensor("Zs", [L, 2048], F32).ap()
    c.Ys = nc.dram_tensor("Ys", [L, 2048], F32).ap()
    c.RW = nc.dram_tensor("RW", [7, D, L], F32).ap()

    with contextlib.ExitStack() as es:
        p = Prog(nc, es)
        c.p = p
        c.ident = sb(c, es, "ident", [128, 128], F32)
        c.ones = sb(c, es, "ones", [128, 128], F32)
        c.mhalf = sb(c, es, "mhalf", [128, 512], F32)
        c.identb = sb(c, es, "identb", [128, 128], BF16)
        p.dma('sp', c.ident[:], c_ident[:, :], [], ["ident"])
        p.copy('dve', c.identb[:], c.ident[:], ["ident"], ["identb"])
        p.memset('pool', c.ones[:], 1.0, ["ones"])
        p.memset('pool', c.mhalf[:], -0.5, ["mhalf"])
        make_masks(c, es)
        p.barrier()

        srcs = [(c.xp, b) for b in range(nseq_prompt)] + [(c.xs, b) for b in range(nseq_sample)]
        dsts = [(c.yp, b) for b in range(nseq_prompt)] + [(c.ys, b) for b in range(nseq_sample)]
        stage_in(c, srcs)
        for i in range(depth):
            if cfg.get('ffn', True) and cfg.get('ffn1', True):
                stage_ffn(c, W['ffn1_norm'][i], W['ffn1_w_gu'][i], W['ffn1_w_down'][i], ntok)
            if i in cfg.get('mixers', [0, 1, 2, 3]):
                m, jj = i % 4, i // 4
                if m == 0:
                    stage_ssd(c, i, jj, nseq)
                if m == 1:
                    stage_gdn(c, i, jj, nseq)
                if m == 2:
                    stage_attn(c, i, jj, nseq)
                if m == 3:
                    stage_rwkv(c, i, jj, nseq)
            if cfg.get('ffn', True) and cfg.get('ffn2', True):
                stage_ffn(c, W['ffn2_norm'][i], W['ffn2_w_gu'][i], W['ffn2_w_down'][i], ntok)
        stage_out(c, dsts, W['final_norm'])
        p.emit()
    return nc


WEIGHT_SHAPES = {
    'ffn1_norm': (4, 1024), 'ffn1_w_gu': (4, 1024, 5632), 'ffn1_w_down': (4, 2816, 1024),
    'mix_norm': (4, 1024), 'ffn2_norm': (4, 1024), 'ffn2_w_gu': (4, 1024, 5632), 'ffn2_w_down': (4, 2816, 1024),
    'ssd_w_in': (1, 1024, 6208), 'ssd_conv_w': (1, 5, 4096), 'ssd_conv_b': (1, 4096), 'ssd_a_log': (1, 2, 32),
    'ssd_dt_bias': (1, 2, 32), 'ssd_d': (1, 32), 'ssd_norm': (1, 2048), 'ssd_w_out': (1, 2048, 1024),
    'gdn_w_in': (1, 1024, 6192), 'gdn_conv_w': (1, 5, 4096), 'gdn_conv_b': (1, 4096), 'gdn_a_log': (1, 2, 16),
    'gdn_dt_bias': (1, 2, 16), 'gdn_norm': (1, 128), 'gdn_w_out': (1, 2048, 1024),
    'att_w_qkv': (1, 1024, 1536), 'att_sinks': (1, 16), 'att_w_out': (1, 1024, 1024),
    'rwkv_x_mu': (1, 6, 1024), 'rwkv_w_rkv': (1, 3, 1024, 1024), 'rwkv_w0': (1, 2, 1024),
    'rwkv_w1': (1, 2, 1024, 64), 'rwkv_w2': (1, 2, 64, 1024), 'rwkv_a0': (1, 1024), 'rwkv_a1': (1, 1024, 64),
    'rwkv_a2': (1, 64, 1024), 'rwkv_g1': (1, 1024, 128), 'rwkv_g2': (1, 128, 1024), 'rwkv_k_k': (1, 1024),
    'rwkv_k_a': (1, 1024), 'rwkv_r_k': (1, 1024), 'rwkv_lnx_w': (1, 1024), 'rwkv_lnx_b': (1, 1024),
    'rwkv_w_out': (1, 1024, 1024), 'final_norm': (1024,),
}


def consts():
    r = np.arange(128) % 64
    i = (r % 32).astype(np.float64)
    inv_freq = 10000.0 ** (-i / 32.0)
    ang = (np.arange(L, dtype=np.float64)[None, :] * inv_freq[:, None]).astype(np.float32).astype(np.float64)
    sgn = np.where(r < 32, -1.0, 1.0)[:, None]
    return {"c_ident": np.eye(128, dtype=np.float32),
            "c_cos": np.cos(ang).astype(np.float32),
            "c_sin": (np.sin(ang) * sgn).astype(np.float32)}


def kernel(**inputs):
    nc = build(1, 4)
    xp = np.ascontiguousarray(inputs['x_prompt'], dtype=np.float32)
    xs = np.ascontiguousarray(inputs['x_sample'], dtype=np.float32)
    shared = {k: np.ascontiguousarray(inputs[k], dtype=np.float32) for k in WEIGHT_SHAPES}
    shared.update(consts())
    in_maps = []
    for c in range(NCORES):
        m = dict(shared)
        m['x_prompt'] = xp[c:c + 1]
        m['x_sample'] = xs[4 * c:4 * c + 4]
        in_maps.append(m)
    res = run_bass_kernel_spmd(nc, in_maps, core_ids=list(range(NCORES)))
    yp = np.concatenate([r['y_prompt'] for r in res.results], axis=0)
    ys = np.concatenate([r['y_sample'] for r in res.results], axis=0)
    return yp.astype(np.float32), ys.astype(np.float32)
```
